# Optimizing a Trainium2 kernel written in Bass

```python
import jax, jax.numpy as jnp
from jax import lax
import numpy as np

D_MODEL = 1024
BATCH = 2
SEQ = 16384
DEPTH = 2
DEC_BATCH = 16
DEC_SEQ = 32
PAST_LEN = 4096

CHUNK = 64
N_MIXERS = 2
N_LRU_LAYERS = (DEPTH + 1) // 2
N_FOX_LAYERS = DEPTH // 2
LRU_WIDTH = 3 * D_MODEL // 2
LRU_BLOCKS = 12
LRU_BLOCK_W = LRU_WIDTH // LRU_BLOCKS
CONV_W = 4
LRU_C = 8.0
FOX_HEADS = 16
FOX_HEAD_DIM = D_MODEL // FOX_HEADS
FOX_WIDTH = FOX_HEADS * FOX_HEAD_DIM
FOX_SCALE = FOX_HEAD_DIM ** -0.5
Q_BLOCK = 128
EPS = 1e-6

kernel_name = "hybrid_rglru_fox_stream_step"


def _rmsnorm(x, g):
    xf = x.astype(jnp.float32)
    y = xf * lax.rsqrt(jnp.mean(xf * xf, axis=-1, keepdims=True) + EPS)
    return (y * g.astype(jnp.float32)).astype(x.dtype)


def _adaln(c, w, b):
    mod = jax.nn.silu(c) @ w + b
    shift, scale, gate = jnp.split(mod[:, None, :], 3, axis=-1)
    return shift, scale, gate


def _lru_combine(e1, e2):
    a1, b1 = e1
    a2, b2 = e2
    return a1 * a2, a2 * b1 + b2


def _rglru_branch(h, conv_buf, h0, w_in, conv_w, conv_b, w_a, b_a, w_x, b_x, lam, w_out):
    B, T, _ = h.shape
    xb, gate = jnp.split(h @ w_in, 2, axis=-1)
    xp = jnp.concatenate([conv_buf.astype(xb.dtype), xb], axis=1)
    xc = conv_b + xp[:, 0:T] * conv_w[0]
    for k in range(1, CONV_W):
        xc = xc + xp[:, k:k + T] * conv_w[k]
    xblk = xc.reshape(B, T, LRU_BLOCKS, LRU_BLOCK_W)
    r = jax.nn.sigmoid(jnp.einsum("btnd,nde->btne", xblk, w_a).reshape(B, T, LRU_WIDTH) + b_a)
    ig = jax.nn.sigmoid(jnp.einsum("btnd,nde->btne", xblk, w_x).reshape(B, T, LRU_WIDTH) + b_x)
    log_a = -LRU_C * r.astype(jnp.float32) * jax.nn.softplus(-lam.astype(jnp.float32))
    a = jnp.exp(log_a)
    u = jnp.sqrt(-jnp.expm1(2.0 * log_a)) * (ig * xc).astype(jnp.float32)
    a_cum, u_cum = lax.associative_scan(_lru_combine, (a, u), axis=1)
    hs = a_cum * h0[:, None, :].astype(jnp.float32) + u_cum
    y = (hs.astype(h.dtype) * jax.nn.silu(gate)) @ w_out
    return y, xp[:, T:], hs[:, -1]


def _fox_attend_block(q, cq, qpos, k, v, ck, kpos):
    s = jnp.einsum("bqhd,bkhd->bhqk", q, k).astype(jnp.float32) * FOX_SCALE
    s = s + jnp.swapaxes(cq, 1, 2)[..., :, None] - jnp.swapaxes(ck, 1, 2)[..., None, :]
    s = jnp.where(kpos[None, None, None, :] <= qpos[None, None, :, None], s, -jnp.inf)
    p = jax.nn.softmax(s, axis=-1)
    return jnp.einsum("bhqk,bkhd->bqhd", p.astype(v.dtype), v)


def _fox_branch(h, past_k, past_v, past_lf, w_in, b_f, w_out):
    B, T, _ = h.shape
    P = past_k.shape[1]
    W = FOX_WIDTH
    q, k, v, g, fl = jnp.split(h @ w_in, [W, 2 * W, 3 * W, 4 * W], axis=-1)
    q = q.reshape(B, T, FOX_HEADS, FOX_HEAD_DIM)
    k = k.reshape(B, T, FOX_HEADS, FOX_HEAD_DIM)
    v = v.reshape(B, T, FOX_HEADS, FOX_HEAD_DIM)
    log_f = jax.nn.log_sigmoid(fl.astype(jnp.float32) + b_f.astype(jnp.float32))
    k_all = jnp.concatenate([past_k.astype(k.dtype), k], axis=1)
    v_all = jnp.concatenate([past_v.astype(v.dtype), v], axis=1)
    cum = jnp.cumsum(jnp.concatenate([past_lf.astype(jnp.float32), log_f], axis=1), axis=1)
    kpos = jnp.arange(P + T)
    qb = Q_BLOCK if T % Q_BLOCK == 0 else T
    nb = T // qb
    qs = jnp.swapaxes(q.reshape(B, nb, qb, FOX_HEADS, FOX_HEAD_DIM), 0, 1)
    cqs = jnp.swapaxes(cum[:, P:].reshape(B, nb, qb, FOX_HEADS), 0, 1)
    qposs = (P + jnp.arange(T)).reshape(nb, qb)
    o = lax.map(lambda blk: _fox_attend_block(blk[0], blk[1], blk[2], k_all, v_all, cum, kpos),
                (qs, cqs, qposs))
    o = jnp.swapaxes(o, 0, 1).reshape(B, T, W)
    y = (o * jax.nn.silu(g)) @ w_out
    return y, k, v, log_f


def setup_inputs(seed: int = 0) -> dict:
    key = jax.random.key(seed)
    ks = jax.random.split(key, 32)
    f32 = jnp.float32
    R, W, H, dh = LRU_WIDTH, FOX_WIDTH, FOX_HEADS, FOX_HEAD_DIM
    NA, NF = N_LRU_LAYERS, N_FOX_LAYERS

    def nrm(k, shape, s):
        return s * jax.random.normal(k, shape, f32)

    a0 = jax.random.uniform(ks[20], (NA, R), f32, 0.9, 0.999) ** (1.0 / LRU_C)
    lru_lambda = jnp.log(a0) - jnp.log1p(-a0)
    return {
        "x_prompt": nrm(ks[0], (BATCH, SEQ, D_MODEL), 1.0),
        "x_sample": nrm(ks[1], (DEC_BATCH, DEC_SEQ, D_MODEL), 1.0),
        "c_prompt": nrm(ks[2], (BATCH, D_MODEL), 1.0),
        "c_sample": nrm(ks[3], (DEC_BATCH, D_MODEL), 1.0),
        "state_lru_h": nrm(ks[4], (NA, DEC_BATCH, R), 0.5),
        "state_lru_conv": nrm(ks[5], (NA, DEC_BATCH, CONV_W - 1, R), 1.0),
        "cache_fox_k": nrm(ks[6], (NF, DEC_BATCH, PAST_LEN, H, dh), 1.0),
        "cache_fox_v": nrm(ks[7], (NF, DEC_BATCH, PAST_LEN, H, dh), 1.0),
        "cache_fox_logf": jax.nn.log_sigmoid(3.0 + jax.random.normal(ks[8], (NF, DEC_BATCH, PAST_LEN, H), f32)),
        "norm_pre": 1.0 + nrm(ks[9], (DEPTH, D_MODEL), 0.05),
        "norm_post": 1.0 + nrm(ks[10], (DEPTH, D_MODEL), 0.05),
        "ada_w": nrm(ks[11], (DEPTH, D_MODEL, 3 * D_MODEL), 0.5 * D_MODEL ** -0.5),
        "ada_b": nrm(ks[12], (DEPTH, 3 * D_MODEL), 0.01),
        "lru_w_in": nrm(ks[13], (NA, D_MODEL, 2 * R), D_MODEL ** -0.5),
        "lru_conv_w": nrm(ks[14], (NA, CONV_W, R), CONV_W ** -0.5),
        "lru_conv_b": nrm(ks[15], (NA, R), 0.01),
        "lru_w_a": nrm(ks[16], (NA, LRU_BLOCKS, LRU_BLOCK_W, LRU_BLOCK_W), LRU_BLOCK_W ** -0.5),
        "lru_b_a": nrm(ks[17], (NA, R), 0.01),
        "lru_w_x": nrm(ks[18], (NA, LRU_BLOCKS, LRU_BLOCK_W, LRU_BLOCK_W), LRU_BLOCK_W ** -0.5),
        "lru_b_x": nrm(ks[19], (NA, R), 0.01),
        "lru_lambda": lru_lambda,
        "lru_w_out": nrm(ks[21], (NA, R, D_MODEL), R ** -0.5),
        "fox_w_in": nrm(ks[22], (NF, D_MODEL, 4 * W + H), D_MODEL ** -0.5),
        "fox_b_f": jax.random.uniform(ks[23], (NF, H), f32, 2.0, 5.0),
        "fox_w_out": nrm(ks[24], (NF, W, D_MODEL), W ** -0.5),
    }


def reference(x_prompt, x_sample, c_prompt, c_sample, state_lru_h, state_lru_conv,
              cache_fox_k, cache_fox_v, cache_fox_logf, norm_pre, norm_post, ada_w, ada_b,
              lru_w_in, lru_conv_w, lru_conv_b, lru_w_a, lru_b_a, lru_w_x, lru_b_x,
              lru_lambda, lru_w_out, fox_w_in, fox_b_f, fox_w_out):
    B = x_prompt.shape[0]
    xp, xs = x_prompt, x_sample
    lru_h_p, lru_c_p, lru_h_s, lru_c_s = [], [], [], []
    fk_p, fv_p, ff_p, fk_s, fv_s, ff_s = [], [], [], [], [], []
    for i in range(DEPTH):
        j = i // N_MIXERS
        sh_p, sc_p, gt_p = _adaln(c_prompt, ada_w[i], ada_b[i])
        sh_s, sc_s, gt_s = _adaln(c_sample, ada_w[i], ada_b[i])
        hp = _rmsnorm(xp, norm_pre[i]) * (1.0 + sc_p) + sh_p
        hs = _rmsnorm(xs, norm_pre[i]) * (1.0 + sc_s) + sh_s
        if i % N_MIXERS == 0:
            lw = (lru_w_in[j], lru_conv_w[j], lru_conv_b[j], lru_w_a[j], lru_b_a[j],
                  lru_w_x[j], lru_b_x[j], lru_lambda[j], lru_w_out[j])
            yp, cbp, hlp = _rglru_branch(hp, jnp.zeros((B, CONV_W - 1, LRU_WIDTH), hp.dtype),
                                         jnp.zeros((B, LRU_WIDTH), jnp.float32), *lw)
            ys, cbs, hls = _rglru_branch(hs, state_lru_conv[j], state_lru_h[j], *lw)
            lru_h_p.append(hlp)
            lru_c_p.append(cbp)
            lru_h_s.append(hls)
            lru_c_s.append(cbs)
        else:
            fw = (fox_w_in[j], fox_b_f[j], fox_w_out[j])
            yp, kp, vp, lfp = _fox_branch(
                hp, jnp.zeros((B, 0, FOX_HEADS, FOX_HEAD_DIM), hp.dtype),
                jnp.zeros((B, 0, FOX_HEADS, FOX_HEAD_DIM), hp.dtype),
                jnp.zeros((B, 0, FOX_HEADS), jnp.float32), *fw)
            ys, ksn, vsn, lfs = _fox_branch(hs, cache_fox_k[j], cache_fox_v[j], cache_fox_logf[j], *fw)
            fk_p.append(kp)
            fv_p.append(vp)
            ff_p.append(lfp)
            fk_s.append(ksn)
            fv_s.append(vsn)
            ff_s.append(lfs)
        xp = xp + gt_p * _rmsnorm(yp, norm_post[i])
        xs = xs + gt_s * _rmsnorm(ys, norm_post[i])
    return (xp, xs,
            jnp.stack(lru_h_p), jnp.stack(lru_c_p), jnp.stack(fk_p), jnp.stack(fv_p), jnp.stack(ff_p),
            jnp.stack(lru_h_s), jnp.stack(lru_c_s), jnp.stack(fk_s), jnp.stack(fv_s), jnp.stack(ff_s))
```

```python
import contextlib
import numpy as np
import ml_dtypes
import concourse.bass as bass
import concourse.mybir as mybir
from concourse.bass_utils import run_bass_kernel_spmd

F32 = mybir.dt.float32
BF16 = mybir.dt.bfloat16
AF = mybir.ActivationFunctionType
ALU = mybir.AluOpType

D = 1024
R = 1536
NCH = 12
H = 16
DH = 64
T = 16384
TT = 512
NT = T // TT
ST = 32
PAST = 4096
EPS = 1e-6
NEG = -30000.0
VS = 66
KA = 70
COMPUTE = ("pe", "act", "dve", "pool", "sp")


class Buf:
    __slots__ = ("name", "last_w", "readers")

    def __init__(self, name):
        self.name = name
        self.last_w = None
        self.readers = []


class Op:
    __slots__ = ("eng", "fn", "deps", "is_dma", "idx", "need_inc", "val", "sem", "semval", "prev_on_sem", "inc")

    def __init__(self, eng, fn, is_dma, inc=16):
        self.eng = eng
        self.fn = fn
        self.deps = set()
        self.is_dma = is_dma
        self.need_inc = False
        self.val = None
        self.sem = None
        self.semval = None
        self.prev_on_sem = None
        self.inc = inc


class Prog:
    def __init__(self, nc, n_dma_sems=(("sp", 32), ("act", 12), ("pool", 12))):
        self.nc = nc
        self.ops = []
        self.by_eng = {e: [] for e in COMPUTE}
        self.dma_ring = {e: n for e, n in n_dma_sems}
        self.dma_count = {e: 0 for e, _ in n_dma_sems}
        self.dma_last_on_slot = {}
        self.all_bufs = []

    def buf(self, name=""):
        b = Buf(name)
        self.all_bufs.append(b)
        return b

    def _add(self, op, reads, writes):
        op.idx = len(self.ops)
        for b in reads:
            if b.last_w is not None:
                op.deps.add(b.last_w)
        for b in writes:
            if b.last_w is not None:
                op.deps.add(b.last_w)
            for r in b.readers:
                op.deps.add(r)
        for b in reads:
            b.readers.append(op)
        for b in writes:
            b.last_w = op
            b.readers = []
        op.deps.discard(op)
        self.ops.append(op)
        self.by_eng[op.eng].append(op)
        return op

    def op(self, eng, fn, reads=(), writes=()):
        return self._add(Op(eng, fn, False), reads, writes)

    def dma(self, eng, fn, reads=(), writes=(), inc=16):
        op = Op(eng, fn, True, inc)
        n = self.dma_count[eng]
        self.dma_count[eng] = n + 1
        slot = (eng, n % self.dma_ring[eng])
        op.sem = slot
        op.prev_on_sem = self.dma_last_on_slot.get(slot)
        op.semval = (op.prev_on_sem.semval if op.prev_on_sem else 0) + inc
        self.dma_last_on_slot[slot] = op
        return self._add(op, reads, writes)

    def barrier(self):
        lasts = [self.by_eng[e][-1] for e in COMPUTE if self.by_eng[e]]
        lasts += list(self.dma_last_on_slot.values())
        for e in COMPUTE:
            o = Op(e, None, False)
            o.idx = len(self.ops)
            o.deps = set(lasts)
            self.ops.append(o)
            self.by_eng[e].append(o)
        for b in self.all_bufs:
            if str(b.name).startswith("dram_"):
                continue
            b.last_w = None
            b.readers = []

    def emit(self, final_wait_eng="sp"):
        nc = self.nc
        lasts = [self.by_eng[e][-1] for e in COMPUTE if self.by_eng[e]]
        lasts += list(self.dma_last_on_slot.values())
        fin = Op(final_wait_eng, None, False)
        fin.idx = len(self.ops)
        fin.deps = set(lasts)
        self.ops.append(fin)
        self.by_eng[final_wait_eng].append(fin)
        for o in self.ops:
            for d in o.deps:
                if not d.is_dma:
                    if d.eng == o.eng and o.eng == "pe":
                        continue
                    d.need_inc = True
        for e in COMPUTE:
            c = 0
            for o in self.by_eng[e]:
                if not o.is_dma and o.need_inc:
                    c += 1
                    o.val = c
        with contextlib.ExitStack() as st:
            esem = {e: st.enter_context(nc.semaphore("s_" + e)) for e in COMPUTE}
            dsem = {}
            for e, n in self.dma_ring.items():
                for i in range(n):
                    dsem[(e, i)] = st.enter_context(nc.semaphore("d_%s_%d" % (e, i)))
            block = st.enter_context(nc.Block())
            engobj = {"pe": nc.tensor, "act": nc.scalar, "dve": nc.vector, "pool": nc.gpsimd, "sp": nc.sync}

            def run_engine(e):
                eng = engobj[e]
                waited = {}

                def wait(key, sem, val):
                    if waited.get(key, 0) >= val:
                        return
                    waited[key] = val
                    eng.wait_ge(sem, val)

                for o in self.by_eng[e]:
                    for d in sorted(o.deps, key=lambda d: d.idx):
                        if d.is_dma:
                            wait(d.sem, dsem[d.sem], d.semval)
                        else:
                            if d.eng == e and e == "pe":
                                continue
                            if d.val is None:
                                continue
                            wait(d.eng, esem[d.eng], d.val)
                    if o.is_dma and o.prev_on_sem is not None:
                        wait(o.sem, dsem[o.sem], o.prev_on_sem.semval)
                    if o.fn is None:
                        if o.need_inc:
                            eng.nop().then_inc(esem[e], 1)
                        continue
                    ins = o.fn(eng)
                    if o.is_dma:
                        ins.then_inc(dsem[o.sem], o.inc)
                    elif o.need_inc:
                        ins.then_inc(esem[e], 1)

            @block.tensor
            def _(x):
                run_engine("pe")

            @block.scalar
            def _(x):
                run_engine("act")

            @block.vector
            def _(x):
                run_engine("dve")

            @block.gpsimd
            def _(x):
                run_engine("pool")

            @block.sync
            def _(x):
                run_engine("sp")


class Arena:
    def __init__(self, nc, base=16640, limit=229376 - 2048):
        self.nc = nc
        self.ptr = base
        self.limit = limit
        self.n = 0

    def alloc(self, shape, dtype, name="t"):
        size = int(np.prod(shape[1:])) * (4 if dtype == F32 else 2)
        size = (size + 63) // 64 * 64
        off = self.ptr
        self.ptr += size
        assert self.ptr <= self.limit, ("SBUF overflow", name, self.ptr)
        self.n += 1
        return self.nc.alloc_sbuf_tensor_at("%s_%d" % (name, self.n), list(shape), dtype, offset=off)

    def mark(self):
        return self.ptr

    def reset(self, m):
        self.ptr = m


def build(nq_tiles=None, do_attn=True, do_sample=True, nt_prompt=None, stop=9, Tn=16384, PASTn=4096, dbg=0):
    T, PAST = Tn, PASTn
    NPR = PAST // 2048
    NQ = T // 2048
    if nq_tiles is None:
        nq_tiles = NQ
    if nt_prompt is None:
        nt_prompt = T // TT
    nc = bass.Bass("TRN2", target_bir_lowering=False)
    P = Prog(nc)
    A = Arena(nc)

    def din(name, shape, dt=F32):
        return nc.dram_tensor(name, list(shape), dt, kind="ExternalInput").ap()

    def dout(name, shape, dt=F32):
        return nc.dram_tensor(name, list(shape), dt, kind="ExternalOutput").ap()

    def dscr(name, shape, dt=F32):
        return nc.dram_tensor(name, list(shape), dt).ap()

    i_xp = din("xp", [T, D])
    i_xs = din("xs", [2 * ST, D])
    i_cc = din("cc", [3, D])
    i_sth = din("st_h", [2, R])
    i_stc = din("st_conv", [2, 3, R])
    i_ck = din("ck_k", [2, PAST, D])
    i_cv = din("ck_v", [2, PAST, D])
    i_clf = din("ck_lf", [2, PAST, H])
    i_npre = din("norm_pre", [2, D])
    i_npost = din("norm_post", [2, D])
    i_adaw = din("ada_w", [2, D, 3 * D])
    i_adab = din("ada_b", [2, 3 * D])
    i_win = din("lru_w_in", [D, 2 * R])
    i_lvec = din("lru_vecs", [8, R])
    i_wa = din("lru_w_a", [NCH, 128, 128])
    i_wx = din("lru_w_x", [NCH, 128, 128])
    i_wout = din("lru_w_out", [R, D])
    i_fwin = din("fox_w_in", [D, 4 * D + H])
    i_fbf = din("fox_b_f", [1, H])
    i_fwout = din("fox_w_out", [D, D])
    i_ident = din("c_ident", [128, 128])
    i_sel = din("c_sel", [3, 2, 128])
    i_pmask = din("c_pmask", [128, 16, TT], BF16)
    i_smask = din("c_smask", [ST, ST], BF16)
    i_onehot = din("c_onehot", [128, 4])
    o_yp = dout("y_p", [NQ * TT, D])
    o_ys = dout("y_s", [2 * ST, D])
    o_lhp = dout("lru_h_p", [R])
    o_lcp = dout("lru_conv_p", [3, R])
    o_fkp = dout("fk_p", [T, D])
    o_fvp = dout("fv_p", [T, D])
    o_flp = dout("flf_p", [T, H])
    o_lhs = dout("lru_h_s", [2, R])
    o_lcs = dout("lru_conv_s", [2, 3, R])
    o_fks = dout("fk_s", [2 * ST, D])
    o_fvs = dout("fv_s", [2 * ST, D])
    o_fls = dout("flf_s", [2 * ST, H])
    s_x1 = dscr("s_x1", [T, D])
    s_kT = dscr("s_kT", [H, KA, T], BF16)
    s_vv = dscr("s_vv", [H, T // 2048, 128, 16 * VS], BF16)
    s_cum = dscr("s_cum", [H, T])
    NK_S = PAST + 128
    s_kTs = dscr("s_kTs", [2, H, KA, NK_S], BF16)
    s_vvs = dscr("s_vvs", [2, H, NPR + 1, 128, 16 * VS], BF16)

    b_sx1, b_skT, b_svv, b_scm, b_skTs, b_svvs = [P.buf("dram_%d" % i) for i in range(6)]
    PS = [nc.alloc_psum_tensor("ps%d" % i, [128, 1024], F32) for i in range(4)]
    bPS = [[P.buf("ps%d_%d" % (i, j)) for j in range(2)] for i in range(4)]

    def bank(i):
        return PS[i // 2][:, (i % 2) * 512:(i % 2) * 512 + 512], bPS[i // 2][i % 2]

    identf = A.alloc([128, 128], F32, "ident")
    b_const = P.buf("const")
    P.dma("sp", lambda e: e.dma_start(out=identf[:], in_=i_ident[:, :]), writes=[b_const])
    sel = A.alloc([3, 2, 128], F32, "sel")
    P.dma("sp", lambda e: e.dma_start(out=sel[:], in_=i_sel[:, :, :]), writes=[b_const])
    onehot = A.alloc([128, 4], F32, "onehot")
    P.dma("sp", lambda e: e.dma_start(out=onehot[:], in_=i_onehot[:, :]), writes=[b_const])
    ones_bf = A.alloc([128, 512], BF16, "ones")
    ones_f = A.alloc([128, 512], F32, "onesf")
    P.op("pool", lambda e: e.memset(ones_bf[:], 1.0), writes=[b_const])
    P.op("pool", lambda e: e.memset(ones_f[:], 1.0), writes=[b_const])
    ident_bf = A.alloc([128, 128], BF16, "identb")
    P.op("dve", lambda e: e.tensor_copy(ident_bf[:], identf[:]), reads=[b_const], writes=[b_const])
    gcol = A.alloc([128, 2, 2, 8, 3], F32, "gcol")
    gprow = A.alloc([128, 2, 2, D], F32, "gprow")
    lcol = A.alloc([128, NCH, 10], F32, "lcol")
    bfcol = A.alloc([16, 2], F32, "bfcol")
    bfrow = A.alloc([128, H], F32, "bfrow")
    b_ada = P.buf("ada")
    epsc = A.alloc([128, 1], F32, "epsc")
    onec = A.alloc([128, 1], F32, "onec")
    P.op("pool", lambda e: e.memset(epsc[:], EPS), writes=[b_const])
    P.op("pool", lambda e: e.memset(onec[:], 1.0), writes=[b_const])
    xs_t = A.alloc([128, 1, D], F32, "xs_t")
    b_xs = P.buf("xs")
    scum = A.alloc([16, 2 * ST], F32, "scum")
    b_scum = P.buf("scum")

    cx = {}

    def nb(*names):
        for n in names:
            cx["b_" + n] = P.buf(n)

    def front(N, nsub, Pt, layer, seqcols, xsrc, b_x):
        xn, junk, stat, hT = cx["xn"], cx["junk"], cx["stat"], cx["hT"]
        b_xn, b_junk, b_stat, b_hT = cx["b_xn"], cx["b_junk"], cx["b_stat"], cx["b_hT"]
        bxs = b_x if isinstance(b_x, list) else [b_x] * nsub
        for half in range(2):
            for s in range(nsub):
                b_x = bxs[s]
                if half == 0:
                    P.op("dve", lambda e, s=s: e.scalar_tensor_tensor(out=junk[0:Pt, :], in0=xsrc[0:Pt, s, :], scalar=1.0, in1=xsrc[0:Pt, s, :], op0=ALU.mult, op1=ALU.mult, accum_out=stat[0:Pt, s:s + 1]),
                         reads=[b_x], writes=[b_junk, b_stat])
                    P.op("act", lambda e, s=s: e.activation(out=stat[0:Pt, 4 + s:5 + s], in_=stat[0:Pt, s:s + 1], func=AF.Sqrt, scale=1.0 / D, bias=epsc[0:Pt, 0:1]), reads=[b_stat, b_const], writes=[b_stat])
                    P.op("dve", lambda e, s=s: e.reciprocal(stat[0:Pt, 8 + s:9 + s], stat[0:Pt, 4 + s:5 + s]), reads=[b_stat], writes=[b_stat])
                P.op("dve", lambda e, s=s: e.tensor_scalar(out=xn[0:Pt, :], in0=xsrc[0:Pt, s, :], scalar1=stat[0:Pt, 8 + s:9 + s], scalar2=None, op0=ALU.mult), reads=[b_x, b_stat], writes=[b_xn])

                def trs(e, s=s, half=half):
                    ins = None
                    for j in range(4):
                        kc = half * 4 + j
                        ins = e.transpose(out=bank(j)[0][:, s * Pt:(s + 1) * Pt], in_=xn[0:Pt, kc * 128:(kc + 1) * 128], identity=identf[0:Pt, 0:Pt])
                    return ins
                P.op("pe", trs, reads=[b_xn, b_const], writes=[bank(j)[1] for j in range(4)])
            for j in range(4):
                kc = half * 4 + j
                for (c0, c1, sq) in seqcols:
                    P.op("act", lambda e, j=j, kc=kc, c0=c0, c1=c1, sq=sq: e.activation(out=hT[:, kc, c0:c1], in_=bank(j)[0][:, c0:c1], func=AF.Identity, scale=gcol[:, layer, 0, kc, sq:sq + 1], bias=gcol[:, layer, 1, kc, sq:sq + 1]),
                         reads=[bank(j)[1], b_ada], writes=[b_hT])

    def post(N, nsub, Pt, layer, grp, wsb, b_wsb, nkc, srcT, b_srcT, ksz, xres, b_xres):
        junk, stat, ytmp = cx["junk"], cx["stat"], cx["ytmp"]
        b_junk, b_stat, b_ytmp = cx["b_junk"], cx["b_stat"], cx["b_ytmp"]
        bxr = b_xres if isinstance(b_xres, list) else [b_xres] * nsub
        for s in range(nsub):
            b_xres = bxr[s]
            pp = PS[2 + (s % 2)]
            bpp = bPS[2 + (s % 2)]

            def mm(e, s=s, pp=pp):
                ins = None
                for hf in range(2):
                    for c in range(nkc):
                        ins = e.matmul(pp[0:Pt, hf * 512:(hf + 1) * 512], lhsT=srcT[0:ksz, c, s * Pt:(s + 1) * Pt], rhs=wsb[0:ksz, c, hf * 512:(hf + 1) * 512], start=(c == 0), stop=(c == nkc - 1))
                return ins
            P.op("pe", mm, reads=[b_srcT, b_wsb], writes=bpp)
            P.op("act", lambda e, pp=pp: e.activation(out=junk[0:Pt, :], in_=pp[0:Pt, :], func=AF.Square, accum_out=stat[0:Pt, 12:13]), reads=bpp, writes=[b_junk, b_stat])
            P.op("act", lambda e: e.activation(out=stat[0:Pt, 13:14], in_=stat[0:Pt, 12:13], func=AF.Sqrt, scale=1.0 / D, bias=epsc[0:Pt, 0:1]), reads=[b_stat, b_const], writes=[b_stat])
            P.op("dve", lambda e: e.reciprocal(stat[0:Pt, 14:15], stat[0:Pt, 13:14]), reads=[b_stat], writes=[b_stat])
            P.op("dve", lambda e, pp=pp: e.scalar_tensor_tensor(out=ytmp[0:Pt, :], in0=pp[0:Pt, :], scalar=stat[0:Pt, 14:15], in1=gprow[0:Pt, layer, grp, :], op0=ALU.mult, op1=ALU.mult), reads=bpp + [b_stat, b_ada], writes=[b_ytmp])
            P.op("pool", lambda e, s=s: e.tensor_tensor(out=xres[0:Pt, s, :], in0=ytmp[0:Pt, :], in1=xres[0:Pt, s, :], op=ALU.add), reads=[b_ytmp, b_xres], writes=[b_xres])

    def split3(src, b_src, N, sign):
        spl, spf, b_spl, b_spf = cx["spl"], cx["spf"], cx["b_spl"], cx["b_spf"]
        P.op("dve", lambda e: e.tensor_scalar(out=spf[:, 0, 0:N], in0=src, scalar1=sign, scalar2=None, op0=ALU.mult), reads=[b_src], writes=[b_spf])
        for j in range(3):
            P.op("dve", lambda e, j=j: e.tensor_copy(spl[:, j, 0:N], spf[:, 0, 0:N]), reads=[b_spf], writes=[b_spl])
            if j < 2:
                P.op("dve", lambda e, j=j: e.tensor_copy(spf[:, 1, 0:N], spl[:, j, 0:N]), reads=[b_spl], writes=[b_spf])
                P.op("dve", lambda e: e.tensor_tensor(out=spf[:, 0, 0:N], in0=spf[:, 0, 0:N], in1=spf[:, 1, 0:N], op=ALU.subtract), reads=[b_spf], writes=[b_spf])

    m_persist = A.mark()
    adaw = A.alloc([128, 8, 3 * D], F32, "adaw")
    crow = A.alloc([3, D], F32, "crow")
    ccol = A.alloc([128, 8, 3], F32, "ccol")
    vrow = A.alloc([12, R], F32, "vrow")
    adab = A.alloc([1, 2, 3 * D], F32, "adab")
    grow = A.alloc([3, D], F32, "grow")
    npb = A.alloc([128, 2, D], F32, "npb")
    tmpc = A.alloc([128, 8, 3], F32, "tmpc")
    b_crow, b_ccol, b_vrow, b_adaw, b_adab, b_grow, b_npb = [P.buf(n) for n in "crow ccol vrow adaw adab grow npb".split()]
    P.dma("sp", lambda e: e.dma_start(out=crow[:], in_=i_cc[:, :]), writes=[b_crow])
    P.op("pool", lambda e: e.memset(vrow[:], 0.0), writes=[b_vrow])
    P.dma("sp", lambda e: e.dma_start(out=vrow[0:8, :], in_=i_lvec[:, :]), reads=[b_vrow], writes=[b_vrow])
    P.dma("sp", lambda e: e.dma_start(out=vrow[8:10, 0:D], in_=i_npre[:, :]), reads=[b_vrow], writes=[b_vrow])
    P.dma("sp", lambda e: e.dma_start(out=adab[:], in_=i_adab.rearrange("(o l) f -> o l f", o=1)), writes=[b_adab])
    P.dma("sp", lambda e: e.dma_start(out=npb[:], in_=i_npost.rearrange("(o l) f -> o l f", o=1).broadcast_to([128, 2, D])), writes=[b_npb])
    P.dma("sp", lambda e: e.dma_start(out=bfrow[:], in_=i_fbf.broadcast_to([128, H])), writes=[b_const])
    P.op("act", lambda e: e.activation(out=crow[:], in_=crow[:], func=AF.Silu), reads=[b_crow], writes=[b_crow])
    pa, ba = bank(0)

    def tr_c(e):
        ins = None
        for kc in range(8):
            ins = e.transpose(out=pa[:, kc * 3:kc * 3 + 3], in_=crow[0:3, kc * 128:(kc + 1) * 128], identity=identf[0:3, 0:3])
        return ins
    P.op("pe", tr_c, reads=[b_crow, b_const], writes=[ba])
    P.op("dve", lambda e: e.tensor_copy(ccol[:].rearrange("p k s -> p (k s)"), pa[:, 0:24]), reads=[ba], writes=[b_ccol])
    pb_, bb = bank(1)

    def tr_v(e):
        ins = None
        for c in range(NCH):
            ins = e.transpose(out=pb_[:, c * 10:c * 10 + 10], in_=vrow[0:10, c * 128:(c + 1) * 128], identity=identf[0:10, 0:10])
        return ins
    P.op("pe", tr_v, reads=[b_vrow, b_const], writes=[bb])
    vcol = A.alloc([128, NCH, 10], F32, "vcol")
    b_vcol = P.buf("vcol")
    P.op("dve", lambda e: e.tensor_copy(vcol[:].rearrange("p c v -> p (c v)"), pb_[:, 0:NCH * 10]), reads=[bb], writes=[b_vcol])
    P.op("dve", lambda e: e.tensor_copy(lcol[:, :, 0:8], vcol[:, :, 0:8]), reads=[b_vcol], writes=[b_const])
    P.op("act", lambda e: e.activation(out=lcol[:, :, 8], in_=vcol[:, :, 7], func=AF.Exp, scale=-1.0), reads=[b_vcol], writes=[b_const])
    P.op("act", lambda e: e.activation(out=lcol[:, :, 8], in_=lcol[:, :, 8], func=AF.Ln, bias=onec[:, 0:1]), reads=[b_const], writes=[b_const])
    P.op("dve", lambda e: e.tensor_scalar(out=lcol[:, :, 8], in0=lcol[:, :, 8], scalar1=-8.0, scalar2=None, op0=ALU.mult), reads=[b_const], writes=[b_const])
    pc_, bc = bank(2)
    P.op("pe", lambda e: e.matmul(pc_[0:16, 0:1], lhsT=bfrow[0:1, 0:16], rhs=ones_f[0:1, 0:1], start=True, stop=True), reads=[b_const], writes=[bc])
    P.op("dve", lambda e: e.tensor_scalar(out=bfcol[:, 0:1], in0=pc_[0:16, 0:1], scalar1=-1.0, scalar2=None, op0=ALU.mult), reads=[bc], writes=[b_const])
    for l in range(2):
        for q4 in range(4):
            P.dma("sp" if q4 % 2 == 0 else "act",
                  lambda e, l=l, q4=q4: e.dma_start(out=adaw[:, 2 * q4:2 * q4 + 2, :], in_=i_adaw[l, q4 * 256:(q4 + 1) * 256, :].rearrange("(k p) f -> p k f", p=128)),
                  writes=[b_adaw])
        pm, bm = bank(4 + l * 2)

        def ada_cols(e, l=l, pm=pm):
            ins = None
            for fc in range(16):
                for kc in range(8):
                    ins = e.matmul(pm[:, fc * 3:fc * 3 + 3], lhsT=adaw[:, kc, fc * 128:(fc + 1) * 128], rhs=ccol[:, kc, :], start=(kc == 0), stop=False)
                ins = e.matmul(pm[:, fc * 3:fc * 3 + 3], lhsT=adab[0:1, l, fc * 128:(fc + 1) * 128], rhs=ones_f[0:1, 0:3], start=False, stop=True)
            return ins
        P.op("pe", ada_cols, reads=[b_adaw, b_ccol, b_adab, b_const], writes=[bm])
        P.op("dve", lambda e, l=l, pm=pm: e.tensor_copy(gcol[:, l, 1, :, :].rearrange("p k s -> p (k s)"), pm[:, 0:24]), reads=[bm], writes=[b_ada])
        P.op("dve", lambda e, l=l, pm=pm: e.tensor_scalar(out=tmpc[:].rearrange("p k s -> p (k s)"), in0=pm[:, 24:48], scalar1=1.0, scalar2=None, op0=ALU.add), reads=[bm], writes=[b_grow])
        for s in range(3):
            P.op("dve", lambda e, l=l, s=s: e.tensor_tensor(out=gcol[:, l, 0, :, s], in0=tmpc[:, :, s], in1=vcol[0:128, 0:8, 8 + l], op=ALU.mult), reads=[b_grow, b_vcol], writes=[b_ada])
        for hf in range(2):
            prh, brh = bank(5 + l * 2) if hf == 0 else bank(3)

            def ada_rowh(e, l=l, hf=hf, prh=prh):
                ins = None
                for kc in range(8):
                    ins = e.matmul(prh[0:3, 0:512], lhsT=ccol[:, kc, :], rhs=adaw[:, kc, 2 * D + hf * 512:2 * D + (hf + 1) * 512], start=(kc == 0), stop=False)
                ins = e.matmul(prh[0:3, 0:512], lhsT=ones_f[0:1, 0:3], rhs=adab[0:1, l, 2 * D + hf * 512:2 * D + (hf + 1) * 512], start=False, stop=True)
                return ins
            P.op("pe", ada_rowh, reads=[b_adaw, b_ccol, b_adab, b_const], writes=[brh])
            P.op("dve", lambda e, hf=hf, prh=prh: e.tensor_copy(grow[:, hf * 512:(hf + 1) * 512], prh[0:3, 0:512]), reads=[brh], writes=[b_grow])
        for g in range(2):
            for hf in range(2):
                pg, bg = bank(0 + hf)
                P.op("pe", lambda e, g=g, hf=hf, pg=pg: e.matmul(pg[:, 0:512], lhsT=sel[0:3, g, :], rhs=grow[0:3, hf * 512:(hf + 1) * 512], start=True, stop=True), reads=[b_grow, b_const], writes=[bg])
                P.op("dve", lambda e, g=g, hf=hf, pg=pg, l=l: e.tensor_tensor(out=gprow[:, l, g, hf * 512:(hf + 1) * 512], in0=pg[:, 0:512], in1=npb[:, l, hf * 512:(hf + 1) * 512], op=ALU.mult), reads=[bg, b_npb], writes=[b_ada])
    P.barrier()
    A.reset(m_persist)
    if dbg == 32:
        P.dma("sp", lambda e: e.dma_start(out=o_fkp[2048:2176, 0:96], in_=gcol[:].rearrange("p a b c d -> p (a b c d)")), reads=[b_ada])
        P.dma("sp", lambda e: e.dma_start(out=o_fkp[2176:2304, 0:2 * 2 * D], in_=gprow[:].rearrange("p a b c -> p (a b c)")), reads=[b_ada]) if False else None
    if stop == 0:
        P.emit()
        return nc

    win = A.alloc([128, 8, 2 * R], BF16, "win")
    wg = A.alloc([128, 2, NCH, 128], BF16, "wg")
    wout = A.alloc([128, NCH, D], BF16, "wout")
    b_w = P.buf("weights")
    for kc in range(8):
        for hf in range(2):
            P.dma("pool", lambda e, kc=kc, hf=hf: e.dma_start(out=win[:, kc, hf * R:(hf + 1) * R], in_=i_win[kc * 128:(kc + 1) * 128, hf * R:(hf + 1) * R]), writes=[b_w])
    P.dma("pool", lambda e: e.dma_start(out=wg[:, 0, :, :], in_=i_wa.rearrange("n d e -> d n e")), writes=[b_w])
    P.dma("pool", lambda e: e.dma_start(out=wg[:, 1, :, :], in_=i_wx.rearrange("n d e -> d n e")), writes=[b_w])
    for c in range(NCH):
        P.dma("pool", lambda e, c=c: e.dma_start(out=wout[:, c, :], in_=i_wout[c * 128:(c + 1) * 128, :]), writes=[b_w])
    CG = 2
    NG = NCH // CG
    ENG_CAST = "pool" if dbg & 1024 else "dve"
    ENG_A2 = "pool" if dbg & 2048 else "dve"
    ENG_IM = "pool" if dbg & 4096 else "dve"
    xt_a1 = A.alloc([128, 4, D], F32, "xt_a1")
    cx["xn"] = A.alloc([128, D], F32, "xn")
    cx["junk"] = A.alloc([128, D], BF16, "junk")
    cx["stat"] = A.alloc([128, 16], F32, "stat")
    cx["hT"] = A.alloc([128, 8, TT], BF16, "hT")
    nb("xn", "junk", "stat", "hT")
    cx["ytmp"] = cx["xn"]
    cx["b_ytmp"] = cx["b_xn"]
    halo = A.alloc([128, NCH, 2, 4], F32, "halo")
    hst = A.alloc([128, NCH, 2], F32, "hst")
    XBW = 520
    sets = []
    for k_ in range(2):
        S_ = dict(xb=A.alloc([128, CG, XBW], F32, "xb"), sg=A.alloc([128, CG, TT], BF16, "sg"), xc=A.alloc([128, CG, TT], F32, "xc"),
                  xcb=A.alloc([128, CG, TT], BF16, "xcb"), rr=A.alloc([128, CG, TT], F32, "rr"), ig=A.alloc([128, CG, TT], F32, "ig"),
                  aa=A.alloc([128, CG, TT], F32, "aa"))
        for n_ in ("xb", "xbh", "sg", "xc", "xcb", "rr", "ig", "aa"):
            S_["b_" + n_] = P.buf("%s%d" % (n_, k_))
        sets.append(S_)
    zT = A.alloc([128, NCH, TT], BF16, "zT")
    b_xts = [P.buf("xt%d" % s_) for s_ in range(4)]
    b_halo = [P.buf("halo%d" % c_) for c_ in range(NCH)]
    b_hst, b_zT = P.buf("hst"), P.buf("zT")

    def layer0(N, nseg, L, segw):
        hT, b_hT = cx["hT"], cx["b_hT"]

        def xbv(S, cl, sgm, a_, b__):
            return S["xb"][:, cl, sgm * segw + a_:sgm * segw + b__]

        def stageA_pe(g):
            for cl in range(CG):
                c = g * CG + cl
                pxa, bxa = bank(4 + cl * 2)
                pga, bga = bank(5 + cl * 2)

                def mm(e, c=c, pxa=pxa, pga=pga):
                    ins = None
                    for kc in range(8):
                        ins = e.matmul(pxa[:, 0:N], lhsT=win[:, kc, c * 128:(c + 1) * 128], rhs=hT[:, kc, 0:N], start=(kc == 0), stop=(kc == 7))
                    for kc in range(8):
                        ins = e.matmul(pga[:, 0:N], lhsT=win[:, kc, R + c * 128:R + (c + 1) * 128], rhs=hT[:, kc, 0:N], start=(kc == 0), stop=(kc == 7))
                    return ins
                P.op("pe", mm, reads=[b_hT, b_w], writes=[bxa, bga])

        def stageA_el(g):
            S = sets[g % 2]
            for cl in range(CG):
                c = g * CG + cl
                pxa, bxa = bank(4 + cl * 2)
                pga, bga = bank(5 + cl * 2)
                P.op("act", lambda e, S=S, cl=cl, pga=pga: e.activation(out=S["sg"][:, cl, 0:N], in_=pga[:, 0:N], func=AF.Silu), reads=[bga], writes=[S["b_sg"]])
                for sgm in range(nseg):
                    P.op("dve", lambda e, S=S, cl=cl, sgm=sgm, pxa=pxa: e.tensor_copy(xbv(S, cl, sgm, 3, 3 + L), pxa[:, sgm * L:(sgm + 1) * L]), reads=[bxa], writes=[S["b_xb"]])
                    P.op("pool", lambda e, S=S, c=c, cl=cl, sgm=sgm: e.tensor_copy(xbv(S, cl, sgm, 0, 3), halo[:, c, sgm, 0:3]), reads=[b_halo[c]], writes=[S["b_xbh"]])
            for cl in range(CG):
                c = g * CG + cl
                for sgm in range(nseg):
                    o = S["xc"][:, cl, sgm * L:(sgm + 1) * L]
                    P.op("dve", lambda e, S=S, c=c, cl=cl, sgm=sgm, o=o: e.tensor_scalar(out=o, in0=xbv(S, cl, sgm, 0, L), scalar1=lcol[:, c, 0:1], scalar2=lcol[:, c, 4:5], op0=ALU.mult, op1=ALU.add),
                         reads=[S["b_xb"], S["b_xbh"], b_const], writes=[S["b_xc"]])
                    for k in range(1, 4):
                        P.op("dve", lambda e, S=S, c=c, cl=cl, sgm=sgm, o=o, k=k: e.scalar_tensor_tensor(out=o, in0=xbv(S, cl, sgm, k, k + L), scalar=lcol[:, c, k:k + 1], in1=o, op0=ALU.mult, op1=ALU.add),
                             reads=[S["b_xb"], S["b_xbh"], b_const, S["b_xc"]], writes=[S["b_xc"]])
                    P.op("pool", lambda e, S=S, c=c, cl=cl, sgm=sgm: e.tensor_copy(halo[:, c, sgm, 0:3], xbv(S, cl, sgm, L, L + 3)), reads=[S["b_xb"]], writes=[b_halo[c]])
                P.op(ENG_CAST, lambda e, S=S, cl=cl: e.tensor_copy(S["xcb"][:, cl, 0:N], S["xc"][:, cl, 0:N]), reads=[S["b_xc"]], writes=[S["b_xcb"]])

        def stageB_pe(g):
            S = sets[g % 2]
            xcb = S["xcb"]
            for cl in range(CG):
                c = g * CG + cl
                pra, bra = bank(cl * 2)
                pia, bia = bank(cl * 2 + 1)

                def mg(e, c=c, cl=cl, pra=pra, pia=pia, xcb=xcb):
                    e.matmul(pra[:, 0:N], lhsT=wg[:, 0, c, :], rhs=xcb[:, cl, 0:N], start=True, stop=True)
                    return e.matmul(pia[:, 0:N], lhsT=wg[:, 1, c, :], rhs=xcb[:, cl, 0:N], start=True, stop=True)
                P.op("pe", mg, reads=[S["b_xcb"], b_w], writes=[bra, bia])

        def stageB_el(g):
            S = sets[g % 2]
            rr, ig, aa, xc, sg, xcb = S["rr"], S["ig"], S["aa"], S["xc"], S["sg"], S["xcb"]
            for cl in range(CG):
                c = g * CG + cl
                pra, bra = bank(cl * 2)
                pia, bia = bank(cl * 2 + 1)
                P.op("act", lambda e, c=c, cl=cl, pra=pra, rr=rr: e.activation(out=rr[:, cl, 0:N], in_=pra[:, 0:N], func=AF.Sigmoid, bias=lcol[:, c, 5:6]), reads=[bra, b_const], writes=[S["b_rr"]])
                P.op("act", lambda e, c=c, cl=cl, pia=pia, ig=ig: e.activation(out=ig[:, cl, 0:N], in_=pia[:, 0:N], func=AF.Sigmoid, bias=lcol[:, c, 6:7]), reads=[bia, b_const], writes=[S["b_ig"]])
            for cl in range(CG):
                c = g * CG + cl
                P.op("act", lambda e, c=c, cl=cl, rr=rr, aa=aa: e.activation(out=aa[:, cl, 0:N], in_=rr[:, cl, 0:N], func=AF.Exp, scale=lcol[:, c, 8:9]), reads=[S["b_rr"], b_const], writes=[S["b_aa"]])
                P.op(ENG_A2, lambda e, cl=cl, rr=rr, aa=aa: e.tensor_tensor(out=rr[:, cl, 0:N], in0=aa[:, cl, 0:N], in1=aa[:, cl, 0:N], op=ALU.mult), reads=[S["b_aa"], S["b_rr"]], writes=[S["b_rr"]])
                P.op("pool", lambda e, cl=cl, ig=ig, xc=xc: e.tensor_tensor(out=ig[:, cl, 0:N], in0=ig[:, cl, 0:N], in1=xc[:, cl, 0:N], op=ALU.mult), reads=[S["b_ig"], S["b_xc"]], writes=[S["b_ig"]])
            for cl in range(CG):
                c = g * CG + cl
                P.op("act", lambda e, cl=cl, rr=rr: e.activation(out=rr[:, cl, 0:N], in_=rr[:, cl, 0:N], func=AF.Sqrt, scale=-1.0, bias=onec[:, 0:1]), reads=[S["b_rr"], b_const], writes=[S["b_rr"]])
                P.op(ENG_IM, lambda e, cl=cl, ig=ig, rr=rr: e.tensor_tensor(out=ig[:, cl, 0:N], in0=ig[:, cl, 0:N], in1=rr[:, cl, 0:N], op=ALU.mult), reads=[S["b_ig"], S["b_rr"]], writes=[S["b_ig"]])
                for sgm in range(nseg):
                    P.op("dve", lambda e, c=c, cl=cl, sgm=sgm, xc=xc, aa=aa, ig=ig: e.tensor_tensor_scan(out=xc[:, cl, sgm * L:(sgm + 1) * L], data0=aa[:, cl, sgm * L:(sgm + 1) * L], data1=ig[:, cl, sgm * L:(sgm + 1) * L], initial=hst[:, c, sgm:sgm + 1], op0=ALU.mult, op1=ALU.add),
                         reads=[S["b_aa"], S["b_ig"], b_hst, S["b_xc"]], writes=[S["b_xc"]])
                    P.op("dve", lambda e, c=c, cl=cl, sgm=sgm, xc=xc: e.tensor_copy(hst[:, c, sgm:sgm + 1], xc[:, cl, (sgm + 1) * L - 1:(sgm + 1) * L]), reads=[S["b_xc"], b_hst], writes=[b_hst])
                P.op("pool", lambda e, c=c, cl=cl, xc=xc, sg=sg: e.tensor_tensor(out=zT[:, c, 0:N], in0=xc[:, cl, 0:N], in1=sg[:, cl, 0:N], op=ALU.mult), reads=[S["b_xc"], S["b_sg"]], writes=[b_zT])

        stageA_pe(0)
        stageA_el(0)
        for g in range(NG):
            if not (dbg & 8192):
                if g + 1 < NG:
                    stageA_pe(g + 1)
                    stageA_el(g + 1)
                stageB_pe(g)
                stageB_el(g)
                continue
            stageB_pe(g)
            if g + 1 < NG:
                stageA_pe(g + 1)
            stageB_el(g)
            if g + 1 < NG:
                stageA_el(g + 1)

    P.op("pool", lambda e: e.memset(halo[:], 0.0), writes=b_halo)
    P.op("pool", lambda e: e.memset(hst[:], 0.0), writes=[b_hst])
    for i in range(nt_prompt):
        for s4 in range(4):
            P.dma("sp", lambda e, i=i, s4=s4: e.dma_start(out=xt_a1[:, s4, :], in_=i_xp[i * TT + s4 * 128:i * TT + (s4 + 1) * 128, :]), writes=[b_xts[s4]])
        front(TT, 4, 128, 0, [(0, TT, 0)], xt_a1, b_xts)
        layer0(TT, 1, TT, 0)
        post(TT, 4, 128, 0, 0, wout, b_w, NCH, zT, b_zT, 128, xt_a1, b_xts)
        for s4 in range(4):
            P.dma("pool", lambda e, i=i, s4=s4: e.dma_start(out=s_x1[i * TT + s4 * 128:i * TT + (s4 + 1) * 128, :], in_=xt_a1[:, s4, :]), reads=[b_xts[s4]], writes=[b_sx1])
        if dbg == 31 and i < NQ:
            P.dma("sp", lambda e, i=i: e.dma_start(out=o_yp[i * TT:(i + 1) * TT, :].rearrange("(s p) f -> p s f", p=128), in_=xt_a1[:]), reads=b_xts)
    P.dma("sp", lambda e: e.dma_start(out=o_lhp.rearrange("(c p) -> p c", p=128), in_=hst[:, :, 0], allow_slow_non_contiguous=True), reads=[b_hst])
    for k3 in range(3):
        P.dma("sp", lambda e, k3=k3: e.dma_start(out=o_lcp[k3].rearrange("(c p) -> p c", p=128), in_=halo[:, :, 0, k3], allow_slow_non_contiguous=True), reads=b_halo)
    sc = [(0, ST, 1), (ST, 2 * ST, 2)]
    if do_sample:
        for sq in range(2):
            P.dma("sp", lambda e, sq=sq: e.dma_start(out=hst[:, :, sq], in_=i_sth[sq].rearrange("(c p) -> p c", p=128), allow_slow_non_contiguous=True), reads=[b_hst], writes=[b_hst])
            for k3 in range(3):
                P.dma("sp", lambda e, sq=sq, k3=k3: e.dma_start(out=halo[:, :, sq, k3], in_=i_stc[sq, k3].rearrange("(c p) -> p c", p=128), allow_slow_non_contiguous=True), reads=b_halo, writes=b_halo)
        P.dma("sp", lambda e: e.dma_start(out=xs_t[0:64, 0, :], in_=i_xs[:, :]), writes=[b_xs])
        front(2 * ST, 1, 64, 0, sc, xs_t, b_xs)
        layer0(2 * ST, 2, ST, 40)
        post(2 * ST, 1, 64, 0, 1, wout, b_w, NCH, zT, b_zT, 128, xs_t, b_xs)
        for sq in range(2):
            P.dma("sp", lambda e, sq=sq: e.dma_start(out=o_lhs[sq].rearrange("(c p) -> p c", p=128), in_=hst[:, :, sq], allow_slow_non_contiguous=True), reads=[b_hst])
            for k3 in range(3):
                P.dma("sp", lambda e, sq=sq, k3=k3: e.dma_start(out=o_lcs[sq, k3].rearrange("(c p) -> p c", p=128), in_=halo[:, :, sq, k3], allow_slow_non_contiguous=True), reads=b_halo)
    P.barrier()
    A.reset(m_persist)
    if stop == 1:
        P.emit()
        return nc

    fwin = A.alloc([128, 8, 2 * D + H], BF16, "fwin")
    b_w = P.buf("weights2")
    for kc in range(8):
        for hf in range(2):
            P.dma("pool", lambda e, kc=kc, hf=hf: e.dma_start(out=fwin[:, kc, hf * D:(hf + 1) * D], in_=i_fwin[kc * 128:(kc + 1) * 128, D + hf * D:D + (hf + 1) * D]), writes=[b_w])
        P.dma("pool", lambda e, kc=kc: e.dma_start(out=fwin[:, kc, 2 * D:2 * D + H], in_=i_fwin[kc * 128:(kc + 1) * 128, 4 * D:4 * D + H]), writes=[b_w])
    xt_a2 = A.alloc([128, 4, D], F32, "xt_a2")
    cx["xn"] = A.alloc([128, D], F32, "xn")
    cx["junk"] = A.alloc([128, D], BF16, "junk")
    cx["stat"] = A.alloc([128, 16], F32, "stat")
    cx["hT"] = A.alloc([128, 8, TT], BF16, "hT")
    kst = A.alloc([128, 4, D], F32, "kst")
    kstb = A.alloc([128, 4, D], BF16, "kstb")
    b_kstb = P.buf("kstb")
    vst = A.alloc([128, 4, D], F32, "vst")
    v1st = A.alloc([128, H, 4, VS], BF16, "v1st")
    kTst = A.alloc([KA, H, TT], BF16, "kTst")
    lft = A.alloc([128, 4, H], F32, "lft")
    lfT = A.alloc([16, TT], F32, "lfT")
    cumT = A.alloc([16, TT], F32, "cumT")
    ccar = A.alloc([16, 2], F32, "ccar")
    cx["spl"] = A.alloc([16, 3, TT], BF16, "spl")
    cx["spf"] = A.alloc([16, 2, TT], F32, "spf")
    nb("xn", "junk", "stat", "hT", "spl", "spf")
    b_xt, b_kst, b_vst, b_v1st, b_kTst, b_lft, b_lfT, b_cumT, b_ccar = [P.buf(n) for n in range(9)]
    P.op("pool", lambda e: e.memset(v1st[:], 1.0), writes=[b_v1st])
    P.op("pool", lambda e: e.memset(kTst[:], 1.0), writes=[b_kTst])
    P.op("pool", lambda e: e.memset(ccar[:], 0.0), writes=[b_ccar])

    def ktrans(N, nsub, Pt, cast=False):
        if cast:
            P.op("pool", lambda e: e.tensor_copy(kstb[0:Pt, 0:nsub, :], kst[0:Pt, 0:nsub, :]), reads=[b_kst], writes=[b_kstb])
        for h in range(H):
            pk, bk = bank(h % 2)

            def trk(e, h=h, pk=pk):
                ins = None
                for s in range(nsub):
                    ins = e.matmul(pk[0:64, s * Pt:(s + 1) * Pt], lhsT=kstb[0:Pt, s, h * 64:(h + 1) * 64], rhs=ident_bf[0:Pt, 0:Pt], start=True, stop=True)
                return ins
            P.op("pe", trk, reads=[b_kstb, b_const], writes=[bk])
            if h % 2 == 0:
                P.op("act", lambda e, h=h, pk=pk: e.activation(out=kTst[0:64, h, 0:N], in_=pk[0:64, 0:N], func=AF.Identity), reads=[bk], writes=[b_kTst])
            else:
                P.op("dve", lambda e, h=h, pk=pk: e.tensor_copy(kTst[0:64, h, 0:N], pk[0:64, 0:N]), reads=[bk], writes=[b_kTst])

    def ckrows(N):
        spl, b_spl = cx["spl"], cx["b_spl"]
        for j in range(3):
            P.dma("sp", lambda e, j=j: e.dma_start(out=kTst[67 + j:68 + j, :, 0:N], in_=spl[:, j, 0:N]), reads=[b_spl, b_kTst], writes=[b_kTst])

    def kvproj(N, nsub, Pt, nseg, L, x1src, b_x1, seqcols, o_fk, o_fv, o_fl, tok0, kT_dsts, vv_dst, cum_dst):
        front(N, nsub, Pt, 1, seqcols, x1src, b_x1)
        if dbg == 33:
            for q_ in range(2):
                P.dma("pool", lambda e, q_=q_: e.dma_start(out=o_fvp[2048:2176, q_ * 2048:(q_ + 1) * 2048].rearrange("p (k n) -> p k n", k=4) if False else o_fvp[2048 + q_ * 128:2176 + q_ * 128, 0:1024].rearrange("p (k n) -> p k n", k=4)[:, :, 0:N // 2 if False else 256], in_=cx["hT"][:, q_ * 4:(q_ + 1) * 4, 0:256]), reads=[cx["b_hT"]])
            return
        if dbg == 1:
            return
        hT, b_hT = cx["hT"], cx["b_hT"]
        for s in range(nsub):
            for which, (wofs, stg, b_stg, o_d) in enumerate([(0, kst, b_kst, o_fk), (D, vst, b_vst, o_fv)]):
                pp = PS[2 + which]
                bpp = bPS[2 + which]

                def mm(e, s=s, pp=pp, wofs=wofs):
                    ins = None
                    for hf in range(2):
                        for kc in range(8):
                            ins = e.matmul(pp[0:Pt, hf * 512:(hf + 1) * 512], lhsT=hT[:, kc, s * Pt:(s + 1) * Pt], rhs=fwin[:, kc, wofs + hf * 512:wofs + (hf + 1) * 512], start=(kc == 0), stop=(kc == 7))
                    return ins
                P.op("pe", mm, reads=[b_hT, b_w], writes=bpp)
                P.op("act", lambda e, s=s, pp=pp, stg=stg: e.activation(out=stg[0:Pt, s, :], in_=pp[0:Pt, :], func=AF.Identity), reads=bpp, writes=[b_stg])
                if which == 0:
                    P.op("dve", lambda e, s=s: e.tensor_copy(kstb[0:Pt, s, :], kst[0:Pt, s, :]), reads=[b_kst], writes=[b_kstb])
                if which == 1 and dbg != 21:
                    if dbg == 22:
                        P.op("act", lambda e, s=s, pp=pp: e.activation(out=v1st[0:Pt, :, s, 0:64], in_=pp[0:Pt, :].rearrange("p (h d) -> p h d", h=H), func=AF.Identity), reads=bpp, writes=[b_v1st])
                    elif dbg == 23:
                        P.op("pool", lambda e, s=s: e.tensor_copy(v1st[0:Pt, :, s, 0:64], vst[0:Pt, s, :].rearrange("p (h d) -> p h d", h=H)), reads=[b_vst], writes=[b_v1st])
                    else:
                        P.op("dve", lambda e, s=s: e.tensor_copy(v1st[0:Pt, :, s, 0:64], vst[0:Pt, s, :].rearrange("p (h d) -> p h d", h=H)), reads=[b_vst], writes=[b_v1st])
                P.dma("sp", lambda e, s=s, stg=stg, o_d=o_d: e.dma_start(out=o_d[tok0 + s * Pt:tok0 + (s + 1) * Pt, :], in_=stg[0:Pt, s, :]), reads=[b_stg])
            if dbg in (2, 21, 22, 23):
                continue
            pf, bf_ = bank(0)

            def mfl(e, s=s, pf=pf):
                ins = None
                for kc in range(8):
                    ins = e.matmul(pf[0:Pt, 0:H], lhsT=hT[:, kc, s * Pt:(s + 1) * Pt], rhs=fwin[:, kc, 2 * D:2 * D + H], start=(kc == 0), stop=(kc == 7))
                return ins
            P.op("pe", mfl, reads=[b_hT, b_w], writes=[bf_])
            P.op("dve", lambda e, s=s, pf=pf: e.tensor_tensor(out=lft[0:Pt, s, :], in0=pf[0:Pt, 0:H], in1=bfrow[0:Pt, :], op=ALU.add), reads=[bf_, b_const], writes=[b_lft])
        if dbg in (2, 21, 22, 23):
            return
        P.op("act", lambda e: e.activation(out=lft[0:Pt, 0:nsub, :], in_=lft[0:Pt, 0:nsub, :], func=AF.Exp, scale=-1.0), reads=[b_lft], writes=[b_lft])
        P.op("act", lambda e: e.activation(out=lft[0:Pt, 0:nsub, :], in_=lft[0:Pt, 0:nsub, :], func=AF.Ln, bias=onec[0:Pt, 0:1]), reads=[b_lft, b_const], writes=[b_lft])
        P.op("dve", lambda e: e.tensor_scalar(out=lft[0:Pt, 0:nsub, :], in0=lft[0:Pt, 0:nsub, :], scalar1=-1.0, scalar2=None, op0=ALU.mult), reads=[b_lft], writes=[b_lft])
        with nc.allow_non_contiguous_dma(reason="64B rows"):
            P.dma("sp", lambda e: e.dma_start(out=o_fl[tok0:tok0 + nsub * Pt, :].rearrange("(s p) h -> p s h", p=Pt), in_=lft[0:Pt, 0:nsub, :], allow_slow_non_contiguous=True), reads=[b_lft])
        if dbg == 3:
            return
        pf, bf_ = bank(1)

        def mflT(e, pf=pf):
            ins = None
            for kc in range(8):
                ins = e.matmul(pf[0:H, 0:N], lhsT=fwin[:, kc, 2 * D:2 * D + H], rhs=hT[:, kc, 0:N], start=(kc == 0), stop=(kc == 7))
            return ins
        P.op("pe", mflT, reads=[b_hT, b_w], writes=[bf_])
        P.op("act", lambda e, pf=pf: e.activation(out=lfT[:, 0:N], in_=pf[0:H, 0:N], func=AF.Exp, scale=-1.0, bias=bfcol[:, 0:1]), reads=[bf_, b_const], writes=[b_lfT])
        P.op("act", lambda e: e.activation(out=lfT[:, 0:N], in_=lfT[:, 0:N], func=AF.Ln, bias=onec[0:16, 0:1]), reads=[b_lfT, b_const], writes=[b_lfT])
        for sgm in range(nseg):
            P.op("dve", lambda e, sgm=sgm: e.tensor_tensor_scan(out=cumT[:, sgm * L:(sgm + 1) * L], data0=ones_f[0:16, 0:L], data1=lfT[:, sgm * L:(sgm + 1) * L], initial=ccar[:, sgm:sgm + 1], op0=ALU.mult, op1=ALU.subtract),
                 reads=[b_lfT, b_ccar, b_const, b_cumT], writes=[b_cumT])
            P.op("dve", lambda e, sgm=sgm: e.tensor_copy(ccar[:, sgm:sgm + 1], cumT[:, (sgm + 1) * L - 1:(sgm + 1) * L]), reads=[b_cumT, b_ccar], writes=[b_ccar])
        if cum_dst is not None:
            P.dma("sp", lambda e: e.dma_start(out=cum_dst, in_=cumT[:, 0:N]), reads=[b_cumT])
        if dbg == 4:
            return
        split3(cumT[:, 0:N], b_cumT, N, -1.0)
        if dbg == 5:
            return
        ktrans(N, nsub, Pt)
        if dbg == 6:
            return
        ckrows(N)
        if dbg == 7:
            return
        for sgm, kd in enumerate(kT_dsts if not (dbg == 9 and Pt == 64) else []):
            P.dma("sp", lambda e, sgm=sgm, kd=kd: e.dma_start(out=kd, in_=kTst[:, :, sgm * L:(sgm + 1) * L]), reads=[b_kTst])
        for (vd, vsrc) in (vv_dst if not (dbg == 8 and Pt == 64) else []):
            P.dma("sp", lambda e, vd=vd, vsrc=vsrc: e.dma_start(out=vd, in_=vsrc), reads=[b_v1st])

    for i in range(nt_prompt):
        P.dma("pool", lambda e, i=i: e.dma_start(out=xt_a2[:], in_=s_x1[i * TT:(i + 1) * TT, :].rearrange("(s p) f -> p s f", p=128)), reads=[b_sx1], writes=[b_xt])
        if dbg == 34 and i < NQ:
            P.dma("sp", lambda e, i=i: e.dma_start(out=o_yp[i * TT:(i + 1) * TT, :].rearrange("(s p) f -> p s f", p=128), in_=xt_a2[:]), reads=[b_xt])
        m_ = i // 4
        vvd = [(s_vv[:, m_, :, (i % 4) * 4 * VS:(i % 4 + 1) * 4 * VS].rearrange("h p x -> p h x"), v1st[:].rearrange("p h s d -> p h (s d)"))]
        kvproj(TT, 4, 128, 1, TT, xt_a2, b_xt, [(0, TT, 0)], o_fkp, o_fvp, o_flp, i * TT,
               [s_kT[:, :, i * TT:(i + 1) * TT].rearrange("h r n -> r h n")], vvd, s_cum[:, i * TT:(i + 1) * TT])
    if do_sample:
        clf = A.alloc([128, 2, PAST // 128, H], F32, "clf")
        pcum = A.alloc([16, 2, PAST], F32, "pcum")
        b_clf, b_pcum = P.buf("clf"), P.buf("pcum")
        with nc.allow_non_contiguous_dma(reason="64B rows"):
            for sq in range(2):
                P.dma("sp", lambda e, sq=sq: e.dma_start(out=clf[:, sq, :, :], in_=i_clf[sq].rearrange("(j p) h -> p j h", p=128), allow_slow_non_contiguous=True), writes=[b_clf])
        for sq in range(2):
            for q8 in range(PAST // 512):
                pk, bk = bank(q8 % 2)

                def trl(e, sq=sq, q8=q8, pk=pk):
                    ins = None
                    for j in range(4):
                        ins = e.matmul(pk[0:16, j * 128:(j + 1) * 128], lhsT=clf[:, sq, q8 * 4 + j, :], rhs=identf[:, :], start=True, stop=True)
                    return ins
                P.op("pe", trl, reads=[b_clf, b_const], writes=[bk])
                P.op("act", lambda e, sq=sq, q8=q8, pk=pk: e.activation(out=pcum[:, sq, q8 * 512:(q8 + 1) * 512], in_=pk[0:16, 0:512], func=AF.Identity), reads=[bk], writes=[b_pcum])
            for q8 in range(PAST // 512):
                init = 0.0 if q8 == 0 else pcum[:, sq, q8 * 512 - 1:q8 * 512]
                P.op("dve", lambda e, sq=sq, q8=q8, init=init: e.tensor_tensor_scan(out=pcum[:, sq, q8 * 512:(q8 + 1) * 512], data0=ones_f[0:16, 0:512], data1=pcum[:, sq, q8 * 512:(q8 + 1) * 512], initial=init, op0=ALU.mult, op1=ALU.add),
                     reads=[b_pcum, b_const], writes=[b_pcum])
            P.op("dve", lambda e, sq=sq: e.tensor_copy(ccar[:, sq:sq + 1], pcum[:, sq, PAST - 1:PAST]), reads=[b_pcum, b_ccar], writes=[b_ccar])
        for sq in range(2 if dbg != 41 else 0):
            for q8 in range(PAST // 512):
                P.dma("sp", lambda e, sq=sq, q8=q8: e.dma_start(out=kst[:], in_=i_ck[sq, q8 * 512:(q8 + 1) * 512, :].rearrange("(s p) f -> p s f", p=128)), writes=[b_kst])
                P.dma("sp", lambda e, sq=sq, q8=q8: e.dma_start(out=vst[:], in_=i_cv[sq, q8 * 512:(q8 + 1) * 512, :].rearrange("(s p) f -> p s f", p=128)), writes=[b_vst])
                P.op("pool", lambda e: e.tensor_copy(v1st[:, :, :, 0:64], vst[:].rearrange("p s (h d) -> p h s d", h=H)), reads=[b_vst], writes=[b_v1st])
                P.dma("sp", lambda e, sq=sq, q8=q8: e.dma_start(out=s_vvs[sq, :, q8 // 4, :, (q8 % 4) * 4 * VS:(q8 % 4 + 1) * 4 * VS].rearrange("h p x -> p h x"), in_=v1st[:].rearrange("p h s d -> p h (s d)")), reads=[b_v1st])
                ktrans(512, 4, 128, cast=True)
                split3(pcum[:, sq, q8 * 512:(q8 + 1) * 512], b_pcum, 512, -1.0)
                ckrows(512)
                P.dma("sp", lambda e, sq=sq, q8=q8: e.dma_start(out=s_kTs[sq, :, :, q8 * 512:(q8 + 1) * 512].rearrange("h r n -> r h n"), in_=kTst[:]), reads=[b_kTst])
        vvd = [(s_vvs[sq, :, NPR, 0:ST, 0:VS].rearrange("h p x -> p h x"), v1st[sq * ST:(sq + 1) * ST, :, 0, :]) for sq in range(2)]
        if dbg not in (41, 42):
          kvproj(2 * ST, 1, 64, 2, ST, xs_t, b_xs, sc, o_fks, o_fvs, o_fls, 0,
               [s_kTs[sq, :, :, PAST:PAST + ST].rearrange("h r n -> r h n") for sq in range(2)], vvd, None)
        P.op("dve", lambda e: e.tensor_copy(scum[:, :], cumT[:, 0:2 * ST]), reads=[b_cumT], writes=[b_scum])
    P.barrier()
    A.reset(m_persist)
    if not do_attn:
        P.emit()
        return nc

    wq = A.alloc([128, 8, 2 * D], BF16, "wq")
    wo = A.alloc([64, H, D], BF16, "wo")
    pmask = A.alloc([128, 16, TT], BF16, "pmask")
    smask = A.alloc([ST, ST], BF16, "smask")
    b_w2 = P.buf("w2")
    for kc in range(8):
        P.dma("pool", lambda e, kc=kc: e.dma_start(out=wq[:, kc, 0:D], in_=i_fwin[kc * 128:(kc + 1) * 128, 0:D]), writes=[b_w2])
        P.dma("pool", lambda e, kc=kc: e.dma_start(out=wq[:, kc, D:2 * D], in_=i_fwin[kc * 128:(kc + 1) * 128, 3 * D:4 * D]), writes=[b_w2])
    for h in range(H):
        P.dma("pool", lambda e, h=h: e.dma_start(out=wo[:, h, :], in_=i_fwout[h * 64:(h + 1) * 64, :]), writes=[b_w2])
    P.dma("sp", lambda e: e.dma_start(out=pmask[:], in_=i_pmask[:, :, :]), writes=[b_w2])
    P.dma("sp", lambda e: e.dma_start(out=smask[:], in_=i_smask[:, :]), writes=[b_w2])
    xt_c = A.alloc([128, 4, D], F32, "xt2")
    cx["xn"] = A.alloc([128, D], F32, "xn")
    cx["junk"] = A.alloc([128, D], BF16, "junk")
    cx["stat"] = A.alloc([128, 16], F32, "stat")
    cx["hT"] = A.alloc([128, 8, TT], BF16, "hT")
    cx["spl"] = A.alloc([16, 3, TT], BF16, "spl")
    cx["spf"] = A.alloc([16, 2, TT], F32, "spf")
    nb("xn", "junk", "stat", "hT", "spl", "spf")
    cx["ytmp"] = cx["xn"]
    cx["b_ytmp"] = cx["b_xn"]
    xl = cx["xn"]
    qaug = A.alloc([KA, H, TT], BF16, "qaug")
    sgT = A.alloc([64, H, TT], BF16, "sgT")
    zT2 = sgT
    csel = A.alloc([16, 2, TT], F32, "csel")
    NKB = 2
    kch = [A.alloc([KA, 2048], BF16, "kch") for _ in range(NKB)]
    vch = [A.alloc([128, 16 * VS], BF16, "vch") for _ in range(NKB)]
    NSG = 4
    pT = [A.alloc([128, 512], BF16, "pT") for _ in range(NSG)]
    osb = A.alloc([65, TT], F32, "osb")
    rl = A.alloc([65, TT], F32, "rl")
    b_xt, b_xl, b_qaug, b_sgT, b_zT2, b_csel, b_osb, b_rl = [P.buf(n) for n in range(8)]
    b_xl = cx["b_xn"]
    b_zT2 = b_sgT
    b_kch = [P.buf("kch") for _ in range(NKB)]
    b_vch = [P.buf("vch") for _ in range(NKB)]
    b_pT = [P.buf("pT") for _ in range(NSG)]
    P.op("pool", lambda e: e.memset(qaug[:], 1.0), writes=[b_qaug])
    chunk_ctr = [0]
    grp_ctr = [0]

    def attend(N, nsub, Pt, seqs, x_res, b_xres, layer_grp, o_y, y_tok0):
        flat = []
        item_ctr = [0]
        for h in range(H):
            for sq in seqs:
                q0, q1 = sq["q0"], sq["q1"]
                nq = q1 - q0
                work = [(k_ap, v_ap, 128, 16, None) for (k_ap, v_ap) in sq["rows"]] + list(sq["diag"])
                first = True
                for (k_src, v_src, nk, ntile, mask_fn) in work:
                    item = dict(k_src=k_src, v_src=v_src, nk=nk, ntile=ntile, cb=None, idx=item_ctr[0])
                    item_ctr[0] += 1
                    G = min(ntile, 512 // nq if nq > 256 else 16)
                    for g0 in range(0, ntile, G):
                        flat.append(dict(h=h, q0=q0, nq=nq, item=item, g0=g0, G=G, nk=nk, mask_fn=mask_fn, first=first, last_of_head=False))
                        first = False
            flat[-1]["last_of_head"] = True

        def emit_S(en):
            it = en["item"]
            h, nk, ntile = en["h"], en["nk"], it["ntile"]
            if it["cb"] is None:
                cb_ = chunk_ctr[0] % NKB
                chunk_ctr[0] += 1
                it["cb"] = cb_
                P.dma("sp", lambda e, cb_=cb_, k_src=it["k_src"], nk=nk, ntile=ntile, h=h: e.dma_start(out=kch[cb_][:, 0:nk * ntile], in_=k_src[h]), writes=[b_kch[cb_]])
                P.dma("sp", lambda e, cb_=cb_, v_src=it["v_src"], nk=nk, ntile=ntile, h=h: e.dma_start(out=vch[cb_][0:nk, 0:ntile * VS], in_=v_src[h]), writes=[b_vch[cb_]])
            cb_ = it["cb"]
            gi = grp_ctr[0] % NSG
            grp_ctr[0] += 1
            en["gi"] = gi
            psg, bpsg = bank(gi)

            def mmS(e, cb_=cb_, g0=en["g0"], psg=psg, h=h, q0=en["q0"], nq=en["nq"], G=en["G"], nk=nk, mask_fn=en["mask_fn"]):
                ins = None
                for j in range(G):
                    kt = g0 + j
                    ins = e.matmul(psg[0:nk, j * nq:(j + 1) * nq], lhsT=kch[cb_][:, kt * nk:(kt + 1) * nk], rhs=qaug[:, h, q0:q0 + nq], start=True, stop=(mask_fn is None))
                    if mask_fn is not None:
                        ins = e.matmul(psg[0:nk, j * nq:(j + 1) * nq], lhsT=ident_bf[0:nk, 0:nk], rhs=mask_fn(kt), start=False, stop=True)
                return ins
            P.op("pe", mmS, reads=[b_kch[cb_], b_qaug, b_w2, b_const], writes=[bpsg])

        def emit_EV(en):
            h, nk, nq, G, gi, q0 = en["h"], en["nk"], en["nq"], en["G"], en["gi"], en["q0"]
            cb_ = en["item"]["cb"]
            psg, bpsg = bank(gi)
            po, bo = bank(6 + (h % 2))
            P.op("act", lambda e, psg=psg, gi=gi, G=G, nq=nq, nk=nk: e.activation(out=pT[gi][0:nk, 0:G * nq], in_=psg[0:nk, 0:G * nq], func=AF.Exp), reads=[bpsg], writes=[b_pT[gi]])

            def mmV(e, cb_=cb_, g0=en["g0"], gi=gi, po=po, q0=q0, nq=nq, G=G, nk=nk, fst=en["first"]):
                ins = None
                for j in range(G):
                    kt = g0 + j
                    ins = e.matmul(po[0:65, q0:q0 + nq], lhsT=vch[cb_][0:nk, kt * VS:kt * VS + 65], rhs=pT[gi][0:nk, j * nq:(j + 1) * nq], start=(fst and j == 0), stop=False, skip_group_check=True)
                return ins
            P.op("pe", mmV, reads=[b_vch[cb_], b_pT[gi]], writes=[bo])
            if en["last_of_head"]:
                P.op("act", lambda e, po=po: e.activation(out=osb[0:65, 0:N], in_=po[0:65, 0:N], func=AF.Identity), reads=[bo], writes=[b_osb])
                P.op("dve", lambda e: e.reciprocal(rl[64:65, 0:N], osb[64:65, 0:N]), reads=[b_osb], writes=[b_rl])
                pbq, bbq = bank(5)
                P.op("pe", lambda e, pbq=pbq: e.matmul(pbq[0:64, 0:N], lhsT=ones_f[64:65, 0:64], rhs=rl[64:65, 0:N], start=True, stop=True), reads=[b_rl, b_const], writes=[bbq])
                P.op("dve", lambda e, pbq=pbq: e.tensor_tensor(out=osb[0:64, 0:N], in0=osb[0:64, 0:N], in1=pbq[0:64, 0:N], op=ALU.mult), reads=[b_osb, bbq], writes=[b_osb])
                P.op("pool", lambda e, h=h: e.tensor_tensor(out=zT2[:, h, 0:N], in0=osb[0:64, 0:N], in1=sgT[:, h, 0:N], op=ALU.mult), reads=[b_osb, b_sgT], writes=[b_zT2])

        nxt = 0
        for i, en in enumerate(flat):
            while nxt < len(flat) and nxt <= i + NSG - 1 and flat[nxt]["item"]["idx"] <= en["item"]["idx"] + NKB - 1:
                emit_S(flat[nxt])
                nxt += 1
            emit_EV(en)
        post(N, nsub, Pt, 1, layer_grp, wo, b_w2, H, zT2, b_zT2, 64, x_res, b_xres)
        P.dma("sp", lambda e: e.dma_start(out=o_y[y_tok0:y_tok0 + nsub * Pt, :].rearrange("(s p) f -> p s f", p=Pt), in_=x_res[0:Pt, 0:nsub, :]), reads=[b_xres])

    def qproj(N, seqcols, xsrc, b_x, nsub, Pt, cq_src, b_cq):
        front(N, nsub, Pt, 1, seqcols, xsrc, b_x)
        hT, b_hT = cx["hT"], cx["b_hT"]
        spl, b_spl = cx["spl"], cx["b_spl"]
        for h in range(H):
            pq, bq = bank(6)
            pg, bg = bank(7)

            def mq(e, h=h, pq=pq, pg=pg):
                ins = None
                for kc in range(8):
                    ins = e.matmul(pq[0:64, 0:N], lhsT=wq[:, kc, h * 64:(h + 1) * 64], rhs=hT[:, kc, 0:N], start=(kc == 0), stop=(kc == 7))
                for kc in range(8):
                    ins = e.matmul(pg[0:64, 0:N], lhsT=wq[:, kc, D + h * 64:D + (h + 1) * 64], rhs=hT[:, kc, 0:N], start=(kc == 0), stop=(kc == 7))
                return ins
            P.op("pe", mq, reads=[b_hT, b_w2], writes=[bq, bg])
            P.op("dve", lambda e, h=h, pq=pq: e.tensor_scalar(out=qaug[0:64, h, 0:N], in0=pq[0:64, 0:N], scalar1=0.125, scalar2=None, op0=ALU.mult), reads=[bq], writes=[b_qaug])
            P.op("act", lambda e, h=h, pg=pg: e.activation(out=sgT[:, h, 0:N], in_=pg[0:64, 0:N], func=AF.Silu), reads=[bg], writes=[b_sgT])
        split3(cq_src, b_cq, N, 1.0)
        for j in range(3):
            P.dma("sp", lambda e, j=j: e.dma_start(out=qaug[64 + j:65 + j, :, 0:N], in_=spl[:, j, 0:N]), reads=[b_spl, b_qaug], writes=[b_qaug])

    if do_sample:
        qproj(2 * ST, sc, xs_t, b_xs, 1, 64, scum[:, :], b_scum)
        seqs = []
        for sq in range(2):
            rows = [(s_kTs[sq, :, :, rw * 2048:(rw + 1) * 2048], s_vvs[sq, :, rw, :, :]) for rw in range(NPR)]
            diag = [(s_kTs[sq, :, :, PAST:PAST + ST], s_vvs[sq, :, NPR, 0:ST, 0:VS], ST, 1, (lambda kt: smask[:, :]))]
            seqs.append(dict(q0=sq * ST, q1=(sq + 1) * ST, rows=rows, diag=diag))
        attend(2 * ST, 1, 64, seqs, xs_t, b_xs, 1, o_ys, 0)
    for m in range(nq_tiles):
        for r_ in range(4):
            i = 4 * m + r_
            P.dma("sp", lambda e, i=i: e.dma_start(out=csel[:, 1, :], in_=s_cum[:, i * TT:(i + 1) * TT]), writes=[b_csel])
            if r_ == 0:
                P.op("dve", lambda e: e.tensor_scalar(out=csel[:, 0, :], in0=csel[:, 1, :], scalar1=onehot[0:16, 0:1], scalar2=None, op0=ALU.mult), reads=[b_csel, b_const], writes=[b_csel])
            else:
                P.op("dve", lambda e, r_=r_: e.scalar_tensor_tensor(out=csel[:, 0, :], in0=csel[:, 1, :], scalar=onehot[0:16, r_:r_ + 1], in1=csel[:, 0, :], op0=ALU.mult, op1=ALU.add), reads=[b_csel, b_const], writes=[b_csel])
            for s4 in range(4):
                P.dma("act", lambda e, i=i, s4=s4: e.dma_start(out=xl[:, :], in_=s_x1[i * TT + s4 * 128:i * TT + (s4 + 1) * 128, :]), writes=[b_xl])
                if r_ == 0:
                    P.op("dve", lambda e, s4=s4: e.tensor_scalar(out=xt_c[:, s4, :], in0=xl[:, :], scalar1=onehot[:, 0:1], scalar2=None, op0=ALU.mult), reads=[b_xl, b_const], writes=[b_xt])
                else:
                    P.op("dve", lambda e, r_=r_, s4=s4: e.scalar_tensor_tensor(out=xt_c[:, s4, :], in0=xl[:, :], scalar=onehot[:, r_:r_ + 1], in1=xt_c[:, s4, :], op0=ALU.mult, op1=ALU.add), reads=[b_xl, b_const, b_xt], writes=[b_xt])
        qproj(TT, [(0, TT, 0)], xt_c, b_xt, 4, 128, csel[:, 0, :], b_csel)
        rows = [(s_kT[:, :, mm_ * 2048:(mm_ + 1) * 2048], s_vv[:, mm_, :, :]) for mm_ in range(m)]
        diag = [(s_kT[:, :, m * 2048:(m + 1) * 2048], s_vv[:, m, :, :], 128, 16, (lambda kt: pmask[:, kt, :]))]
        attend(TT, 4, 128, [dict(q0=0, q1=TT, rows=rows, diag=diag)], xt_c, b_xt, 0, o_yp, m * TT)
    P.emit()
    return nc


_CACHE = {}


def _get_nc(**kw):
    key = tuple(sorted(kw.items()))
    if key not in _CACHE:
        _CACHE[key] = build(**kw)
    return _CACHE[key]


def _consts(core):
    r = core % 4
    ident = np.eye(128, dtype=np.float32)
    sel = np.zeros((3, 2, 128), np.float32)
    sel[0, 0, :] = 1.0
    sel[1, 1, 0:ST] = 1.0
    sel[2, 1, ST:2 * ST] = 1.0
    kpos = (np.arange(16)[None, :, None] * 128 + np.arange(128)[:, None, None])
    qpos = r * TT + np.arange(TT)[None, None, :]
    pmask = np.where(kpos > qpos, NEG, 0.0).astype(ml_dtypes.bfloat16)
    smask = np.where(np.arange(ST)[:, None] > np.arange(ST)[None, :], NEG, 0.0).astype(ml_dtypes.bfloat16)
    onehot = np.zeros((128, 4), np.float32)
    onehot[:, r] = 1.0
    return dict(c_ident=ident, c_sel=sel, c_pmask=np.ascontiguousarray(pmask), c_smask=smask, c_onehot=onehot)


def _in_map(c, I, shared, T, PAST):
    f = lambda a: np.ascontiguousarray(np.asarray(a, dtype=np.float32))
    b = c // 4
    s0 = 2 * c
    m = dict(shared)
    m.update(_consts(c))
    m["xp"] = f(I["x_prompt"][b])
    m["xs"] = f(I["x_sample"][s0:s0 + 2]).reshape(2 * ST, D)
    m["cc"] = np.concatenate([f(I["c_prompt"])[b:b + 1], f(I["c_sample"])[s0:s0 + 2]], axis=0)
    m["st_h"] = f(I["state_lru_h"][0, s0:s0 + 2])
    m["st_conv"] = f(I["state_lru_conv"][0, s0:s0 + 2])
    m["ck_k"] = f(I["cache_fox_k"][0, s0:s0 + 2]).reshape(2, PAST, D)
    m["ck_v"] = f(I["cache_fox_v"][0, s0:s0 + 2]).reshape(2, PAST, D)
    m["ck_lf"] = f(I["cache_fox_logf"][0, s0:s0 + 2])
    return m


def _shared(I):
    f = lambda a: np.ascontiguousarray(np.asarray(a, dtype=np.float32))
    lvec = np.concatenate([f(I["lru_conv_w"])[0], f(I["lru_conv_b"]), f(I["lru_b_a"]), f(I["lru_b_x"]), f(I["lru_lambda"])], axis=0)
    return dict(norm_pre=f(I["norm_pre"]), norm_post=f(I["norm_post"]), ada_w=f(I["ada_w"]), ada_b=f(I["ada_b"]), lru_w_in=f(I["lru_w_in"])[0],
                lru_vecs=f(lvec), lru_w_a=f(I["lru_w_a"])[0], lru_w_x=f(I["lru_w_x"])[0], lru_w_out=f(I["lru_w_out"])[0],
                fox_w_in=f(I["fox_w_in"])[0], fox_b_f=f(I["fox_b_f"]), fox_w_out=f(I["fox_w_out"])[0])


def run_cores(I, cores, build_kw):
    T = I["x_prompt"].shape[1]
    PAST = I["cache_fox_k"].shape[2]
    kw = dict(build_kw)
    kw.update(Tn=T, PASTn=PAST)
    nc = _get_nc(**kw)
    shared = _shared(I)
    in_maps = [_in_map(c, I, shared, T, PAST) for c in cores]
    res = run_bass_kernel_spmd(nc, in_maps, core_ids=list(range(len(cores)))).results
    return {c: res[i] for i, c in enumerate(cores)}


def kernel(x_prompt, x_sample, c_prompt, c_sample, state_lru_h, state_lru_conv, cache_fox_k, cache_fox_v, cache_fox_logf,
           norm_pre, norm_post, ada_w, ada_b, lru_w_in, lru_conv_w, lru_conv_b, lru_w_a, lru_b_a, lru_w_x, lru_b_x,
           lru_lambda, lru_w_out, fox_w_in, fox_b_f, fox_w_out, _build_kw=None):
    I = dict(x_prompt=x_prompt, x_sample=x_sample, c_prompt=c_prompt, c_sample=c_sample, state_lru_h=state_lru_h, state_lru_conv=state_lru_conv,
             cache_fox_k=cache_fox_k, cache_fox_v=cache_fox_v, cache_fox_logf=cache_fox_logf, norm_pre=norm_pre, norm_post=norm_post,
             ada_w=ada_w, ada_b=ada_b, lru_w_in=lru_w_in, lru_conv_w=lru_conv_w, lru_conv_b=lru_conv_b, lru_w_a=lru_w_a, lru_b_a=lru_b_a,
             lru_w_x=lru_w_x, lru_b_x=lru_b_x, lru_lambda=lru_lambda, lru_w_out=lru_w_out, fox_w_in=fox_w_in, fox_b_f=fox_b_f, fox_w_out=fox_w_out)
    I = {k: np.asarray(v) for k, v in I.items()}
    res = run_cores(I, list(range(8)), _build_kw if _build_kw is not None else {})
    return assemble(res, I)


def assemble(res, I):
    B, T = I["x_prompt"].shape[0], I["x_prompt"].shape[1]
    NQ = T // 2048
    cores = sorted(res.keys())
    y_p = np.zeros((B, T, D), np.float32)
    for c in cores:
        b, r = c // 4, c % 4
        yp = res[c]["y_p"].reshape(NQ, TT, D)
        for m_ in range(NQ):
            i = 4 * m_ + r
            y_p[b, i * TT:(i + 1) * TT] = yp[m_]
    nb = len(cores) // 4 if len(cores) >= 4 else 1
    bs = sorted(set(c // 4 for c in cores))
    first = {b: min(c for c in cores if c // 4 == b) for b in bs}
    y_s = np.concatenate([res[c]["y_s"].reshape(2, ST, D) for c in cores], axis=0)
    lru_h_p = np.stack([res[first[b]]["lru_h_p"] for b in bs])[None]
    lru_c_p = np.stack([res[first[b]]["lru_conv_p"] for b in bs])[None]
    fk_p = np.stack([res[first[b]]["fk_p"].reshape(T, H, DH) for b in bs])[None]
    fv_p = np.stack([res[first[b]]["fv_p"].reshape(T, H, DH) for b in bs])[None]
    fl_p = np.stack([res[first[b]]["flf_p"] for b in bs])[None]
    lru_h_s = np.concatenate([res[c]["lru_h_s"] for c in cores], axis=0)[None]
    lru_c_s = np.concatenate([res[c]["lru_conv_s"] for c in cores], axis=0)[None]
    fk_s = np.concatenate([res[c]["fk_s"].reshape(2, ST, H, DH) for c in cores], axis=0)[None]
    fv_s = np.concatenate([res[c]["fv_s"].reshape(2, ST, H, DH) for c in cores], axis=0)[None]
    fl_s = np.concatenate([res[c]["flf_s"].reshape(2, ST, H) for c in cores], axis=0)[None]
    return (y_p, y_s, lru_h_p, lru_c_p, fk_p, fv_p, fl_p, lru_h_s, lru_c_s, fk_s, fv_s, fl_s)
```

```python
import contextlib
import numpy as np
import ml_dtypes
import concourse.bass as bass
import concourse.mybir as mybir
from concourse.bass_utils import run_bass_kernel_spmd

F32 = mybir.dt.float32
BF16 = mybir.dt.bfloat16
AF = mybir.ActivationFunctionType
ALU = mybir.AluOpType

D = 1024
R = 1536
NCH = 12
H = 16
DH = 64
T = 16384
TT = 512
NT = T // TT
ST = 32
PAST = 4096
EPS = 1e-6
NEG = -30000.0
VS = 66
KA = 70
COMPUTE = ("pe", "act", "dve", "pool", "sp")


class Buf:
    __slots__ = ("name", "last_w", "readers")

    def __init__(self, name):
        self.name = name
        self.last_w = None
        self.readers = []


class Op:
    __slots__ = ("eng", "fn", "deps", "is_dma", "idx", "need_inc", "val", "sem", "semval", "prev_on_sem", "inc")

    def __init__(self, eng, fn, is_dma, inc=16):
        self.eng = eng
        self.fn = fn
        self.deps = set()
        self.is_dma = is_dma
        self.need_inc = False
        self.val = None
        self.sem = None
        self.semval = None
        self.prev_on_sem = None
        self.inc = inc


class Prog:
    def __init__(self, nc, n_dma_sems=(("sp", 32), ("act", 12), ("pool", 12))):
        self.nc = nc
        self.ops = []
        self.by_eng = {e: [] for e in COMPUTE}
        self.dma_ring = {e: n for e, n in n_dma_sems}
        self.dma_count = {e: 0 for e, _ in n_dma_sems}
        self.dma_last_on_slot = {}
        self.all_bufs = []

    def buf(self, name=""):
        b = Buf(name)
        self.all_bufs.append(b)
        return b

    def _add(self, op, reads, writes):
        op.idx = len(self.ops)
        for b in reads:
            if b.last_w is not None:
                op.deps.add(b.last_w)
        for b in writes:
            if b.last_w is not None:
                op.deps.add(b.last_w)
            for r in b.readers:
                op.deps.add(r)
        for b in reads:
            b.readers.append(op)
        for b in writes:
            b.last_w = op
            b.readers = []
        op.deps.discard(op)
        self.ops.append(op)
        self.by_eng[op.eng].append(op)
        return op

    def op(self, eng, fn, reads=(), writes=()):
        return self._add(Op(eng, fn, False), reads, writes)

    def dma(self, eng, fn, reads=(), writes=(), inc=16):
        op = Op(eng, fn, True, inc)
        n = self.dma_count[eng]
        self.dma_count[eng] = n + 1
        slot = (eng, n % self.dma_ring[eng])
        op.sem = slot
        op.prev_on_sem = self.dma_last_on_slot.get(slot)
        op.semval = (op.prev_on_sem.semval if op.prev_on_sem else 0) + inc
        self.dma_last_on_slot[slot] = op
        return self._add(op, reads, writes)

    def barrier(self):
        lasts = [self.by_eng[e][-1] for e in COMPUTE if self.by_eng[e]]
        lasts += list(self.dma_last_on_slot.values())
        for e in COMPUTE:
            o = Op(e, None, False)
            o.idx = len(self.ops)
            o.deps = set(lasts)
            self.ops.append(o)
            self.by_eng[e].append(o)
        for b in self.all_bufs:
            if str(b.name).startswith("dram_"):
                continue
            b.last_w = None
            b.readers = []

    def emit(self, final_wait_eng="sp"):
        nc = self.nc
        lasts = [self.by_eng[e][-1] for e in COMPUTE if self.by_eng[e]]
        lasts += list(self.dma_last_on_slot.values())
        fin = Op(final_wait_eng, None, False)
        fin.idx = len(self.ops)
        fin.deps = set(lasts)
        self.ops.append(fin)
        self.by_eng[final_wait_eng].append(fin)
        for o in self.ops:
            for d in o.deps:
                if not d.is_dma:
                    if d.eng == o.eng and o.eng == "pe":
                        continue
                    d.need_inc = True
        for e in COMPUTE:
            c = 0
            for o in self.by_eng[e]:
                if not o.is_dma and o.need_inc:
                    c += 1
                    o.val = c
        with contextlib.ExitStack() as st:
            esem = {e: st.enter_context(nc.semaphore("s_" + e)) for e in COMPUTE}
            dsem = {}
            for e, n in self.dma_ring.items():
                for i in range(n):
                    dsem[(e, i)] = st.enter_context(nc.semaphore("d_%s_%d" % (e, i)))
            block = st.enter_context(nc.Block())
            engobj = {"pe": nc.tensor, "act": nc.scalar, "dve": nc.vector, "pool": nc.gpsimd, "sp": nc.sync}

            def run_engine(e):
                eng = engobj[e]
                waited = {}

                def wait(key, sem, val):
                    if waited.get(key, 0) >= val:
                        return
                    waited[key] = val
                    eng.wait_ge(sem, val)

                for o in self.by_eng[e]:
                    for d in sorted(o.deps, key=lambda d: d.idx):
                        if d.is_dma:
                            wait(d.sem, dsem[d.sem], d.semval)
                        else:
                            if d.eng == e and e == "pe":
                                continue
                            if d.val is None:
                                continue
                            wait(d.eng, esem[d.eng], d.val)
                    if o.is_dma and o.prev_on_sem is not None:
                        wait(o.sem, dsem[o.sem], o.prev_on_sem.semval)
                    if o.fn is None:
                        if o.need_inc:
                            eng.nop().then_inc(esem[e], 1)
                        continue
                    ins = o.fn(eng)
                    if o.is_dma:
                        ins.then_inc(dsem[o.sem], o.inc)
                    elif o.need_inc:
                        ins.then_inc(esem[e], 1)

            @block.tensor
            def _(x):
                run_engine("pe")

            @block.scalar
            def _(x):
                run_engine("act")

            @block.vector
            def _(x):
                run_engine("dve")

            @block.gpsimd
            def _(x):
                run_engine("pool")

            @block.sync
            def _(x):
                run_engine("sp")


class Arena:
    def __init__(self, nc, base=16640, limit=229376 - 2048):
        self.nc = nc
        self.ptr = base
        self.limit = limit
        self.n = 0

    def alloc(self, shape, dtype, name="t"):
        size = int(np.prod(shape[1:])) * (4 if dtype == F32 else 2)
        size = (size + 63) // 64 * 64
        off = self.ptr
        self.ptr += size
        assert self.ptr <= self.limit, ("SBUF overflow", name, self.ptr)
        self.n += 1
        return self.nc.alloc_sbuf_tensor_at("%s_%d" % (name, self.n), list(shape), dtype, offset=off)

    def mark(self):
        return self.ptr

    def reset(self, m):
        self.ptr = m


def build(nq_tiles=None, do_attn=True, do_sample=True, nt_prompt=None, stop=9, Tn=16384, PASTn=4096, dbg=0):
    T, PAST = Tn, PASTn
    NPR = PAST // 2048
    NQ = T // 2048
    if nq_tiles is None:
        nq_tiles = NQ
    if nt_prompt is None:
        nt_prompt = T // TT
    nc = bass.Bass("TRN2", target_bir_lowering=False)
    P = Prog(nc)
    A = Arena(nc)

    def din(name, shape, dt=F32):
        return nc.dram_tensor(name, list(shape), dt, kind="ExternalInput").ap()

    def dout(name, shape, dt=F32):
        return nc.dram_tensor(name, list(shape), dt, kind="ExternalOutput").ap()

    def dscr(name, shape, dt=F32):
        return nc.dram_tensor(name, list(shape), dt).ap()

    i_xp = din("xp", [T, D])
    i_xs = din("xs", [2 * ST, D])
    i_cc = din("cc", [3, D])
    i_sth = din("st_h", [2, R])
    i_stc = din("st_conv", [2, 3, R])
    i_ck = din("ck_k", [2, PAST, D])
    i_cv = din("ck_v", [2, PAST, D])
    i_clf = din("ck_lf", [2, PAST, H])
    i_npre = din("norm_pre", [2, D])
    i_npost = din("norm_post", [2, D])
    i_adaw = din("ada_w", [2, D, 3 * D])
    i_adab = din("ada_b", [2, 3 * D])
    i_win = din("lru_w_in", [D, 2 * R])
    i_lvec = din("lru_vecs", [8, R])
    i_wa = din("lru_w_a", [NCH, 128, 128])
    i_wx = din("lru_w_x", [NCH, 128, 128])
    i_wout = din("lru_w_out", [R, D])
    i_fwin = din("fox_w_in", [D, 4 * D + H])
    i_fbf = din("fox_b_f", [1, H])
    i_fwout = din("fox_w_out", [D, D])
    i_ident = din("c_ident", [128, 128])
    i_sel = din("c_sel", [3, 2, 128])
    i_pmask = din("c_pmask", [128, 16, TT], BF16)
    i_smask = din("c_smask", [ST, ST], BF16)
    i_onehot = din("c_onehot", [128, 4])
    o_yp = dout("y_p", [NQ * TT, D])
    o_ys = dout("y_s", [2 * ST, D])
    o_lhp = dout("lru_h_p", [R])
    o_lcp = dout("lru_conv_p", [3, R])
    o_fkp = dout("fk_p", [T, D])
    o_fvp = dout("fv_p", [T, D])
    o_flp = dout("flf_p", [T, H])
    o_lhs = dout("lru_h_s", [2, R])
    o_lcs = dout("lru_conv_s", [2, 3, R])
    o_fks = dout("fk_s", [2 * ST, D])
    o_fvs = dout("fv_s", [2 * ST, D])
    o_fls = dout("flf_s", [2 * ST, H])
    s_x1 = dscr("s_x1", [T, D])
    s_kT = dscr("s_kT", [H, KA, T], BF16)
    s_vv = dscr("s_vv", [H, T // 2048, 128, 16 * VS], BF16)
    s_cum = dscr("s_cum", [H, T])
    NK_S = PAST + 128
    s_kTs = dscr("s_kTs", [2, H, KA, NK_S], BF16)
    s_vvs = dscr("s_vvs", [2, H, NPR + 1, 128, 16 * VS], BF16)

    b_sx1, b_skT, b_svv, b_scm, b_skTs, b_svvs = [P.buf("dram_%d" % i) for i in range(6)]
    PS = [nc.alloc_psum_tensor("ps%d" % i, [128, 1024], F32) for i in range(4)]
    bPS = [[P.buf("ps%d_%d" % (i, j)) for j in range(2)] for i in range(4)]

    def bank(i):
        return PS[i // 2][:, (i % 2) * 512:(i % 2) * 512 + 512], bPS[i // 2][i % 2]

    identf = A.alloc([128, 128], F32, "ident")
    b_const = P.buf("const")
    P.dma("sp", lambda e: e.dma_start(out=identf[:], in_=i_ident[:, :]), writes=[b_const])
    sel = A.alloc([3, 2, 128], F32, "sel")
    P.dma("sp", lambda e: e.dma_start(out=sel[:], in_=i_sel[:, :, :]), writes=[b_const])
    onehot = A.alloc([128, 4], F32, "onehot")
    P.dma("sp", lambda e: e.dma_start(out=onehot[:], in_=i_onehot[:, :]), writes=[b_const])
    ones_bf = A.alloc([128, 512], BF16, "ones")
    ones_f = A.alloc([128, 512], F32, "onesf")
    P.op("pool", lambda e: e.memset(ones_bf[:], 1.0), writes=[b_const])
    P.op("pool", lambda e: e.memset(ones_f[:], 1.0), writes=[b_const])
    ident_bf = A.alloc([128, 128], BF16, "identb")
    P.op("dve", lambda e: e.tensor_copy(ident_bf[:], identf[:]), reads=[b_const], writes=[b_const])
    gcol = A.alloc([128, 2, 2, 8, 3], F32, "gcol")
    gprow = A.alloc([128, 2, 2, D], F32, "gprow")
    lcol = A.alloc([128, NCH, 10], F32, "lcol")
    bfcol = A.alloc([16, 2], F32, "bfcol")
    bfrow = A.alloc([128, H], F32, "bfrow")
    b_ada = P.buf("ada")
    epsc = A.alloc([128, 1], F32, "epsc")
    onec = A.alloc([128, 1], F32, "onec")
    P.op("pool", lambda e: e.memset(epsc[:], EPS), writes=[b_const])
    P.op("pool", lambda e: e.memset(onec[:], 1.0), writes=[b_const])
    xs_t = A.alloc([128, 1, D], F32, "xs_t")
    b_xs = P.buf("xs")
    scum = A.alloc([16, 2 * ST], F32, "scum")
    b_scum = P.buf("scum")

    cx = {}

    def nb(*names):
        for n in names:
            cx["b_" + n] = P.buf(n)

    def front(N, nsub, Pt, layer, seqcols, xsrc, b_x):
        xn, junk, stat, hT = cx["xn"], cx["junk"], cx["stat"], cx["hT"]
        b_xn, b_junk, b_stat, b_hT = cx["b_xn"], cx["b_junk"], cx["b_stat"], cx["b_hT"]
        bxs = b_x if isinstance(b_x, list) else [b_x] * nsub
        for half in range(2):
            for s in range(nsub):
                b_x = bxs[s]
                if half == 0:
                    P.op("dve", lambda e, s=s: e.scalar_tensor_tensor(out=junk[0:Pt, :], in0=xsrc[0:Pt, s, :], scalar=1.0, in1=xsrc[0:Pt, s, :], op0=ALU.mult, op1=ALU.mult, accum_out=stat[0:Pt, s:s + 1]),
                         reads=[b_x], writes=[b_junk, b_stat])
                    P.op("act", lambda e, s=s: e.activation(out=stat[0:Pt, 4 + s:5 + s], in_=stat[0:Pt, s:s + 1], func=AF.Sqrt, scale=1.0 / D, bias=epsc[0:Pt, 0:1]), reads=[b_stat, b_const], writes=[b_stat])
                    P.op("dve", lambda e, s=s: e.reciprocal(stat[0:Pt, 8 + s:9 + s], stat[0:Pt, 4 + s:5 + s]), reads=[b_stat], writes=[b_stat])
                P.op("dve", lambda e, s=s: e.tensor_scalar(out=xn[0:Pt, :], in0=xsrc[0:Pt, s, :], scalar1=stat[0:Pt, 8 + s:9 + s], scalar2=None, op0=ALU.mult), reads=[b_x, b_stat], writes=[b_xn])

                def trs(e, s=s, half=half):
                    ins = None
                    for j in range(4):
                        kc = half * 4 + j
                        ins = e.transpose(out=bank(j)[0][:, s * Pt:(s + 1) * Pt], in_=xn[0:Pt, kc * 128:(kc + 1) * 128], identity=identf[0:Pt, 0:Pt])
                    return ins
                P.op("pe", trs, reads=[b_xn, b_const], writes=[bank(j)[1] for j in range(4)])
            for j in range(4):
                kc = half * 4 + j
                for (c0, c1, sq) in seqcols:
                    P.op("act", lambda e, j=j, kc=kc, c0=c0, c1=c1, sq=sq: e.activation(out=hT[:, kc, c0:c1], in_=bank(j)[0][:, c0:c1], func=AF.Identity, scale=gcol[:, layer, 0, kc, sq:sq + 1], bias=gcol[:, layer, 1, kc, sq:sq + 1]),
                         reads=[bank(j)[1], b_ada], writes=[b_hT])

    def post(N, nsub, Pt, layer, grp, wsb, b_wsb, nkc, srcT, b_srcT, ksz, xres, b_xres):
        junk, stat, ytmp = cx["junk"], cx["stat"], cx["ytmp"]
        b_junk, b_stat, b_ytmp = cx["b_junk"], cx["b_stat"], cx["b_ytmp"]
        bxr = b_xres if isinstance(b_xres, list) else [b_xres] * nsub
        for s in range(nsub):
            b_xres = bxr[s]
            pp = PS[2 + (s % 2)]
            bpp = bPS[2 + (s % 2)]

            def mm(e, s=s, pp=pp):
                ins = None
                for hf in range(2):
                    for c in range(nkc):
                        ins = e.matmul(pp[0:Pt, hf * 512:(hf + 1) * 512], lhsT=srcT[0:ksz, c, s * Pt:(s + 1) * Pt], rhs=wsb[0:ksz, c, hf * 512:(hf + 1) * 512], start=(c == 0), stop=(c == nkc - 1))
                return ins
            P.op("pe", mm, reads=[b_srcT, b_wsb], writes=bpp)
            P.op("act", lambda e, pp=pp: e.activation(out=junk[0:Pt, :], in_=pp[0:Pt, :], func=AF.Square, accum_out=stat[0:Pt, 12:13]), reads=bpp, writes=[b_junk, b_stat])
            P.op("act", lambda e: e.activation(out=stat[0:Pt, 13:14], in_=stat[0:Pt, 12:13], func=AF.Sqrt, scale=1.0 / D, bias=epsc[0:Pt, 0:1]), reads=[b_stat, b_const], writes=[b_stat])
            P.op("dve", lambda e: e.reciprocal(stat[0:Pt, 14:15], stat[0:Pt, 13:14]), reads=[b_stat], writes=[b_stat])
            P.op("dve", lambda e, pp=pp: e.scalar_tensor_tensor(out=ytmp[0:Pt, :], in0=pp[0:Pt, :], scalar=stat[0:Pt, 14:15], in1=gprow[0:Pt, layer, grp, :], op0=ALU.mult, op1=ALU.mult), reads=bpp + [b_stat, b_ada], writes=[b_ytmp])
            P.op("pool", lambda e, s=s: e.tensor_tensor(out=xres[0:Pt, s, :], in0=ytmp[0:Pt, :], in1=xres[0:Pt, s, :], op=ALU.add), reads=[b_ytmp, b_xres], writes=[b_xres])

    def split3(src, b_src, N, sign):
        spl, spf, b_spl, b_spf = cx["spl"], cx["spf"], cx["b_spl"], cx["b_spf"]
        P.op("dve", lambda e: e.tensor_scalar(out=spf[:, 0, 0:N], in0=src, scalar1=sign, scalar2=None, op0=ALU.mult), reads=[b_src], writes=[b_spf])
        for j in range(3):
            P.op("dve", lambda e, j=j: e.tensor_copy(spl[:, j, 0:N], spf[:, 0, 0:N]), reads=[b_spf], writes=[b_spl])
            if j < 2:
                P.op("dve", lambda e, j=j: e.tensor_copy(spf[:, 1, 0:N], spl[:, j, 0:N]), reads=[b_spl], writes=[b_spf])
                P.op("dve", lambda e: e.tensor_tensor(out=spf[:, 0, 0:N], in0=spf[:, 0, 0:N], in1=spf[:, 1, 0:N], op=ALU.subtract), reads=[b_spf], writes=[b_spf])

    m_persist = A.mark()
    adaw = A.alloc([128, 8, 3 * D], F32, "adaw")
    crow = A.alloc([3, D], F32, "crow")
    ccol = A.alloc([128, 8, 3], F32, "ccol")
    vrow = A.alloc([12, R], F32, "vrow")
    adab = A.alloc([1, 2, 3 * D], F32, "adab")
    grow = A.alloc([3, D], F32, "grow")
    npb = A.alloc([128, 2, D], F32, "npb")
    tmpc = A.alloc([128, 8, 3], F32, "tmpc")
    b_crow, b_ccol, b_vrow, b_adaw, b_adab, b_grow, b_npb = [P.buf(n) for n in "crow ccol vrow adaw adab grow npb".split()]
    P.dma("sp", lambda e: e.dma_start(out=crow[:], in_=i_cc[:, :]), writes=[b_crow])
    P.op("pool", lambda e: e.memset(vrow[:], 0.0), writes=[b_vrow])
    P.dma("sp", lambda e: e.dma_start(out=vrow[0:8, :], in_=i_lvec[:, :]), reads=[b_vrow], writes=[b_vrow])
    P.dma("sp", lambda e: e.dma_start(out=vrow[8:10, 0:D], in_=i_npre[:, :]), reads=[b_vrow], writes=[b_vrow])
    P.dma("sp", lambda e: e.dma_start(out=adab[:], in_=i_adab.rearrange("(o l) f -> o l f", o=1)), writes=[b_adab])
    P.dma("sp", lambda e: e.dma_start(out=npb[:], in_=i_npost.rearrange("(o l) f -> o l f", o=1).broadcast_to([128, 2, D])), writes=[b_npb])
    P.dma("sp", lambda e: e.dma_start(out=bfrow[:], in_=i_fbf.broadcast_to([128, H])), writes=[b_const])
    P.op("act", lambda e: e.activation(out=crow[:], in_=crow[:], func=AF.Silu), reads=[b_crow], writes=[b_crow])
    pa, ba = bank(0)

    def tr_c(e):
        ins = None
        for kc in range(8):
            ins = e.transpose(out=pa[:, kc * 3:kc * 3 + 3], in_=crow[0:3, kc * 128:(kc + 1) * 128], identity=identf[0:3, 0:3])
        return ins
    P.op("pe", tr_c, reads=[b_crow, b_const], writes=[ba])
    P.op("dve", lambda e: e.tensor_copy(ccol[:].rearrange("p k s -> p (k s)"), pa[:, 0:24]), reads=[ba], writes=[b_ccol])
    pb_, bb = bank(1)

    def tr_v(e):
        ins = None
        for c in range(NCH):
            ins = e.transpose(out=pb_[:, c * 10:c * 10 + 10], in_=vrow[0:10, c * 128:(c + 1) * 128], identity=identf[0:10, 0:10])
        return ins
    P.op("pe", tr_v, reads=[b_vrow, b_const], writes=[bb])
    vcol = A.alloc([128, NCH, 10], F32, "vcol")
    b_vcol = P.buf("vcol")
    P.op("dve", lambda e: e.tensor_copy(vcol[:].rearrange("p c v -> p (c v)"), pb_[:, 0:NCH * 10]), reads=[bb], writes=[b_vcol])
    P.op("dve", lambda e: e.tensor_copy(lcol[:, :, 0:8], vcol[:, :, 0:8]), reads=[b_vcol], writes=[b_const])
    P.op("act", lambda e: e.activation(out=lcol[:, :, 8], in_=vcol[:, :, 7], func=AF.Exp, scale=-1.0), reads=[b_vcol], writes=[b_const])
    P.op("act", lambda e: e.activation(out=lcol[:, :, 8], in_=lcol[:, :, 8], func=AF.Ln, bias=onec[:, 0:1]), reads=[b_const], writes=[b_const])
    P.op("dve", lambda e: e.tensor_scalar(out=lcol[:, :, 8], in0=lcol[:, :, 8], scalar1=-8.0, scalar2=None, op0=ALU.mult), reads=[b_const], writes=[b_const])
    pc_, bc = bank(2)
    P.op("pe", lambda e: e.matmul(pc_[0:16, 0:1], lhsT=bfrow[0:1, 0:16], rhs=ones_f[0:1, 0:1], start=True, stop=True), reads=[b_const], writes=[bc])
    P.op("dve", lambda e: e.tensor_scalar(out=bfcol[:, 0:1], in0=pc_[0:16, 0:1], scalar1=-1.0, scalar2=None, op0=ALU.mult), reads=[bc], writes=[b_const])
    for l in range(2):
        for q4 in range(4):
            P.dma("sp" if q4 % 2 == 0 else "act",
                  lambda e, l=l, q4=q4: e.dma_start(out=adaw[:, 2 * q4:2 * q4 + 2, :], in_=i_adaw[l, q4 * 256:(q4 + 1) * 256, :].rearrange("(k p) f -> p k f", p=128)),
                  writes=[b_adaw])
        pm, bm = bank(4 + l * 2)

        def ada_cols(e, l=l, pm=pm):
            ins = None
            for fc in range(16):
                for kc in range(8):
                    ins = e.matmul(pm[:, fc * 3:fc * 3 + 3], lhsT=adaw[:, kc, fc * 128:(fc + 1) * 128], rhs=ccol[:, kc, :], start=(kc == 0), stop=False)
                ins = e.matmul(pm[:, fc * 3:fc * 3 + 3], lhsT=adab[0:1, l, fc * 128:(fc + 1) * 128], rhs=ones_f[0:1, 0:3], start=False, stop=True)
            return ins
        P.op("pe", ada_cols, reads=[b_adaw, b_ccol, b_adab, b_const], writes=[bm])
        P.op("dve", lambda e, l=l, pm=pm: e.tensor_copy(gcol[:, l, 1, :, :].rearrange("p k s -> p (k s)"), pm[:, 0:24]), reads=[bm], writes=[b_ada])
        P.op("dve", lambda e, l=l, pm=pm: e.tensor_scalar(out=tmpc[:].rearrange("p k s -> p (k s)"), in0=pm[:, 24:48], scalar1=1.0, scalar2=None, op0=ALU.add), reads=[bm], writes=[b_grow])
        for s in range(3):
            P.op("dve", lambda e, l=l, s=s: e.tensor_tensor(out=gcol[:, l, 0, :, s], in0=tmpc[:, :, s], in1=vcol[0:128, 0:8, 8 + l], op=ALU.mult), reads=[b_grow, b_vcol], writes=[b_ada])
        for hf in range(2):
            prh, brh = bank(5 + l * 2) if hf == 0 else bank(3)

            def ada_rowh(e, l=l, hf=hf, prh=prh):
                ins = None
                for kc in range(8):
                    ins = e.matmul(prh[0:3, 0:512], lhsT=ccol[:, kc, :], rhs=adaw[:, kc, 2 * D + hf * 512:2 * D + (hf + 1) * 512], start=(kc == 0), stop=False)
                ins = e.matmul(prh[0:3, 0:512], lhsT=ones_f[0:1, 0:3], rhs=adab[0:1, l, 2 * D + hf * 512:2 * D + (hf + 1) * 512], start=False, stop=True)
                return ins
            P.op("pe", ada_rowh, reads=[b_adaw, b_ccol, b_adab, b_const], writes=[brh])
            P.op("dve", lambda e, hf=hf, prh=prh: e.tensor_copy(grow[:, hf * 512:(hf + 1) * 512], prh[0:3, 0:512]), reads=[brh], writes=[b_grow])
        for g in range(2):
            for hf in range(2):
                pg, bg = bank(0 + hf)
                P.op("pe", lambda e, g=g, hf=hf, pg=pg: e.matmul(pg[:, 0:512], lhsT=sel[0:3, g, :], rhs=grow[0:3, hf * 512:(hf + 1) * 512], start=True, stop=True), reads=[b_grow, b_const], writes=[bg])
                P.op("dve", lambda e, g=g, hf=hf, pg=pg, l=l: e.tensor_tensor(out=gprow[:, l, g, hf * 512:(hf + 1) * 512], in0=pg[:, 0:512], in1=npb[:, l, hf * 512:(hf + 1) * 512], op=ALU.mult), reads=[bg, b_npb], writes=[b_ada])
    P.barrier()
    A.reset(m_persist)
    if dbg == 32:
        P.dma("sp", lambda e: e.dma_start(out=o_fkp[2048:2176, 0:96], in_=gcol[:].rearrange("p a b c d -> p (a b c d)")), reads=[b_ada])
        P.dma("sp", lambda e: e.dma_start(out=o_fkp[2176:2304, 0:2 * 2 * D], in_=gprow[:].rearrange("p a b c -> p (a b c)")), reads=[b_ada]) if False else None
    if stop == 0:
        P.emit()
        return nc

    win = A.alloc([128, 8, 2 * R], BF16, "win")
    wg = A.alloc([128, 2, NCH, 128], BF16, "wg")
    wout = A.alloc([128, NCH, D], BF16, "wout")
    b_w = P.buf("weights")
    for kc in range(8):
        for hf in range(2):
            P.dma("pool", lambda e, kc=kc, hf=hf: e.dma_start(out=win[:, kc, hf * R:(hf + 1) * R], in_=i_win[kc * 128:(kc + 1) * 128, hf * R:(hf + 1) * R]), writes=[b_w])
    P.dma("pool", lambda e: e.dma_start(out=wg[:, 0, :, :], in_=i_wa.rearrange("n d e -> d n e")), writes=[b_w])
    P.dma("pool", lambda e: e.dma_start(out=wg[:, 1, :, :], in_=i_wx.rearrange("n d e -> d n e")), writes=[b_w])
    for c in range(NCH):
        P.dma("pool", lambda e, c=c: e.dma_start(out=wout[:, c, :], in_=i_wout[c * 128:(c + 1) * 128, :]), writes=[b_w])
    CG = 2
    NG = NCH // CG
    ENG_CAST = "pool" if dbg & 1024 else "dve"
    ENG_A2 = "pool" if dbg & 2048 else "dve"
    ENG_IM = "pool" if dbg & 4096 else "dve"
    xt_a1 = A.alloc([128, 4, D], F32, "xt_a1")
    cx["xn"] = A.alloc([128, D], F32, "xn")
    cx["junk"] = A.alloc([128, D], BF16, "junk")
    cx["stat"] = A.alloc([128, 16], F32, "stat")
    cx["hT"] = A.alloc([128, 8, TT], BF16, "hT")
    nb("xn", "junk", "stat", "hT")
    cx["ytmp"] = cx["xn"]
    cx["b_ytmp"] = cx["b_xn"]
    halo = A.alloc([128, NCH, 2, 4], F32, "halo")
    hst = A.alloc([128, NCH, 2], F32, "hst")
    XBW = 520
    sets = []
    for k_ in range(2):
        S_ = dict(xb=A.alloc([128, CG, XBW], F32, "xb"), sg=A.alloc([128, CG, TT], BF16, "sg"), xc=A.alloc([128, CG, TT], F32, "xc"),
                  xcb=A.alloc([128, CG, TT], BF16, "xcb"), rr=A.alloc([128, CG, TT], F32, "rr"), ig=A.alloc([128, CG, TT], F32, "ig"),
                  aa=A.alloc([128, CG, TT], F32, "aa"))
        for n_ in ("xb", "xbh", "sg", "xc", "xcb", "rr", "ig", "aa"):
            S_["b_" + n_] = P.buf("%s%d" % (n_, k_))
        sets.append(S_)
    zT = A.alloc([128, NCH, TT], BF16, "zT")
    b_xts = [P.buf("xt%d" % s_) for s_ in range(4)]
    b_halo = [P.buf("halo%d" % c_) for c_ in range(NCH)]
    b_hst, b_zT = P.buf("hst"), P.buf("zT")

    def layer0(N, nseg, L, segw):
        hT, b_hT = cx["hT"], cx["b_hT"]

        def xbv(S, cl, sgm, a_, b__):
            return S["xb"][:, cl, sgm * segw + a_:sgm * segw + b__]

        def stageA_pe(g):
            for cl in range(CG):
                c = g * CG + cl
                pxa, bxa = bank(4 + cl * 2)
                pga, bga = bank(5 + cl * 2)

                def mm(e, c=c, pxa=pxa, pga=pga):
                    ins = None
                    for kc in range(8):
                        ins = e.matmul(pxa[:, 0:N], lhsT=win[:, kc, c * 128:(c + 1) * 128], rhs=hT[:, kc, 0:N], start=(kc == 0), stop=(kc == 7))
                    for kc in range(8):
                        ins = e.matmul(pga[:, 0:N], lhsT=win[:, kc, R + c * 128:R + (c + 1) * 128], rhs=hT[:, kc, 0:N], start=(kc == 0), stop=(kc == 7))
                    return ins
                P.op("pe", mm, reads=[b_hT, b_w], writes=[bxa, bga])

        def stageA_el(g):
            S = sets[g % 2]
            for cl in range(CG):
                c = g * CG + cl
                pxa, bxa = bank(4 + cl * 2)
                pga, bga = bank(5 + cl * 2)
                P.op("act", lambda e, S=S, cl=cl, pga=pga: e.activation(out=S["sg"][:, cl, 0:N], in_=pga[:, 0:N], func=AF.Silu), reads=[bga], writes=[S["b_sg"]])
                for sgm in range(nseg):
                    P.op("dve", lambda e, S=S, cl=cl, sgm=sgm, pxa=pxa: e.tensor_copy(xbv(S, cl, sgm, 3, 3 + L), pxa[:, sgm * L:(sgm + 1) * L]), reads=[bxa], writes=[S["b_xb"]])
                    P.op("pool", lambda e, S=S, c=c, cl=cl, sgm=sgm: e.tensor_copy(xbv(S, cl, sgm, 0, 3), halo[:, c, sgm, 0:3]), reads=[b_halo[c]], writes=[S["b_xbh"]])
            for cl in range(CG):
                c = g * CG + cl
                for sgm in range(nseg):
                    o = S["xc"][:, cl, sgm * L:(sgm + 1) * L]
                    P.op("dve", lambda e, S=S, c=c, cl=cl, sgm=sgm, o=o: e.tensor_scalar(out=o, in0=xbv(S, cl, sgm, 0, L), scalar1=lcol[:, c, 0:1], scalar2=lcol[:, c, 4:5], op0=ALU.mult, op1=ALU.add),
                         reads=[S["b_xb"], S["b_xbh"], b_const], writes=[S["b_xc"]])
                    for k in range(1, 4):
                        P.op("dve", lambda e, S=S, c=c, cl=cl, sgm=sgm, o=o, k=k: e.scalar_tensor_tensor(out=o, in0=xbv(S, cl, sgm, k, k + L), scalar=lcol[:, c, k:k + 1], in1=o, op0=ALU.mult, op1=ALU.add),
                             reads=[S["b_xb"], S["b_xbh"], b_const, S["b_xc"]], writes=[S["b_xc"]])
                    P.op("pool", lambda e, S=S, c=c, cl=cl, sgm=sgm: e.tensor_copy(halo[:, c, sgm, 0:3], xbv(S, cl, sgm, L, L + 3)), reads=[S["b_xb"]], writes=[b_halo[c]])
                P.op(ENG_CAST, lambda e, S=S, cl=cl: e.tensor_copy(S["xcb"][:, cl, 0:N], S["xc"][:, cl, 0:N]), reads=[S["b_xc"]], writes=[S["b_xcb"]])

        def stageB_pe(g):
            S = sets[g % 2]
            xcb = S["xcb"]
            for cl in range(CG):
                c = g * CG + cl
                pra, bra = bank(cl * 2)
                pia, bia = bank(cl * 2 + 1)

                def mg(e, c=c, cl=cl, pra=pra, pia=pia, xcb=xcb):
                    e.matmul(pra[:, 0:N], lhsT=wg[:, 0, c, :], rhs=xcb[:, cl, 0:N], start=True, stop=True)
                    return e.matmul(pia[:, 0:N], lhsT=wg[:, 1, c, :], rhs=xcb[:, cl, 0:N], start=True, stop=True)
                P.op("pe", mg, reads=[S["b_xcb"], b_w], writes=[bra, bia])

        def stageB_el(g):
            S = sets[g % 2]
            rr, ig, aa, xc, sg, xcb = S["rr"], S["ig"], S["aa"], S["xc"], S["sg"], S["xcb"]
            for cl in range(CG):
                c = g * CG + cl
                pra, bra = bank(cl * 2)
                pia, bia = bank(cl * 2 + 1)
                P.op("act", lambda e, c=c, cl=cl, pra=pra, rr=rr: e.activation(out=rr[:, cl, 0:N], in_=pra[:, 0:N], func=AF.Sigmoid, bias=lcol[:, c, 5:6]), reads=[bra, b_const], writes=[S["b_rr"]])
                P.op("act", lambda e, c=c, cl=cl, pia=pia, ig=ig: e.activation(out=ig[:, cl, 0:N], in_=pia[:, 0:N], func=AF.Sigmoid, bias=lcol[:, c, 6:7]), reads=[bia, b_const], writes=[S["b_ig"]])
            for cl in range(CG):
                c = g * CG + cl
                P.op("act", lambda e, c=c, cl=cl, rr=rr, aa=aa: e.activation(out=aa[:, cl, 0:N], in_=rr[:, cl, 0:N], func=AF.Exp, scale=lcol[:, c, 8:9]), reads=[S["b_rr"], b_const], writes=[S["b_aa"]])
                P.op(ENG_A2, lambda e, cl=cl, rr=rr, aa=aa: e.tensor_tensor(out=rr[:, cl, 0:N], in0=aa[:, cl, 0:N], in1=aa[:, cl, 0:N], op=ALU.mult), reads=[S["b_aa"], S["b_rr"]], writes=[S["b_rr"]])
                P.op("pool", lambda e, cl=cl, ig=ig, xc=xc: e.tensor_tensor(out=ig[:, cl, 0:N], in0=ig[:, cl, 0:N], in1=xc[:, cl, 0:N], op=ALU.mult), reads=[S["b_ig"], S["b_xc"]], writes=[S["b_ig"]])
            for cl in range(CG):
                c = g * CG + cl
                P.op("act", lambda e, cl=cl, rr=rr: e.activation(out=rr[:, cl, 0:N], in_=rr[:, cl, 0:N], func=AF.Sqrt, scale=-1.0, bias=onec[:, 0:1]), reads=[S["b_rr"], b_const], writes=[S["b_rr"]])
                P.op(ENG_IM, lambda e, cl=cl, ig=ig, rr=rr: e.tensor_tensor(out=ig[:, cl, 0:N], in0=ig[:, cl, 0:N], in1=rr[:, cl, 0:N], op=ALU.mult), reads=[S["b_ig"], S["b_rr"]], writes=[S["b_ig"]])
                for sgm in range(nseg):
                    P.op("dve", lambda e, c=c, cl=cl, sgm=sgm, xc=xc, aa=aa, ig=ig: e.tensor_tensor_scan(out=xc[:, cl, sgm * L:(sgm + 1) * L], data0=aa[:, cl, sgm * L:(sgm + 1) * L], data1=ig[:, cl, sgm * L:(sgm + 1) * L], initial=hst[:, c, sgm:sgm + 1], op0=ALU.mult, op1=ALU.add),
                         reads=[S["b_aa"], S["b_ig"], b_hst, S["b_xc"]], writes=[S["b_xc"]])
                    P.op("dve", lambda e, c=c, cl=cl, sgm=sgm, xc=xc: e.tensor_copy(hst[:, c, sgm:sgm + 1], xc[:, cl, (sgm + 1) * L - 1:(sgm + 1) * L]), reads=[S["b_xc"], b_hst], writes=[b_hst])
                P.op("pool", lambda e, c=c, cl=cl, xc=xc, sg=sg: e.tensor_tensor(out=zT[:, c, 0:N], in0=xc[:, cl, 0:N], in1=sg[:, cl, 0:N], op=ALU.mult), reads=[S["b_xc"], S["b_sg"]], writes=[b_zT])

        stageA_pe(0)
        stageA_el(0)
        for g in range(NG):
            if not (dbg & 8192):
                if g + 1 < NG:
                    stageA_pe(g + 1)
                    stageA_el(g + 1)
                stageB_pe(g)
                stageB_el(g)
                continue
            stageB_pe(g)
            if g + 1 < NG:
                stageA_pe(g + 1)
            stageB_el(g)
            if g + 1 < NG:
                stageA_el(g + 1)

    P.op("pool", lambda e: e.memset(halo[:], 0.0), writes=b_halo)
    P.op("pool", lambda e: e.memset(hst[:], 0.0), writes=[b_hst])
    for i in range(nt_prompt):
        for s4 in range(4):
            P.dma("sp", lambda e, i=i, s4=s4: e.dma_start(out=xt_a1[:, s4, :], in_=i_xp[i * TT + s4 * 128:i * TT + (s4 + 1) * 128, :]), writes=[b_xts[s4]])
        front(TT, 4, 128, 0, [(0, TT, 0)], xt_a1, b_xts)
        layer0(TT, 1, TT, 0)
        post(TT, 4, 128, 0, 0, wout, b_w, NCH, zT, b_zT, 128, xt_a1, b_xts)
        for s4 in range(4):
            P.dma("pool", lambda e, i=i, s4=s4: e.dma_start(out=s_x1[i * TT + s4 * 128:i * TT + (s4 + 1) * 128, :], in_=xt_a1[:, s4, :]), reads=[b_xts[s4]], writes=[b_sx1])
        if dbg == 31 and i < NQ:
            P.dma("sp", lambda e, i=i: e.dma_start(out=o_yp[i * TT:(i + 1) * TT, :].rearrange("(s p) f -> p s f", p=128), in_=xt_a1[:]), reads=b_xts)
    P.dma("sp", lambda e: e.dma_start(out=o_lhp.rearrange("(c p) -> p c", p=128), in_=hst[:, :, 0], allow_slow_non_contiguous=True), reads=[b_hst])
    for k3 in range(3):
        P.dma("sp", lambda e, k3=k3: e.dma_start(out=o_lcp[k3].rearrange("(c p) -> p c", p=128), in_=halo[:, :, 0, k3], allow_slow_non_contiguous=True), reads=b_halo)
    sc = [(0, ST, 1), (ST, 2 * ST, 2)]
    if do_sample:
        for sq in range(2):
            P.dma("sp", lambda e, sq=sq: e.dma_start(out=hst[:, :, sq], in_=i_sth[sq].rearrange("(c p) -> p c", p=128), allow_slow_non_contiguous=True), reads=[b_hst], writes=[b_hst])
            for k3 in range(3):
                P.dma("sp", lambda e, sq=sq, k3=k3: e.dma_start(out=halo[:, :, sq, k3], in_=i_stc[sq, k3].rearrange("(c p) -> p c", p=128), allow_slow_non_contiguous=True), reads=b_halo, writes=b_halo)
        P.dma("sp", lambda e: e.dma_start(out=xs_t[0:64, 0, :], in_=i_xs[:, :]), writes=[b_xs])
        front(2 * ST, 1, 64, 0, sc, xs_t, b_xs)
        layer0(2 * ST, 2, ST, 40)
        post(2 * ST, 1, 64, 0, 1, wout, b_w, NCH, zT, b_zT, 128, xs_t, b_xs)
        for sq in range(2):
            P.dma("sp", lambda e, sq=sq: e.dma_start(out=o_lhs[sq].rearrange("(c p) -> p c", p=128), in_=hst[:, :, sq], allow_slow_non_contiguous=True), reads=[b_hst])
            for k3 in range(3):
                P.dma("sp", lambda e, sq=sq, k3=k3: e.dma_start(out=o_lcs[sq, k3].rearrange("(c p) -> p c", p=128), in_=halo[:, :, sq, k3], allow_slow_non_contiguous=True), reads=b_halo)
    P.barrier()
    A.reset(m_persist)
    if stop == 1:
        P.emit()
        return nc

    fwin = A.alloc([128, 8, 2 * D + H], BF16, "fwin")
    b_w = P.buf("weights2")
    for kc in range(8):
        for hf in range(2):
            P.dma("pool", lambda e, kc=kc, hf=hf: e.dma_start(out=fwin[:, kc, hf * D:(hf + 1) * D], in_=i_fwin[kc * 128:(kc + 1) * 128, D + hf * D:D + (hf + 1) * D]), writes=[b_w])
        P.dma("pool", lambda e, kc=kc: e.dma_start(out=fwin[:, kc, 2 * D:2 * D + H], in_=i_fwin[kc * 128:(kc + 1) * 128, 4 * D:4 * D + H]), writes=[b_w])
    xt_a2s = [A.alloc([128, 4, D], F32, "xt_a2")]
    xt_a2 = xt_a2s[0]
    cx["xn"] = A.alloc([128, D], F32, "xn")
    cx["junk"] = A.alloc([128, D], BF16, "junk")
    cx["stat"] = A.alloc([128, 16], F32, "stat")
    cx["hT"] = A.alloc([128, 8, TT], BF16, "hT")
    kst = A.alloc([128, 4, D], F32, "kst")
    kstb = A.alloc([128, 4, D], BF16, "kstb")
    b_kstb = P.buf("kstb")
    vst = A.alloc([128, 4, D], F32, "vst")
    v1st = A.alloc([128, H, 4, VS], BF16, "v1st")
    kTst = A.alloc([KA, H, TT], BF16, "kTst")
    lft = A.alloc([128, 4, H], F32, "lft")
    lfT = A.alloc([16, TT], F32, "lfT")
    cumT = A.alloc([16, TT], F32, "cumT")
    ccar = A.alloc([16, 2], F32, "ccar")
    cx["spl"] = A.alloc([16, 3, TT], BF16, "spl")
    cx["spf"] = A.alloc([16, 2, TT], F32, "spf")
    nb("xn", "junk", "stat", "hT", "spl", "spf")
    b_xt, b_kst, b_vst, b_v1st, b_kTst, b_lft, b_lfT, b_cumT, b_ccar = [P.buf(n) for n in range(9)]
    P.op("pool", lambda e: e.memset(v1st[:], 1.0), writes=[b_v1st])
    P.op("pool", lambda e: e.memset(kTst[:], 1.0), writes=[b_kTst])
    P.op("pool", lambda e: e.memset(ccar[:], 0.0), writes=[b_ccar])

    def ktrans(N, nsub, Pt, cast=False):
        if cast:
            P.op("pool", lambda e: e.tensor_copy(kstb[0:Pt, 0:nsub, :], kst[0:Pt, 0:nsub, :]), reads=[b_kst], writes=[b_kstb])
        for h in range(H):
            pk, bk = bank(h % 2)

            def trk(e, h=h, pk=pk):
                ins = None
                for s in range(nsub):
                    ins = e.matmul(pk[0:64, s * Pt:(s + 1) * Pt], lhsT=kstb[0:Pt, s, h * 64:(h + 1) * 64], rhs=ident_bf[0:Pt, 0:Pt], start=True, stop=True)
                return ins
            P.op("pe", trk, reads=[b_kstb, b_const], writes=[bk])
            if h % 2 == 0:
                P.op("act", lambda e, h=h, pk=pk: e.activation(out=kTst[0:64, h, 0:N], in_=pk[0:64, 0:N], func=AF.Identity), reads=[bk], writes=[b_kTst])
            else:
                P.op("dve", lambda e, h=h, pk=pk: e.tensor_copy(kTst[0:64, h, 0:N], pk[0:64, 0:N]), reads=[bk], writes=[b_kTst])

    def ckrows(N):
        spl, b_spl = cx["spl"], cx["b_spl"]
        for j in range(3):
            P.dma("sp", lambda e, j=j: e.dma_start(out=kTst[67 + j:68 + j, :, 0:N], in_=spl[:, j, 0:N]), reads=[b_spl, b_kTst], writes=[b_kTst])

    def kvproj(N, nsub, Pt, nseg, L, x1src, b_x1, seqcols, o_fk, o_fv, o_fl, tok0, kT_dsts, vv_dst, cum_dst):
        front(N, nsub, Pt, 1, seqcols, x1src, b_x1)
        if dbg == 33:
            for q_ in range(2):
                P.dma("pool", lambda e, q_=q_: e.dma_start(out=o_fvp[2048:2176, q_ * 2048:(q_ + 1) * 2048].rearrange("p (k n) -> p k n", k=4) if False else o_fvp[2048 + q_ * 128:2176 + q_ * 128, 0:1024].rearrange("p (k n) -> p k n", k=4)[:, :, 0:N // 2 if False else 256], in_=cx["hT"][:, q_ * 4:(q_ + 1) * 4, 0:256]), reads=[cx["b_hT"]])
            return
        if dbg == 1:
            return
        hT, b_hT = cx["hT"], cx["b_hT"]
        for s in range(nsub):
            for which, (wofs, stg, b_stg, o_d) in enumerate([(0, kst, b_kst, o_fk), (D, vst, b_vst, o_fv)]):
                pp = PS[2 + which]
                bpp = bPS[2 + which]

                def mm(e, s=s, pp=pp, wofs=wofs):
                    ins = None
                    for hf in range(2):
                        for kc in range(8):
                            ins = e.matmul(pp[0:Pt, hf * 512:(hf + 1) * 512], lhsT=hT[:, kc, s * Pt:(s + 1) * Pt], rhs=fwin[:, kc, wofs + hf * 512:wofs + (hf + 1) * 512], start=(kc == 0), stop=(kc == 7))
                    return ins
                P.op("pe", mm, reads=[b_hT, b_w], writes=bpp)
                P.op("act", lambda e, s=s, pp=pp, stg=stg: e.activation(out=stg[0:Pt, s, :], in_=pp[0:Pt, :], func=AF.Identity), reads=bpp, writes=[b_stg])
                if which == 0:
                    P.op("dve", lambda e, s=s: e.tensor_copy(kstb[0:Pt, s, :], kst[0:Pt, s, :]), reads=[b_kst], writes=[b_kstb])
                if which == 1 and dbg != 21:
                    if dbg == 22:
                        P.op("act", lambda e, s=s, pp=pp: e.activation(out=v1st[0:Pt, :, s, 0:64], in_=pp[0:Pt, :].rearrange("p (h d) -> p h d", h=H), func=AF.Identity), reads=bpp, writes=[b_v1st])
                    elif dbg == 23:
                        P.op("pool", lambda e, s=s: e.tensor_copy(v1st[0:Pt, :, s, 0:64], vst[0:Pt, s, :].rearrange("p (h d) -> p h d", h=H)), reads=[b_vst], writes=[b_v1st])
                    else:
                        P.op("dve", lambda e, s=s: e.tensor_copy(v1st[0:Pt, :, s, 0:64], vst[0:Pt, s, :].rearrange("p (h d) -> p h d", h=H)), reads=[b_vst], writes=[b_v1st])
                P.dma("sp", lambda e, s=s, stg=stg, o_d=o_d: e.dma_start(out=o_d[tok0 + s * Pt:tok0 + (s + 1) * Pt, :], in_=stg[0:Pt, s, :]), reads=[b_stg])
            if dbg in (2, 21, 22, 23):
                continue
            pf, bf_ = bank(0)

            def mfl(e, s=s, pf=pf):
                ins = None
                for kc in range(8):
                    ins = e.matmul(pf[0:Pt, 0:H], lhsT=hT[:, kc, s * Pt:(s + 1) * Pt], rhs=fwin[:, kc, 2 * D:2 * D + H], start=(kc == 0), stop=(kc == 7))
                return ins
            P.op("pe", mfl, reads=[b_hT, b_w], writes=[bf_])
            P.op("dve", lambda e, s=s, pf=pf: e.tensor_tensor(out=lft[0:Pt, s, :], in0=pf[0:Pt, 0:H], in1=bfrow[0:Pt, :], op=ALU.add), reads=[bf_, b_const], writes=[b_lft])
        if dbg in (2, 21, 22, 23):
            return
        P.op("act", lambda e: e.activation(out=lft[0:Pt, 0:nsub, :], in_=lft[0:Pt, 0:nsub, :], func=AF.Exp, scale=-1.0), reads=[b_lft], writes=[b_lft])
        P.op("act", lambda e: e.activation(out=lft[0:Pt, 0:nsub, :], in_=lft[0:Pt, 0:nsub, :], func=AF.Ln, bias=onec[0:Pt, 0:1]), reads=[b_lft, b_const], writes=[b_lft])
        P.op("dve", lambda e: e.tensor_scalar(out=lft[0:Pt, 0:nsub, :], in0=lft[0:Pt, 0:nsub, :], scalar1=-1.0, scalar2=None, op0=ALU.mult), reads=[b_lft], writes=[b_lft])
        with nc.allow_non_contiguous_dma(reason="64B rows"):
            P.dma("sp", lambda e: e.dma_start(out=o_fl[tok0:tok0 + nsub * Pt, :].rearrange("(s p) h -> p s h", p=Pt), in_=lft[0:Pt, 0:nsub, :], allow_slow_non_contiguous=True), reads=[b_lft])
        if dbg == 3:
            return
        pf, bf_ = bank(1)

        def mflT(e, pf=pf):
            ins = None
            for kc in range(8):
                ins = e.matmul(pf[0:H, 0:N], lhsT=fwin[:, kc, 2 * D:2 * D + H], rhs=hT[:, kc, 0:N], start=(kc == 0), stop=(kc == 7))
            return ins
        P.op("pe", mflT, reads=[b_hT, b_w], writes=[bf_])
        P.op("act", lambda e, pf=pf: e.activation(out=lfT[:, 0:N], in_=pf[0:H, 0:N], func=AF.Exp, scale=-1.0, bias=bfcol[:, 0:1]), reads=[bf_, b_const], writes=[b_lfT])
        P.op("act", lambda e: e.activation(out=lfT[:, 0:N], in_=lfT[:, 0:N], func=AF.Ln, bias=onec[0:16, 0:1]), reads=[b_lfT, b_const], writes=[b_lfT])
        for sgm in range(nseg):
            P.op("dve", lambda e, sgm=sgm: e.tensor_tensor_scan(out=cumT[:, sgm * L:(sgm + 1) * L], data0=ones_f[0:16, 0:L], data1=lfT[:, sgm * L:(sgm + 1) * L], initial=ccar[:, sgm:sgm + 1], op0=ALU.mult, op1=ALU.subtract),
                 reads=[b_lfT, b_ccar, b_const, b_cumT], writes=[b_cumT])
            P.op("dve", lambda e, sgm=sgm: e.tensor_copy(ccar[:, sgm:sgm + 1], cumT[:, (sgm + 1) * L - 1:(sgm + 1) * L]), reads=[b_cumT, b_ccar], writes=[b_ccar])
        if cum_dst is not None:
            P.dma("sp", lambda e: e.dma_start(out=cum_dst, in_=cumT[:, 0:N]), reads=[b_cumT])
        if dbg == 4:
            return
        split3(cumT[:, 0:N], b_cumT, N, -1.0)
        if dbg == 5:
            return
        ktrans(N, nsub, Pt)
        if dbg == 6:
            return
        ckrows(N)
        if dbg == 7:
            return
        for sgm, kd in enumerate(kT_dsts if not (dbg == 9 and Pt == 64) else []):
            P.dma("sp", lambda e, sgm=sgm, kd=kd: e.dma_start(out=kd, in_=kTst[:, :, sgm * L:(sgm + 1) * L]), reads=[b_kTst])
        for (vd, vsrc) in (vv_dst if not (dbg == 8 and Pt == 64) else []):
            P.dma("sp", lambda e, vd=vd, vsrc=vsrc: e.dma_start(out=vd, in_=vsrc), reads=[b_v1st])

    m_a2 = A.mark()
    xt_a2s.append(A.alloc([128, 4, D], F32, "xt_a2b"))
    hT_a2 = [cx["hT"], A.alloc([128, 8, TT], BF16, "hT2b")]
    b_hT_a2 = [cx["b_hT"], P.buf("hT2b")]
    b_xt2 = [b_xt, P.buf("xt2b")]
    for i in range(nt_prompt):
        xt_a2, b_xt = xt_a2s[i % 2], b_xt2[i % 2]
        cx["hT"], cx["b_hT"] = hT_a2[i % 2], b_hT_a2[i % 2]
        P.dma("pool", lambda e, i=i, xt_a2=xt_a2: e.dma_start(out=xt_a2[:], in_=s_x1[i * TT:(i + 1) * TT, :].rearrange("(s p) f -> p s f", p=128)), reads=[b_sx1], writes=[b_xt])
        if dbg == 34 and i < NQ:
            P.dma("sp", lambda e, i=i, xt_a2=xt_a2: e.dma_start(out=o_yp[i * TT:(i + 1) * TT, :].rearrange("(s p) f -> p s f", p=128), in_=xt_a2[:]), reads=[b_xt])
        m_ = i // 4
        vvd = [(s_vv[:, m_, :, (i % 4) * 4 * VS:(i % 4 + 1) * 4 * VS].rearrange("h p x -> p h x"), v1st[:].rearrange("p h s d -> p h (s d)"))]
        kvproj(TT, 4, 128, 1, TT, xt_a2, b_xt, [(0, TT, 0)], o_fkp, o_fvp, o_flp, i * TT,
               [s_kT[:, :, i * TT:(i + 1) * TT].rearrange("h r n -> r h n")], vvd, s_cum[:, i * TT:(i + 1) * TT])
    cx["hT"], cx["b_hT"] = hT_a2[0], b_hT_a2[0]
    P.barrier()
    A.reset(m_a2)
    if do_sample:
        clf = A.alloc([128, 2, PAST // 128, H], F32, "clf")
        pcum = A.alloc([16, 2, PAST], F32, "pcum")
        b_clf, b_pcum = P.buf("clf"), P.buf("pcum")
        with nc.allow_non_contiguous_dma(reason="64B rows"):
            for sq in range(2):
                P.dma("sp", lambda e, sq=sq: e.dma_start(out=clf[:, sq, :, :], in_=i_clf[sq].rearrange("(j p) h -> p j h", p=128), allow_slow_non_contiguous=True), writes=[b_clf])
        for sq in range(2):
            for q8 in range(PAST // 512):
                pk, bk = bank(q8 % 2)

                def trl(e, sq=sq, q8=q8, pk=pk):
                    ins = None
                    for j in range(4):
                        ins = e.matmul(pk[0:16, j * 128:(j + 1) * 128], lhsT=clf[:, sq, q8 * 4 + j, :], rhs=identf[:, :], start=True, stop=True)
                    return ins
                P.op("pe", trl, reads=[b_clf, b_const], writes=[bk])
                P.op("act", lambda e, sq=sq, q8=q8, pk=pk: e.activation(out=pcum[:, sq, q8 * 512:(q8 + 1) * 512], in_=pk[0:16, 0:512], func=AF.Identity), reads=[bk], writes=[b_pcum])
            for q8 in range(PAST // 512):
                init = 0.0 if q8 == 0 else pcum[:, sq, q8 * 512 - 1:q8 * 512]
                P.op("dve", lambda e, sq=sq, q8=q8, init=init: e.tensor_tensor_scan(out=pcum[:, sq, q8 * 512:(q8 + 1) * 512], data0=ones_f[0:16, 0:512], data1=pcum[:, sq, q8 * 512:(q8 + 1) * 512], initial=init, op0=ALU.mult, op1=ALU.add),
                     reads=[b_pcum, b_const], writes=[b_pcum])
            P.op("dve", lambda e, sq=sq: e.tensor_copy(ccar[:, sq:sq + 1], pcum[:, sq, PAST - 1:PAST]), reads=[b_pcum, b_ccar], writes=[b_ccar])
        for sq in range(2 if dbg != 41 else 0):
            for q8 in range(PAST // 512):
                P.dma("sp", lambda e, sq=sq, q8=q8: e.dma_start(out=kst[:], in_=i_ck[sq, q8 * 512:(q8 + 1) * 512, :].rearrange("(s p) f -> p s f", p=128)), writes=[b_kst])
                P.dma("sp", lambda e, sq=sq, q8=q8: e.dma_start(out=vst[:], in_=i_cv[sq, q8 * 512:(q8 + 1) * 512, :].rearrange("(s p) f -> p s f", p=128)), writes=[b_vst])
                P.op("pool", lambda e: e.tensor_copy(v1st[:, :, :, 0:64], vst[:].rearrange("p s (h d) -> p h s d", h=H)), reads=[b_vst], writes=[b_v1st])
                P.dma("sp", lambda e, sq=sq, q8=q8: e.dma_start(out=s_vvs[sq, :, q8 // 4, :, (q8 % 4) * 4 * VS:(q8 % 4 + 1) * 4 * VS].rearrange("h p x -> p h x"), in_=v1st[:].rearrange("p h s d -> p h (s d)")), reads=[b_v1st])
                ktrans(512, 4, 128, cast=True)
                split3(pcum[:, sq, q8 * 512:(q8 + 1) * 512], b_pcum, 512, -1.0)
                ckrows(512)
                P.dma("sp", lambda e, sq=sq, q8=q8: e.dma_start(out=s_kTs[sq, :, :, q8 * 512:(q8 + 1) * 512].rearrange("h r n -> r h n"), in_=kTst[:]), reads=[b_kTst])
        vvd = [(s_vvs[sq, :, NPR, 0:ST, 0:VS].rearrange("h p x -> p h x"), v1st[sq * ST:(sq + 1) * ST, :, 0, :]) for sq in range(2)]
        if dbg not in (41, 42):
          kvproj(2 * ST, 1, 64, 2, ST, xs_t, b_xs, sc, o_fks, o_fvs, o_fls, 0,
               [s_kTs[sq, :, :, PAST:PAST + ST].rearrange("h r n -> r h n") for sq in range(2)], vvd, None)
        P.op("dve", lambda e: e.tensor_copy(scum[:, :], cumT[:, 0:2 * ST]), reads=[b_cumT], writes=[b_scum])
    P.barrier()
    A.reset(m_persist)
    if not do_attn:
        P.emit()
        return nc

    wq = A.alloc([128, 8, 2 * D], BF16, "wq")
    wo = A.alloc([64, H, D], BF16, "wo")
    pmask = A.alloc([128, 16, TT], BF16, "pmask")
    smask = A.alloc([ST, ST], BF16, "smask")
    b_w2 = P.buf("w2")
    for kc in range(8):
        P.dma("pool", lambda e, kc=kc: e.dma_start(out=wq[:, kc, 0:D], in_=i_fwin[kc * 128:(kc + 1) * 128, 0:D]), writes=[b_w2])
        P.dma("pool", lambda e, kc=kc: e.dma_start(out=wq[:, kc, D:2 * D], in_=i_fwin[kc * 128:(kc + 1) * 128, 3 * D:4 * D]), writes=[b_w2])
    for h in range(H):
        P.dma("pool", lambda e, h=h: e.dma_start(out=wo[:, h, :], in_=i_fwout[h * 64:(h + 1) * 64, :]), writes=[b_w2])
    P.dma("sp", lambda e: e.dma_start(out=pmask[:], in_=i_pmask[:, :, :]), writes=[b_w2])
    P.dma("sp", lambda e: e.dma_start(out=smask[:], in_=i_smask[:, :]), writes=[b_w2])
    xt_c = A.alloc([128, 4, D], F32, "xt2")
    cx["xn"] = A.alloc([128, D], F32, "xn")
    cx["junk"] = A.alloc([128, D], BF16, "junk")
    cx["stat"] = A.alloc([128, 16], F32, "stat")
    cx["hT"] = A.alloc([128, 8, TT], BF16, "hT")
    cx["spl"] = A.alloc([16, 3, TT], BF16, "spl")
    cx["spf"] = A.alloc([16, 2, TT], F32, "spf")
    nb("xn", "junk", "stat", "hT", "spl", "spf")
    cx["ytmp"] = cx["xn"]
    cx["b_ytmp"] = cx["b_xn"]
    xl = cx["xn"]
    qaug = A.alloc([KA, H, TT], BF16, "qaug")
    sgT = A.alloc([64, H, TT], BF16, "sgT")
    zT2 = sgT
    csel = A.alloc([16, 2, TT], F32, "csel")
    NKB = 2
    kch = [A.alloc([KA, 2048], BF16, "kch") for _ in range(NKB)]
    vch = [A.alloc([128, 16 * VS], BF16, "vch") for _ in range(NKB)]
    NSG = 4
    pT = [A.alloc([128, 512], BF16, "pT") for _ in range(NSG)]
    osb = A.alloc([65, TT], F32, "osb")
    rl = A.alloc([65, TT], F32, "rl")
    b_xt, b_xl, b_qaug, b_sgT, b_zT2, b_csel, b_osb, b_rl = [P.buf(n) for n in range(8)]
    b_xl = cx["b_xn"]
    b_zT2 = b_sgT
    b_kch = [P.buf("kch") for _ in range(NKB)]
    b_vch = [P.buf("vch") for _ in range(NKB)]
    b_pT = [P.buf("pT") for _ in range(NSG)]
    P.op("pool", lambda e: e.memset(qaug[:], 1.0), writes=[b_qaug])
    chunk_ctr = [0]
    grp_ctr = [0]

    def attend(N, nsub, Pt, seqs, x_res, b_xres, layer_grp, o_y, y_tok0):
        flat = []
        item_ctr = [0]
        pending = []
        EPI_DELAY = 3
        for h in range(H):
            for sq in seqs:
                q0, q1 = sq["q0"], sq["q1"]
                nq = q1 - q0
                work = [(k_ap, v_ap, 128, 16, None) for (k_ap, v_ap) in sq["rows"]] + list(sq["diag"])
                first = True
                for (k_src, v_src, nk, ntile, mask_fn) in work:
                    item = dict(k_src=k_src, v_src=v_src, nk=nk, ntile=ntile, cb=None, idx=item_ctr[0])
                    item_ctr[0] += 1
                    G = min(ntile, 512 // nq if nq > 256 else 16)
                    for g0 in range(0, ntile, G):
                        flat.append(dict(h=h, q0=q0, nq=nq, item=item, g0=g0, G=G, nk=nk, mask_fn=mask_fn, first=first, last_of_head=False))
                        first = False
            flat[-1]["last_of_head"] = True

        def emit_S(en):
            it = en["item"]
            h, nk, ntile = en["h"], en["nk"], it["ntile"]
            if it["cb"] is None:
                cb_ = chunk_ctr[0] % NKB
                chunk_ctr[0] += 1
                it["cb"] = cb_
                P.dma("sp", lambda e, cb_=cb_, k_src=it["k_src"], nk=nk, ntile=ntile, h=h: e.dma_start(out=kch[cb_][:, 0:nk * ntile], in_=k_src[h]), writes=[b_kch[cb_]])
                P.dma("sp", lambda e, cb_=cb_, v_src=it["v_src"], nk=nk, ntile=ntile, h=h: e.dma_start(out=vch[cb_][0:nk, 0:ntile * VS], in_=v_src[h]), writes=[b_vch[cb_]])
            cb_ = it["cb"]
            gi = grp_ctr[0] % NSG
            grp_ctr[0] += 1
            en["gi"] = gi
            psg, bpsg = bank(gi)

            def mmS(e, cb_=cb_, g0=en["g0"], psg=psg, h=h, q0=en["q0"], nq=en["nq"], G=en["G"], nk=nk, mask_fn=en["mask_fn"]):
                ins = None
                for j in range(G):
                    kt = g0 + j
                    ins = e.matmul(psg[0:nk, j * nq:(j + 1) * nq], lhsT=kch[cb_][:, kt * nk:(kt + 1) * nk], rhs=qaug[:, h, q0:q0 + nq], start=True, stop=(mask_fn is None))
                    if mask_fn is not None:
                        ins = e.matmul(psg[0:nk, j * nq:(j + 1) * nq], lhsT=ident_bf[0:nk, 0:nk], rhs=mask_fn(kt), start=False, stop=True)
                return ins
            P.op("pe", mmS, reads=[b_kch[cb_], b_qaug, b_w2, b_const], writes=[bpsg])

        def emit_EV(en):
            h, nk, nq, G, gi, q0 = en["h"], en["nk"], en["nq"], en["G"], en["gi"], en["q0"]
            cb_ = en["item"]["cb"]
            psg, bpsg = bank(gi)
            po, bo = bank(6 + (h % 2))
            P.op("act", lambda e, psg=psg, gi=gi, G=G, nq=nq, nk=nk: e.activation(out=pT[gi][0:nk, 0:G * nq], in_=psg[0:nk, 0:G * nq], func=AF.Exp), reads=[bpsg], writes=[b_pT[gi]])

            def mmV(e, cb_=cb_, g0=en["g0"], gi=gi, po=po, q0=q0, nq=nq, G=G, nk=nk, fst=en["first"]):
                ins = None
                for j in range(G):
                    kt = g0 + j
                    ins = e.matmul(po[0:65, q0:q0 + nq], lhsT=vch[cb_][0:nk, kt * VS:kt * VS + 65], rhs=pT[gi][0:nk, j * nq:(j + 1) * nq], start=(fst and j == 0), stop=False, skip_group_check=True)
                return ins
            P.op("pe", mmV, reads=[b_vch[cb_], b_pT[gi]], writes=[bo])
            if en["last_of_head"]:
                P.op("act", lambda e, po=po: e.activation(out=osb[0:65, 0:N], in_=po[0:65, 0:N], func=AF.Identity), reads=[bo], writes=[b_osb])
                P.op("dve", lambda e: e.reciprocal(rl[64:65, 0:N], osb[64:65, 0:N]), reads=[b_osb], writes=[b_rl])
                pending.append([EPI_DELAY, h])

        def emit_epi_tail(h):
            if True:
                pbq, bbq = bank(5)
                P.op("pe", lambda e, pbq=pbq: e.matmul(pbq[0:64, 0:N], lhsT=ones_f[64:65, 0:64], rhs=rl[64:65, 0:N], start=True, stop=True), reads=[b_rl, b_const], writes=[bbq])
                P.op("dve", lambda e, pbq=pbq: e.tensor_tensor(out=osb[0:64, 0:N], in0=osb[0:64, 0:N], in1=pbq[0:64, 0:N], op=ALU.mult), reads=[b_osb, bbq], writes=[b_osb])
                P.op("pool", lambda e, h=h: e.tensor_tensor(out=zT2[:, h, 0:N], in0=osb[0:64, 0:N], in1=sgT[:, h, 0:N], op=ALU.mult), reads=[b_osb, b_sgT], writes=[b_zT2])

        nxt = 0
        for i, en in enumerate(flat):
            while nxt < len(flat) and nxt <= i + NSG - 1 and flat[nxt]["item"]["idx"] <= en["item"]["idx"] + NKB - 1:
                emit_S(flat[nxt])
                nxt += 1
            for p_ in pending:
                p_[0] -= 1
            while pending and pending[0][0] <= 0:
                emit_epi_tail(pending.pop(0)[1])
            emit_EV(en)
        while pending:
            emit_epi_tail(pending.pop(0)[1])
        post(N, nsub, Pt, 1, layer_grp, wo, b_w2, H, zT2, b_zT2, 64, x_res, b_xres)
        P.dma("sp", lambda e: e.dma_start(out=o_y[y_tok0:y_tok0 + nsub * Pt, :].rearrange("(s p) f -> p s f", p=Pt), in_=x_res[0:Pt, 0:nsub, :]), reads=[b_xres])

    def qproj(N, seqcols, xsrc, b_x, nsub, Pt, cq_src, b_cq):
        front(N, nsub, Pt, 1, seqcols, xsrc, b_x)
        hT, b_hT = cx["hT"], cx["b_hT"]
        spl, b_spl = cx["spl"], cx["b_spl"]
        for h in range(H):
            pq, bq = bank(6)
            pg, bg = bank(7)

            def mq(e, h=h, pq=pq, pg=pg):
                ins = None
                for kc in range(8):
                    ins = e.matmul(pq[0:64, 0:N], lhsT=wq[:, kc, h * 64:(h + 1) * 64], rhs=hT[:, kc, 0:N], start=(kc == 0), stop=(kc == 7))
                for kc in range(8):
                    ins = e.matmul(pg[0:64, 0:N], lhsT=wq[:, kc, D + h * 64:D + (h + 1) * 64], rhs=hT[:, kc, 0:N], start=(kc == 0), stop=(kc == 7))
                return ins
            P.op("pe", mq, reads=[b_hT, b_w2], writes=[bq, bg])
            P.op("dve", lambda e, h=h, pq=pq: e.tensor_scalar(out=qaug[0:64, h, 0:N], in0=pq[0:64, 0:N], scalar1=0.125, scalar2=None, op0=ALU.mult), reads=[bq], writes=[b_qaug])
            P.op("act", lambda e, h=h, pg=pg: e.activation(out=sgT[:, h, 0:N], in_=pg[0:64, 0:N], func=AF.Silu), reads=[bg], writes=[b_sgT])
        split3(cq_src, b_cq, N, 1.0)
        for j in range(3):
            P.dma("sp", lambda e, j=j: e.dma_start(out=qaug[64 + j:65 + j, :, 0:N], in_=spl[:, j, 0:N]), reads=[b_spl, b_qaug], writes=[b_qaug])

    if do_sample:
        qproj(2 * ST, sc, xs_t, b_xs, 1, 64, scum[:, :], b_scum)
        seqs = []
        for sq in range(2):
            rows = [(s_kTs[sq, :, :, rw * 2048:(rw + 1) * 2048], s_vvs[sq, :, rw, :, :]) for rw in range(NPR)]
            diag = [(s_kTs[sq, :, :, PAST:PAST + ST], s_vvs[sq, :, NPR, 0:ST, 0:VS], ST, 1, (lambda kt: smask[:, :]))]
            seqs.append(dict(q0=sq * ST, q1=(sq + 1) * ST, rows=rows, diag=diag))
        attend(2 * ST, 1, 64, seqs, xs_t, b_xs, 1, o_ys, 0)
    for m in range(nq_tiles):
        for r_ in range(4):
            i = 4 * m + r_
            P.dma("sp", lambda e, i=i: e.dma_start(out=csel[:, 1, :], in_=s_cum[:, i * TT:(i + 1) * TT]), writes=[b_csel])
            if r_ == 0:
                P.op("dve", lambda e: e.tensor_scalar(out=csel[:, 0, :], in0=csel[:, 1, :], scalar1=onehot[0:16, 0:1], scalar2=None, op0=ALU.mult), reads=[b_csel, b_const], writes=[b_csel])
            else:
                P.op("dve", lambda e, r_=r_: e.scalar_tensor_tensor(out=csel[:, 0, :], in0=csel[:, 1, :], scalar=onehot[0:16, r_:r_ + 1], in1=csel[:, 0, :], op0=ALU.mult, op1=ALU.add), reads=[b_csel, b_const], writes=[b_csel])
            for s4 in range(4):
                P.dma("act", lambda e, i=i, s4=s4: e.dma_start(out=xl[:, :], in_=s_x1[i * TT + s4 * 128:i * TT + (s4 + 1) * 128, :]), writes=[b_xl])
                if r_ == 0:
                    P.op("dve", lambda e, s4=s4: e.tensor_scalar(out=xt_c[:, s4, :], in0=xl[:, :], scalar1=onehot[:, 0:1], scalar2=None, op0=ALU.mult), reads=[b_xl, b_const], writes=[b_xt])
                else:
                    P.op("dve", lambda e, r_=r_, s4=s4: e.scalar_tensor_tensor(out=xt_c[:, s4, :], in0=xl[:, :], scalar=onehot[:, r_:r_ + 1], in1=xt_c[:, s4, :], op0=ALU.mult, op1=ALU.add), reads=[b_xl, b_const, b_xt], writes=[b_xt])
        qproj(TT, [(0, TT, 0)], xt_c, b_xt, 4, 128, csel[:, 0, :], b_csel)
        rows = [(s_kT[:, :, mm_ * 2048:(mm_ + 1) * 2048], s_vv[:, mm_, :, :]) for mm_ in range(m)]
        diag = [(s_kT[:, :, m * 2048:(m + 1) * 2048], s_vv[:, m, :, :], 128, 16, (lambda kt: pmask[:, kt, :]))]
        attend(TT, 4, 128, [dict(q0=0, q1=TT, rows=rows, diag=diag)], xt_c, b_xt, 0, o_yp, m * TT)
    P.emit()
    return nc


_CACHE = {}


def _get_nc(**kw):
    key = tuple(sorted(kw.items()))
    if key not in _CACHE:
        _CACHE[key] = build(**kw)
    return _CACHE[key]


def _consts(core):
    r = core % 4
    ident = np.eye(128, dtype=np.float32)
    sel = np.zeros((3, 2, 128), np.float32)
    sel[0, 0, :] = 1.0
    sel[1, 1, 0:ST] = 1.0
    sel[2, 1, ST:2 * ST] = 1.0
    kpos = (np.arange(16)[None, :, None] * 128 + np.arange(128)[:, None, None])
    qpos = r * TT + np.arange(TT)[None, None, :]
    pmask = np.where(kpos > qpos, NEG, 0.0).astype(ml_dtypes.bfloat16)
    smask = np.where(np.arange(ST)[:, None] > np.arange(ST)[None, :], NEG, 0.0).astype(ml_dtypes.bfloat16)
    onehot = np.zeros((128, 4), np.float32)
    onehot[:, r] = 1.0
    return dict(c_ident=ident, c_sel=sel, c_pmask=np.ascontiguousarray(pmask), c_smask=smask, c_onehot=onehot)


def _in_map(c, I, shared, T, PAST):
    f = lambda a: np.ascontiguousarray(np.asarray(a, dtype=np.float32))
    b = c // 4
    s0 = 2 * c
    m = dict(shared)
    m.update(_consts(c))
    m["xp"] = f(I["x_prompt"][b])
    m["xs"] = f(I["x_sample"][s0:s0 + 2]).reshape(2 * ST, D)
    m["cc"] = np.concatenate([f(I["c_prompt"])[b:b + 1], f(I["c_sample"])[s0:s0 + 2]], axis=0)
    m["st_h"] = f(I["state_lru_h"][0, s0:s0 + 2])
    m["st_conv"] = f(I["state_lru_conv"][0, s0:s0 + 2])
    m["ck_k"] = f(I["cache_fox_k"][0, s0:s0 + 2]).reshape(2, PAST, D)
    m["ck_v"] = f(I["cache_fox_v"][0, s0:s0 + 2]).reshape(2, PAST, D)
    m["ck_lf"] = f(I["cache_fox_logf"][0, s0:s0 + 2])
    return m


def _shared(I):
    f = lambda a: np.ascontiguousarray(np.asarray(a, dtype=np.float32))
    lvec = np.concatenate([f(I["lru_conv_w"])[0], f(I["lru_conv_b"]), f(I["lru_b_a"]), f(I["lru_b_x"]), f(I["lru_lambda"])], axis=0)
    return dict(norm_pre=f(I["norm_pre"]), norm_post=f(I["norm_post"]), ada_w=f(I["ada_w"]), ada_b=f(I["ada_b"]), lru_w_in=f(I["lru_w_in"])[0],
                lru_vecs=f(lvec), lru_w_a=f(I["lru_w_a"])[0], lru_w_x=f(I["lru_w_x"])[0], lru_w_out=f(I["lru_w_out"])[0],
                fox_w_in=f(I["fox_w_in"])[0], fox_b_f=f(I["fox_b_f"]), fox_w_out=f(I["fox_w_out"])[0])


def run_cores(I, cores, build_kw):
    T = I["x_prompt"].shape[1]
    PAST = I["cache_fox_k"].shape[2]
    kw = dict(build_kw)
    kw.update(Tn=T, PASTn=PAST)
    nc = _get_nc(**kw)
    shared = _shared(I)
    in_maps = [_in_map(c, I, shared, T, PAST) for c in cores]
    res = run_bass_kernel_spmd(nc, in_maps, core_ids=list(range(len(cores)))).results
    return {c: res[i] for i, c in enumerate(cores)}


def kernel(x_prompt, x_sample, c_prompt, c_sample, state_lru_h, state_lru_conv, cache_fox_k, cache_fox_v, cache_fox_logf,
           norm_pre, norm_post, ada_w, ada_b, lru_w_in, lru_conv_w, lru_conv_b, lru_w_a, lru_b_a, lru_w_x, lru_b_x,
           lru_lambda, lru_w_out, fox_w_in, fox_b_f, fox_w_out, _build_kw=None):
    I = dict(x_prompt=x_prompt, x_sample=x_sample, c_prompt=c_prompt, c_sample=c_sample, state_lru_h=state_lru_h, state_lru_conv=state_lru_conv,
             cache_fox_k=cache_fox_k, cache_fox_v=cache_fox_v, cache_fox_logf=cache_fox_logf, norm_pre=norm_pre, norm_post=norm_post,
             ada_w=ada_w, ada_b=ada_b, lru_w_in=lru_w_in, lru_conv_w=lru_conv_w, lru_conv_b=lru_conv_b, lru_w_a=lru_w_a, lru_b_a=lru_b_a,
             lru_w_x=lru_w_x, lru_b_x=lru_b_x, lru_lambda=lru_lambda, lru_w_out=lru_w_out, fox_w_in=fox_w_in, fox_b_f=fox_b_f, fox_w_out=fox_w_out)
    I = {k: np.asarray(v) for k, v in I.items()}
    res = run_cores(I, list(range(8)), _build_kw if _build_kw is not None else {})
    return assemble(res, I)


def assemble(res, I):
    B, T = I["x_prompt"].shape[0], I["x_prompt"].shape[1]
    NQ = T // 2048
    cores = sorted(res.keys())
    y_p = np.zeros((B, T, D), np.float32)
    for c in cores:
        b, r = c // 4, c % 4
        yp = res[c]["y_p"].reshape(NQ, TT, D)
        for m_ in range(NQ):
            i = 4 * m_ + r
            y_p[b, i * TT:(i + 1) * TT] = yp[m_]
    nb = len(cores) // 4 if len(cores) >= 4 else 1
    bs = sorted(set(c // 4 for c in cores))
    first = {b: min(c for c in cores if c // 4 == b) for b in bs}
    y_s = np.concatenate([res[c]["y_s"].reshape(2, ST, D) for c in cores], axis=0)
    lru_h_p = np.stack([res[first[b]]["lru_h_p"] for b in bs])[None]
    lru_c_p = np.stack([res[first[b]]["lru_conv_p"] for b in bs])[None]
    fk_p = np.stack([res[first[b]]["fk_p"].reshape(T, H, DH) for b in bs])[None]
    fv_p = np.stack([res[first[b]]["fv_p"].reshape(T, H, DH) for b in bs])[None]
    fl_p = np.stack([res[first[b]]["flf_p"] for b in bs])[None]
    lru_h_s = np.concatenate([res[c]["lru_h_s"] for c in cores], axis=0)[None]
    lru_c_s = np.concatenate([res[c]["lru_conv_s"] for c in cores], axis=0)[None]
    fk_s = np.concatenate([res[c]["fk_s"].reshape(2, ST, H, DH) for c in cores], axis=0)[None]
    fv_s = np.concatenate([res[c]["fv_s"].reshape(2, ST, H, DH) for c in cores], axis=0)[None]
    fl_s = np.concatenate([res[c]["flf_s"].reshape(2, ST, H) for c in cores], axis=0)[None]
    return (y_p, y_s, lru_h_p, lru_c_p, fk_p, fv_p, fl_p, lru_h_s, lru_c_s, fk_s, fv_s, fl_s)
```

```python
import contextlib
import numpy as np
import ml_dtypes
import concourse.bass as bass
import concourse.mybir as mybir
from concourse.bass_utils import run_bass_kernel_spmd

F32 = mybir.dt.float32
BF16 = mybir.dt.bfloat16
AF = mybir.ActivationFunctionType
ALU = mybir.AluOpType

D = 1024
R = 1536
NCH = 12
H = 16
DH = 64
T = 16384
TT = 512
NT = T // TT
ST = 32
PAST = 4096
EPS = 1e-6
NEG = -30000.0
VS = 66
KA = 70
COMPUTE = ("pe", "act", "dve", "pool", "sp")


class Buf:
    __slots__ = ("name", "last_w", "readers")

    def __init__(self, name):
        self.name = name
        self.last_w = None
        self.readers = []


class Op:
    __slots__ = ("eng", "fn", "deps", "is_dma", "idx", "need_inc", "val", "sem", "semval", "prev_on_sem", "inc")

    def __init__(self, eng, fn, is_dma, inc=16):
        self.eng = eng
        self.fn = fn
        self.deps = set()
        self.is_dma = is_dma
        self.need_inc = False
        self.val = None
        self.sem = None
        self.semval = None
        self.prev_on_sem = None
        self.inc = inc


class Prog:
    def __init__(self, nc, n_dma_sems=(("sp", 32), ("act", 12), ("pool", 12))):
        self.nc = nc
        self.ops = []
        self.by_eng = {e: [] for e in COMPUTE}
        self.dma_ring = {e: n for e, n in n_dma_sems}
        self.dma_count = {e: 0 for e, _ in n_dma_sems}
        self.dma_last_on_slot = {}
        self.all_bufs = []

    def buf(self, name=""):
        b = Buf(name)
        self.all_bufs.append(b)
        return b

    def _add(self, op, reads, writes):
        op.idx = len(self.ops)
        for b in reads:
            if b.last_w is not None:
                op.deps.add(b.last_w)
        for b in writes:
            if b.last_w is not None:
                op.deps.add(b.last_w)
            for r in b.readers:
                op.deps.add(r)
        for b in reads:
            b.readers.append(op)
        for b in writes:
            b.last_w = op
            b.readers = []
        op.deps.discard(op)
        self.ops.append(op)
        self.by_eng[op.eng].append(op)
        return op

    def op(self, eng, fn, reads=(), writes=()):
        return self._add(Op(eng, fn, False), reads, writes)

    def dma(self, eng, fn, reads=(), writes=(), inc=16):
        op = Op(eng, fn, True, inc)
        n = self.dma_count[eng]
        self.dma_count[eng] = n + 1
        slot = (eng, n % self.dma_ring[eng])
        op.sem = slot
        op.prev_on_sem = self.dma_last_on_slot.get(slot)
        op.semval = (op.prev_on_sem.semval if op.prev_on_sem else 0) + inc
        self.dma_last_on_slot[slot] = op
        return self._add(op, reads, writes)

    def barrier(self):
        lasts = [self.by_eng[e][-1] for e in COMPUTE if self.by_eng[e]]
        lasts += list(self.dma_last_on_slot.values())
        for e in COMPUTE:
            o = Op(e, None, False)
            o.idx = len(self.ops)
            o.deps = set(lasts)
            self.ops.append(o)
            self.by_eng[e].append(o)
        for b in self.all_bufs:
            if str(b.name).startswith("dram_"):
                continue
            b.last_w = None
            b.readers = []

    def emit(self, final_wait_eng="sp"):
        nc = self.nc
        lasts = [self.by_eng[e][-1] for e in COMPUTE if self.by_eng[e]]
        lasts += list(self.dma_last_on_slot.values())
        fin = Op(final_wait_eng, None, False)
        fin.idx = len(self.ops)
        fin.deps = set(lasts)
        self.ops.append(fin)
        self.by_eng[final_wait_eng].append(fin)
        for o in self.ops:
            for d in o.deps:
                if not d.is_dma:
                    if d.eng == o.eng and o.eng == "pe":
                        continue
                    d.need_inc = True
        for e in COMPUTE:
            c = 0
            for o in self.by_eng[e]:
                if not o.is_dma and o.need_inc:
                    c += 1
                    o.val = c
        with contextlib.ExitStack() as st:
            esem = {e: st.enter_context(nc.semaphore("s_" + e)) for e in COMPUTE}
            dsem = {}
            for e, n in self.dma_ring.items():
                for i in range(n):
                    dsem[(e, i)] = st.enter_context(nc.semaphore("d_%s_%d" % (e, i)))
            block = st.enter_context(nc.Block())
            engobj = {"pe": nc.tensor, "act": nc.scalar, "dve": nc.vector, "pool": nc.gpsimd, "sp": nc.sync}

            def run_engine(e):
                eng = engobj[e]
                waited = {}

                def wait(key, sem, val):
                    if waited.get(key, 0) >= val:
                        return
                    waited[key] = val
                    eng.wait_ge(sem, val)

                for o in self.by_eng[e]:
                    for d in sorted(o.deps, key=lambda d: d.idx):
                        if d.is_dma:
                            wait(d.sem, dsem[d.sem], d.semval)
                        else:
                            if d.eng == e and e == "pe":
                                continue
                            if d.val is None:
                                continue
                            wait(d.eng, esem[d.eng], d.val)
                    if o.is_dma and o.prev_on_sem is not None:
                        wait(o.sem, dsem[o.sem], o.prev_on_sem.semval)
                    if o.fn is None:
                        if o.need_inc:
                            eng.nop().then_inc(esem[e], 1)
                        continue
                    ins = o.fn(eng)
                    if o.is_dma:
                        ins.then_inc(dsem[o.sem], o.inc)
                    elif o.need_inc:
                        ins.then_inc(esem[e], 1)

            @block.tensor
            def _(x):
                run_engine("pe")

            @block.scalar
            def _(x):
                run_engine("act")

            @block.vector
            def _(x):
                run_engine("dve")

            @block.gpsimd
            def _(x):
                run_engine("pool")

            @block.sync
            def _(x):
                run_engine("sp")


class Arena:
    def __init__(self, nc, base=16640, limit=229376 - 2048):
        self.nc = nc
        self.ptr = base
        self.limit = limit
        self.n = 0

    def alloc(self, shape, dtype, name="t"):
        size = int(np.prod(shape[1:])) * (4 if dtype == F32 else 2)
        size = (size + 63) // 64 * 64
        off = self.ptr
        self.ptr += size
        assert self.ptr <= self.limit, ("SBUF overflow", name, self.ptr)
        self.n += 1
        return self.nc.alloc_sbuf_tensor_at("%s_%d" % (name, self.n), list(shape), dtype, offset=off)

    def mark(self):
        return self.ptr

    def reset(self, m):
        self.ptr = m


def build(nq_tiles=None, do_attn=True, do_sample=True, nt_prompt=None, stop=9, Tn=16384, PASTn=4096, dbg=0):
    T, PAST = Tn, PASTn
    NPR = PAST // 2048
    NQ = T // 2048
    if nq_tiles is None:
        nq_tiles = NQ
    if nt_prompt is None:
        nt_prompt = T // TT
    nc = bass.Bass("TRN2", target_bir_lowering=False)
    P = Prog(nc)
    A = Arena(nc)

    def din(name, shape, dt=F32):
        return nc.dram_tensor(name, list(shape), dt, kind="ExternalInput").ap()

    def dout(name, shape, dt=F32):
        return nc.dram_tensor(name, list(shape), dt, kind="ExternalOutput").ap()

    def dscr(name, shape, dt=F32):
        return nc.dram_tensor(name, list(shape), dt).ap()

    i_xp = din("xp", [T, D])
    i_xs = din("xs", [2 * ST, D])
    i_cc = din("cc", [3, D])
    i_sth = din("st_h", [2, R])
    i_stc = din("st_conv", [2, 3, R])
    i_ck = din("ck_k", [2, PAST, D])
    i_cv = din("ck_v", [2, PAST, D])
    i_clf = din("ck_lf", [2, PAST, H])
    i_npre = din("norm_pre", [2, D])
    i_npost = din("norm_post", [2, D])
    i_adaw = din("ada_w", [2, D, 3 * D])
    i_adab = din("ada_b", [2, 3 * D])
    i_win = din("lru_w_in", [D, 2 * R])
    i_lvec = din("lru_vecs", [8, R])
    i_wa = din("lru_w_a", [NCH, 128, 128])
    i_wx = din("lru_w_x", [NCH, 128, 128])
    i_wout = din("lru_w_out", [R, D])
    i_fwin = din("fox_w_in", [D, 4 * D + H])
    i_fbf = din("fox_b_f", [1, H])
    i_fwout = din("fox_w_out", [D, D])
    i_ident = din("c_ident", [128, 128])
    i_sel = din("c_sel", [3, 2, 128])
    i_pmask = din("c_pmask", [128, 16, TT], BF16)
    i_smask = din("c_smask", [ST, ST], BF16)
    i_onehot = din("c_onehot", [128, 4])
    o_yp = dout("y_p", [NQ * TT, D])
    o_ys = dout("y_s", [2 * ST, D])
    o_lhp = dout("lru_h_p", [R])
    o_lcp = dout("lru_conv_p", [3, R])
    o_fkp = dout("fk_p", [T, D])
    o_fvp = dout("fv_p", [T, D])
    o_flp = dout("flf_p", [T, H])
    o_lhs = dout("lru_h_s", [2, R])
    o_lcs = dout("lru_conv_s", [2, 3, R])
    o_fks = dout("fk_s", [2 * ST, D])
    o_fvs = dout("fv_s", [2 * ST, D])
    o_fls = dout("flf_s", [2 * ST, H])
    s_x1 = dscr("s_x1", [T, D])
    s_kT = dscr("s_kT", [H, KA, T], BF16)
    s_vv = dscr("s_vv", [H, T // 2048, 128, 16 * VS], BF16)
    s_cum = dscr("s_cum", [H, T])
    NK_S = PAST + 128
    s_kTs = dscr("s_kTs", [2, H, KA, NK_S], BF16)
    s_vvs = dscr("s_vvs", [2, H, NPR + 1, 128, 16 * VS], BF16)

    b_sx1, b_skT, b_svv, b_scm, b_skTs, b_svvs = [P.buf("dram_%d" % i) for i in range(6)]
    PS = [nc.alloc_psum_tensor("ps%d" % i, [128, 1024], F32) for i in range(4)]
    bPS = [[P.buf("ps%d_%d" % (i, j)) for j in range(2)] for i in range(4)]

    def bank(i):
        return PS[i // 2][:, (i % 2) * 512:(i % 2) * 512 + 512], bPS[i // 2][i % 2]

    identf = A.alloc([128, 128], F32, "ident")
    b_const = P.buf("const")
    P.dma("sp", lambda e: e.dma_start(out=identf[:], in_=i_ident[:, :]), writes=[b_const])
    sel = A.alloc([3, 2, 128], F32, "sel")
    P.dma("sp", lambda e: e.dma_start(out=sel[:], in_=i_sel[:, :, :]), writes=[b_const])
    onehot = A.alloc([128, 4], F32, "onehot")
    P.dma("sp", lambda e: e.dma_start(out=onehot[:], in_=i_onehot[:, :]), writes=[b_const])
    ones_bf = A.alloc([128, 512], BF16, "ones")
    ones_f = A.alloc([128, 512], F32, "onesf")
    P.op("pool", lambda e: e.memset(ones_bf[:], 1.0), writes=[b_const])
    P.op("pool", lambda e: e.memset(ones_f[:], 1.0), writes=[b_const])
    ident_bf = A.alloc([128, 128], BF16, "identb")
    P.op("dve", lambda e: e.tensor_copy(ident_bf[:], identf[:]), reads=[b_const], writes=[b_const])
    gcol = A.alloc([128, 2, 2, 8, 3], F32, "gcol")
    gprow = A.alloc([128, 2, 2, D], F32, "gprow")
    lcol = A.alloc([128, NCH, 10], F32, "lcol")
    bfcol = A.alloc([16, 2], F32, "bfcol")
    bfrow = A.alloc([128, H], F32, "bfrow")
    b_ada = P.buf("ada")
    epsc = A.alloc([128, 1], F32, "epsc")
    onec = A.alloc([128, 1], F32, "onec")
    P.op("pool", lambda e: e.memset(epsc[:], EPS), writes=[b_const])
    P.op("pool", lambda e: e.memset(onec[:], 1.0), writes=[b_const])
    xs_t = A.alloc([128, 1, D], F32, "xs_t")
    b_xs = P.buf("xs")
    scum = A.alloc([16, 2 * ST], F32, "scum")
    b_scum = P.buf("scum")

    cx = {}

    def nb(*names):
        for n in names:
            cx["b_" + n] = P.buf(n)

    def front(N, nsub, Pt, layer, seqcols, xsrc, b_x):
        xn, junk, stat, hT = cx["xn"], cx["junk"], cx["stat"], cx["hT"]
        b_xn, b_junk, b_stat, b_hT = cx["b_xn"], cx["b_junk"], cx["b_stat"], cx["b_hT"]
        bxs = b_x if isinstance(b_x, list) else [b_x] * nsub
        for half in range(2):
            for s in range(nsub):
                b_x = bxs[s]
                if half == 0:
                    P.op("dve", lambda e, s=s: e.scalar_tensor_tensor(out=junk[0:Pt, :], in0=xsrc[0:Pt, s, :], scalar=1.0, in1=xsrc[0:Pt, s, :], op0=ALU.mult, op1=ALU.mult, accum_out=stat[0:Pt, s:s + 1]),
                         reads=[b_x], writes=[b_junk, b_stat])
                    P.op("act", lambda e, s=s: e.activation(out=stat[0:Pt, 4 + s:5 + s], in_=stat[0:Pt, s:s + 1], func=AF.Sqrt, scale=1.0 / D, bias=epsc[0:Pt, 0:1]), reads=[b_stat, b_const], writes=[b_stat])
                    P.op("dve", lambda e, s=s: e.reciprocal(stat[0:Pt, 8 + s:9 + s], stat[0:Pt, 4 + s:5 + s]), reads=[b_stat], writes=[b_stat])
                P.op("dve", lambda e, s=s: e.tensor_scalar(out=xn[0:Pt, :], in0=xsrc[0:Pt, s, :], scalar1=stat[0:Pt, 8 + s:9 + s], scalar2=None, op0=ALU.mult), reads=[b_x, b_stat], writes=[b_xn])

                def trs(e, s=s, half=half):
                    ins = None
                    for j in range(4):
                        kc = half * 4 + j
                        ins = e.transpose(out=bank(j)[0][:, s * Pt:(s + 1) * Pt], in_=xn[0:Pt, kc * 128:(kc + 1) * 128], identity=identf[0:Pt, 0:Pt])
                    return ins
                P.op("pe", trs, reads=[b_xn, b_const], writes=[bank(j)[1] for j in range(4)])
            for j in range(4):
                kc = half * 4 + j
                for (c0, c1, sq) in seqcols:
                    P.op("act", lambda e, j=j, kc=kc, c0=c0, c1=c1, sq=sq: e.activation(out=hT[:, kc, c0:c1], in_=bank(j)[0][:, c0:c1], func=AF.Identity, scale=gcol[:, layer, 0, kc, sq:sq + 1], bias=gcol[:, layer, 1, kc, sq:sq + 1]),
                         reads=[bank(j)[1], b_ada], writes=[b_hT])

    def post(N, nsub, Pt, layer, grp, wsb, b_wsb, nkc, srcT, b_srcT, ksz, xres, b_xres):
        junk, stat, ytmp = cx["junk"], cx["stat"], cx["ytmp"]
        b_junk, b_stat, b_ytmp = cx["b_junk"], cx["b_stat"], cx["b_ytmp"]
        bxr = b_xres if isinstance(b_xres, list) else [b_xres] * nsub
        for s in range(nsub):
            b_xres = bxr[s]
            pp = PS[2 + (s % 2)]
            bpp = bPS[2 + (s % 2)]

            def mm(e, s=s, pp=pp):
                ins = None
                for hf in range(2):
                    for c in range(nkc):
                        ins = e.matmul(pp[0:Pt, hf * 512:(hf + 1) * 512], lhsT=srcT[0:ksz, c, s * Pt:(s + 1) * Pt], rhs=wsb[0:ksz, c, hf * 512:(hf + 1) * 512], start=(c == 0), stop=(c == nkc - 1))
                return ins
            P.op("pe", mm, reads=[b_srcT, b_wsb], writes=bpp)
            P.op("act", lambda e, pp=pp: e.activation(out=junk[0:Pt, :], in_=pp[0:Pt, :], func=AF.Square, accum_out=stat[0:Pt, 12:13]), reads=bpp, writes=[b_junk, b_stat])
            P.op("act", lambda e: e.activation(out=stat[0:Pt, 13:14], in_=stat[0:Pt, 12:13], func=AF.Sqrt, scale=1.0 / D, bias=epsc[0:Pt, 0:1]), reads=[b_stat, b_const], writes=[b_stat])
            P.op("dve", lambda e: e.reciprocal(stat[0:Pt, 14:15], stat[0:Pt, 13:14]), reads=[b_stat], writes=[b_stat])
            P.op("dve", lambda e, pp=pp: e.scalar_tensor_tensor(out=ytmp[0:Pt, :], in0=pp[0:Pt, :], scalar=stat[0:Pt, 14:15], in1=gprow[0:Pt, layer, grp, :], op0=ALU.mult, op1=ALU.mult), reads=bpp + [b_stat, b_ada], writes=[b_ytmp])
            P.op("pool", lambda e, s=s: e.tensor_tensor(out=xres[0:Pt, s, :], in0=ytmp[0:Pt, :], in1=xres[0:Pt, s, :], op=ALU.add), reads=[b_ytmp, b_xres], writes=[b_xres])

    def split3(src, b_src, N, sign):
        spl, spf, b_spl, b_spf = cx["spl"], cx["spf"], cx["b_spl"], cx["b_spf"]
        P.op("dve", lambda e: e.tensor_scalar(out=spf[:, 0, 0:N], in0=src, scalar1=sign, scalar2=None, op0=ALU.mult), reads=[b_src], writes=[b_spf])
        for j in range(3):
            P.op("dve", lambda e, j=j: e.tensor_copy(spl[:, j, 0:N], spf[:, 0, 0:N]), reads=[b_spf], writes=[b_spl])
            if j < 2:
                P.op("dve", lambda e, j=j: e.tensor_copy(spf[:, 1, 0:N], spl[:, j, 0:N]), reads=[b_spl], writes=[b_spf])
                P.op("dve", lambda e: e.tensor_tensor(out=spf[:, 0, 0:N], in0=spf[:, 0, 0:N], in1=spf[:, 1, 0:N], op=ALU.subtract), reads=[b_spf], writes=[b_spf])

    m_persist = A.mark()
    adaw = A.alloc([128, 8, 3 * D], F32, "adaw")
    crow = A.alloc([3, D], F32, "crow")
    ccol = A.alloc([128, 8, 3], F32, "ccol")
    vrow = A.alloc([12, R], F32, "vrow")
    adab = A.alloc([1, 2, 3 * D], F32, "adab")
    grow = A.alloc([3, D], F32, "grow")
    npb = A.alloc([128, 2, D], F32, "npb")
    tmpc = A.alloc([128, 8, 3], F32, "tmpc")
    b_crow, b_ccol, b_vrow, b_adaw, b_adab, b_grow, b_npb = [P.buf(n) for n in "crow ccol vrow adaw adab grow npb".split()]
    P.dma("sp", lambda e: e.dma_start(out=crow[:], in_=i_cc[:, :]), writes=[b_crow])
    P.op("pool", lambda e: e.memset(vrow[:], 0.0), writes=[b_vrow])
    P.dma("sp", lambda e: e.dma_start(out=vrow[0:8, :], in_=i_lvec[:, :]), reads=[b_vrow], writes=[b_vrow])
    P.dma("sp", lambda e: e.dma_start(out=vrow[8:10, 0:D], in_=i_npre[:, :]), reads=[b_vrow], writes=[b_vrow])
    P.dma("sp", lambda e: e.dma_start(out=adab[:], in_=i_adab.rearrange("(o l) f -> o l f", o=1)), writes=[b_adab])
    P.dma("sp", lambda e: e.dma_start(out=npb[:], in_=i_npost.rearrange("(o l) f -> o l f", o=1).broadcast_to([128, 2, D])), writes=[b_npb])
    P.dma("sp", lambda e: e.dma_start(out=bfrow[:], in_=i_fbf.broadcast_to([128, H])), writes=[b_const])
    P.op("act", lambda e: e.activation(out=crow[:], in_=crow[:], func=AF.Silu), reads=[b_crow], writes=[b_crow])
    pa, ba = bank(0)

    def tr_c(e):
        ins = None
        for kc in range(8):
            ins = e.transpose(out=pa[:, kc * 3:kc * 3 + 3], in_=crow[0:3, kc * 128:(kc + 1) * 128], identity=identf[0:3, 0:3])
        return ins
    P.op("pe", tr_c, reads=[b_crow, b_const], writes=[ba])
    P.op("dve", lambda e: e.tensor_copy(ccol[:].rearrange("p k s -> p (k s)"), pa[:, 0:24]), reads=[ba], writes=[b_ccol])
    pb_, bb = bank(1)

    def tr_v(e):
        ins = None
        for c in range(NCH):
            ins = e.transpose(out=pb_[:, c * 10:c * 10 + 10], in_=vrow[0:10, c * 128:(c + 1) * 128], identity=identf[0:10, 0:10])
        return ins
    P.op("pe", tr_v, reads=[b_vrow, b_const], writes=[bb])
    vcol = A.alloc([128, NCH, 10], F32, "vcol")
    b_vcol = P.buf("vcol")
    P.op("dve", lambda e: e.tensor_copy(vcol[:].rearrange("p c v -> p (c v)"), pb_[:, 0:NCH * 10]), reads=[bb], writes=[b_vcol])
    P.op("dve", lambda e: e.tensor_copy(lcol[:, :, 0:8], vcol[:, :, 0:8]), reads=[b_vcol], writes=[b_const])
    P.op("act", lambda e: e.activation(out=lcol[:, :, 8], in_=vcol[:, :, 7], func=AF.Exp, scale=-1.0), reads=[b_vcol], writes=[b_const])
    P.op("act", lambda e: e.activation(out=lcol[:, :, 8], in_=lcol[:, :, 8], func=AF.Ln, bias=onec[:, 0:1]), reads=[b_const], writes=[b_const])
    P.op("dve", lambda e: e.tensor_scalar(out=lcol[:, :, 8], in0=lcol[:, :, 8], scalar1=-8.0, scalar2=None, op0=ALU.mult), reads=[b_const], writes=[b_const])
    pc_, bc = bank(2)
    P.op("pe", lambda e: e.matmul(pc_[0:16, 0:1], lhsT=bfrow[0:1, 0:16], rhs=ones_f[0:1, 0:1], start=True, stop=True), reads=[b_const], writes=[bc])
    P.op("dve", lambda e: e.tensor_scalar(out=bfcol[:, 0:1], in0=pc_[0:16, 0:1], scalar1=-1.0, scalar2=None, op0=ALU.mult), reads=[bc], writes=[b_const])
    for l in range(2):
        for q4 in range(4):
            P.dma("sp" if q4 % 2 == 0 else "act",
                  lambda e, l=l, q4=q4: e.dma_start(out=adaw[:, 2 * q4:2 * q4 + 2, :], in_=i_adaw[l, q4 * 256:(q4 + 1) * 256, :].rearrange("(k p) f -> p k f", p=128)),
                  writes=[b_adaw])
        pm, bm = bank(4 + l * 2)

        def ada_cols(e, l=l, pm=pm):
            ins = None
            for fc in range(16):
                for kc in range(8):
                    ins = e.matmul(pm[:, fc * 3:fc * 3 + 3], lhsT=adaw[:, kc, fc * 128:(fc + 1) * 128], rhs=ccol[:, kc, :], start=(kc == 0), stop=False)
                ins = e.matmul(pm[:, fc * 3:fc * 3 + 3], lhsT=adab[0:1, l, fc * 128:(fc + 1) * 128], rhs=ones_f[0:1, 0:3], start=False, stop=True)
            return ins
        P.op("pe", ada_cols, reads=[b_adaw, b_ccol, b_adab, b_const], writes=[bm])
        P.op("dve", lambda e, l=l, pm=pm: e.tensor_copy(gcol[:, l, 1, :, :].rearrange("p k s -> p (k s)"), pm[:, 0:24]), reads=[bm], writes=[b_ada])
        P.op("dve", lambda e, l=l, pm=pm: e.tensor_scalar(out=tmpc[:].rearrange("p k s -> p (k s)"), in0=pm[:, 24:48], scalar1=1.0, scalar2=None, op0=ALU.add), reads=[bm], writes=[b_grow])
        for s in range(3):
            P.op("dve", lambda e, l=l, s=s: e.tensor_tensor(out=gcol[:, l, 0, :, s], in0=tmpc[:, :, s], in1=vcol[0:128, 0:8, 8 + l], op=ALU.mult), reads=[b_grow, b_vcol], writes=[b_ada])
        for hf in range(2):
            prh, brh = bank(5 + l * 2) if hf == 0 else bank(3)

            def ada_rowh(e, l=l, hf=hf, prh=prh):
                ins = None
                for kc in range(8):
                    ins = e.matmul(prh[0:3, 0:512], lhsT=ccol[:, kc, :], rhs=adaw[:, kc, 2 * D + hf * 512:2 * D + (hf + 1) * 512], start=(kc == 0), stop=False)
                ins = e.matmul(prh[0:3, 0:512], lhsT=ones_f[0:1, 0:3], rhs=adab[0:1, l, 2 * D + hf * 512:2 * D + (hf + 1) * 512], start=False, stop=True)
                return ins
            P.op("pe", ada_rowh, reads=[b_adaw, b_ccol, b_adab, b_const], writes=[brh])
            P.op("dve", lambda e, hf=hf, prh=prh: e.tensor_copy(grow[:, hf * 512:(hf + 1) * 512], prh[0:3, 0:512]), reads=[brh], writes=[b_grow])
        for g in range(2):
            for hf in range(2):
                pg, bg = bank(0 + hf)
                P.op("pe", lambda e, g=g, hf=hf, pg=pg: e.matmul(pg[:, 0:512], lhsT=sel[0:3, g, :], rhs=grow[0:3, hf * 512:(hf + 1) * 512], start=True, stop=True), reads=[b_grow, b_const], writes=[bg])
                P.op("dve", lambda e, g=g, hf=hf, pg=pg, l=l: e.tensor_tensor(out=gprow[:, l, g, hf * 512:(hf + 1) * 512], in0=pg[:, 0:512], in1=npb[:, l, hf * 512:(hf + 1) * 512], op=ALU.mult), reads=[bg, b_npb], writes=[b_ada])
    P.barrier()
    A.reset(m_persist)
    if dbg == 32:
        P.dma("sp", lambda e: e.dma_start(out=o_fkp[2048:2176, 0:96], in_=gcol[:].rearrange("p a b c d -> p (a b c d)")), reads=[b_ada])
        P.dma("sp", lambda e: e.dma_start(out=o_fkp[2176:2304, 0:2 * 2 * D], in_=gprow[:].rearrange("p a b c -> p (a b c)")), reads=[b_ada]) if False else None
    if stop == 0:
        P.emit()
        return nc

    win = A.alloc([128, 8, 2 * R], BF16, "win")
    wg = A.alloc([128, 2, NCH, 128], BF16, "wg")
    wout = A.alloc([128, NCH, D], BF16, "wout")
    b_w = P.buf("weights")
    for kc in range(8):
        for hf in range(2):
            P.dma("pool", lambda e, kc=kc, hf=hf: e.dma_start(out=win[:, kc, hf * R:(hf + 1) * R], in_=i_win[kc * 128:(kc + 1) * 128, hf * R:(hf + 1) * R]), writes=[b_w])
    P.dma("pool", lambda e: e.dma_start(out=wg[:, 0, :, :], in_=i_wa.rearrange("n d e -> d n e")), writes=[b_w])
    P.dma("pool", lambda e: e.dma_start(out=wg[:, 1, :, :], in_=i_wx.rearrange("n d e -> d n e")), writes=[b_w])
    for c in range(NCH):
        P.dma("pool", lambda e, c=c: e.dma_start(out=wout[:, c, :], in_=i_wout[c * 128:(c + 1) * 128, :]), writes=[b_w])
    CG = 2
    NG = NCH // CG
    ENG_CAST = "pool" if dbg & 1024 else "dve"
    ENG_A2 = "pool" if dbg & 2048 else "dve"
    ENG_IM = "pool" if dbg & 4096 else "dve"
    xt_a1 = A.alloc([128, 4, D], F32, "xt_a1")
    cx["xn"] = A.alloc([128, D], F32, "xn")
    cx["junk"] = A.alloc([128, D], BF16, "junk")
    cx["stat"] = A.alloc([128, 16], F32, "stat")
    cx["hT"] = A.alloc([128, 8, TT], BF16, "hT")
    nb("xn", "junk", "stat", "hT")
    cx["ytmp"] = cx["xn"]
    cx["b_ytmp"] = cx["b_xn"]
    halo = A.alloc([128, NCH, 2, 4], F32, "halo")
    hst = A.alloc([128, NCH, 2], F32, "hst")
    XBW = 520
    sets = []
    for k_ in range(2):
        S_ = dict(xb=A.alloc([128, CG, XBW], F32, "xb"), sg=A.alloc([128, CG, TT], BF16, "sg"), xc=A.alloc([128, CG, TT], F32, "xc"),
                  xcb=A.alloc([128, CG, TT], BF16, "xcb"), rr=A.alloc([128, CG, TT], F32, "rr"), ig=A.alloc([128, CG, TT], F32, "ig"),
                  aa=A.alloc([128, CG, TT], F32, "aa"))
        for n_ in ("xb", "xbh", "sg", "xc", "xcb", "rr", "ig", "aa"):
            S_["b_" + n_] = P.buf("%s%d" % (n_, k_))
        sets.append(S_)
    zT = A.alloc([128, NCH, TT], BF16, "zT")
    b_xts = [P.buf("xt%d" % s_) for s_ in range(4)]
    b_halo = [P.buf("halo%d" % c_) for c_ in range(NCH)]
    b_hst, b_zT = P.buf("hst"), P.buf("zT")

    def layer0(N, nseg, L, segw):
        hT, b_hT = cx["hT"], cx["b_hT"]

        def xbv(S, cl, sgm, a_, b__):
            return S["xb"][:, cl, sgm * segw + a_:sgm * segw + b__]

        def stageA_pe(g):
            for cl in range(CG):
                c = g * CG + cl
                pxa, bxa = bank(4 + cl * 2)
                pga, bga = bank(5 + cl * 2)

                def mm(e, c=c, pxa=pxa, pga=pga):
                    ins = None
                    for kc in range(8):
                        ins = e.matmul(pxa[:, 0:N], lhsT=win[:, kc, c * 128:(c + 1) * 128], rhs=hT[:, kc, 0:N], start=(kc == 0), stop=(kc == 7))
                    for kc in range(8):
                        ins = e.matmul(pga[:, 0:N], lhsT=win[:, kc, R + c * 128:R + (c + 1) * 128], rhs=hT[:, kc, 0:N], start=(kc == 0), stop=(kc == 7))
                    return ins
                P.op("pe", mm, reads=[b_hT, b_w], writes=[bxa, bga])

        def stageA_el(g):
            S = sets[g % 2]
            for cl in range(CG):
                c = g * CG + cl
                pxa, bxa = bank(4 + cl * 2)
                pga, bga = bank(5 + cl * 2)
                P.op("act", lambda e, S=S, cl=cl, pga=pga: e.activation(out=S["sg"][:, cl, 0:N], in_=pga[:, 0:N], func=AF.Silu), reads=[bga], writes=[S["b_sg"]])
                for sgm in range(nseg):
                    P.op("dve", lambda e, S=S, cl=cl, sgm=sgm, pxa=pxa: e.tensor_copy(xbv(S, cl, sgm, 3, 3 + L), pxa[:, sgm * L:(sgm + 1) * L]), reads=[bxa], writes=[S["b_xb"]])
                    P.op("pool", lambda e, S=S, c=c, cl=cl, sgm=sgm: e.tensor_copy(xbv(S, cl, sgm, 0, 3), halo[:, c, sgm, 0:3]), reads=[b_halo[c]], writes=[S["b_xbh"]])
            for cl in range(CG):
                c = g * CG + cl
                for sgm in range(nseg):
                    o = S["xc"][:, cl, sgm * L:(sgm + 1) * L]
                    P.op("dve", lambda e, S=S, c=c, cl=cl, sgm=sgm, o=o: e.tensor_scalar(out=o, in0=xbv(S, cl, sgm, 0, L), scalar1=lcol[:, c, 0:1], scalar2=lcol[:, c, 4:5], op0=ALU.mult, op1=ALU.add),
                         reads=[S["b_xb"], S["b_xbh"], b_const], writes=[S["b_xc"]])
                    for k in range(1, 4):
                        P.op("dve", lambda e, S=S, c=c, cl=cl, sgm=sgm, o=o, k=k: e.scalar_tensor_tensor(out=o, in0=xbv(S, cl, sgm, k, k + L), scalar=lcol[:, c, k:k + 1], in1=o, op0=ALU.mult, op1=ALU.add),
                             reads=[S["b_xb"], S["b_xbh"], b_const, S["b_xc"]], writes=[S["b_xc"]])
                    P.op("pool", lambda e, S=S, c=c, cl=cl, sgm=sgm: e.tensor_copy(halo[:, c, sgm, 0:3], xbv(S, cl, sgm, L, L + 3)), reads=[S["b_xb"]], writes=[b_halo[c]])
                P.op(ENG_CAST, lambda e, S=S, cl=cl: e.tensor_copy(S["xcb"][:, cl, 0:N], S["xc"][:, cl, 0:N]), reads=[S["b_xc"]], writes=[S["b_xcb"]])

        def stageB_pe(g):
            S = sets[g % 2]
            xcb = S["xcb"]
            for cl in range(CG):
                c = g * CG + cl
                pra, bra = bank(cl * 2)
                pia, bia = bank(cl * 2 + 1)

                def mg(e, c=c, cl=cl, pra=pra, pia=pia, xcb=xcb):
                    e.matmul(pra[:, 0:N], lhsT=wg[:, 0, c, :], rhs=xcb[:, cl, 0:N], start=True, stop=True)
                    return e.matmul(pia[:, 0:N], lhsT=wg[:, 1, c, :], rhs=xcb[:, cl, 0:N], start=True, stop=True)
                P.op("pe", mg, reads=[S["b_xcb"], b_w], writes=[bra, bia])

        def stageB_el(g):
            S = sets[g % 2]
            rr, ig, aa, xc, sg, xcb = S["rr"], S["ig"], S["aa"], S["xc"], S["sg"], S["xcb"]
            for cl in range(CG):
                c = g * CG + cl
                pra, bra = bank(cl * 2)
                pia, bia = bank(cl * 2 + 1)
                P.op("act", lambda e, c=c, cl=cl, pra=pra, rr=rr: e.activation(out=rr[:, cl, 0:N], in_=pra[:, 0:N], func=AF.Sigmoid, bias=lcol[:, c, 5:6]), reads=[bra, b_const], writes=[S["b_rr"]])
                P.op("act", lambda e, c=c, cl=cl, pia=pia, ig=ig: e.activation(out=ig[:, cl, 0:N], in_=pia[:, 0:N], func=AF.Sigmoid, bias=lcol[:, c, 6:7]), reads=[bia, b_const], writes=[S["b_ig"]])
            for cl in range(CG):
                c = g * CG + cl
                P.op("act", lambda e, c=c, cl=cl, rr=rr, aa=aa: e.activation(out=aa[:, cl, 0:N], in_=rr[:, cl, 0:N], func=AF.Exp, scale=lcol[:, c, 8:9]), reads=[S["b_rr"], b_const], writes=[S["b_aa"]])
                P.op(ENG_A2, lambda e, cl=cl, rr=rr, aa=aa: e.tensor_tensor(out=rr[:, cl, 0:N], in0=aa[:, cl, 0:N], in1=aa[:, cl, 0:N], op=ALU.mult), reads=[S["b_aa"], S["b_rr"]], writes=[S["b_rr"]])
                P.op("pool", lambda e, cl=cl, ig=ig, xc=xc: e.tensor_tensor(out=ig[:, cl, 0:N], in0=ig[:, cl, 0:N], in1=xc[:, cl, 0:N], op=ALU.mult), reads=[S["b_ig"], S["b_xc"]], writes=[S["b_ig"]])
            for cl in range(CG):
                c = g * CG + cl
                P.op("act", lambda e, cl=cl, rr=rr: e.activation(out=rr[:, cl, 0:N], in_=rr[:, cl, 0:N], func=AF.Sqrt, scale=-1.0, bias=onec[:, 0:1]), reads=[S["b_rr"], b_const], writes=[S["b_rr"]])
                P.op(ENG_IM, lambda e, cl=cl, ig=ig, rr=rr: e.tensor_tensor(out=ig[:, cl, 0:N], in0=ig[:, cl, 0:N], in1=rr[:, cl, 0:N], op=ALU.mult), reads=[S["b_ig"], S["b_rr"]], writes=[S["b_ig"]])
                for sgm in range(nseg):
                    P.op("dve", lambda e, c=c, cl=cl, sgm=sgm, xc=xc, aa=aa, ig=ig: e.tensor_tensor_scan(out=xc[:, cl, sgm * L:(sgm + 1) * L], data0=aa[:, cl, sgm * L:(sgm + 1) * L], data1=ig[:, cl, sgm * L:(sgm + 1) * L], initial=hst[:, c, sgm:sgm + 1], op0=ALU.mult, op1=ALU.add),
                         reads=[S["b_aa"], S["b_ig"], b_hst, S["b_xc"]], writes=[S["b_xc"]])
                    P.op("dve", lambda e, c=c, cl=cl, sgm=sgm, xc=xc: e.tensor_copy(hst[:, c, sgm:sgm + 1], xc[:, cl, (sgm + 1) * L - 1:(sgm + 1) * L]), reads=[S["b_xc"], b_hst], writes=[b_hst])
                P.op("pool", lambda e, c=c, cl=cl, xc=xc, sg=sg: e.tensor_tensor(out=zT[:, c, 0:N], in0=xc[:, cl, 0:N], in1=sg[:, cl, 0:N], op=ALU.mult), reads=[S["b_xc"], S["b_sg"]], writes=[b_zT])

        stageA_pe(0)
        stageA_el(0)
        for g in range(NG):
            if not (dbg & 8192):
                if g + 1 < NG:
                    stageA_pe(g + 1)
                    stageA_el(g + 1)
                stageB_pe(g)
                stageB_el(g)
                continue
            stageB_pe(g)
            if g + 1 < NG:
                stageA_pe(g + 1)
            stageB_el(g)
            if g + 1 < NG:
                stageA_el(g + 1)

    P.op("pool", lambda e: e.memset(halo[:], 0.0), writes=b_halo)
    P.op("pool", lambda e: e.memset(hst[:], 0.0), writes=[b_hst])
    for i in range(nt_prompt):
        for s4 in range(4):
            P.dma("sp", lambda e, i=i, s4=s4: e.dma_start(out=xt_a1[:, s4, :], in_=i_xp[i * TT + s4 * 128:i * TT + (s4 + 1) * 128, :]), writes=[b_xts[s4]])
        front(TT, 4, 128, 0, [(0, TT, 0)], xt_a1, b_xts)
        layer0(TT, 1, TT, 0)
        post(TT, 4, 128, 0, 0, wout, b_w, NCH, zT, b_zT, 128, xt_a1, b_xts)
        for s4 in range(4):
            P.dma("pool", lambda e, i=i, s4=s4: e.dma_start(out=s_x1[i * TT + s4 * 128:i * TT + (s4 + 1) * 128, :], in_=xt_a1[:, s4, :]), reads=[b_xts[s4]], writes=[b_sx1])
        if dbg == 31 and i < NQ:
            P.dma("sp", lambda e, i=i: e.dma_start(out=o_yp[i * TT:(i + 1) * TT, :].rearrange("(s p) f -> p s f", p=128), in_=xt_a1[:]), reads=b_xts)
    P.dma("sp", lambda e: e.dma_start(out=o_lhp.rearrange("(c p) -> p c", p=128), in_=hst[:, :, 0], allow_slow_non_contiguous=True), reads=[b_hst])
    for k3 in range(3):
        P.dma("sp", lambda e, k3=k3: e.dma_start(out=o_lcp[k3].rearrange("(c p) -> p c", p=128), in_=halo[:, :, 0, k3], allow_slow_non_contiguous=True), reads=b_halo)
    sc = [(0, ST, 1), (ST, 2 * ST, 2)]
    if do_sample:
        for sq in range(2):
            P.dma("sp", lambda e, sq=sq: e.dma_start(out=hst[:, :, sq], in_=i_sth[sq].rearrange("(c p) -> p c", p=128), allow_slow_non_contiguous=True), reads=[b_hst], writes=[b_hst])
            for k3 in range(3):
                P.dma("sp", lambda e, sq=sq, k3=k3: e.dma_start(out=halo[:, :, sq, k3], in_=i_stc[sq, k3].rearrange("(c p) -> p c", p=128), allow_slow_non_contiguous=True), reads=b_halo, writes=b_halo)
        P.dma("sp", lambda e: e.dma_start(out=xs_t[0:64, 0, :], in_=i_xs[:, :]), writes=[b_xs])
        front(2 * ST, 1, 64, 0, sc, xs_t, b_xs)
        layer0(2 * ST, 2, ST, 40)
        post(2 * ST, 1, 64, 0, 1, wout, b_w, NCH, zT, b_zT, 128, xs_t, b_xs)
        for sq in range(2):
            P.dma("sp", lambda e, sq=sq: e.dma_start(out=o_lhs[sq].rearrange("(c p) -> p c", p=128), in_=hst[:, :, sq], allow_slow_non_contiguous=True), reads=[b_hst])
            for k3 in range(3):
                P.dma("sp", lambda e, sq=sq, k3=k3: e.dma_start(out=o_lcs[sq, k3].rearrange("(c p) -> p c", p=128), in_=halo[:, :, sq, k3], allow_slow_non_contiguous=True), reads=b_halo)
    P.barrier()
    A.reset(m_persist)
    if stop == 1:
        P.emit()
        return nc

    fwin = A.alloc([128, 8, 2 * D + H], BF16, "fwin")
    b_w = P.buf("weights2")
    for kc in range(8):
        for hf in range(2):
            P.dma("pool", lambda e, kc=kc, hf=hf: e.dma_start(out=fwin[:, kc, hf * D:(hf + 1) * D], in_=i_fwin[kc * 128:(kc + 1) * 128, D + hf * D:D + (hf + 1) * D]), writes=[b_w])
        P.dma("pool", lambda e, kc=kc: e.dma_start(out=fwin[:, kc, 2 * D:2 * D + H], in_=i_fwin[kc * 128:(kc + 1) * 128, 4 * D:4 * D + H]), writes=[b_w])
    xt_a2s = [A.alloc([128, 4, D], F32, "xt_a2")]
    xt_a2 = xt_a2s[0]
    cx["xn"] = A.alloc([128, D], F32, "xn")
    cx["junk"] = A.alloc([128, D], BF16, "junk")
    cx["stat"] = A.alloc([128, 16], F32, "stat")
    cx["hT"] = A.alloc([128, 8, TT], BF16, "hT")
    kst = A.alloc([128, 4, D], F32, "kst")
    kstb = A.alloc([128, 4, D], BF16, "kstb")
    b_kstb = P.buf("kstb")
    vst = A.alloc([128, 4, D], F32, "vst")
    v1st = A.alloc([128, H, 4, VS], BF16, "v1st")
    kTst = A.alloc([KA, H, TT], BF16, "kTst")
    lft = A.alloc([128, 4, H], F32, "lft")
    lfT = A.alloc([16, TT], F32, "lfT")
    cumT = A.alloc([16, TT], F32, "cumT")
    ccar = A.alloc([16, 2], F32, "ccar")
    cx["spl"] = A.alloc([16, 3, TT], BF16, "spl")
    cx["spf"] = A.alloc([16, 2, TT], F32, "spf")
    nb("xn", "junk", "stat", "hT", "spl", "spf")
    b_xt, b_kst, b_vst, b_v1st, b_kTst, b_lft, b_lfT, b_cumT, b_ccar = [P.buf(n) for n in range(9)]
    P.op("pool", lambda e: e.memset(v1st[:], 1.0), writes=[b_v1st])
    P.op("pool", lambda e: e.memset(kTst[:], 1.0), writes=[b_kTst])
    P.op("pool", lambda e: e.memset(ccar[:], 0.0), writes=[b_ccar])

    def ktrans(N, nsub, Pt, cast=False):
        if cast:
            for s_ in range(nsub):
                P.op("dve", lambda e, s_=s_: e.tensor_copy(kstb[0:Pt, s_, :], kst[0:Pt, s_, :]), reads=[b_kst], writes=[b_kstb])
        for h in range(H):
            pk, bk = bank(h % 2)

            def trk(e, h=h, pk=pk):
                ins = None
                for s in range(nsub):
                    ins = e.matmul(pk[0:64, s * Pt:(s + 1) * Pt], lhsT=kstb[0:Pt, s, h * 64:(h + 1) * 64], rhs=ident_bf[0:Pt, 0:Pt], start=True, stop=True)
                return ins
            P.op("pe", trk, reads=[b_kstb, b_const], writes=[bk])
            if h % 2 == 0:
                P.op("act", lambda e, h=h, pk=pk: e.activation(out=kTst[0:64, h, 0:N], in_=pk[0:64, 0:N], func=AF.Identity), reads=[bk], writes=[b_kTst])
            else:
                P.op("dve", lambda e, h=h, pk=pk: e.tensor_copy(kTst[0:64, h, 0:N], pk[0:64, 0:N]), reads=[bk], writes=[b_kTst])

    def ckrows(N):
        spl, b_spl = cx["spl"], cx["b_spl"]
        for j in range(3):
            P.dma("sp", lambda e, j=j: e.dma_start(out=kTst[67 + j:68 + j, :, 0:N], in_=spl[:, j, 0:N]), reads=[b_spl, b_kTst], writes=[b_kTst])

    def kvproj(N, nsub, Pt, nseg, L, x1src, b_x1, seqcols, o_fk, o_fv, o_fl, tok0, kT_dsts, vv_dst, cum_dst):
        front(N, nsub, Pt, 1, seqcols, x1src, b_x1)
        if dbg == 33:
            for q_ in range(2):
                P.dma("pool", lambda e, q_=q_: e.dma_start(out=o_fvp[2048:2176, q_ * 2048:(q_ + 1) * 2048].rearrange("p (k n) -> p k n", k=4) if False else o_fvp[2048 + q_ * 128:2176 + q_ * 128, 0:1024].rearrange("p (k n) -> p k n", k=4)[:, :, 0:N // 2 if False else 256], in_=cx["hT"][:, q_ * 4:(q_ + 1) * 4, 0:256]), reads=[cx["b_hT"]])
            return
        if dbg == 1:
            return
        hT, b_hT = cx["hT"], cx["b_hT"]
        for s in range(nsub):
            for which, (wofs, stg, b_stg, o_d) in enumerate([(0, kst, b_kst, o_fk), (D, vst, b_vst, o_fv)]):
                pp = PS[2 + which]
                bpp = bPS[2 + which]

                def mm(e, s=s, pp=pp, wofs=wofs):
                    ins = None
                    for hf in range(2):
                        for kc in range(8):
                            ins = e.matmul(pp[0:Pt, hf * 512:(hf + 1) * 512], lhsT=hT[:, kc, s * Pt:(s + 1) * Pt], rhs=fwin[:, kc, wofs + hf * 512:wofs + (hf + 1) * 512], start=(kc == 0), stop=(kc == 7))
                    return ins
                P.op("pe", mm, reads=[b_hT, b_w], writes=bpp)
                P.op("act", lambda e, s=s, pp=pp, stg=stg: e.activation(out=stg[0:Pt, s, :], in_=pp[0:Pt, :], func=AF.Identity), reads=bpp, writes=[b_stg])
                if which == 0:
                    P.op("dve", lambda e, s=s: e.tensor_copy(kstb[0:Pt, s, :], kst[0:Pt, s, :]), reads=[b_kst], writes=[b_kstb])
                if which == 1 and dbg != 21:
                    if dbg == 22:
                        P.op("act", lambda e, s=s, pp=pp: e.activation(out=v1st[0:Pt, :, s, 0:64], in_=pp[0:Pt, :].rearrange("p (h d) -> p h d", h=H), func=AF.Identity), reads=bpp, writes=[b_v1st])
                    elif dbg == 23:
                        P.op("pool", lambda e, s=s: e.tensor_copy(v1st[0:Pt, :, s, 0:64], vst[0:Pt, s, :].rearrange("p (h d) -> p h d", h=H)), reads=[b_vst], writes=[b_v1st])
                    else:
                        P.op("dve", lambda e, s=s: e.tensor_copy(v1st[0:Pt, :, s, 0:64], vst[0:Pt, s, :].rearrange("p (h d) -> p h d", h=H)), reads=[b_vst], writes=[b_v1st])
                P.dma("sp", lambda e, s=s, stg=stg, o_d=o_d: e.dma_start(out=o_d[tok0 + s * Pt:tok0 + (s + 1) * Pt, :], in_=stg[0:Pt, s, :]), reads=[b_stg])
            if dbg in (2, 21, 22, 23):
                continue
            pf, bf_ = bank(0)

            def mfl(e, s=s, pf=pf):
                ins = None
                for kc in range(8):
                    ins = e.matmul(pf[0:Pt, 0:H], lhsT=hT[:, kc, s * Pt:(s + 1) * Pt], rhs=fwin[:, kc, 2 * D:2 * D + H], start=(kc == 0), stop=(kc == 7))
                return ins
            P.op("pe", mfl, reads=[b_hT, b_w], writes=[bf_])
            P.op("dve", lambda e, s=s, pf=pf: e.tensor_tensor(out=lft[0:Pt, s, :], in0=pf[0:Pt, 0:H], in1=bfrow[0:Pt, :], op=ALU.add), reads=[bf_, b_const], writes=[b_lft])
        if dbg in (2, 21, 22, 23):
            return
        P.op("act", lambda e: e.activation(out=lft[0:Pt, 0:nsub, :], in_=lft[0:Pt, 0:nsub, :], func=AF.Exp, scale=-1.0), reads=[b_lft], writes=[b_lft])
        P.op("act", lambda e: e.activation(out=lft[0:Pt, 0:nsub, :], in_=lft[0:Pt, 0:nsub, :], func=AF.Ln, bias=onec[0:Pt, 0:1]), reads=[b_lft, b_const], writes=[b_lft])
        P.op("dve", lambda e: e.tensor_scalar(out=lft[0:Pt, 0:nsub, :], in0=lft[0:Pt, 0:nsub, :], scalar1=-1.0, scalar2=None, op0=ALU.mult), reads=[b_lft], writes=[b_lft])
        with nc.allow_non_contiguous_dma(reason="64B rows"):
            P.dma("sp", lambda e: e.dma_start(out=o_fl[tok0:tok0 + nsub * Pt, :].rearrange("(s p) h -> p s h", p=Pt), in_=lft[0:Pt, 0:nsub, :], allow_slow_non_contiguous=True), reads=[b_lft])
        if dbg == 3:
            return
        pf, bf_ = bank(1)

        def mflT(e, pf=pf):
            ins = None
            for kc in range(8):
                ins = e.matmul(pf[0:H, 0:N], lhsT=fwin[:, kc, 2 * D:2 * D + H], rhs=hT[:, kc, 0:N], start=(kc == 0), stop=(kc == 7))
            return ins
        P.op("pe", mflT, reads=[b_hT, b_w], writes=[bf_])
        P.op("act", lambda e, pf=pf: e.activation(out=lfT[:, 0:N], in_=pf[0:H, 0:N], func=AF.Exp, scale=-1.0, bias=bfcol[:, 0:1]), reads=[bf_, b_const], writes=[b_lfT])
        P.op("act", lambda e: e.activation(out=lfT[:, 0:N], in_=lfT[:, 0:N], func=AF.Ln, bias=onec[0:16, 0:1]), reads=[b_lfT, b_const], writes=[b_lfT])
        for sgm in range(nseg):
            P.op("dve", lambda e, sgm=sgm: e.tensor_tensor_scan(out=cumT[:, sgm * L:(sgm + 1) * L], data0=ones_f[0:16, 0:L], data1=lfT[:, sgm * L:(sgm + 1) * L], initial=ccar[:, sgm:sgm + 1], op0=ALU.mult, op1=ALU.subtract),
                 reads=[b_lfT, b_ccar, b_const, b_cumT], writes=[b_cumT])
            P.op("dve", lambda e, sgm=sgm: e.tensor_copy(ccar[:, sgm:sgm + 1], cumT[:, (sgm + 1) * L - 1:(sgm + 1) * L]), reads=[b_cumT, b_ccar], writes=[b_ccar])
        if cum_dst is not None:
            P.dma("sp", lambda e: e.dma_start(out=cum_dst, in_=cumT[:, 0:N]), reads=[b_cumT])
        if dbg == 4:
            return
        split3(cumT[:, 0:N], b_cumT, N, -1.0)
        if dbg == 5:
            return
        ktrans(N, nsub, Pt)
        if dbg == 6:
            return
        ckrows(N)
        if dbg == 7:
            return
        for sgm, kd in enumerate(kT_dsts if not (dbg == 9 and Pt == 64) else []):
            P.dma("sp", lambda e, sgm=sgm, kd=kd: e.dma_start(out=kd, in_=kTst[:, :, sgm * L:(sgm + 1) * L]), reads=[b_kTst])
        for (vd, vsrc) in (vv_dst if not (dbg == 8 and Pt == 64) else []):
            P.dma("sp", lambda e, vd=vd, vsrc=vsrc: e.dma_start(out=vd, in_=vsrc), reads=[b_v1st])

    m_a2 = A.mark()
    xt_a2s.append(A.alloc([128, 4, D], F32, "xt_a2b"))
    hT_a2 = [cx["hT"], A.alloc([128, 8, TT], BF16, "hT2b")]
    b_hT_a2 = [cx["b_hT"], P.buf("hT2b")]
    b_xt2 = [b_xt, P.buf("xt2b")]
    for i in range(nt_prompt):
        xt_a2, b_xt = xt_a2s[i % 2], b_xt2[i % 2]
        cx["hT"], cx["b_hT"] = hT_a2[i % 2], b_hT_a2[i % 2]
        P.dma("pool", lambda e, i=i, xt_a2=xt_a2: e.dma_start(out=xt_a2[:], in_=s_x1[i * TT:(i + 1) * TT, :].rearrange("(s p) f -> p s f", p=128)), reads=[b_sx1], writes=[b_xt])
        if dbg == 34 and i < NQ:
            P.dma("sp", lambda e, i=i, xt_a2=xt_a2: e.dma_start(out=o_yp[i * TT:(i + 1) * TT, :].rearrange("(s p) f -> p s f", p=128), in_=xt_a2[:]), reads=[b_xt])
        m_ = i // 4
        vvd = [(s_vv[:, m_, :, (i % 4) * 4 * VS:(i % 4 + 1) * 4 * VS].rearrange("h p x -> p h x"), v1st[:].rearrange("p h s d -> p h (s d)"))]
        kvproj(TT, 4, 128, 1, TT, xt_a2, b_xt, [(0, TT, 0)], o_fkp, o_fvp, o_flp, i * TT,
               [s_kT[:, :, i * TT:(i + 1) * TT].rearrange("h r n -> r h n")], vvd, s_cum[:, i * TT:(i + 1) * TT])
    cx["hT"], cx["b_hT"] = hT_a2[0], b_hT_a2[0]
    P.barrier()
    A.reset(m_a2)
    if do_sample:
        clf = A.alloc([128, 2, PAST // 128, H], F32, "clf")
        pcum = A.alloc([16, 2, PAST], F32, "pcum")
        b_clf, b_pcum = P.buf("clf"), P.buf("pcum")
        with nc.allow_non_contiguous_dma(reason="64B rows"):
            for sq in range(2):
                P.dma("sp", lambda e, sq=sq: e.dma_start(out=clf[:, sq, :, :], in_=i_clf[sq].rearrange("(j p) h -> p j h", p=128), allow_slow_non_contiguous=True), writes=[b_clf])
        for sq in range(2):
            for q8 in range(PAST // 512):
                pk, bk = bank(q8 % 2)

                def trl(e, sq=sq, q8=q8, pk=pk):
                    ins = None
                    for j in range(4):
                        ins = e.matmul(pk[0:16, j * 128:(j + 1) * 128], lhsT=clf[:, sq, q8 * 4 + j, :], rhs=identf[:, :], start=True, stop=True)
                    return ins
                P.op("pe", trl, reads=[b_clf, b_const], writes=[bk])
                P.op("act", lambda e, sq=sq, q8=q8, pk=pk: e.activation(out=pcum[:, sq, q8 * 512:(q8 + 1) * 512], in_=pk[0:16, 0:512], func=AF.Identity), reads=[bk], writes=[b_pcum])
            for q8 in range(PAST // 512):
                init = 0.0 if q8 == 0 else pcum[:, sq, q8 * 512 - 1:q8 * 512]
                P.op("dve", lambda e, sq=sq, q8=q8, init=init: e.tensor_tensor_scan(out=pcum[:, sq, q8 * 512:(q8 + 1) * 512], data0=ones_f[0:16, 0:512], data1=pcum[:, sq, q8 * 512:(q8 + 1) * 512], initial=init, op0=ALU.mult, op1=ALU.add),
                     reads=[b_pcum, b_const], writes=[b_pcum])
            P.op("dve", lambda e, sq=sq: e.tensor_copy(ccar[:, sq:sq + 1], pcum[:, sq, PAST - 1:PAST]), reads=[b_pcum, b_ccar], writes=[b_ccar])
        for sq in range(2 if dbg != 41 else 0):
            for q8 in range(PAST // 512):
                P.dma("sp", lambda e, sq=sq, q8=q8: e.dma_start(out=kst[:], in_=i_ck[sq, q8 * 512:(q8 + 1) * 512, :].rearrange("(s p) f -> p s f", p=128)), writes=[b_kst])
                P.dma("sp", lambda e, sq=sq, q8=q8: e.dma_start(out=vst[:], in_=i_cv[sq, q8 * 512:(q8 + 1) * 512, :].rearrange("(s p) f -> p s f", p=128)), writes=[b_vst])
                for s_ in range(4):
                    P.op("act", lambda e, s_=s_: e.activation(out=v1st[:, :, s_, 0:64], in_=vst[:, s_, :].rearrange("p (h d) -> p h d", h=H), func=AF.Identity), reads=[b_vst], writes=[b_v1st])
                P.dma("sp", lambda e, sq=sq, q8=q8: e.dma_start(out=s_vvs[sq, :, q8 // 4, :, (q8 % 4) * 4 * VS:(q8 % 4 + 1) * 4 * VS].rearrange("h p x -> p h x"), in_=v1st[:].rearrange("p h s d -> p h (s d)")), reads=[b_v1st])
                ktrans(512, 4, 128, cast=True)
                split3(pcum[:, sq, q8 * 512:(q8 + 1) * 512], b_pcum, 512, -1.0)
                ckrows(512)
                P.dma("sp", lambda e, sq=sq, q8=q8: e.dma_start(out=s_kTs[sq, :, :, q8 * 512:(q8 + 1) * 512].rearrange("h r n -> r h n"), in_=kTst[:]), reads=[b_kTst])
        vvd = [(s_vvs[sq, :, NPR, 0:ST, 0:VS].rearrange("h p x -> p h x"), v1st[sq * ST:(sq + 1) * ST, :, 0, :]) for sq in range(2)]
        if dbg not in (41, 42):
          kvproj(2 * ST, 1, 64, 2, ST, xs_t, b_xs, sc, o_fks, o_fvs, o_fls, 0,
               [s_kTs[sq, :, :, PAST:PAST + ST].rearrange("h r n -> r h n") for sq in range(2)], vvd, None)
        P.op("dve", lambda e: e.tensor_copy(scum[:, :], cumT[:, 0:2 * ST]), reads=[b_cumT], writes=[b_scum])
    P.barrier()
    A.reset(m_persist)
    if not do_attn:
        P.emit()
        return nc

    wq = A.alloc([128, 8, 2 * D], BF16, "wq")
    wo = A.alloc([64, H, D], BF16, "wo")
    pmask = A.alloc([128, 16, TT], BF16, "pmask")
    smask = A.alloc([ST, ST], BF16, "smask")
    b_w2 = P.buf("w2")
    for kc in range(8):
        P.dma("pool", lambda e, kc=kc: e.dma_start(out=wq[:, kc, 0:D], in_=i_fwin[kc * 128:(kc + 1) * 128, 0:D]), writes=[b_w2])
        P.dma("pool", lambda e, kc=kc: e.dma_start(out=wq[:, kc, D:2 * D], in_=i_fwin[kc * 128:(kc + 1) * 128, 3 * D:4 * D]), writes=[b_w2])
    for h in range(H):
        P.dma("pool", lambda e, h=h: e.dma_start(out=wo[:, h, :], in_=i_fwout[h * 64:(h + 1) * 64, :]), writes=[b_w2])
    P.dma("sp", lambda e: e.dma_start(out=pmask[:], in_=i_pmask[:, :, :]), writes=[b_w2])
    P.dma("sp", lambda e: e.dma_start(out=smask[:], in_=i_smask[:, :]), writes=[b_w2])
    xt_c = A.alloc([128, 4, D], F32, "xt2")
    cx["xn"] = A.alloc([128, D], F32, "xn")
    cx["junk"] = A.alloc([128, D], BF16, "junk")
    cx["stat"] = A.alloc([128, 16], F32, "stat")
    cx["hT"] = A.alloc([128, 8, TT], BF16, "hT")
    cx["spl"] = A.alloc([16, 3, TT], BF16, "spl")
    cx["spf"] = A.alloc([16, 2, TT], F32, "spf")
    nb("xn", "junk", "stat", "hT", "spl", "spf")
    cx["ytmp"] = cx["xn"]
    cx["b_ytmp"] = cx["b_xn"]
    xl = cx["xn"]
    qaug = A.alloc([KA, H, TT], BF16, "qaug")
    sgT = A.alloc([64, H, TT], BF16, "sgT")
    zT2 = sgT
    csel = A.alloc([16, 2, TT], F32, "csel")
    NKB = 2
    kch = [A.alloc([KA, 2048], BF16, "kch") for _ in range(NKB)]
    vch = [A.alloc([128, 16 * VS], BF16, "vch") for _ in range(NKB)]
    NSG = 4
    pT = [A.alloc([128, 512], BF16, "pT") for _ in range(NSG)]
    osb = A.alloc([65, TT], F32, "osb")
    rl = A.alloc([65, TT], F32, "rl")
    b_xt, b_xl, b_qaug, b_sgT, b_zT2, b_csel, b_osb, b_rl = [P.buf(n) for n in range(8)]
    b_xl = cx["b_xn"]
    b_zT2 = b_sgT
    b_kch = [P.buf("kch") for _ in range(NKB)]
    b_vch = [P.buf("vch") for _ in range(NKB)]
    b_pT = [P.buf("pT") for _ in range(NSG)]
    P.op("pool", lambda e: e.memset(qaug[:], 1.0), writes=[b_qaug])
    chunk_ctr = [0]
    grp_ctr = [0]

    def attend(N, nsub, Pt, seqs, x_res, b_xres, layer_grp, o_y, y_tok0):
        flat = []
        item_ctr = [0]
        pending = []
        EPI_DELAY = 3
        for h in range(H):
            for sq in seqs:
                q0, q1 = sq["q0"], sq["q1"]
                nq = q1 - q0
                work = [(k_ap, v_ap, 128, 16, None) for (k_ap, v_ap) in sq["rows"]] + list(sq["diag"])
                first = True
                for (k_src, v_src, nk, ntile, mask_fn) in work:
                    item = dict(k_src=k_src, v_src=v_src, nk=nk, ntile=ntile, cb=None, idx=item_ctr[0])
                    item_ctr[0] += 1
                    G = min(ntile, 512 // nq if nq > 256 else 16)
                    for g0 in range(0, ntile, G):
                        flat.append(dict(h=h, q0=q0, nq=nq, item=item, g0=g0, G=G, nk=nk, mask_fn=mask_fn, first=first, last_of_head=False))
                        first = False
            flat[-1]["last_of_head"] = True

        def emit_S(en):
            it = en["item"]
            h, nk, ntile = en["h"], en["nk"], it["ntile"]
            if it["cb"] is None:
                cb_ = chunk_ctr[0] % NKB
                chunk_ctr[0] += 1
                it["cb"] = cb_
                P.dma("sp", lambda e, cb_=cb_, k_src=it["k_src"], nk=nk, ntile=ntile, h=h: e.dma_start(out=kch[cb_][:, 0:nk * ntile], in_=k_src[h]), writes=[b_kch[cb_]])
                P.dma("sp", lambda e, cb_=cb_, v_src=it["v_src"], nk=nk, ntile=ntile, h=h: e.dma_start(out=vch[cb_][0:nk, 0:ntile * VS], in_=v_src[h]), writes=[b_vch[cb_]])
            cb_ = it["cb"]
            gi = grp_ctr[0] % NSG
            grp_ctr[0] += 1
            en["gi"] = gi
            psg, bpsg = bank(gi)

            def mmS(e, cb_=cb_, g0=en["g0"], psg=psg, h=h, q0=en["q0"], nq=en["nq"], G=en["G"], nk=nk, mask_fn=en["mask_fn"]):
                ins = None
                for j in range(G):
                    kt = g0 + j
                    ins = e.matmul(psg[0:nk, j * nq:(j + 1) * nq], lhsT=kch[cb_][:, kt * nk:(kt + 1) * nk], rhs=qaug[:, h, q0:q0 + nq], start=True, stop=(mask_fn is None))
                    if mask_fn is not None:
                        ins = e.matmul(psg[0:nk, j * nq:(j + 1) * nq], lhsT=ident_bf[0:nk, 0:nk], rhs=mask_fn(kt), start=False, stop=True)
                return ins
            P.op("pe", mmS, reads=[b_kch[cb_], b_qaug, b_w2, b_const], writes=[bpsg])

        def emit_EV(en):
            h, nk, nq, G, gi, q0 = en["h"], en["nk"], en["nq"], en["G"], en["gi"], en["q0"]
            cb_ = en["item"]["cb"]
            psg, bpsg = bank(gi)
            po, bo = bank(6 + (h % 2))
            P.op("act", lambda e, psg=psg, gi=gi, G=G, nq=nq, nk=nk: e.activation(out=pT[gi][0:nk, 0:G * nq], in_=psg[0:nk, 0:G * nq], func=AF.Exp), reads=[bpsg], writes=[b_pT[gi]])

            def mmV(e, cb_=cb_, g0=en["g0"], gi=gi, po=po, q0=q0, nq=nq, G=G, nk=nk, fst=en["first"]):
                ins = None
                for j in range(G):
                    kt = g0 + j
                    ins = e.matmul(po[0:65, q0:q0 + nq], lhsT=vch[cb_][0:nk, kt * VS:kt * VS + 65], rhs=pT[gi][0:nk, j * nq:(j + 1) * nq], start=(fst and j == 0), stop=False, skip_group_check=True)
                return ins
            P.op("pe", mmV, reads=[b_vch[cb_], b_pT[gi]], writes=[bo])
            if en["last_of_head"]:
                P.op("act", lambda e, po=po: e.activation(out=osb[0:65, 0:N], in_=po[0:65, 0:N], func=AF.Identity), reads=[bo], writes=[b_osb])
                P.op("dve", lambda e: e.reciprocal(rl[64:65, 0:N], osb[64:65, 0:N]), reads=[b_osb], writes=[b_rl])
                pending.append([EPI_DELAY, h])

        def emit_epi_tail(h):
            if True:
                pbq, bbq = bank(5)
                P.op("pe", lambda e, pbq=pbq: e.matmul(pbq[0:64, 0:N], lhsT=ones_f[64:65, 0:64], rhs=rl[64:65, 0:N], start=True, stop=True), reads=[b_rl, b_const], writes=[bbq])
                P.op("dve", lambda e, pbq=pbq: e.tensor_tensor(out=osb[0:64, 0:N], in0=osb[0:64, 0:N], in1=pbq[0:64, 0:N], op=ALU.mult), reads=[b_osb, bbq], writes=[b_osb])
                P.op("pool", lambda e, h=h: e.tensor_tensor(out=zT2[:, h, 0:N], in0=osb[0:64, 0:N], in1=sgT[:, h, 0:N], op=ALU.mult), reads=[b_osb, b_sgT], writes=[b_zT2])

        nxt = 0
        for i, en in enumerate(flat):
            while nxt < len(flat) and nxt <= i + NSG - 1 and flat[nxt]["item"]["idx"] <= en["item"]["idx"] + NKB - 1:
                emit_S(flat[nxt])
                nxt += 1
            for p_ in pending:
                p_[0] -= 1
            while pending and pending[0][0] <= 0:
                emit_epi_tail(pending.pop(0)[1])
            emit_EV(en)
        while pending:
            emit_epi_tail(pending.pop(0)[1])
        post(N, nsub, Pt, 1, layer_grp, wo, b_w2, H, zT2, b_zT2, 64, x_res, b_xres)
        P.dma("sp", lambda e: e.dma_start(out=o_y[y_tok0:y_tok0 + nsub * Pt, :].rearrange("(s p) f -> p s f", p=Pt), in_=x_res[0:Pt, 0:nsub, :]), reads=[b_xres])

    def qproj(N, seqcols, xsrc, b_x, nsub, Pt, cq_src, b_cq):
        front(N, nsub, Pt, 1, seqcols, xsrc, b_x)
        hT, b_hT = cx["hT"], cx["b_hT"]
        spl, b_spl = cx["spl"], cx["b_spl"]
        for h in range(H):
            pq, bq = bank(6)
            pg, bg = bank(7)

            def mq(e, h=h, pq=pq, pg=pg):
                ins = None
                for kc in range(8):
                    ins = e.matmul(pq[0:64, 0:N], lhsT=wq[:, kc, h * 64:(h + 1) * 64], rhs=hT[:, kc, 0:N], start=(kc == 0), stop=(kc == 7))
                for kc in range(8):
                    ins = e.matmul(pg[0:64, 0:N], lhsT=wq[:, kc, D + h * 64:D + (h + 1) * 64], rhs=hT[:, kc, 0:N], start=(kc == 0), stop=(kc == 7))
                return ins
            P.op("pe", mq, reads=[b_hT, b_w2], writes=[bq, bg])
            P.op("dve", lambda e, h=h, pq=pq: e.tensor_scalar(out=qaug[0:64, h, 0:N], in0=pq[0:64, 0:N], scalar1=0.125, scalar2=None, op0=ALU.mult), reads=[bq], writes=[b_qaug])
            P.op("act", lambda e, h=h, pg=pg: e.activation(out=sgT[:, h, 0:N], in_=pg[0:64, 0:N], func=AF.Silu), reads=[bg], writes=[b_sgT])
        split3(cq_src, b_cq, N, 1.0)
        for j in range(3):
            P.dma("sp", lambda e, j=j: e.dma_start(out=qaug[64 + j:65 + j, :, 0:N], in_=spl[:, j, 0:N]), reads=[b_spl, b_qaug], writes=[b_qaug])

    if do_sample:
        qproj(2 * ST, sc, xs_t, b_xs, 1, 64, scum[:, :], b_scum)
        seqs = []
        for sq in range(2):
            rows = [(s_kTs[sq, :, :, rw * 2048:(rw + 1) * 2048], s_vvs[sq, :, rw, :, :]) for rw in range(NPR)]
            diag = [(s_kTs[sq, :, :, PAST:PAST + ST], s_vvs[sq, :, NPR, 0:ST, 0:VS], ST, 1, (lambda kt: smask[:, :]))]
            seqs.append(dict(q0=sq * ST, q1=(sq + 1) * ST, rows=rows, diag=diag))
        attend(2 * ST, 1, 64, seqs, xs_t, b_xs, 1, o_ys, 0)
    for m in range(nq_tiles):
        for r_ in range(4):
            i = 4 * m + r_
            P.dma("sp", lambda e, i=i: e.dma_start(out=csel[:, 1, :], in_=s_cum[:, i * TT:(i + 1) * TT]), writes=[b_csel])
            if r_ == 0:
                P.op("dve", lambda e: e.tensor_scalar(out=csel[:, 0, :], in0=csel[:, 1, :], scalar1=onehot[0:16, 0:1], scalar2=None, op0=ALU.mult), reads=[b_csel, b_const], writes=[b_csel])
            else:
                P.op("dve", lambda e, r_=r_: e.scalar_tensor_tensor(out=csel[:, 0, :], in0=csel[:, 1, :], scalar=onehot[0:16, r_:r_ + 1], in1=csel[:, 0, :], op0=ALU.mult, op1=ALU.add), reads=[b_csel, b_const], writes=[b_csel])
            for s4 in range(4):
                P.dma("act", lambda e, i=i, s4=s4: e.dma_start(out=xl[:, :], in_=s_x1[i * TT + s4 * 128:i * TT + (s4 + 1) * 128, :]), writes=[b_xl])
                if r_ == 0:
                    P.op("dve", lambda e, s4=s4: e.tensor_scalar(out=xt_c[:, s4, :], in0=xl[:, :], scalar1=onehot[:, 0:1], scalar2=None, op0=ALU.mult), reads=[b_xl, b_const], writes=[b_xt])
                else:
                    P.op("dve", lambda e, r_=r_, s4=s4: e.scalar_tensor_tensor(out=xt_c[:, s4, :], in0=xl[:, :], scalar=onehot[:, r_:r_ + 1], in1=xt_c[:, s4, :], op0=ALU.mult, op1=ALU.add), reads=[b_xl, b_const, b_xt], writes=[b_xt])
        qproj(TT, [(0, TT, 0)], xt_c, b_xt, 4, 128, csel[:, 0, :], b_csel)
        rows = [(s_kT[:, :, mm_ * 2048:(mm_ + 1) * 2048], s_vv[:, mm_, :, :]) for mm_ in range(m)]
        diag = [(s_kT[:, :, m * 2048:(m + 1) * 2048], s_vv[:, m, :, :], 128, 16, (lambda kt: pmask[:, kt, :]))]
        attend(TT, 4, 128, [dict(q0=0, q1=TT, rows=rows, diag=diag)], xt_c, b_xt, 0, o_yp, m * TT)
    P.emit()
    return nc


_CACHE = {}


def _get_nc(**kw):
    key = tuple(sorted(kw.items()))
    if key not in _CACHE:
        _CACHE[key] = build(**kw)
    return _CACHE[key]


def _consts(core):
    r = core % 4
    ident = np.eye(128, dtype=np.float32)
    sel = np.zeros((3, 2, 128), np.float32)
    sel[0, 0, :] = 1.0
    sel[1, 1, 0:ST] = 1.0
    sel[2, 1, ST:2 * ST] = 1.0
    kpos = (np.arange(16)[None, :, None] * 128 + np.arange(128)[:, None, None])
    qpos = r * TT + np.arange(TT)[None, None, :]
    pmask = np.where(kpos > qpos, NEG, 0.0).astype(ml_dtypes.bfloat16)
    smask = np.where(np.arange(ST)[:, None] > np.arange(ST)[None, :], NEG, 0.0).astype(ml_dtypes.bfloat16)
    onehot = np.zeros((128, 4), np.float32)
    onehot[:, r] = 1.0
    return dict(c_ident=ident, c_sel=sel, c_pmask=np.ascontiguousarray(pmask), c_smask=smask, c_onehot=onehot)


def _in_map(c, I, shared, T, PAST):
    f = lambda a: np.ascontiguousarray(np.asarray(a, dtype=np.float32))
    b = c // 4
    s0 = 2 * c
    m = dict(shared)
    m.update(_consts(c))
    m["xp"] = f(I["x_prompt"][b])
    m["xs"] = f(I["x_sample"][s0:s0 + 2]).reshape(2 * ST, D)
    m["cc"] = np.concatenate([f(I["c_prompt"])[b:b + 1], f(I["c_sample"])[s0:s0 + 2]], axis=0)
    m["st_h"] = f(I["state_lru_h"][0, s0:s0 + 2])
    m["st_conv"] = f(I["state_lru_conv"][0, s0:s0 + 2])
    m["ck_k"] = f(I["cache_fox_k"][0, s0:s0 + 2]).reshape(2, PAST, D)
    m["ck_v"] = f(I["cache_fox_v"][0, s0:s0 + 2]).reshape(2, PAST, D)
    m["ck_lf"] = f(I["cache_fox_logf"][0, s0:s0 + 2])
    return m


def _shared(I):
    f = lambda a: np.ascontiguousarray(np.asarray(a, dtype=np.float32))
    lvec = np.concatenate([f(I["lru_conv_w"])[0], f(I["lru_conv_b"]), f(I["lru_b_a"]), f(I["lru_b_x"]), f(I["lru_lambda"])], axis=0)
    return dict(norm_pre=f(I["norm_pre"]), norm_post=f(I["norm_post"]), ada_w=f(I["ada_w"]), ada_b=f(I["ada_b"]), lru_w_in=f(I["lru_w_in"])[0],
                lru_vecs=f(lvec), lru_w_a=f(I["lru_w_a"])[0], lru_w_x=f(I["lru_w_x"])[0], lru_w_out=f(I["lru_w_out"])[0],
                fox_w_in=f(I["fox_w_in"])[0], fox_b_f=f(I["fox_b_f"]), fox_w_out=f(I["fox_w_out"])[0])


def run_cores(I, cores, build_kw):
    T = I["x_prompt"].shape[1]
    PAST = I["cache_fox_k"].shape[2]
    kw = dict(build_kw)
    kw.update(Tn=T, PASTn=PAST)
    nc = _get_nc(**kw)
    shared = _shared(I)
    in_maps = [_in_map(c, I, shared, T, PAST) for c in cores]
    res = run_bass_kernel_spmd(nc, in_maps, core_ids=list(range(len(cores)))).results
    return {c: res[i] for i, c in enumerate(cores)}


def kernel(x_prompt, x_sample, c_prompt, c_sample, state_lru_h, state_lru_conv, cache_fox_k, cache_fox_v, cache_fox_logf,
           norm_pre, norm_post, ada_w, ada_b, lru_w_in, lru_conv_w, lru_conv_b, lru_w_a, lru_b_a, lru_w_x, lru_b_x,
           lru_lambda, lru_w_out, fox_w_in, fox_b_f, fox_w_out, _build_kw=None):
    I = dict(x_prompt=x_prompt, x_sample=x_sample, c_prompt=c_prompt, c_sample=c_sample, state_lru_h=state_lru_h, state_lru_conv=state_lru_conv,
             cache_fox_k=cache_fox_k, cache_fox_v=cache_fox_v, cache_fox_logf=cache_fox_logf, norm_pre=norm_pre, norm_post=norm_post,
             ada_w=ada_w, ada_b=ada_b, lru_w_in=lru_w_in, lru_conv_w=lru_conv_w, lru_conv_b=lru_conv_b, lru_w_a=lru_w_a, lru_b_a=lru_b_a,
             lru_w_x=lru_w_x, lru_b_x=lru_b_x, lru_lambda=lru_lambda, lru_w_out=lru_w_out, fox_w_in=fox_w_in, fox_b_f=fox_b_f, fox_w_out=fox_w_out)
    I = {k: np.asarray(v) for k, v in I.items()}
    res = run_cores(I, list(range(8)), _build_kw if _build_kw is not None else {})
    return assemble(res, I)


def assemble(res, I):
    B, T = I["x_prompt"].shape[0], I["x_prompt"].shape[1]
    NQ = T // 2048
    cores = sorted(res.keys())
    y_p = np.zeros((B, T, D), np.float32)
    for c in cores:
        b, r = c // 4, c % 4
        yp = res[c]["y_p"].reshape(NQ, TT, D)
        for m_ in range(NQ):
            i = 4 * m_ + r
            y_p[b, i * TT:(i + 1) * TT] = yp[m_]
    nb = len(cores) // 4 if len(cores) >= 4 else 1
    bs = sorted(set(c // 4 for c in cores))
    first = {b: min(c for c in cores if c // 4 == b) for b in bs}
    y_s = np.concatenate([res[c]["y_s"].reshape(2, ST, D) for c in cores], axis=0)
    lru_h_p = np.stack([res[first[b]]["lru_h_p"] for b in bs])[None]
    lru_c_p = np.stack([res[first[b]]["lru_conv_p"] for b in bs])[None]
    fk_p = np.stack([res[first[b]]["fk_p"].reshape(T, H, DH) for b in bs])[None]
    fv_p = np.stack([res[first[b]]["fv_p"].reshape(T, H, DH) for b in bs])[None]
    fl_p = np.stack([res[first[b]]["flf_p"] for b in bs])[None]
    lru_h_s = np.concatenate([res[c]["lru_h_s"] for c in cores], axis=0)[None]
    lru_c_s = np.concatenate([res[c]["lru_conv_s"] for c in cores], axis=0)[None]
    fk_s = np.concatenate([res[c]["fk_s"].reshape(2, ST, H, DH) for c in cores], axis=0)[None]
    fv_s = np.concatenate([res[c]["fv_s"].reshape(2, ST, H, DH) for c in cores], axis=0)[None]
    fl_s = np.concatenate([res[c]["flf_s"].reshape(2, ST, H) for c in cores], axis=0)[None]
    return (y_p, y_s, lru_h_p, lru_c_p, fk_p, fv_p, fl_p, lru_h_s, lru_c_s, fk_s, fv_s, fl_s)
```

```python
import contextlib
import numpy as np
import ml_dtypes
import concourse.bass as bass
import concourse.mybir as mybir
from concourse.bass_utils import run_bass_kernel_spmd

F32 = mybir.dt.float32
BF16 = mybir.dt.bfloat16
AF = mybir.ActivationFunctionType
ALU = mybir.AluOpType

D = 1024
R = 1536
NCH = 12
H = 16
DH = 64
T = 16384
TT = 512
NT = T // TT
ST = 32
PAST = 4096
EPS = 1e-6
NEG = -30000.0
VS = 66
KA = 70
COMPUTE = ("pe", "act", "dve", "pool", "sp")


class Buf:
    __slots__ = ("name", "last_w", "readers")

    def __init__(self, name):
        self.name = name
        self.last_w = None
        self.readers = []


class Op:
    __slots__ = ("eng", "fn", "deps", "is_dma", "idx", "need_inc", "val", "sem", "semval", "prev_on_sem", "inc")

    def __init__(self, eng, fn, is_dma, inc=16):
        self.eng = eng
        self.fn = fn
        self.deps = set()
        self.is_dma = is_dma
        self.need_inc = False
        self.val = None
        self.sem = None
        self.semval = None
        self.prev_on_sem = None
        self.inc = inc


class Prog:
    def __init__(self, nc, n_dma_sems=(("sp", 32), ("act", 12), ("pool", 12))):
        self.nc = nc
        self.ops = []
        self.by_eng = {e: [] for e in COMPUTE}
        self.dma_ring = {e: n for e, n in n_dma_sems}
        self.dma_count = {e: 0 for e, _ in n_dma_sems}
        self.dma_last_on_slot = {}
        self.all_bufs = []

    def buf(self, name=""):
        b = Buf(name)
        self.all_bufs.append(b)
        return b

    def _add(self, op, reads, writes):
        op.idx = len(self.ops)
        for b in reads:
            if b.last_w is not None:
                op.deps.add(b.last_w)
        for b in writes:
            if b.last_w is not None:
                op.deps.add(b.last_w)
            for r in b.readers:
                op.deps.add(r)
        for b in reads:
            b.readers.append(op)
        for b in writes:
            b.last_w = op
            b.readers = []
        op.deps.discard(op)
        self.ops.append(op)
        self.by_eng[op.eng].append(op)
        return op

    def op(self, eng, fn, reads=(), writes=()):
        return self._add(Op(eng, fn, False), reads, writes)

    def dma(self, eng, fn, reads=(), writes=(), inc=16):
        op = Op(eng, fn, True, inc)
        n = self.dma_count[eng]
        self.dma_count[eng] = n + 1
        slot = (eng, n % self.dma_ring[eng])
        op.sem = slot
        op.prev_on_sem = self.dma_last_on_slot.get(slot)
        op.semval = (op.prev_on_sem.semval if op.prev_on_sem else 0) + inc
        self.dma_last_on_slot[slot] = op
        return self._add(op, reads, writes)

    def barrier(self):
        lasts = [self.by_eng[e][-1] for e in COMPUTE if self.by_eng[e]]
        lasts += list(self.dma_last_on_slot.values())
        for e in COMPUTE:
            o = Op(e, None, False)
            o.idx = len(self.ops)
            o.deps = set(lasts)
            self.ops.append(o)
            self.by_eng[e].append(o)
        for b in self.all_bufs:
            if str(b.name).startswith("dram_"):
                continue
            b.last_w = None
            b.readers = []

    def emit(self, final_wait_eng="sp"):
        nc = self.nc
        lasts = [self.by_eng[e][-1] for e in COMPUTE if self.by_eng[e]]
        lasts += list(self.dma_last_on_slot.values())
        fin = Op(final_wait_eng, None, False)
        fin.idx = len(self.ops)
        fin.deps = set(lasts)
        self.ops.append(fin)
        self.by_eng[final_wait_eng].append(fin)
        for o in self.ops:
            for d in o.deps:
                if not d.is_dma:
                    if d.eng == o.eng and o.eng == "pe":
                        continue
                    d.need_inc = True
        for e in COMPUTE:
            c = 0
            for o in self.by_eng[e]:
                if not o.is_dma and o.need_inc:
                    c += 1
                    o.val = c
        with contextlib.ExitStack() as st:
            esem = {e: st.enter_context(nc.semaphore("s_" + e)) for e in COMPUTE}
            dsem = {}
            for e, n in self.dma_ring.items():
                for i in range(n):
                    dsem[(e, i)] = st.enter_context(nc.semaphore("d_%s_%d" % (e, i)))
            block = st.enter_context(nc.Block())
            engobj = {"pe": nc.tensor, "act": nc.scalar, "dve": nc.vector, "pool": nc.gpsimd, "sp": nc.sync}

            def run_engine(e):
                eng = engobj[e]
                waited = {}

                def wait(key, sem, val):
                    if waited.get(key, 0) >= val:
                        return
                    waited[key] = val
                    eng.wait_ge(sem, val)

                for o in self.by_eng[e]:
                    for d in sorted(o.deps, key=lambda d: d.idx):
                        if d.is_dma:
                            wait(d.sem, dsem[d.sem], d.semval)
                        else:
                            if d.eng == e and e == "pe":
                                continue
                            if d.val is None:
                                continue
                            wait(d.eng, esem[d.eng], d.val)
                    if o.is_dma and o.prev_on_sem is not None:
                        wait(o.sem, dsem[o.sem], o.prev_on_sem.semval)
                    if o.fn is None:
                        if o.need_inc:
                            eng.nop().then_inc(esem[e], 1)
                        continue
                    ins = o.fn(eng)
                    if o.is_dma:
                        ins.then_inc(dsem[o.sem], o.inc)
                    elif o.need_inc:
                        ins.then_inc(esem[e], 1)

            @block.tensor
            def _(x):
                run_engine("pe")

            @block.scalar
            def _(x):
                run_engine("act")

            @block.vector
            def _(x):
                run_engine("dve")

            @block.gpsimd
            def _(x):
                run_engine("pool")

            @block.sync
            def _(x):
                run_engine("sp")


class Arena:
    def __init__(self, nc, base=16640, limit=229376 - 2048):
        self.nc = nc
        self.ptr = base
        self.limit = limit
        self.n = 0

    def alloc(self, shape, dtype, name="t"):
        size = int(np.prod(shape[1:])) * (4 if dtype == F32 else 2)
        size = (size + 63) // 64 * 64
        off = self.ptr
        self.ptr += size
        assert self.ptr <= self.limit, ("SBUF overflow", name, self.ptr)
        self.n += 1
        return self.nc.alloc_sbuf_tensor_at("%s_%d" % (name, self.n), list(shape), dtype, offset=off)

    def mark(self):
        return self.ptr

    def reset(self, m):
        self.ptr = m


def build(nq_tiles=None, do_attn=True, do_sample=True, nt_prompt=None, stop=9, Tn=16384, PASTn=4096, dbg=0):
    T, PAST = Tn, PASTn
    NPR = PAST // 2048
    NQ = T // 2048
    if nq_tiles is None:
        nq_tiles = NQ
    if nt_prompt is None:
        nt_prompt = T // TT
    nc = bass.Bass("TRN2", target_bir_lowering=False)
    P = Prog(nc)
    A = Arena(nc)

    def din(name, shape, dt=F32):
        return nc.dram_tensor(name, list(shape), dt, kind="ExternalInput").ap()

    def dout(name, shape, dt=F32):
        return nc.dram_tensor(name, list(shape), dt, kind="ExternalOutput").ap()

    def dscr(name, shape, dt=F32):
        return nc.dram_tensor(name, list(shape), dt).ap()

    i_xp = din("xp", [T, D])
    i_xs = din("xs", [2 * ST, D])
    i_cc = din("cc", [3, D])
    i_sth = din("st_h", [2, R])
    i_stc = din("st_conv", [2, 3, R])
    i_ck = din("ck_k", [2, PAST, D])
    i_cv = din("ck_v", [2, PAST, D])
    i_clf = din("ck_lf", [2, PAST, H])
    i_npre = din("norm_pre", [2, D])
    i_npost = din("norm_post", [2, D])
    i_adaw = din("ada_w", [2, D, 3 * D])
    i_adab = din("ada_b", [2, 3 * D])
    i_win = din("lru_w_in", [D, 2 * R])
    i_lvec = din("lru_vecs", [8, R])
    i_wa = din("lru_w_a", [NCH, 128, 128])
    i_wx = din("lru_w_x", [NCH, 128, 128])
    i_wout = din("lru_w_out", [R, D])
    i_fwin = din("fox_w_in", [D, 4 * D + H])
    i_fbf = din("fox_b_f", [1, H])
    i_fwout = din("fox_w_out", [D, D])
    i_ident = din("c_ident", [128, 128])
    i_sel = din("c_sel", [3, 2, 128])
    i_pmask = din("c_pmask", [128, 16, TT], BF16)
    i_smask = din("c_smask", [ST, ST], BF16)
    i_onehot = din("c_onehot", [128, 4])
    o_yp = dout("y_p", [NQ * TT, D])
    o_ys = dout("y_s", [2 * ST, D])
    o_lhp = dout("lru_h_p", [R])
    o_lcp = dout("lru_conv_p", [3, R])
    o_fkp = dout("fk_p", [T, D])
    o_fvp = dout("fv_p", [T, D])
    o_flp = dout("flf_p", [T, H])
    o_lhs = dout("lru_h_s", [2, R])
    o_lcs = dout("lru_conv_s", [2, 3, R])
    o_fks = dout("fk_s", [2 * ST, D])
    o_fvs = dout("fv_s", [2 * ST, D])
    o_fls = dout("flf_s", [2 * ST, H])
    s_x1 = dscr("s_x1", [T, D])
    s_kT = dscr("s_kT", [H, KA, T], BF16)
    s_vv = dscr("s_vv", [H, T // 2048, 128, 16 * VS], BF16)
    s_cum = dscr("s_cum", [H, T])
    NK_S = PAST + 128
    s_kTs = dscr("s_kTs", [2, H, KA, NK_S], BF16)
    s_vvs = dscr("s_vvs", [2, H, NPR + 1, 128, 16 * VS], BF16)

    b_sx1, b_skT, b_svv, b_scm, b_skTs, b_svvs = [P.buf("dram_%d" % i) for i in range(6)]
    PS = [nc.alloc_psum_tensor("ps%d" % i, [128, 1024], F32) for i in range(4)]
    bPS = [[P.buf("ps%d_%d" % (i, j)) for j in range(2)] for i in range(4)]

    def bank(i):
        return PS[i // 2][:, (i % 2) * 512:(i % 2) * 512 + 512], bPS[i // 2][i % 2]

    identf = A.alloc([128, 128], F32, "ident")
    b_const = P.buf("const")
    P.dma("sp", lambda e: e.dma_start(out=identf[:], in_=i_ident[:, :]), writes=[b_const])
    sel = A.alloc([3, 2, 128], F32, "sel")
    P.dma("sp", lambda e: e.dma_start(out=sel[:], in_=i_sel[:, :, :]), writes=[b_const])
    onehot = A.alloc([128, 4], F32, "onehot")
    P.dma("sp", lambda e: e.dma_start(out=onehot[:], in_=i_onehot[:, :]), writes=[b_const])
    ones_bf = A.alloc([128, 512], BF16, "ones")
    ones_f = A.alloc([128, 512], F32, "onesf")
    P.op("pool", lambda e: e.memset(ones_bf[:], 1.0), writes=[b_const])
    P.op("pool", lambda e: e.memset(ones_f[:], 1.0), writes=[b_const])
    ident_bf = A.alloc([128, 128], BF16, "identb")
    P.op("dve", lambda e: e.tensor_copy(ident_bf[:], identf[:]), reads=[b_const], writes=[b_const])
    gcol = A.alloc([128, 2, 2, 8, 3], F32, "gcol")
    gprow = A.alloc([128, 2, 2, D], F32, "gprow")
    lcol = A.alloc([128, NCH, 10], F32, "lcol")
    bfcol = A.alloc([16, 2], F32, "bfcol")
    bfrow = A.alloc([128, H], F32, "bfrow")
    b_ada = P.buf("ada")
    epsc = A.alloc([128, 1], F32, "epsc")
    onec = A.alloc([128, 1], F32, "onec")
    P.op("pool", lambda e: e.memset(epsc[:], EPS), writes=[b_const])
    P.op("pool", lambda e: e.memset(onec[:], 1.0), writes=[b_const])
    xs_t = A.alloc([128, 1, D], F32, "xs_t")
    b_xs = P.buf("xs")
    scum = A.alloc([16, 2 * ST], F32, "scum")
    b_scum = P.buf("scum")

    cx = {}

    def nb(*names):
        for n in names:
            cx["b_" + n] = P.buf(n)

    def front(N, nsub, Pt, layer, seqcols, xsrc, b_x):
        xn, junk, stat, hT = cx["xn"], cx["junk"], cx["stat"], cx["hT"]
        b_xn, b_junk, b_stat, b_hT = cx["b_xn"], cx["b_junk"], cx["b_stat"], cx["b_hT"]
        bxs = b_x if isinstance(b_x, list) else [b_x] * nsub
        for half in range(2):
            for s in range(nsub):
                b_x = bxs[s]
                if half == 0:
                    P.op("dve", lambda e, s=s: e.scalar_tensor_tensor(out=junk[0:Pt, :], in0=xsrc[0:Pt, s, :], scalar=1.0, in1=xsrc[0:Pt, s, :], op0=ALU.mult, op1=ALU.mult, accum_out=stat[0:Pt, s:s + 1]),
                         reads=[b_x], writes=[b_junk, b_stat])
                    P.op("act", lambda e, s=s: e.activation(out=stat[0:Pt, 4 + s:5 + s], in_=stat[0:Pt, s:s + 1], func=AF.Sqrt, scale=1.0 / D, bias=epsc[0:Pt, 0:1]), reads=[b_stat, b_const], writes=[b_stat])
                    P.op("dve", lambda e, s=s: e.reciprocal(stat[0:Pt, 8 + s:9 + s], stat[0:Pt, 4 + s:5 + s]), reads=[b_stat], writes=[b_stat])
                P.op("dve", lambda e, s=s: e.tensor_scalar(out=xn[0:Pt, :], in0=xsrc[0:Pt, s, :], scalar1=stat[0:Pt, 8 + s:9 + s], scalar2=None, op0=ALU.mult), reads=[b_x, b_stat], writes=[b_xn])

                def trs(e, s=s, half=half):
                    ins = None
                    for j in range(4):
                        kc = half * 4 + j
                        ins = e.transpose(out=bank(j)[0][:, s * Pt:(s + 1) * Pt], in_=xn[0:Pt, kc * 128:(kc + 1) * 128], identity=identf[0:Pt, 0:Pt])
                    return ins
                P.op("pe", trs, reads=[b_xn, b_const], writes=[bank(j)[1] for j in range(4)])
            for j in range(4):
                kc = half * 4 + j
                for (c0, c1, sq) in seqcols:
                    P.op("act", lambda e, j=j, kc=kc, c0=c0, c1=c1, sq=sq: e.activation(out=hT[:, kc, c0:c1], in_=bank(j)[0][:, c0:c1], func=AF.Identity, scale=gcol[:, layer, 0, kc, sq:sq + 1], bias=gcol[:, layer, 1, kc, sq:sq + 1]),
                         reads=[bank(j)[1], b_ada], writes=[b_hT])

    def post(N, nsub, Pt, layer, grp, wsb, b_wsb, nkc, srcT, b_srcT, ksz, xres, b_xres):
        junk, stat, ytmp = cx["junk"], cx["stat"], cx["ytmp"]
        b_junk, b_stat, b_ytmp = cx["b_junk"], cx["b_stat"], cx["b_ytmp"]
        bxr = b_xres if isinstance(b_xres, list) else [b_xres] * nsub
        for s in range(nsub):
            b_xres = bxr[s]
            pp = PS[2 + (s % 2)]
            bpp = bPS[2 + (s % 2)]

            def mm(e, s=s, pp=pp):
                ins = None
                for hf in range(2):
                    for c in range(nkc):
                        ins = e.matmul(pp[0:Pt, hf * 512:(hf + 1) * 512], lhsT=srcT[0:ksz, c, s * Pt:(s + 1) * Pt], rhs=wsb[0:ksz, c, hf * 512:(hf + 1) * 512], start=(c == 0), stop=(c == nkc - 1))
                return ins
            P.op("pe", mm, reads=[b_srcT, b_wsb], writes=bpp)
            P.op("act", lambda e, pp=pp: e.activation(out=junk[0:Pt, :], in_=pp[0:Pt, :], func=AF.Square, accum_out=stat[0:Pt, 12:13]), reads=bpp, writes=[b_junk, b_stat])
            P.op("act", lambda e: e.activation(out=stat[0:Pt, 13:14], in_=stat[0:Pt, 12:13], func=AF.Sqrt, scale=1.0 / D, bias=epsc[0:Pt, 0:1]), reads=[b_stat, b_const], writes=[b_stat])
            P.op("dve", lambda e: e.reciprocal(stat[0:Pt, 14:15], stat[0:Pt, 13:14]), reads=[b_stat], writes=[b_stat])
            P.op("dve", lambda e, pp=pp: e.scalar_tensor_tensor(out=ytmp[0:Pt, :], in0=pp[0:Pt, :], scalar=stat[0:Pt, 14:15], in1=gprow[0:Pt, layer, grp, :], op0=ALU.mult, op1=ALU.mult), reads=bpp + [b_stat, b_ada], writes=[b_ytmp])
            P.op("pool", lambda e, s=s: e.tensor_tensor(out=xres[0:Pt, s, :], in0=ytmp[0:Pt, :], in1=xres[0:Pt, s, :], op=ALU.add), reads=[b_ytmp, b_xres], writes=[b_xres])

    def split3(src, b_src, N, sign):
        spl, spf, b_spl, b_spf = cx["spl"], cx["spf"], cx["b_spl"], cx["b_spf"]
        P.op("dve", lambda e: e.tensor_scalar(out=spf[:, 0, 0:N], in0=src, scalar1=sign, scalar2=None, op0=ALU.mult), reads=[b_src], writes=[b_spf])
        for j in range(3):
            P.op("dve", lambda e, j=j: e.tensor_copy(spl[:, j, 0:N], spf[:, 0, 0:N]), reads=[b_spf], writes=[b_spl])
            if j < 2:
                P.op("dve", lambda e, j=j: e.tensor_copy(spf[:, 1, 0:N], spl[:, j, 0:N]), reads=[b_spl], writes=[b_spf])
                P.op("dve", lambda e: e.tensor_tensor(out=spf[:, 0, 0:N], in0=spf[:, 0, 0:N], in1=spf[:, 1, 0:N], op=ALU.subtract), reads=[b_spf], writes=[b_spf])

    m_persist = A.mark()
    adaw = A.alloc([128, 8, 3 * D], F32, "adaw")
    crow = A.alloc([3, D], F32, "crow")
    ccol = A.alloc([128, 8, 3], F32, "ccol")
    vrow = A.alloc([12, R], F32, "vrow")
    adab = A.alloc([1, 2, 3 * D], F32, "adab")
    grow = A.alloc([3, D], F32, "grow")
    npb = A.alloc([128, 2, D], F32, "npb")
    tmpc = A.alloc([128, 8, 3], F32, "tmpc")
    b_crow, b_ccol, b_vrow, b_adaw, b_adab, b_grow, b_npb = [P.buf(n) for n in "crow ccol vrow adaw adab grow npb".split()]
    P.dma("sp", lambda e: e.dma_start(out=crow[:], in_=i_cc[:, :]), writes=[b_crow])
    P.op("pool", lambda e: e.memset(vrow[:], 0.0), writes=[b_vrow])
    P.dma("sp", lambda e: e.dma_start(out=vrow[0:8, :], in_=i_lvec[:, :]), reads=[b_vrow], writes=[b_vrow])
    P.dma("sp", lambda e: e.dma_start(out=vrow[8:10, 0:D], in_=i_npre[:, :]), reads=[b_vrow], writes=[b_vrow])
    P.dma("sp", lambda e: e.dma_start(out=adab[:], in_=i_adab.rearrange("(o l) f -> o l f", o=1)), writes=[b_adab])
    P.dma("sp", lambda e: e.dma_start(out=npb[:], in_=i_npost.rearrange("(o l) f -> o l f", o=1).broadcast_to([128, 2, D])), writes=[b_npb])
    P.dma("sp", lambda e: e.dma_start(out=bfrow[:], in_=i_fbf.broadcast_to([128, H])), writes=[b_const])
    P.op("act", lambda e: e.activation(out=crow[:], in_=crow[:], func=AF.Silu), reads=[b_crow], writes=[b_crow])
    pa, ba = bank(0)

    def tr_c(e):
        ins = None
        for kc in range(8):
            ins = e.transpose(out=pa[:, kc * 3:kc * 3 + 3], in_=crow[0:3, kc * 128:(kc + 1) * 128], identity=identf[0:3, 0:3])
        return ins
    P.op("pe", tr_c, reads=[b_crow, b_const], writes=[ba])
    P.op("dve", lambda e: e.tensor_copy(ccol[:].rearrange("p k s -> p (k s)"), pa[:, 0:24]), reads=[ba], writes=[b_ccol])
    pb_, bb = bank(1)

    def tr_v(e):
        ins = None
        for c in range(NCH):
            ins = e.transpose(out=pb_[:, c * 10:c * 10 + 10], in_=vrow[0:10, c * 128:(c + 1) * 128], identity=identf[0:10, 0:10])
        return ins
    P.op("pe", tr_v, reads=[b_vrow, b_const], writes=[bb])
    vcol = A.alloc([128, NCH, 10], F32, "vcol")
    b_vcol = P.buf("vcol")
    P.op("dve", lambda e: e.tensor_copy(vcol[:].rearrange("p c v -> p (c v)"), pb_[:, 0:NCH * 10]), reads=[bb], writes=[b_vcol])
    P.op("dve", lambda e: e.tensor_copy(lcol[:, :, 0:8], vcol[:, :, 0:8]), reads=[b_vcol], writes=[b_const])
    P.op("act", lambda e: e.activation(out=lcol[:, :, 8], in_=vcol[:, :, 7], func=AF.Exp, scale=-1.0), reads=[b_vcol], writes=[b_const])
    P.op("act", lambda e: e.activation(out=lcol[:, :, 8], in_=lcol[:, :, 8], func=AF.Ln, bias=onec[:, 0:1]), reads=[b_const], writes=[b_const])
    P.op("dve", lambda e: e.tensor_scalar(out=lcol[:, :, 8], in0=lcol[:, :, 8], scalar1=-8.0, scalar2=None, op0=ALU.mult), reads=[b_const], writes=[b_const])
    pc_, bc = bank(2)
    P.op("pe", lambda e: e.matmul(pc_[0:16, 0:1], lhsT=bfrow[0:1, 0:16], rhs=ones_f[0:1, 0:1], start=True, stop=True), reads=[b_const], writes=[bc])
    P.op("dve", lambda e: e.tensor_scalar(out=bfcol[:, 0:1], in0=pc_[0:16, 0:1], scalar1=-1.0, scalar2=None, op0=ALU.mult), reads=[bc], writes=[b_const])
    for l in range(2):
        for q4 in range(4):
            P.dma("sp" if q4 % 2 == 0 else "act",
                  lambda e, l=l, q4=q4: e.dma_start(out=adaw[:, 2 * q4:2 * q4 + 2, :], in_=i_adaw[l, q4 * 256:(q4 + 1) * 256, :].rearrange("(k p) f -> p k f", p=128)),
                  writes=[b_adaw])
        pm, bm = bank(4 + l * 2)

        def ada_cols(e, l=l, pm=pm):
            ins = None
            for fc in range(16):
                for kc in range(8):
                    ins = e.matmul(pm[:, fc * 3:fc * 3 + 3], lhsT=adaw[:, kc, fc * 128:(fc + 1) * 128], rhs=ccol[:, kc, :], start=(kc == 0), stop=False)
                ins = e.matmul(pm[:, fc * 3:fc * 3 + 3], lhsT=adab[0:1, l, fc * 128:(fc + 1) * 128], rhs=ones_f[0:1, 0:3], start=False, stop=True)
            return ins
        P.op("pe", ada_cols, reads=[b_adaw, b_ccol, b_adab, b_const], writes=[bm])
        P.op("dve", lambda e, l=l, pm=pm: e.tensor_copy(gcol[:, l, 1, :, :].rearrange("p k s -> p (k s)"), pm[:, 0:24]), reads=[bm], writes=[b_ada])
        P.op("dve", lambda e, l=l, pm=pm: e.tensor_scalar(out=tmpc[:].rearrange("p k s -> p (k s)"), in0=pm[:, 24:48], scalar1=1.0, scalar2=None, op0=ALU.add), reads=[bm], writes=[b_grow])
        for s in range(3):
            P.op("dve", lambda e, l=l, s=s: e.tensor_tensor(out=gcol[:, l, 0, :, s], in0=tmpc[:, :, s], in1=vcol[0:128, 0:8, 8 + l], op=ALU.mult), reads=[b_grow, b_vcol], writes=[b_ada])
        for hf in range(2):
            prh, brh = bank(5 + l * 2) if hf == 0 else bank(3)

            def ada_rowh(e, l=l, hf=hf, prh=prh):
                ins = None
                for kc in range(8):
                    ins = e.matmul(prh[0:3, 0:512], lhsT=ccol[:, kc, :], rhs=adaw[:, kc, 2 * D + hf * 512:2 * D + (hf + 1) * 512], start=(kc == 0), stop=False)
                ins = e.matmul(prh[0:3, 0:512], lhsT=ones_f[0:1, 0:3], rhs=adab[0:1, l, 2 * D + hf * 512:2 * D + (hf + 1) * 512], start=False, stop=True)
                return ins
            P.op("pe", ada_rowh, reads=[b_adaw, b_ccol, b_adab, b_const], writes=[brh])
            P.op("dve", lambda e, hf=hf, prh=prh: e.tensor_copy(grow[:, hf * 512:(hf + 1) * 512], prh[0:3, 0:512]), reads=[brh], writes=[b_grow])
        for g in range(2):
            for hf in range(2):
                pg, bg = bank(0 + hf)
                P.op("pe", lambda e, g=g, hf=hf, pg=pg: e.matmul(pg[:, 0:512], lhsT=sel[0:3, g, :], rhs=grow[0:3, hf * 512:(hf + 1) * 512], start=True, stop=True), reads=[b_grow, b_const], writes=[bg])
                P.op("dve", lambda e, g=g, hf=hf, pg=pg, l=l: e.tensor_tensor(out=gprow[:, l, g, hf * 512:(hf + 1) * 512], in0=pg[:, 0:512], in1=npb[:, l, hf * 512:(hf + 1) * 512], op=ALU.mult), reads=[bg, b_npb], writes=[b_ada])
    P.barrier()
    A.reset(m_persist)
    if dbg == 32:
        P.dma("sp", lambda e: e.dma_start(out=o_fkp[2048:2176, 0:96], in_=gcol[:].rearrange("p a b c d -> p (a b c d)")), reads=[b_ada])
        P.dma("sp", lambda e: e.dma_start(out=o_fkp[2176:2304, 0:2 * 2 * D], in_=gprow[:].rearrange("p a b c -> p (a b c)")), reads=[b_ada]) if False else None
    if stop == 0:
        P.emit()
        return nc

    win = A.alloc([128, 8, 2 * R], BF16, "win")
    wg = A.alloc([128, 2, NCH, 128], BF16, "wg")
    wout = A.alloc([128, NCH, D], BF16, "wout")
    b_w = P.buf("weights")
    for kc in range(8):
        for hf in range(2):
            P.dma("pool", lambda e, kc=kc, hf=hf: e.dma_start(out=win[:, kc, hf * R:(hf + 1) * R], in_=i_win[kc * 128:(kc + 1) * 128, hf * R:(hf + 1) * R]), writes=[b_w])
    P.dma("pool", lambda e: e.dma_start(out=wg[:, 0, :, :], in_=i_wa.rearrange("n d e -> d n e")), writes=[b_w])
    P.dma("pool", lambda e: e.dma_start(out=wg[:, 1, :, :], in_=i_wx.rearrange("n d e -> d n e")), writes=[b_w])
    for c in range(NCH):
        P.dma("pool", lambda e, c=c: e.dma_start(out=wout[:, c, :], in_=i_wout[c * 128:(c + 1) * 128, :]), writes=[b_w])
    CG = 2
    NG = NCH // CG
    ENG_CAST = "pool" if dbg & 1024 else "dve"
    ENG_A2 = "pool" if dbg & 2048 else "dve"
    ENG_IM = "pool" if dbg & 4096 else "dve"
    xt_a1 = A.alloc([128, 4, D], F32, "xt_a1")
    cx["xn"] = A.alloc([128, D], F32, "xn")
    cx["junk"] = A.alloc([128, D], BF16, "junk")
    cx["stat"] = A.alloc([128, 16], F32, "stat")
    cx["hT"] = A.alloc([128, 8, TT], BF16, "hT")
    nb("xn", "junk", "stat", "hT")
    cx["ytmp"] = cx["xn"]
    cx["b_ytmp"] = cx["b_xn"]
    halo = A.alloc([128, NCH, 2, 4], F32, "halo")
    hst = A.alloc([128, NCH, 2], F32, "hst")
    XBW = 520
    sets = []
    for k_ in range(2):
        S_ = dict(xb=A.alloc([128, CG, XBW], F32, "xb"), sg=A.alloc([128, CG, TT], BF16, "sg"), xc=A.alloc([128, CG, TT], F32, "xc"),
                  xcb=A.alloc([128, CG, TT], BF16, "xcb"), rr=A.alloc([128, CG, TT], F32, "rr"), ig=A.alloc([128, CG, TT], F32, "ig"),
                  aa=A.alloc([128, CG, TT], F32, "aa"))
        for n_ in ("xb", "xbh", "sg", "xc", "xcb", "rr", "ig", "aa"):
            S_["b_" + n_] = P.buf("%s%d" % (n_, k_))
        sets.append(S_)
    zT = A.alloc([128, NCH, TT], BF16, "zT")
    b_xts = [P.buf("xt%d" % s_) for s_ in range(4)]
    b_halo = [P.buf("halo%d" % c_) for c_ in range(NCH)]
    b_hst, b_zT = P.buf("hst"), P.buf("zT")

    def layer0(N, nseg, L, segw):
        hT, b_hT = cx["hT"], cx["b_hT"]

        def xbv(S, cl, sgm, a_, b__):
            return S["xb"][:, cl, sgm * segw + a_:sgm * segw + b__]

        def stageA_pe(g):
            for cl in range(CG):
                c = g * CG + cl
                pxa, bxa = bank(4 + cl * 2)
                pga, bga = bank(5 + cl * 2)

                def mm(e, c=c, pxa=pxa, pga=pga):
                    ins = None
                    for kc in range(8):
                        ins = e.matmul(pxa[:, 0:N], lhsT=win[:, kc, c * 128:(c + 1) * 128], rhs=hT[:, kc, 0:N], start=(kc == 0), stop=(kc == 7))
                    for kc in range(8):
                        ins = e.matmul(pga[:, 0:N], lhsT=win[:, kc, R + c * 128:R + (c + 1) * 128], rhs=hT[:, kc, 0:N], start=(kc == 0), stop=(kc == 7))
                    return ins
                P.op("pe", mm, reads=[b_hT, b_w], writes=[bxa, bga])

        def stageA_el(g):
            S = sets[g % 2]
            for cl in range(CG):
                c = g * CG + cl
                pxa, bxa = bank(4 + cl * 2)
                pga, bga = bank(5 + cl * 2)
                P.op("act", lambda e, S=S, cl=cl, pga=pga: e.activation(out=S["sg"][:, cl, 0:N], in_=pga[:, 0:N], func=AF.Silu), reads=[bga], writes=[S["b_sg"]])
                for sgm in range(nseg):
                    P.op("dve", lambda e, S=S, cl=cl, sgm=sgm, pxa=pxa: e.tensor_copy(xbv(S, cl, sgm, 3, 3 + L), pxa[:, sgm * L:(sgm + 1) * L]), reads=[bxa], writes=[S["b_xb"]])
                    P.op("pool", lambda e, S=S, c=c, cl=cl, sgm=sgm: e.tensor_copy(xbv(S, cl, sgm, 0, 3), halo[:, c, sgm, 0:3]), reads=[b_halo[c]], writes=[S["b_xbh"]])
            for cl in range(CG):
                c = g * CG + cl
                for sgm in range(nseg):
                    o = S["xc"][:, cl, sgm * L:(sgm + 1) * L]
                    P.op("dve", lambda e, S=S, c=c, cl=cl, sgm=sgm, o=o: e.tensor_scalar(out=o, in0=xbv(S, cl, sgm, 0, L), scalar1=lcol[:, c, 0:1], scalar2=lcol[:, c, 4:5], op0=ALU.mult, op1=ALU.add),
                         reads=[S["b_xb"], S["b_xbh"], b_const], writes=[S["b_xc"]])
                    for k in range(1, 4):
                        P.op("dve", lambda e, S=S, c=c, cl=cl, sgm=sgm, o=o, k=k: e.scalar_tensor_tensor(out=o, in0=xbv(S, cl, sgm, k, k + L), scalar=lcol[:, c, k:k + 1], in1=o, op0=ALU.mult, op1=ALU.add),
                             reads=[S["b_xb"], S["b_xbh"], b_const, S["b_xc"]], writes=[S["b_xc"]])
                    P.op("pool", lambda e, S=S, c=c, cl=cl, sgm=sgm: e.tensor_copy(halo[:, c, sgm, 0:3], xbv(S, cl, sgm, L, L + 3)), reads=[S["b_xb"]], writes=[b_halo[c]])
                P.op(ENG_CAST, lambda e, S=S, cl=cl: e.tensor_copy(S["xcb"][:, cl, 0:N], S["xc"][:, cl, 0:N]), reads=[S["b_xc"]], writes=[S["b_xcb"]])

        def stageB_pe(g):
            S = sets[g % 2]
            xcb = S["xcb"]
            for cl in range(CG):
                c = g * CG + cl
                pra, bra = bank(cl * 2)
                pia, bia = bank(cl * 2 + 1)

                def mg(e, c=c, cl=cl, pra=pra, pia=pia, xcb=xcb):
                    e.matmul(pra[:, 0:N], lhsT=wg[:, 0, c, :], rhs=xcb[:, cl, 0:N], start=True, stop=True)
                    return e.matmul(pia[:, 0:N], lhsT=wg[:, 1, c, :], rhs=xcb[:, cl, 0:N], start=True, stop=True)
                P.op("pe", mg, reads=[S["b_xcb"], b_w], writes=[bra, bia])

        def stageB_el(g):
            S = sets[g % 2]
            rr, ig, aa, xc, sg, xcb = S["rr"], S["ig"], S["aa"], S["xc"], S["sg"], S["xcb"]
            for cl in range(CG):
                c = g * CG + cl
                pra, bra = bank(cl * 2)
                pia, bia = bank(cl * 2 + 1)
                P.op("act", lambda e, c=c, cl=cl, pra=pra, rr=rr: e.activation(out=rr[:, cl, 0:N], in_=pra[:, 0:N], func=AF.Sigmoid, bias=lcol[:, c, 5:6]), reads=[bra, b_const], writes=[S["b_rr"]])
                P.op("act", lambda e, c=c, cl=cl, pia=pia, ig=ig: e.activation(out=ig[:, cl, 0:N], in_=pia[:, 0:N], func=AF.Sigmoid, bias=lcol[:, c, 6:7]), reads=[bia, b_const], writes=[S["b_ig"]])
            for cl in range(CG):
                c = g * CG + cl
                P.op("act", lambda e, c=c, cl=cl, rr=rr, aa=aa: e.activation(out=aa[:, cl, 0:N], in_=rr[:, cl, 0:N], func=AF.Exp, scale=lcol[:, c, 8:9]), reads=[S["b_rr"], b_const], writes=[S["b_aa"]])
                P.op(ENG_A2, lambda e, cl=cl, rr=rr, aa=aa: e.tensor_tensor(out=rr[:, cl, 0:N], in0=aa[:, cl, 0:N], in1=aa[:, cl, 0:N], op=ALU.mult), reads=[S["b_aa"], S["b_rr"]], writes=[S["b_rr"]])
                P.op("pool", lambda e, cl=cl, ig=ig, xc=xc: e.tensor_tensor(out=ig[:, cl, 0:N], in0=ig[:, cl, 0:N], in1=xc[:, cl, 0:N], op=ALU.mult), reads=[S["b_ig"], S["b_xc"]], writes=[S["b_ig"]])
            for cl in range(CG):
                c = g * CG + cl
                P.op("act", lambda e, cl=cl, rr=rr: e.activation(out=rr[:, cl, 0:N], in_=rr[:, cl, 0:N], func=AF.Sqrt, scale=-1.0, bias=onec[:, 0:1]), reads=[S["b_rr"], b_const], writes=[S["b_rr"]])
                P.op(ENG_IM, lambda e, cl=cl, ig=ig, rr=rr: e.tensor_tensor(out=ig[:, cl, 0:N], in0=ig[:, cl, 0:N], in1=rr[:, cl, 0:N], op=ALU.mult), reads=[S["b_ig"], S["b_rr"]], writes=[S["b_ig"]])
                for sgm in range(nseg):
                    P.op("dve", lambda e, c=c, cl=cl, sgm=sgm, xc=xc, aa=aa, ig=ig: e.tensor_tensor_scan(out=xc[:, cl, sgm * L:(sgm + 1) * L], data0=aa[:, cl, sgm * L:(sgm + 1) * L], data1=ig[:, cl, sgm * L:(sgm + 1) * L], initial=hst[:, c, sgm:sgm + 1], op0=ALU.mult, op1=ALU.add),
                         reads=[S["b_aa"], S["b_ig"], b_hst, S["b_xc"]], writes=[S["b_xc"]])
                    P.op("dve", lambda e, c=c, cl=cl, sgm=sgm, xc=xc: e.tensor_copy(hst[:, c, sgm:sgm + 1], xc[:, cl, (sgm + 1) * L - 1:(sgm + 1) * L]), reads=[S["b_xc"], b_hst], writes=[b_hst])
                P.op("pool", lambda e, c=c, cl=cl, xc=xc, sg=sg: e.tensor_tensor(out=zT[:, c, 0:N], in0=xc[:, cl, 0:N], in1=sg[:, cl, 0:N], op=ALU.mult), reads=[S["b_xc"], S["b_sg"]], writes=[b_zT])

        stageA_pe(0)
        stageA_el(0)
        for g in range(NG):
            if not (dbg & 8192):
                if g + 1 < NG:
                    stageA_pe(g + 1)
                    stageA_el(g + 1)
                stageB_pe(g)
                stageB_el(g)
                continue
            stageB_pe(g)
            if g + 1 < NG:
                stageA_pe(g + 1)
            stageB_el(g)
            if g + 1 < NG:
                stageA_el(g + 1)

    P.op("pool", lambda e: e.memset(halo[:], 0.0), writes=b_halo)
    P.op("pool", lambda e: e.memset(hst[:], 0.0), writes=[b_hst])
    for i in range(nt_prompt):
        for s4 in range(4):
            P.dma("sp", lambda e, i=i, s4=s4: e.dma_start(out=xt_a1[:, s4, :], in_=i_xp[i * TT + s4 * 128:i * TT + (s4 + 1) * 128, :]), writes=[b_xts[s4]])
        front(TT, 4, 128, 0, [(0, TT, 0)], xt_a1, b_xts)
        layer0(TT, 1, TT, 0)
        post(TT, 4, 128, 0, 0, wout, b_w, NCH, zT, b_zT, 128, xt_a1, b_xts)
        for s4 in range(4):
            P.dma("pool", lambda e, i=i, s4=s4: e.dma_start(out=s_x1[i * TT + s4 * 128:i * TT + (s4 + 1) * 128, :], in_=xt_a1[:, s4, :]), reads=[b_xts[s4]], writes=[b_sx1])
        if dbg == 31 and i < NQ:
            P.dma("sp", lambda e, i=i: e.dma_start(out=o_yp[i * TT:(i + 1) * TT, :].rearrange("(s p) f -> p s f", p=128), in_=xt_a1[:]), reads=b_xts)
    P.dma("sp", lambda e: e.dma_start(out=o_lhp.rearrange("(c p) -> p c", p=128), in_=hst[:, :, 0], allow_slow_non_contiguous=True), reads=[b_hst])
    for k3 in range(3):
        P.dma("sp", lambda e, k3=k3: e.dma_start(out=o_lcp[k3].rearrange("(c p) -> p c", p=128), in_=halo[:, :, 0, k3], allow_slow_non_contiguous=True), reads=b_halo)
    sc = [(0, ST, 1), (ST, 2 * ST, 2)]
    if do_sample:
        for sq in range(2):
            P.dma("sp", lambda e, sq=sq: e.dma_start(out=hst[:, :, sq], in_=i_sth[sq].rearrange("(c p) -> p c", p=128), allow_slow_non_contiguous=True), reads=[b_hst], writes=[b_hst])
            for k3 in range(3):
                P.dma("sp", lambda e, sq=sq, k3=k3: e.dma_start(out=halo[:, :, sq, k3], in_=i_stc[sq, k3].rearrange("(c p) -> p c", p=128), allow_slow_non_contiguous=True), reads=b_halo, writes=b_halo)
        P.dma("sp", lambda e: e.dma_start(out=xs_t[0:64, 0, :], in_=i_xs[:, :]), writes=[b_xs])
        front(2 * ST, 1, 64, 0, sc, xs_t, b_xs)
        layer0(2 * ST, 2, ST, 40)
        post(2 * ST, 1, 64, 0, 1, wout, b_w, NCH, zT, b_zT, 128, xs_t, b_xs)
        for sq in range(2):
            P.dma("sp", lambda e, sq=sq: e.dma_start(out=o_lhs[sq].rearrange("(c p) -> p c", p=128), in_=hst[:, :, sq], allow_slow_non_contiguous=True), reads=[b_hst])
            for k3 in range(3):
                P.dma("sp", lambda e, sq=sq, k3=k3: e.dma_start(out=o_lcs[sq, k3].rearrange("(c p) -> p c", p=128), in_=halo[:, :, sq, k3], allow_slow_non_contiguous=True), reads=b_halo)
    P.barrier()
    A.reset(m_persist)
    if stop == 1:
        P.emit()
        return nc

    fwin = A.alloc([128, 8, 2 * D + H], BF16, "fwin")
    b_w = P.buf("weights2")
    for kc in range(8):
        for hf in range(2):
            P.dma("pool", lambda e, kc=kc, hf=hf: e.dma_start(out=fwin[:, kc, hf * D:(hf + 1) * D], in_=i_fwin[kc * 128:(kc + 1) * 128, D + hf * D:D + (hf + 1) * D]), writes=[b_w])
        P.dma("pool", lambda e, kc=kc: e.dma_start(out=fwin[:, kc, 2 * D:2 * D + H], in_=i_fwin[kc * 128:(kc + 1) * 128, 4 * D:4 * D + H]), writes=[b_w])
    xt_a2s = [A.alloc([128, 4, D], F32, "xt_a2")]
    xt_a2 = xt_a2s[0]
    cx["xn"] = A.alloc([128, D], F32, "xn")
    cx["junk"] = A.alloc([128, D], BF16, "junk")
    cx["stat"] = A.alloc([128, 16], F32, "stat")
    cx["hT"] = A.alloc([128, 8, TT], BF16, "hT")
    kst = A.alloc([128, 4, D], F32, "kst")
    kstb = A.alloc([128, 4, D], BF16, "kstb")
    b_kstb = P.buf("kstb")
    vst = A.alloc([128, 4, D], F32, "vst")
    v1st = A.alloc([128, H, 4, VS], BF16, "v1st")
    kTst = A.alloc([KA, H, TT], BF16, "kTst")
    lft = A.alloc([128, 4, H], F32, "lft")
    lfT = A.alloc([16, TT], F32, "lfT")
    cumT = A.alloc([16, TT], F32, "cumT")
    ccar = A.alloc([16, 2], F32, "ccar")
    cx["spl"] = A.alloc([16, 3, TT], BF16, "spl")
    cx["spf"] = A.alloc([16, 2, TT], F32, "spf")
    nb("xn", "junk", "stat", "hT", "spl", "spf")
    b_xt, b_kst, b_vst, b_v1st, b_kTst, b_lft, b_lfT, b_cumT, b_ccar = [P.buf(n) for n in range(9)]
    P.op("pool", lambda e: e.memset(v1st[:], 1.0), writes=[b_v1st])
    b_kTr = [P.buf("kTr%d" % j_) for j_ in range(3)]
    P.op("pool", lambda e: e.memset(kTst[:], 1.0), writes=[b_kTst] + b_kTr)
    P.op("pool", lambda e: e.memset(ccar[:], 0.0), writes=[b_ccar])

    def ktrans(N, nsub, Pt, cast=False):
        if cast:
            for s_ in range(nsub):
                P.op("dve", lambda e, s_=s_: e.tensor_copy(kstb[0:Pt, s_, :], kst[0:Pt, s_, :]), reads=[b_kst], writes=[b_kstb])
        for h in range(H):
            pk, bk = bank(h % 2)

            def trk(e, h=h, pk=pk):
                ins = None
                for s in range(nsub):
                    ins = e.matmul(pk[0:64, s * Pt:(s + 1) * Pt], lhsT=kstb[0:Pt, s, h * 64:(h + 1) * 64], rhs=ident_bf[0:Pt, 0:Pt], start=True, stop=True)
                return ins
            P.op("pe", trk, reads=[b_kstb, b_const], writes=[bk])
            if h % 2 == 0:
                P.op("act", lambda e, h=h, pk=pk: e.activation(out=kTst[0:64, h, 0:N], in_=pk[0:64, 0:N], func=AF.Identity), reads=[bk], writes=[b_kTst])
            else:
                P.op("dve", lambda e, h=h, pk=pk: e.tensor_copy(kTst[0:64, h, 0:N], pk[0:64, 0:N]), reads=[bk], writes=[b_kTst])

    def ckrows(N):
        spl, b_spl = cx["spl"], cx["b_spl"]
        for j in range(3):
            P.dma("sp", lambda e, j=j: e.dma_start(out=kTst[67 + j:68 + j, :, 0:N], in_=spl[:, j, 0:N]), reads=[b_spl], writes=[b_kTr[j]])

    def kvproj(N, nsub, Pt, nseg, L, x1src, b_x1, seqcols, o_fk, o_fv, o_fl, tok0, kT_dsts, vv_dst, cum_dst):
        front(N, nsub, Pt, 1, seqcols, x1src, b_x1)
        if dbg == 33:
            for q_ in range(2):
                P.dma("pool", lambda e, q_=q_: e.dma_start(out=o_fvp[2048:2176, q_ * 2048:(q_ + 1) * 2048].rearrange("p (k n) -> p k n", k=4) if False else o_fvp[2048 + q_ * 128:2176 + q_ * 128, 0:1024].rearrange("p (k n) -> p k n", k=4)[:, :, 0:N // 2 if False else 256], in_=cx["hT"][:, q_ * 4:(q_ + 1) * 4, 0:256]), reads=[cx["b_hT"]])
            return
        if dbg == 1:
            return
        hT, b_hT = cx["hT"], cx["b_hT"]
        for s in range(nsub):
            for which, (wofs, stg, b_stg, o_d) in enumerate([(0, kst, b_kst, o_fk), (D, vst, b_vst, o_fv)]):
                pp = PS[2 + which]
                bpp = bPS[2 + which]

                def mm(e, s=s, pp=pp, wofs=wofs):
                    ins = None
                    for hf in range(2):
                        for kc in range(8):
                            ins = e.matmul(pp[0:Pt, hf * 512:(hf + 1) * 512], lhsT=hT[:, kc, s * Pt:(s + 1) * Pt], rhs=fwin[:, kc, wofs + hf * 512:wofs + (hf + 1) * 512], start=(kc == 0), stop=(kc == 7))
                    return ins
                P.op("pe", mm, reads=[b_hT, b_w], writes=bpp)
                P.op("act", lambda e, s=s, pp=pp, stg=stg: e.activation(out=stg[0:Pt, s, :], in_=pp[0:Pt, :], func=AF.Identity), reads=bpp, writes=[b_stg])
                if which == 0:
                    P.op("dve", lambda e, s=s: e.tensor_copy(kstb[0:Pt, s, :], kst[0:Pt, s, :]), reads=[b_kst], writes=[b_kstb])
                if which == 1 and dbg != 21:
                    if dbg == 22:
                        P.op("act", lambda e, s=s, pp=pp: e.activation(out=v1st[0:Pt, :, s, 0:64], in_=pp[0:Pt, :].rearrange("p (h d) -> p h d", h=H), func=AF.Identity), reads=bpp, writes=[b_v1st])
                    elif dbg == 23:
                        P.op("pool", lambda e, s=s: e.tensor_copy(v1st[0:Pt, :, s, 0:64], vst[0:Pt, s, :].rearrange("p (h d) -> p h d", h=H)), reads=[b_vst], writes=[b_v1st])
                    else:
                        P.op("dve", lambda e, s=s: e.tensor_copy(v1st[0:Pt, :, s, 0:64], vst[0:Pt, s, :].rearrange("p (h d) -> p h d", h=H)), reads=[b_vst], writes=[b_v1st])
                P.dma("sp", lambda e, s=s, stg=stg, o_d=o_d: e.dma_start(out=o_d[tok0 + s * Pt:tok0 + (s + 1) * Pt, :], in_=stg[0:Pt, s, :]), reads=[b_stg])
            if dbg in (2, 21, 22, 23):
                continue
            pf, bf_ = bank(0)

            def mfl(e, s=s, pf=pf):
                ins = None
                for kc in range(8):
                    ins = e.matmul(pf[0:Pt, 0:H], lhsT=hT[:, kc, s * Pt:(s + 1) * Pt], rhs=fwin[:, kc, 2 * D:2 * D + H], start=(kc == 0), stop=(kc == 7))
                return ins
            P.op("pe", mfl, reads=[b_hT, b_w], writes=[bf_])
            P.op("dve", lambda e, s=s, pf=pf: e.tensor_tensor(out=lft[0:Pt, s, :], in0=pf[0:Pt, 0:H], in1=bfrow[0:Pt, :], op=ALU.add), reads=[bf_, b_const], writes=[b_lft])
        if dbg in (2, 21, 22, 23):
            return
        P.op("act", lambda e: e.activation(out=lft[0:Pt, 0:nsub, :], in_=lft[0:Pt, 0:nsub, :], func=AF.Exp, scale=-1.0), reads=[b_lft], writes=[b_lft])
        P.op("act", lambda e: e.activation(out=lft[0:Pt, 0:nsub, :], in_=lft[0:Pt, 0:nsub, :], func=AF.Ln, bias=onec[0:Pt, 0:1]), reads=[b_lft, b_const], writes=[b_lft])
        P.op("dve", lambda e: e.tensor_scalar(out=lft[0:Pt, 0:nsub, :], in0=lft[0:Pt, 0:nsub, :], scalar1=-1.0, scalar2=None, op0=ALU.mult), reads=[b_lft], writes=[b_lft])
        with nc.allow_non_contiguous_dma(reason="64B rows"):
            P.dma("sp", lambda e: e.dma_start(out=o_fl[tok0:tok0 + nsub * Pt, :].rearrange("(s p) h -> p s h", p=Pt), in_=lft[0:Pt, 0:nsub, :], allow_slow_non_contiguous=True), reads=[b_lft])
        if dbg == 3:
            return
        pf, bf_ = bank(1)

        def mflT(e, pf=pf):
            ins = None
            for kc in range(8):
                ins = e.matmul(pf[0:H, 0:N], lhsT=fwin[:, kc, 2 * D:2 * D + H], rhs=hT[:, kc, 0:N], start=(kc == 0), stop=(kc == 7))
            return ins
        P.op("pe", mflT, reads=[b_hT, b_w], writes=[bf_])
        P.op("act", lambda e, pf=pf: e.activation(out=lfT[:, 0:N], in_=pf[0:H, 0:N], func=AF.Exp, scale=-1.0, bias=bfcol[:, 0:1]), reads=[bf_, b_const], writes=[b_lfT])
        P.op("act", lambda e: e.activation(out=lfT[:, 0:N], in_=lfT[:, 0:N], func=AF.Ln, bias=onec[0:16, 0:1]), reads=[b_lfT, b_const], writes=[b_lfT])
        for sgm in range(nseg):
            P.op("dve", lambda e, sgm=sgm: e.tensor_tensor_scan(out=cumT[:, sgm * L:(sgm + 1) * L], data0=ones_f[0:16, 0:L], data1=lfT[:, sgm * L:(sgm + 1) * L], initial=ccar[:, sgm:sgm + 1], op0=ALU.mult, op1=ALU.subtract),
                 reads=[b_lfT, b_ccar, b_const, b_cumT], writes=[b_cumT])
            P.op("dve", lambda e, sgm=sgm: e.tensor_copy(ccar[:, sgm:sgm + 1], cumT[:, (sgm + 1) * L - 1:(sgm + 1) * L]), reads=[b_cumT, b_ccar], writes=[b_ccar])
        if cum_dst is not None:
            P.dma("sp", lambda e: e.dma_start(out=cum_dst, in_=cumT[:, 0:N]), reads=[b_cumT])
        if dbg == 4:
            return
        split3(cumT[:, 0:N], b_cumT, N, -1.0)
        if dbg == 5:
            return
        ktrans(N, nsub, Pt)
        if dbg == 6:
            return
        ckrows(N)
        if dbg == 7:
            return
        for sgm, kd in enumerate(kT_dsts if not (dbg == 9 and Pt == 64) else []):
            P.dma("sp", lambda e, sgm=sgm, kd=kd: e.dma_start(out=kd, in_=kTst[:, :, sgm * L:(sgm + 1) * L]), reads=[b_kTst] + b_kTr)
        for (vd, vsrc) in (vv_dst if not (dbg == 8 and Pt == 64) else []):
            P.dma("sp", lambda e, vd=vd, vsrc=vsrc: e.dma_start(out=vd, in_=vsrc), reads=[b_v1st])

    m_a2 = A.mark()
    xt_a2s.append(A.alloc([128, 4, D], F32, "xt_a2b"))
    hT_a2 = [cx["hT"], A.alloc([128, 8, TT], BF16, "hT2b")]
    b_hT_a2 = [cx["b_hT"], P.buf("hT2b")]
    b_xt2 = [b_xt, P.buf("xt2b")]
    for i in range(nt_prompt):
        xt_a2, b_xt = xt_a2s[i % 2], b_xt2[i % 2]
        cx["hT"], cx["b_hT"] = hT_a2[i % 2], b_hT_a2[i % 2]
        P.dma("pool", lambda e, i=i, xt_a2=xt_a2: e.dma_start(out=xt_a2[:], in_=s_x1[i * TT:(i + 1) * TT, :].rearrange("(s p) f -> p s f", p=128)), reads=[b_sx1], writes=[b_xt])
        if dbg == 34 and i < NQ:
            P.dma("sp", lambda e, i=i, xt_a2=xt_a2: e.dma_start(out=o_yp[i * TT:(i + 1) * TT, :].rearrange("(s p) f -> p s f", p=128), in_=xt_a2[:]), reads=[b_xt])
        m_ = i // 4
        vvd = [(s_vv[:, m_, :, (i % 4) * 4 * VS:(i % 4 + 1) * 4 * VS].rearrange("h p x -> p h x"), v1st[:].rearrange("p h s d -> p h (s d)"))]
        kvproj(TT, 4, 128, 1, TT, xt_a2, b_xt, [(0, TT, 0)], o_fkp, o_fvp, o_flp, i * TT,
               [s_kT[:, :, i * TT:(i + 1) * TT].rearrange("h r n -> r h n")], vvd, s_cum[:, i * TT:(i + 1) * TT])
    cx["hT"], cx["b_hT"] = hT_a2[0], b_hT_a2[0]
    P.barrier()
    A.reset(m_a2)
    if do_sample:
        clf = A.alloc([128, 2, PAST // 128, H], F32, "clf")
        pcum = A.alloc([16, 2, PAST], F32, "pcum")
        b_clf, b_pcum = P.buf("clf"), P.buf("pcum")
        with nc.allow_non_contiguous_dma(reason="64B rows"):
            for sq in range(2):
                P.dma("sp", lambda e, sq=sq: e.dma_start(out=clf[:, sq, :, :], in_=i_clf[sq].rearrange("(j p) h -> p j h", p=128), allow_slow_non_contiguous=True), writes=[b_clf])
        for sq in range(2):
            for q8 in range(PAST // 512):
                pk, bk = bank(q8 % 2)

                def trl(e, sq=sq, q8=q8, pk=pk):
                    ins = None
                    for j in range(4):
                        ins = e.matmul(pk[0:16, j * 128:(j + 1) * 128], lhsT=clf[:, sq, q8 * 4 + j, :], rhs=identf[:, :], start=True, stop=True)
                    return ins
                P.op("pe", trl, reads=[b_clf, b_const], writes=[bk])
                P.op("act", lambda e, sq=sq, q8=q8, pk=pk: e.activation(out=pcum[:, sq, q8 * 512:(q8 + 1) * 512], in_=pk[0:16, 0:512], func=AF.Identity), reads=[bk], writes=[b_pcum])
            for q8 in range(PAST // 512):
                init = 0.0 if q8 == 0 else pcum[:, sq, q8 * 512 - 1:q8 * 512]
                P.op("dve", lambda e, sq=sq, q8=q8, init=init: e.tensor_tensor_scan(out=pcum[:, sq, q8 * 512:(q8 + 1) * 512], data0=ones_f[0:16, 0:512], data1=pcum[:, sq, q8 * 512:(q8 + 1) * 512], initial=init, op0=ALU.mult, op1=ALU.add),
                     reads=[b_pcum, b_const], writes=[b_pcum])
            P.op("dve", lambda e, sq=sq: e.tensor_copy(ccar[:, sq:sq + 1], pcum[:, sq, PAST - 1:PAST]), reads=[b_pcum, b_ccar], writes=[b_ccar])
        for sq in range(2 if dbg != 41 else 0):
            for q8 in range(PAST // 512):
                P.dma("sp", lambda e, sq=sq, q8=q8: e.dma_start(out=kst[:], in_=i_ck[sq, q8 * 512:(q8 + 1) * 512, :].rearrange("(s p) f -> p s f", p=128)), writes=[b_kst])
                P.dma("sp", lambda e, sq=sq, q8=q8: e.dma_start(out=vst[:], in_=i_cv[sq, q8 * 512:(q8 + 1) * 512, :].rearrange("(s p) f -> p s f", p=128)), writes=[b_vst])
                for s_ in range(4):
                    P.op("act", lambda e, s_=s_: e.activation(out=v1st[:, :, s_, 0:64], in_=vst[:, s_, :].rearrange("p (h d) -> p h d", h=H), func=AF.Identity), reads=[b_vst], writes=[b_v1st])
                P.dma("sp", lambda e, sq=sq, q8=q8: e.dma_start(out=s_vvs[sq, :, q8 // 4, :, (q8 % 4) * 4 * VS:(q8 % 4 + 1) * 4 * VS].rearrange("h p x -> p h x"), in_=v1st[:].rearrange("p h s d -> p h (s d)")), reads=[b_v1st])
                ktrans(512, 4, 128, cast=True)
                split3(pcum[:, sq, q8 * 512:(q8 + 1) * 512], b_pcum, 512, -1.0)
                ckrows(512)
                P.dma("sp", lambda e, sq=sq, q8=q8: e.dma_start(out=s_kTs[sq, :, :, q8 * 512:(q8 + 1) * 512].rearrange("h r n -> r h n"), in_=kTst[:]), reads=[b_kTst] + b_kTr)
        vvd = [(s_vvs[sq, :, NPR, 0:ST, 0:VS].rearrange("h p x -> p h x"), v1st[sq * ST:(sq + 1) * ST, :, 0, :]) for sq in range(2)]
        if dbg not in (41, 42):
          kvproj(2 * ST, 1, 64, 2, ST, xs_t, b_xs, sc, o_fks, o_fvs, o_fls, 0,
               [s_kTs[sq, :, :, PAST:PAST + ST].rearrange("h r n -> r h n") for sq in range(2)], vvd, None)
        P.op("dve", lambda e: e.tensor_copy(scum[:, :], cumT[:, 0:2 * ST]), reads=[b_cumT], writes=[b_scum])
    P.barrier()
    A.reset(m_persist)
    if not do_attn:
        P.emit()
        return nc

    wq = A.alloc([128, 8, 2 * D], BF16, "wq")
    wo = A.alloc([64, H, D], BF16, "wo")
    pmask = A.alloc([128, 16, TT], BF16, "pmask")
    smask = A.alloc([ST, ST], BF16, "smask")
    b_w2 = P.buf("w2")
    for kc in range(8):
        P.dma("pool", lambda e, kc=kc: e.dma_start(out=wq[:, kc, 0:D], in_=i_fwin[kc * 128:(kc + 1) * 128, 0:D]), writes=[b_w2])
        P.dma("pool", lambda e, kc=kc: e.dma_start(out=wq[:, kc, D:2 * D], in_=i_fwin[kc * 128:(kc + 1) * 128, 3 * D:4 * D]), writes=[b_w2])
    for h in range(H):
        P.dma("pool", lambda e, h=h: e.dma_start(out=wo[:, h, :], in_=i_fwout[h * 64:(h + 1) * 64, :]), writes=[b_w2])
    P.dma("sp", lambda e: e.dma_start(out=pmask[:], in_=i_pmask[:, :, :]), writes=[b_w2])
    P.dma("sp", lambda e: e.dma_start(out=smask[:], in_=i_smask[:, :]), writes=[b_w2])
    xt_c = A.alloc([128, 4, D], F32, "xt2")
    cx["xn"] = A.alloc([128, D], F32, "xn")
    cx["junk"] = A.alloc([128, D], BF16, "junk")
    cx["stat"] = A.alloc([128, 16], F32, "stat")
    cx["hT"] = A.alloc([128, 8, TT], BF16, "hT")
    cx["spl"] = A.alloc([16, 3, TT], BF16, "spl")
    cx["spf"] = A.alloc([16, 2, TT], F32, "spf")
    nb("xn", "junk", "stat", "hT", "spl", "spf")
    cx["ytmp"] = cx["xn"]
    cx["b_ytmp"] = cx["b_xn"]
    xl = cx["xn"]
    qaug = A.alloc([KA, H, TT], BF16, "qaug")
    sgT = A.alloc([64, H, TT], BF16, "sgT")
    zT2 = sgT
    csel = A.alloc([16, 2, TT], F32, "csel")
    NKB = 2
    kch = [A.alloc([KA, 2048], BF16, "kch") for _ in range(NKB)]
    vch = [A.alloc([128, 16 * VS], BF16, "vch") for _ in range(NKB)]
    NSG = 4
    pT = [A.alloc([128, 512], BF16, "pT") for _ in range(NSG)]
    osb = A.alloc([65, TT], F32, "osb")
    rl = A.alloc([65, TT], F32, "rl")
    b_xt, b_xl, b_qaug, b_sgT, b_zT2, b_csel, b_osb, b_rl = [P.buf(n) for n in range(8)]
    b_xl = cx["b_xn"]
    b_zT2 = b_sgT
    b_kch = [P.buf("kch") for _ in range(NKB)]
    b_vch = [P.buf("vch") for _ in range(NKB)]
    b_pT = [P.buf("pT") for _ in range(NSG)]
    P.op("pool", lambda e: e.memset(qaug[:], 1.0), writes=[b_qaug])
    chunk_ctr = [0]
    grp_ctr = [0]

    def attend(N, nsub, Pt, seqs, x_res, b_xres, layer_grp, o_y, y_tok0):
        flat = []
        item_ctr = [0]
        pending = []
        EPI_DELAY = 3
        for h in range(H):
            for sq in seqs:
                q0, q1 = sq["q0"], sq["q1"]
                nq = q1 - q0
                work = [(k_ap, v_ap, 128, 16, None) for (k_ap, v_ap) in sq["rows"]] + list(sq["diag"])
                first = True
                for (k_src, v_src, nk, ntile, mask_fn) in work:
                    item = dict(k_src=k_src, v_src=v_src, nk=nk, ntile=ntile, cb=None, idx=item_ctr[0])
                    item_ctr[0] += 1
                    G = min(ntile, 512 // nq if nq > 256 else 16)
                    for g0 in range(0, ntile, G):
                        flat.append(dict(h=h, q0=q0, nq=nq, item=item, g0=g0, G=G, nk=nk, mask_fn=mask_fn, first=first, last_of_head=False))
                        first = False
            flat[-1]["last_of_head"] = True

        def emit_S(en):
            it = en["item"]
            h, nk, ntile = en["h"], en["nk"], it["ntile"]
            if it["cb"] is None:
                cb_ = chunk_ctr[0] % NKB
                chunk_ctr[0] += 1
                it["cb"] = cb_
                P.dma("sp", lambda e, cb_=cb_, k_src=it["k_src"], nk=nk, ntile=ntile, h=h: e.dma_start(out=kch[cb_][:, 0:nk * ntile], in_=k_src[h]), writes=[b_kch[cb_]])
                P.dma("sp", lambda e, cb_=cb_, v_src=it["v_src"], nk=nk, ntile=ntile, h=h: e.dma_start(out=vch[cb_][0:nk, 0:ntile * VS], in_=v_src[h]), writes=[b_vch[cb_]])
            cb_ = it["cb"]
            gi = grp_ctr[0] % NSG
            grp_ctr[0] += 1
            en["gi"] = gi
            psg, bpsg = bank(gi)

            def mmS(e, cb_=cb_, g0=en["g0"], psg=psg, h=h, q0=en["q0"], nq=en["nq"], G=en["G"], nk=nk, mask_fn=en["mask_fn"]):
                ins = None
                for j in range(G):
                    kt = g0 + j
                    ins = e.matmul(psg[0:nk, j * nq:(j + 1) * nq], lhsT=kch[cb_][:, kt * nk:(kt + 1) * nk], rhs=qaug[:, h, q0:q0 + nq], start=True, stop=(mask_fn is None))
                    if mask_fn is not None:
                        ins = e.matmul(psg[0:nk, j * nq:(j + 1) * nq], lhsT=ident_bf[0:nk, 0:nk], rhs=mask_fn(kt), start=False, stop=True)
                return ins
            P.op("pe", mmS, reads=[b_kch[cb_], b_qaug, b_w2, b_const], writes=[bpsg])

        def emit_EV(en):
            h, nk, nq, G, gi, q0 = en["h"], en["nk"], en["nq"], en["G"], en["gi"], en["q0"]
            cb_ = en["item"]["cb"]
            psg, bpsg = bank(gi)
            po, bo = bank(6 + (h % 2))
            P.op("act", lambda e, psg=psg, gi=gi, G=G, nq=nq, nk=nk: e.activation(out=pT[gi][0:nk, 0:G * nq], in_=psg[0:nk, 0:G * nq], func=AF.Exp), reads=[bpsg], writes=[b_pT[gi]])

            def mmV(e, cb_=cb_, g0=en["g0"], gi=gi, po=po, q0=q0, nq=nq, G=G, nk=nk, fst=en["first"]):
                ins = None
                for j in range(G):
                    kt = g0 + j
                    ins = e.matmul(po[0:65, q0:q0 + nq], lhsT=vch[cb_][0:nk, kt * VS:kt * VS + 65], rhs=pT[gi][0:nk, j * nq:(j + 1) * nq], start=(fst and j == 0), stop=False, skip_group_check=True)
                return ins
            P.op("pe", mmV, reads=[b_vch[cb_], b_pT[gi]], writes=[bo])
            if en["last_of_head"]:
                P.op("act", lambda e, po=po: e.activation(out=osb[0:65, 0:N], in_=po[0:65, 0:N], func=AF.Identity), reads=[bo], writes=[b_osb])
                P.op("dve", lambda e: e.reciprocal(rl[64:65, 0:N], osb[64:65, 0:N]), reads=[b_osb], writes=[b_rl])
                pending.append([EPI_DELAY, h])

        def emit_epi_tail(h):
            if True:
                pbq, bbq = bank(5)
                P.op("pe", lambda e, pbq=pbq: e.matmul(pbq[0:64, 0:N], lhsT=ones_f[64:65, 0:64], rhs=rl[64:65, 0:N], start=True, stop=True), reads=[b_rl, b_const], writes=[bbq])
                P.op("dve", lambda e, pbq=pbq: e.tensor_tensor(out=osb[0:64, 0:N], in0=osb[0:64, 0:N], in1=pbq[0:64, 0:N], op=ALU.mult), reads=[b_osb, bbq], writes=[b_osb])
                P.op("pool", lambda e, h=h: e.tensor_tensor(out=zT2[:, h, 0:N], in0=osb[0:64, 0:N], in1=sgT[:, h, 0:N], op=ALU.mult), reads=[b_osb, b_sgT], writes=[b_zT2])

        nxt = 0
        for i, en in enumerate(flat):
            while nxt < len(flat) and nxt <= i + NSG - 1 and flat[nxt]["item"]["idx"] <= en["item"]["idx"] + NKB - 1:
                emit_S(flat[nxt])
                nxt += 1
            for p_ in pending:
                p_[0] -= 1
            while pending and pending[0][0] <= 0:
                emit_epi_tail(pending.pop(0)[1])
            emit_EV(en)
        while pending:
            emit_epi_tail(pending.pop(0)[1])
        post(N, nsub, Pt, 1, layer_grp, wo, b_w2, H, zT2, b_zT2, 64, x_res, b_xres)
        P.dma("sp", lambda e: e.dma_start(out=o_y[y_tok0:y_tok0 + nsub * Pt, :].rearrange("(s p) f -> p s f", p=Pt), in_=x_res[0:Pt, 0:nsub, :]), reads=[b_xres])

    def qproj(N, seqcols, xsrc, b_x, nsub, Pt, cq_src, b_cq):
        front(N, nsub, Pt, 1, seqcols, xsrc, b_x)
        hT, b_hT = cx["hT"], cx["b_hT"]
        spl, b_spl = cx["spl"], cx["b_spl"]
        for h in range(H):
            pq, bq = bank(6)
            pg, bg = bank(7)

            def mq(e, h=h, pq=pq, pg=pg):
                ins = None
                for kc in range(8):
                    ins = e.matmul(pq[0:64, 0:N], lhsT=wq[:, kc, h * 64:(h + 1) * 64], rhs=hT[:, kc, 0:N], start=(kc == 0), stop=(kc == 7))
                for kc in range(8):
                    ins = e.matmul(pg[0:64, 0:N], lhsT=wq[:, kc, D + h * 64:D + (h + 1) * 64], rhs=hT[:, kc, 0:N], start=(kc == 0), stop=(kc == 7))
                return ins
            P.op("pe", mq, reads=[b_hT, b_w2], writes=[bq, bg])
            P.op("dve", lambda e, h=h, pq=pq: e.tensor_scalar(out=qaug[0:64, h, 0:N], in0=pq[0:64, 0:N], scalar1=0.125, scalar2=None, op0=ALU.mult), reads=[bq], writes=[b_qaug])
            P.op("act", lambda e, h=h, pg=pg: e.activation(out=sgT[:, h, 0:N], in_=pg[0:64, 0:N], func=AF.Silu), reads=[bg], writes=[b_sgT])
        split3(cq_src, b_cq, N, 1.0)
        for j in range(3):
            P.dma("sp", lambda e, j=j: e.dma_start(out=qaug[64 + j:65 + j, :, 0:N], in_=spl[:, j, 0:N]), reads=[b_spl, b_qaug], writes=[b_qaug])

    if do_sample:
        qproj(2 * ST, sc, xs_t, b_xs, 1, 64, scum[:, :], b_scum)
        seqs = []
        for sq in range(2):
            rows = [(s_kTs[sq, :, :, rw * 2048:(rw + 1) * 2048], s_vvs[sq, :, rw, :, :]) for rw in range(NPR)]
            diag = [(s_kTs[sq, :, :, PAST:PAST + ST], s_vvs[sq, :, NPR, 0:ST, 0:VS], ST, 1, (lambda kt: smask[:, :]))]
            seqs.append(dict(q0=sq * ST, q1=(sq + 1) * ST, rows=rows, diag=diag))
        attend(2 * ST, 1, 64, seqs, xs_t, b_xs, 1, o_ys, 0)
    for m in range(nq_tiles):
        for r_ in range(4):
            i = 4 * m + r_
            P.dma("sp", lambda e, i=i: e.dma_start(out=csel[:, 1, :], in_=s_cum[:, i * TT:(i + 1) * TT]), writes=[b_csel])
            if r_ == 0:
                P.op("dve", lambda e: e.tensor_scalar(out=csel[:, 0, :], in0=csel[:, 1, :], scalar1=onehot[0:16, 0:1], scalar2=None, op0=ALU.mult), reads=[b_csel, b_const], writes=[b_csel])
            else:
                P.op("dve", lambda e, r_=r_: e.scalar_tensor_tensor(out=csel[:, 0, :], in0=csel[:, 1, :], scalar=onehot[0:16, r_:r_ + 1], in1=csel[:, 0, :], op0=ALU.mult, op1=ALU.add), reads=[b_csel, b_const], writes=[b_csel])
            for s4 in range(4):
                P.dma("act", lambda e, i=i, s4=s4: e.dma_start(out=xl[:, :], in_=s_x1[i * TT + s4 * 128:i * TT + (s4 + 1) * 128, :]), writes=[b_xl])
                if r_ == 0:
                    P.op("dve", lambda e, s4=s4: e.tensor_scalar(out=xt_c[:, s4, :], in0=xl[:, :], scalar1=onehot[:, 0:1], scalar2=None, op0=ALU.mult), reads=[b_xl, b_const], writes=[b_xt])
                else:
                    P.op("dve", lambda e, r_=r_, s4=s4: e.scalar_tensor_tensor(out=xt_c[:, s4, :], in0=xl[:, :], scalar=onehot[:, r_:r_ + 1], in1=xt_c[:, s4, :], op0=ALU.mult, op1=ALU.add), reads=[b_xl, b_const, b_xt], writes=[b_xt])
        qproj(TT, [(0, TT, 0)], xt_c, b_xt, 4, 128, csel[:, 0, :], b_csel)
        rows = [(s_kT[:, :, mm_ * 2048:(mm_ + 1) * 2048], s_vv[:, mm_, :, :]) for mm_ in range(m)]
        diag = [(s_kT[:, :, m * 2048:(m + 1) * 2048], s_vv[:, m, :, :], 128, 16, (lambda kt: pmask[:, kt, :]))]
        attend(TT, 4, 128, [dict(q0=0, q1=TT, rows=rows, diag=diag)], xt_c, b_xt, 0, o_yp, m * TT)
    P.emit()
    return nc


_CACHE = {}


def _get_nc(**kw):
    key = tuple(sorted(kw.items()))
    if key not in _CACHE:
        _CACHE[key] = build(**kw)
    return _CACHE[key]


def _consts(core):
    r = core % 4
    ident = np.eye(128, dtype=np.float32)
    sel = np.zeros((3, 2, 128), np.float32)
    sel[0, 0, :] = 1.0
    sel[1, 1, 0:ST] = 1.0
    sel[2, 1, ST:2 * ST] = 1.0
    kpos = (np.arange(16)[None, :, None] * 128 + np.arange(128)[:, None, None])
    qpos = r * TT + np.arange(TT)[None, None, :]
    pmask = np.where(kpos > qpos, NEG, 0.0).astype(ml_dtypes.bfloat16)
    smask = np.where(np.arange(ST)[:, None] > np.arange(ST)[None, :], NEG, 0.0).astype(ml_dtypes.bfloat16)
    onehot = np.zeros((128, 4), np.float32)
    onehot[:, r] = 1.0
    return dict(c_ident=ident, c_sel=sel, c_pmask=np.ascontiguousarray(pmask), c_smask=smask, c_onehot=onehot)


def _in_map(c, I, shared, T, PAST):
    f = lambda a: np.ascontiguousarray(np.asarray(a, dtype=np.float32))
    b = c // 4
    s0 = 2 * c
    m = dict(shared)
    m.update(_consts(c))
    m["xp"] = f(I["x_prompt"][b])
    m["xs"] = f(I["x_sample"][s0:s0 + 2]).reshape(2 * ST, D)
    m["cc"] = np.concatenate([f(I["c_prompt"])[b:b + 1], f(I["c_sample"])[s0:s0 + 2]], axis=0)
    m["st_h"] = f(I["state_lru_h"][0, s0:s0 + 2])
    m["st_conv"] = f(I["state_lru_conv"][0, s0:s0 + 2])
    m["ck_k"] = f(I["cache_fox_k"][0, s0:s0 + 2]).reshape(2, PAST, D)
    m["ck_v"] = f(I["cache_fox_v"][0, s0:s0 + 2]).reshape(2, PAST, D)
    m["ck_lf"] = f(I["cache_fox_logf"][0, s0:s0 + 2])
    return m


def _shared(I):
    f = lambda a: np.ascontiguousarray(np.asarray(a, dtype=np.float32))
    lvec = np.concatenate([f(I["lru_conv_w"])[0], f(I["lru_conv_b"]), f(I["lru_b_a"]), f(I["lru_b_x"]), f(I["lru_lambda"])], axis=0)
    return dict(norm_pre=f(I["norm_pre"]), norm_post=f(I["norm_post"]), ada_w=f(I["ada_w"]), ada_b=f(I["ada_b"]), lru_w_in=f(I["lru_w_in"])[0],
                lru_vecs=f(lvec), lru_w_a=f(I["lru_w_a"])[0], lru_w_x=f(I["lru_w_x"])[0], lru_w_out=f(I["lru_w_out"])[0],
                fox_w_in=f(I["fox_w_in"])[0], fox_b_f=f(I["fox_b_f"]), fox_w_out=f(I["fox_w_out"])[0])


def run_cores(I, cores, build_kw):
    T = I["x_prompt"].shape[1]
    PAST = I["cache_fox_k"].shape[2]
    kw = dict(build_kw)
    kw.update(Tn=T, PASTn=PAST)
    nc = _get_nc(**kw)
    shared = _shared(I)
    in_maps = [_in_map(c, I, shared, T, PAST) for c in cores]
    res = run_bass_kernel_spmd(nc, in_maps, core_ids=list(range(len(cores)))).results
    return {c: res[i] for i, c in enumerate(cores)}


def kernel(x_prompt, x_sample, c_prompt, c_sample, state_lru_h, state_lru_conv, cache_fox_k, cache_fox_v, cache_fox_logf,
           norm_pre, norm_post, ada_w, ada_b, lru_w_in, lru_conv_w, lru_conv_b, lru_w_a, lru_b_a, lru_w_x, lru_b_x,
           lru_lambda, lru_w_out, fox_w_in, fox_b_f, fox_w_out, _build_kw=None):
    I = dict(x_prompt=x_prompt, x_sample=x_sample, c_prompt=c_prompt, c_sample=c_sample, state_lru_h=state_lru_h, state_lru_conv=state_lru_conv,
             cache_fox_k=cache_fox_k, cache_fox_v=cache_fox_v, cache_fox_logf=cache_fox_logf, norm_pre=norm_pre, norm_post=norm_post,
             ada_w=ada_w, ada_b=ada_b, lru_w_in=lru_w_in, lru_conv_w=lru_conv_w, lru_conv_b=lru_conv_b, lru_w_a=lru_w_a, lru_b_a=lru_b_a,
             lru_w_x=lru_w_x, lru_b_x=lru_b_x, lru_lambda=lru_lambda, lru_w_out=lru_w_out, fox_w_in=fox_w_in, fox_b_f=fox_b_f, fox_w_out=fox_w_out)
    I = {k: np.asarray(v) for k, v in I.items()}
    res = run_cores(I, list(range(8)), _build_kw if _build_kw is not None else {})
    return assemble(res, I)


def assemble(res, I):
    B, T = I["x_prompt"].shape[0], I["x_prompt"].shape[1]
    NQ = T // 2048
    cores = sorted(res.keys())
    y_p = np.zeros((B, T, D), np.float32)
    for c in cores:
        b, r = c // 4, c % 4
        yp = res[c]["y_p"].reshape(NQ, TT, D)
        for m_ in range(NQ):
            i = 4 * m_ + r
            y_p[b, i * TT:(i + 1) * TT] = yp[m_]
    nb = len(cores) // 4 if len(cores) >= 4 else 1
    bs = sorted(set(c // 4 for c in cores))
    first = {b: min(c for c in cores if c // 4 == b) for b in bs}
    y_s = np.concatenate([res[c]["y_s"].reshape(2, ST, D) for c in cores], axis=0)
    lru_h_p = np.stack([res[first[b]]["lru_h_p"] for b in bs])[None]
    lru_c_p = np.stack([res[first[b]]["lru_conv_p"] for b in bs])[None]
    fk_p = np.stack([res[first[b]]["fk_p"].reshape(T, H, DH) for b in bs])[None]
    fv_p = np.stack([res[first[b]]["fv_p"].reshape(T, H, DH) for b in bs])[None]
    fl_p = np.stack([res[first[b]]["flf_p"] for b in bs])[None]
    lru_h_s = np.concatenate([res[c]["lru_h_s"] for c in cores], axis=0)[None]
    lru_c_s = np.concatenate([res[c]["lru_conv_s"] for c in cores], axis=0)[None]
    fk_s = np.concatenate([res[c]["fk_s"].reshape(2, ST, H, DH) for c in cores], axis=0)[None]
    fv_s = np.concatenate([res[c]["fv_s"].reshape(2, ST, H, DH) for c in cores], axis=0)[None]
    fl_s = np.concatenate([res[c]["flf_s"].reshape(2, ST, H) for c in cores], axis=0)[None]
    return (y_p, y_s, lru_h_p, lru_c_p, fk_p, fv_p, fl_p, lru_h_s, lru_c_s, fk_s, fv_s, fl_s)
```

```python
import contextlib
import numpy as np
import ml_dtypes
import concourse.bass as bass
import concourse.mybir as mybir
from concourse.bass_utils import run_bass_kernel_spmd

F32 = mybir.dt.float32
BF16 = mybir.dt.bfloat16
AF = mybir.ActivationFunctionType
ALU = mybir.AluOpType

D = 1024
R = 1536
NCH = 12
H = 16
DH = 64
T = 16384
TT = 512
NT = T // TT
ST = 32
PAST = 4096
EPS = 1e-6
NEG = -30000.0
VS = 66
KA = 70
COMPUTE = ("pe", "act", "dve", "pool", "sp")


class Buf:
    __slots__ = ("name", "last_w", "readers")

    def __init__(self, name):
        self.name = name
        self.last_w = None
        self.readers = []


class Op:
    __slots__ = ("eng", "fn", "deps", "is_dma", "idx", "need_inc", "val", "sem", "semval", "prev_on_sem", "inc")

    def __init__(self, eng, fn, is_dma, inc=16):
        self.eng = eng
        self.fn = fn
        self.deps = set()
        self.is_dma = is_dma
        self.need_inc = False
        self.val = None
        self.sem = None
        self.semval = None
        self.prev_on_sem = None
        self.inc = inc


class Prog:
    def __init__(self, nc, n_dma_sems=(("sp", 32), ("act", 12), ("pool", 12))):
        self.nc = nc
        self.ops = []
        self.by_eng = {e: [] for e in COMPUTE}
        self.dma_ring = {e: n for e, n in n_dma_sems}
        self.dma_count = {e: 0 for e, _ in n_dma_sems}
        self.dma_last_on_slot = {}
        self.all_bufs = []

    def buf(self, name=""):
        b = Buf(name)
        self.all_bufs.append(b)
        return b

    def _add(self, op, reads, writes):
        op.idx = len(self.ops)
        for b in reads:
            if b.last_w is not None:
                op.deps.add(b.last_w)
        for b in writes:
            if b.last_w is not None:
                op.deps.add(b.last_w)
            for r in b.readers:
                op.deps.add(r)
        for b in reads:
            b.readers.append(op)
        for b in writes:
            b.last_w = op
            b.readers = []
        op.deps.discard(op)
        self.ops.append(op)
        self.by_eng[op.eng].append(op)
        return op

    def op(self, eng, fn, reads=(), writes=()):
        return self._add(Op(eng, fn, False), reads, writes)

    def dma(self, eng, fn, reads=(), writes=(), inc=16):
        op = Op(eng, fn, True, inc)
        n = self.dma_count[eng]
        self.dma_count[eng] = n + 1
        slot = (eng, n % self.dma_ring[eng])
        op.sem = slot
        op.prev_on_sem = self.dma_last_on_slot.get(slot)
        op.semval = (op.prev_on_sem.semval if op.prev_on_sem else 0) + inc
        self.dma_last_on_slot[slot] = op
        return self._add(op, reads, writes)

    def barrier(self):
        lasts = [self.by_eng[e][-1] for e in COMPUTE if self.by_eng[e]]
        lasts += list(self.dma_last_on_slot.values())
        for e in COMPUTE:
            o = Op(e, None, False)
            o.idx = len(self.ops)
            o.deps = set(lasts)
            self.ops.append(o)
            self.by_eng[e].append(o)
        for b in self.all_bufs:
            if str(b.name).startswith("dram_"):
                continue
            b.last_w = None
            b.readers = []

    def emit(self, final_wait_eng="sp"):
        nc = self.nc
        lasts = [self.by_eng[e][-1] for e in COMPUTE if self.by_eng[e]]
        lasts += list(self.dma_last_on_slot.values())
        fin = Op(final_wait_eng, None, False)
        fin.idx = len(self.ops)
        fin.deps = set(lasts)
        self.ops.append(fin)
        self.by_eng[final_wait_eng].append(fin)
        for o in self.ops:
            for d in o.deps:
                if not d.is_dma:
                    if d.eng == o.eng and o.eng == "pe":
                        continue
                    d.need_inc = True
        for e in COMPUTE:
            c = 0
            for o in self.by_eng[e]:
                if not o.is_dma and o.need_inc:
                    c += 1
                    o.val = c
        with contextlib.ExitStack() as st:
            esem = {e: st.enter_context(nc.semaphore("s_" + e)) for e in COMPUTE}
            dsem = {}
            for e, n in self.dma_ring.items():
                for i in range(n):
                    dsem[(e, i)] = st.enter_context(nc.semaphore("d_%s_%d" % (e, i)))
            block = st.enter_context(nc.Block())
            engobj = {"pe": nc.tensor, "act": nc.scalar, "dve": nc.vector, "pool": nc.gpsimd, "sp": nc.sync}

            def run_engine(e):
                eng = engobj[e]
                waited = {}

                def wait(key, sem, val):
                    if waited.get(key, 0) >= val:
                        return
                    waited[key] = val
                    eng.wait_ge(sem, val)

                for o in self.by_eng[e]:
                    for d in sorted(o.deps, key=lambda d: d.idx):
                        if d.is_dma:
                            wait(d.sem, dsem[d.sem], d.semval)
                        else:
                            if d.eng == e and e == "pe":
                                continue
                            if d.val is None:
                                continue
                            wait(d.eng, esem[d.eng], d.val)
                    if o.is_dma and o.prev_on_sem is not None:
                        wait(o.sem, dsem[o.sem], o.prev_on_sem.semval)
                    if o.fn is None:
                        if o.need_inc:
                            eng.nop().then_inc(esem[e], 1)
                        continue
                    ins = o.fn(eng)
                    if o.is_dma:
                        ins.then_inc(dsem[o.sem], o.inc)
                    elif o.need_inc:
                        ins.then_inc(esem[e], 1)

            @block.tensor
            def _(x):
                run_engine("pe")

            @block.scalar
            def _(x):
                run_engine("act")

            @block.vector
            def _(x):
                run_engine("dve")

            @block.gpsimd
            def _(x):
                run_engine("pool")

            @block.sync
            def _(x):
                run_engine("sp")


class Arena:
    def __init__(self, nc, base=16640, limit=229376 - 2048):
        self.nc = nc
        self.ptr = base
        self.limit = limit
        self.n = 0

    def alloc(self, shape, dtype, name="t"):
        size = int(np.prod(shape[1:])) * (4 if dtype == F32 else 2)
        size = (size + 63) // 64 * 64
        off = self.ptr
        self.ptr += size
        assert self.ptr <= self.limit, ("SBUF overflow", name, self.ptr)
        self.n += 1
        return self.nc.alloc_sbuf_tensor_at("%s_%d" % (name, self.n), list(shape), dtype, offset=off)

    def mark(self):
        return self.ptr

    def reset(self, m):
        self.ptr = m


def build(nq_tiles=None, do_attn=True, do_sample=True, nt_prompt=None, stop=9, Tn=16384, PASTn=4096, dbg=0):
    T, PAST = Tn, PASTn
    NPR = PAST // 2048
    NQ = T // 2048
    if nq_tiles is None:
        nq_tiles = NQ
    if nt_prompt is None:
        nt_prompt = T // TT
    nc = bass.Bass("TRN2", target_bir_lowering=False)
    P = Prog(nc)
    A = Arena(nc)

    def din(name, shape, dt=F32):
        return nc.dram_tensor(name, list(shape), dt, kind="ExternalInput").ap()

    def dout(name, shape, dt=F32):
        return nc.dram_tensor(name, list(shape), dt, kind="ExternalOutput").ap()

    def dscr(name, shape, dt=F32):
        return nc.dram_tensor(name, list(shape), dt).ap()

    i_xp = din("xp", [T, D])
    i_xs = din("xs", [2 * ST, D])
    i_cc = din("cc", [3, D])
    i_sth = din("st_h", [2, R])
    i_stc = din("st_conv", [2, 3, R])
    i_ck = din("ck_k", [2, PAST, D])
    i_cv = din("ck_v", [2, PAST, D])
    i_clf = din("ck_lf", [2, PAST, H])
    i_npre = din("norm_pre", [2, D])
    i_npost = din("norm_post", [2, D])
    i_adaw = din("ada_w", [2, D, 3 * D])
    i_adab = din("ada_b", [2, 3 * D])
    i_win = din("lru_w_in", [D, 2 * R])
    i_lvec = din("lru_vecs", [8, R])
    i_wa = din("lru_w_a", [NCH, 128, 128])
    i_wx = din("lru_w_x", [NCH, 128, 128])
    i_wout = din("lru_w_out", [R, D])
    i_fwin = din("fox_w_in", [D, 4 * D + H])
    i_fbf = din("fox_b_f", [1, H])
    i_fwout = din("fox_w_out", [D, D])
    i_ident = din("c_ident", [128, 128])
    i_sel = din("c_sel", [3, 2, 128])
    i_pmask = din("c_pmask", [128, 16, TT], BF16)
    i_smask = din("c_smask", [ST, ST], BF16)
    i_onehot = din("c_onehot", [128, 4])
    o_yp = dout("y_p", [NQ * TT, D])
    o_ys = dout("y_s", [2 * ST, D])
    o_lhp = dout("lru_h_p", [R])
    o_lcp = dout("lru_conv_p", [3, R])
    o_fkp = dout("fk_p", [T, D])
    o_fvp = dout("fv_p", [T, D])
    o_flp = dout("flf_p", [T, H])
    o_lhs = dout("lru_h_s", [2, R])
    o_lcs = dout("lru_conv_s", [2, 3, R])
    o_fks = dout("fk_s", [2 * ST, D])
    o_fvs = dout("fv_s", [2 * ST, D])
    o_fls = dout("flf_s", [2 * ST, H])
    s_x1 = dscr("s_x1", [T, D])
    s_kT = dscr("s_kT", [H, KA, T], BF16)
    s_vv = dscr("s_vv", [H, T // 2048, 128, 16 * VS], BF16)
    s_cum = dscr("s_cum", [H, T])
    NK_S = PAST + 128
    s_kTs = dscr("s_kTs", [2, H, KA, NK_S], BF16)
    s_vvs = dscr("s_vvs", [2, H, NPR + 1, 128, 16 * VS], BF16)

    b_sx1, b_skT, b_svv, b_scm, b_skTs, b_svvs = [P.buf("dram_%d" % i) for i in range(6)]
    PS = [nc.alloc_psum_tensor("ps%d" % i, [128, 1024], F32) for i in range(4)]
    bPS = [[P.buf("ps%d_%d" % (i, j)) for j in range(2)] for i in range(4)]

    def bank(i):
        return PS[i // 2][:, (i % 2) * 512:(i % 2) * 512 + 512], bPS[i // 2][i % 2]

    identf = A.alloc([128, 128], F32, "ident")
    b_const = P.buf("const")
    P.dma("sp", lambda e: e.dma_start(out=identf[:], in_=i_ident[:, :]), writes=[b_const])
    sel = A.alloc([3, 2, 128], F32, "sel")
    P.dma("sp", lambda e: e.dma_start(out=sel[:], in_=i_sel[:, :, :]), writes=[b_const])
    onehot = A.alloc([128, 4], F32, "onehot")
    P.dma("sp", lambda e: e.dma_start(out=onehot[:], in_=i_onehot[:, :]), writes=[b_const])
    ones_bf = A.alloc([128, 512], BF16, "ones")
    ones_f = A.alloc([128, 512], F32, "onesf")
    P.op("pool", lambda e: e.memset(ones_bf[:], 1.0), writes=[b_const])
    P.op("pool", lambda e: e.memset(ones_f[:], 1.0), writes=[b_const])
    ident_bf = A.alloc([128, 128], BF16, "identb")
    P.op("dve", lambda e: e.tensor_copy(ident_bf[:], identf[:]), reads=[b_const], writes=[b_const])
    gcol = A.alloc([128, 2, 2, 8, 3], F32, "gcol")
    gprow = A.alloc([128, 2, 2, D], F32, "gprow")
    lcol = A.alloc([128, NCH, 10], F32, "lcol")
    bfcol = A.alloc([16, 2], F32, "bfcol")
    bfrow = A.alloc([128, H], F32, "bfrow")
    b_ada = P.buf("ada")
    epsc = A.alloc([128, 1], F32, "epsc")
    onec = A.alloc([128, 1], F32, "onec")
    P.op("pool", lambda e: e.memset(epsc[:], EPS), writes=[b_const])
    P.op("pool", lambda e: e.memset(onec[:], 1.0), writes=[b_const])
    xs_t = A.alloc([128, 1, D], F32, "xs_t")
    b_xs = P.buf("xs")
    scum = A.alloc([16, 2 * ST], F32, "scum")
    b_scum = P.buf("scum")

    cx = {}

    def nb(*names):
        for n in names:
            cx["b_" + n] = P.buf(n)

    def front(N, nsub, Pt, layer, seqcols, xsrc, b_x):
        xn, junk, stat, hT = cx["xn"], cx["junk"], cx["stat"], cx["hT"]
        b_xn, b_junk, b_stat, b_hT = cx["b_xn"], cx["b_junk"], cx["b_stat"], cx["b_hT"]
        bxs = b_x if isinstance(b_x, list) else [b_x] * nsub
        for half in range(1):
            for s in range(nsub):
                b_x = bxs[s]
                if half == 0:
                    P.op("dve", lambda e, s=s: e.scalar_tensor_tensor(out=junk[0:Pt, :], in0=xsrc[0:Pt, s, :], scalar=1.0, in1=xsrc[0:Pt, s, :], op0=ALU.mult, op1=ALU.mult, accum_out=stat[0:Pt, s:s + 1]),
                         reads=[b_x], writes=[b_junk, b_stat])
                    P.op("act", lambda e, s=s: e.activation(out=stat[0:Pt, 4 + s:5 + s], in_=stat[0:Pt, s:s + 1], func=AF.Sqrt, scale=1.0 / D, bias=epsc[0:Pt, 0:1]), reads=[b_stat, b_const], writes=[b_stat])
                    P.op("dve", lambda e, s=s: e.reciprocal(stat[0:Pt, 8 + s:9 + s], stat[0:Pt, 4 + s:5 + s]), reads=[b_stat], writes=[b_stat])
                P.op("dve", lambda e, s=s: e.tensor_scalar(out=xn[0:Pt, :], in0=xsrc[0:Pt, s, :], scalar1=stat[0:Pt, 8 + s:9 + s], scalar2=None, op0=ALU.mult), reads=[b_x, b_stat], writes=[b_xn])

                def trs(e, s=s, half=half):
                    ins = None
                    for j in range(8):
                        kc = j
                        ins = e.transpose(out=bank(j)[0][:, s * Pt:(s + 1) * Pt], in_=xn[0:Pt, kc * 128:(kc + 1) * 128], identity=identf[0:Pt, 0:Pt])
                    return ins
                P.op("pe", trs, reads=[b_xn, b_const], writes=[bank(j)[1] for j in range(8)])
            for j in range(8):
                kc = j
                for (c0, c1, sq) in seqcols:
                    P.op("act", lambda e, j=j, kc=kc, c0=c0, c1=c1, sq=sq: e.activation(out=hT[:, kc, c0:c1], in_=bank(j)[0][:, c0:c1], func=AF.Identity, scale=gcol[:, layer, 0, kc, sq:sq + 1], bias=gcol[:, layer, 1, kc, sq:sq + 1]),
                         reads=[bank(j)[1], b_ada], writes=[b_hT])

    def post(N, nsub, Pt, layer, grp, wsb, b_wsb, nkc, srcT, b_srcT, ksz, xres, b_xres):
        junk, stat, ytmp = cx["junk"], cx["stat"], cx["ytmp"]
        b_junk, b_stat, b_ytmp = cx["b_junk"], cx["b_stat"], cx["b_ytmp"]
        bxr = b_xres if isinstance(b_xres, list) else [b_xres] * nsub
        for s in range(nsub):
            b_xres = bxr[s]
            pp = PS[2 + (s % 2)]
            bpp = bPS[2 + (s % 2)]

            def mm(e, s=s, pp=pp):
                ins = None
                for hf in range(2):
                    for c in range(nkc):
                        ins = e.matmul(pp[0:Pt, hf * 512:(hf + 1) * 512], lhsT=srcT[0:ksz, c, s * Pt:(s + 1) * Pt], rhs=wsb[0:ksz, c, hf * 512:(hf + 1) * 512], start=(c == 0), stop=(c == nkc - 1))
                return ins
            P.op("pe", mm, reads=[b_srcT, b_wsb], writes=bpp)
            P.op("act", lambda e, pp=pp: e.activation(out=junk[0:Pt, :], in_=pp[0:Pt, :], func=AF.Square, accum_out=stat[0:Pt, 12:13]), reads=bpp, writes=[b_junk, b_stat])
            P.op("act", lambda e: e.activation(out=stat[0:Pt, 13:14], in_=stat[0:Pt, 12:13], func=AF.Sqrt, scale=1.0 / D, bias=epsc[0:Pt, 0:1]), reads=[b_stat, b_const], writes=[b_stat])
            P.op("dve", lambda e: e.reciprocal(stat[0:Pt, 14:15], stat[0:Pt, 13:14]), reads=[b_stat], writes=[b_stat])
            P.op("dve", lambda e, pp=pp: e.scalar_tensor_tensor(out=ytmp[0:Pt, :], in0=pp[0:Pt, :], scalar=stat[0:Pt, 14:15], in1=gprow[0:Pt, layer, grp, :], op0=ALU.mult, op1=ALU.mult), reads=bpp + [b_stat, b_ada], writes=[b_ytmp])
            P.op("pool", lambda e, s=s: e.tensor_tensor(out=xres[0:Pt, s, :], in0=ytmp[0:Pt, :], in1=xres[0:Pt, s, :], op=ALU.add), reads=[b_ytmp, b_xres], writes=[b_xres])

    def split3(src, b_src, N, sign):
        spl, spf, b_spl, b_spf = cx["spl"], cx["spf"], cx["b_spl"], cx["b_spf"]
        P.op("dve", lambda e: e.tensor_scalar(out=spf[:, 0, 0:N], in0=src, scalar1=sign, scalar2=None, op0=ALU.mult), reads=[b_src], writes=[b_spf])
        for j in range(3):
            P.op("dve", lambda e, j=j: e.tensor_copy(spl[:, j, 0:N], spf[:, 0, 0:N]), reads=[b_spf], writes=[b_spl])
            if j < 2:
                P.op("dve", lambda e, j=j: e.tensor_copy(spf[:, 1, 0:N], spl[:, j, 0:N]), reads=[b_spl], writes=[b_spf])
                P.op("dve", lambda e: e.tensor_tensor(out=spf[:, 0, 0:N], in0=spf[:, 0, 0:N], in1=spf[:, 1, 0:N], op=ALU.subtract), reads=[b_spf], writes=[b_spf])

    m_persist = A.mark()
    adaw = A.alloc([128, 8, 3 * D], F32, "adaw")
    crow = A.alloc([3, D], F32, "crow")
    ccol = A.alloc([128, 8, 3], F32, "ccol")
    vrow = A.alloc([12, R], F32, "vrow")
    adab = A.alloc([1, 2, 3 * D], F32, "adab")
    grow = A.alloc([3, D], F32, "grow")
    npb = A.alloc([128, 2, D], F32, "npb")
    tmpc = A.alloc([128, 8, 3], F32, "tmpc")
    b_crow, b_ccol, b_vrow, b_adaw, b_adab, b_grow, b_npb = [P.buf(n) for n in "crow ccol vrow adaw adab grow npb".split()]
    P.dma("sp", lambda e: e.dma_start(out=crow[:], in_=i_cc[:, :]), writes=[b_crow])
    P.op("pool", lambda e: e.memset(vrow[:], 0.0), writes=[b_vrow])
    P.dma("sp", lambda e: e.dma_start(out=vrow[0:8, :], in_=i_lvec[:, :]), reads=[b_vrow], writes=[b_vrow])
    P.dma("sp", lambda e: e.dma_start(out=vrow[8:10, 0:D], in_=i_npre[:, :]), reads=[b_vrow], writes=[b_vrow])
    P.dma("sp", lambda e: e.dma_start(out=adab[:], in_=i_adab.rearrange("(o l) f -> o l f", o=1)), writes=[b_adab])
    P.dma("sp", lambda e: e.dma_start(out=npb[:], in_=i_npost.rearrange("(o l) f -> o l f", o=1).broadcast_to([128, 2, D])), writes=[b_npb])
    P.dma("sp", lambda e: e.dma_start(out=bfrow[:], in_=i_fbf.broadcast_to([128, H])), writes=[b_const])
    P.op("act", lambda e: e.activation(out=crow[:], in_=crow[:], func=AF.Silu), reads=[b_crow], writes=[b_crow])
    pa, ba = bank(0)

    def tr_c(e):
        ins = None
        for kc in range(8):
            ins = e.transpose(out=pa[:, kc * 3:kc * 3 + 3], in_=crow[0:3, kc * 128:(kc + 1) * 128], identity=identf[0:3, 0:3])
        return ins
    P.op("pe", tr_c, reads=[b_crow, b_const], writes=[ba])
    P.op("dve", lambda e: e.tensor_copy(ccol[:].rearrange("p k s -> p (k s)"), pa[:, 0:24]), reads=[ba], writes=[b_ccol])
    pb_, bb = bank(1)

    def tr_v(e):
        ins = None
        for c in range(NCH):
            ins = e.transpose(out=pb_[:, c * 10:c * 10 + 10], in_=vrow[0:10, c * 128:(c + 1) * 128], identity=identf[0:10, 0:10])
        return ins
    P.op("pe", tr_v, reads=[b_vrow, b_const], writes=[bb])
    vcol = A.alloc([128, NCH, 10], F32, "vcol")
    b_vcol = P.buf("vcol")
    P.op("dve", lambda e: e.tensor_copy(vcol[:].rearrange("p c v -> p (c v)"), pb_[:, 0:NCH * 10]), reads=[bb], writes=[b_vcol])
    P.op("dve", lambda e: e.tensor_copy(lcol[:, :, 0:8], vcol[:, :, 0:8]), reads=[b_vcol], writes=[b_const])
    P.op("act", lambda e: e.activation(out=lcol[:, :, 8], in_=vcol[:, :, 7], func=AF.Exp, scale=-1.0), reads=[b_vcol], writes=[b_const])
    P.op("act", lambda e: e.activation(out=lcol[:, :, 8], in_=lcol[:, :, 8], func=AF.Ln, bias=onec[:, 0:1]), reads=[b_const], writes=[b_const])
    P.op("dve", lambda e: e.tensor_scalar(out=lcol[:, :, 8], in0=lcol[:, :, 8], scalar1=-8.0, scalar2=None, op0=ALU.mult), reads=[b_const], writes=[b_const])
    pc_, bc = bank(2)
    P.op("pe", lambda e: e.matmul(pc_[0:16, 0:1], lhsT=bfrow[0:1, 0:16], rhs=ones_f[0:1, 0:1], start=True, stop=True), reads=[b_const], writes=[bc])
    P.op("dve", lambda e: e.tensor_scalar(out=bfcol[:, 0:1], in0=pc_[0:16, 0:1], scalar1=-1.0, scalar2=None, op0=ALU.mult), reads=[bc], writes=[b_const])
    for l in range(2):
        for q4 in range(4):
            P.dma("sp" if q4 % 2 == 0 else "act",
                  lambda e, l=l, q4=q4: e.dma_start(out=adaw[:, 2 * q4:2 * q4 + 2, :], in_=i_adaw[l, q4 * 256:(q4 + 1) * 256, :].rearrange("(k p) f -> p k f", p=128)),
                  writes=[b_adaw])
        pm, bm = bank(4 + l * 2)

        def ada_cols(e, l=l, pm=pm):
            ins = None
            for fc in range(16):
                for kc in range(8):
                    ins = e.matmul(pm[:, fc * 3:fc * 3 + 3], lhsT=adaw[:, kc, fc * 128:(fc + 1) * 128], rhs=ccol[:, kc, :], start=(kc == 0), stop=False)
                ins = e.matmul(pm[:, fc * 3:fc * 3 + 3], lhsT=adab[0:1, l, fc * 128:(fc + 1) * 128], rhs=ones_f[0:1, 0:3], start=False, stop=True)
            return ins
        P.op("pe", ada_cols, reads=[b_adaw, b_ccol, b_adab, b_const], writes=[bm])
        P.op("dve", lambda e, l=l, pm=pm: e.tensor_copy(gcol[:, l, 1, :, :].rearrange("p k s -> p (k s)"), pm[:, 0:24]), reads=[bm], writes=[b_ada])
        P.op("dve", lambda e, l=l, pm=pm: e.tensor_scalar(out=tmpc[:].rearrange("p k s -> p (k s)"), in0=pm[:, 24:48], scalar1=1.0, scalar2=None, op0=ALU.add), reads=[bm], writes=[b_grow])
        for s in range(3):
            P.op("dve", lambda e, l=l, s=s: e.tensor_tensor(out=gcol[:, l, 0, :, s], in0=tmpc[:, :, s], in1=vcol[0:128, 0:8, 8 + l], op=ALU.mult), reads=[b_grow, b_vcol], writes=[b_ada])
        for hf in range(2):
            prh, brh = bank(5 + l * 2) if hf == 0 else bank(3)

            def ada_rowh(e, l=l, hf=hf, prh=prh):
                ins = None
                for kc in range(8):
                    ins = e.matmul(prh[0:3, 0:512], lhsT=ccol[:, kc, :], rhs=adaw[:, kc, 2 * D + hf * 512:2 * D + (hf + 1) * 512], start=(kc == 0), stop=False)
                ins = e.matmul(prh[0:3, 0:512], lhsT=ones_f[0:1, 0:3], rhs=adab[0:1, l, 2 * D + hf * 512:2 * D + (hf + 1) * 512], start=False, stop=True)
                return ins
            P.op("pe", ada_rowh, reads=[b_adaw, b_ccol, b_adab, b_const], writes=[brh])
            P.op("dve", lambda e, hf=hf, prh=prh: e.tensor_copy(grow[:, hf * 512:(hf + 1) * 512], prh[0:3, 0:512]), reads=[brh], writes=[b_grow])
        for g in range(2):
            for hf in range(2):
                pg, bg = bank(0 + hf)
                P.op("pe", lambda e, g=g, hf=hf, pg=pg: e.matmul(pg[:, 0:512], lhsT=sel[0:3, g, :], rhs=grow[0:3, hf * 512:(hf + 1) * 512], start=True, stop=True), reads=[b_grow, b_const], writes=[bg])
                P.op("dve", lambda e, g=g, hf=hf, pg=pg, l=l: e.tensor_tensor(out=gprow[:, l, g, hf * 512:(hf + 1) * 512], in0=pg[:, 0:512], in1=npb[:, l, hf * 512:(hf + 1) * 512], op=ALU.mult), reads=[bg, b_npb], writes=[b_ada])
    P.barrier()
    A.reset(m_persist)
    if dbg == 32:
        P.dma("sp", lambda e: e.dma_start(out=o_fkp[2048:2176, 0:96], in_=gcol[:].rearrange("p a b c d -> p (a b c d)")), reads=[b_ada])
        P.dma("sp", lambda e: e.dma_start(out=o_fkp[2176:2304, 0:2 * 2 * D], in_=gprow[:].rearrange("p a b c -> p (a b c)")), reads=[b_ada]) if False else None
    if stop == 0:
        P.emit()
        return nc

    win = A.alloc([128, 8, 2 * R], BF16, "win")
    wg = A.alloc([128, 2, NCH, 128], BF16, "wg")
    wout = A.alloc([128, NCH, D], BF16, "wout")
    b_w = P.buf("weights")
    for kc in range(8):
        for hf in range(2):
            P.dma("pool", lambda e, kc=kc, hf=hf: e.dma_start(out=win[:, kc, hf * R:(hf + 1) * R], in_=i_win[kc * 128:(kc + 1) * 128, hf * R:(hf + 1) * R]), writes=[b_w])
    P.dma("pool", lambda e: e.dma_start(out=wg[:, 0, :, :], in_=i_wa.rearrange("n d e -> d n e")), writes=[b_w])
    P.dma("pool", lambda e: e.dma_start(out=wg[:, 1, :, :], in_=i_wx.rearrange("n d e -> d n e")), writes=[b_w])
    for c in range(NCH):
        P.dma("pool", lambda e, c=c: e.dma_start(out=wout[:, c, :], in_=i_wout[c * 128:(c + 1) * 128, :]), writes=[b_w])
    CG = 2
    NG = NCH // CG
    ENG_CAST = "pool" if dbg & 1024 else "dve"
    ENG_A2 = "pool" if dbg & 2048 else "dve"
    ENG_IM = "pool" if dbg & 4096 else "dve"
    xt_a1 = A.alloc([128, 4, D], F32, "xt_a1")
    cx["xn"] = A.alloc([128, D], F32, "xn")
    cx["junk"] = A.alloc([128, D], BF16, "junk")
    cx["stat"] = A.alloc([128, 16], F32, "stat")
    cx["hT"] = A.alloc([128, 8, TT], BF16, "hT")
    nb("xn", "junk", "stat", "hT")
    cx["ytmp"] = cx["xn"]
    cx["b_ytmp"] = cx["b_xn"]
    halo = A.alloc([128, NCH, 2, 4], F32, "halo")
    hst = A.alloc([128, NCH, 2], F32, "hst")
    XBW = 520
    sets = []
    for k_ in range(2):
        S_ = dict(xb=A.alloc([128, CG, XBW], F32, "xb"), sg=A.alloc([128, CG, TT], BF16, "sg"), xc=A.alloc([128, CG, TT], F32, "xc"),
                  xcb=A.alloc([128, CG, TT], BF16, "xcb"), rr=A.alloc([128, CG, TT], F32, "rr"), ig=A.alloc([128, CG, TT], F32, "ig"),
                  aa=A.alloc([128, CG, TT], F32, "aa"))
        for n_ in ("xb", "xbh", "sg", "xc", "xcb", "rr", "ig", "aa"):
            S_["b_" + n_] = P.buf("%s%d" % (n_, k_))
        sets.append(S_)
    zT = A.alloc([128, NCH, TT], BF16, "zT")
    b_xts = [P.buf("xt%d" % s_) for s_ in range(4)]
    b_halo = [P.buf("halo%d" % c_) for c_ in range(NCH)]
    b_hst, b_zT = P.buf("hst"), P.buf("zT")

    def layer0(N, nseg, L, segw):
        hT, b_hT = cx["hT"], cx["b_hT"]

        def xbv(S, cl, sgm, a_, b__):
            return S["xb"][:, cl, sgm * segw + a_:sgm * segw + b__]

        def stageA_pe(g):
            for cl in range(CG):
                c = g * CG + cl
                pxa, bxa = bank(4 + cl * 2)
                pga, bga = bank(5 + cl * 2)

                def mm(e, c=c, pxa=pxa, pga=pga):
                    ins = None
                    for kc in range(8):
                        ins = e.matmul(pxa[:, 0:N], lhsT=win[:, kc, c * 128:(c + 1) * 128], rhs=hT[:, kc, 0:N], start=(kc == 0), stop=(kc == 7))
                    for kc in range(8):
                        ins = e.matmul(pga[:, 0:N], lhsT=win[:, kc, R + c * 128:R + (c + 1) * 128], rhs=hT[:, kc, 0:N], start=(kc == 0), stop=(kc == 7))
                    return ins
                P.op("pe", mm, reads=[b_hT, b_w], writes=[bxa, bga])

        def stageA_el(g):
            S = sets[g % 2]
            for cl in range(CG):
                c = g * CG + cl
                pxa, bxa = bank(4 + cl * 2)
                pga, bga = bank(5 + cl * 2)
                P.op("act", lambda e, S=S, cl=cl, pga=pga: e.activation(out=S["sg"][:, cl, 0:N], in_=pga[:, 0:N], func=AF.Silu), reads=[bga], writes=[S["b_sg"]])
                for sgm in range(nseg):
                    P.op("dve", lambda e, S=S, cl=cl, sgm=sgm, pxa=pxa: e.tensor_copy(xbv(S, cl, sgm, 3, 3 + L), pxa[:, sgm * L:(sgm + 1) * L]), reads=[bxa], writes=[S["b_xb"]])
                    P.op("pool", lambda e, S=S, c=c, cl=cl, sgm=sgm: e.tensor_copy(xbv(S, cl, sgm, 0, 3), halo[:, c, sgm, 0:3]), reads=[b_halo[c]], writes=[S["b_xbh"]])
            for cl in range(CG):
                c = g * CG + cl
                for sgm in range(nseg):
                    o = S["xc"][:, cl, sgm * L:(sgm + 1) * L]
                    P.op("dve", lambda e, S=S, c=c, cl=cl, sgm=sgm, o=o: e.tensor_scalar(out=o, in0=xbv(S, cl, sgm, 0, L), scalar1=lcol[:, c, 0:1], scalar2=lcol[:, c, 4:5], op0=ALU.mult, op1=ALU.add),
                         reads=[S["b_xb"], S["b_xbh"], b_const], writes=[S["b_xc"]])
                    for k in range(1, 4):
                        P.op("dve", lambda e, S=S, c=c, cl=cl, sgm=sgm, o=o, k=k: e.scalar_tensor_tensor(out=o, in0=xbv(S, cl, sgm, k, k + L), scalar=lcol[:, c, k:k + 1], in1=o, op0=ALU.mult, op1=ALU.add),
                             reads=[S["b_xb"], S["b_xbh"], b_const, S["b_xc"]], writes=[S["b_xc"]])
                    P.op("pool", lambda e, S=S, c=c, cl=cl, sgm=sgm: e.tensor_copy(halo[:, c, sgm, 0:3], xbv(S, cl, sgm, L, L + 3)), reads=[S["b_xb"]], writes=[b_halo[c]])
                P.op(ENG_CAST, lambda e, S=S, cl=cl: e.tensor_copy(S["xcb"][:, cl, 0:N], S["xc"][:, cl, 0:N]), reads=[S["b_xc"]], writes=[S["b_xcb"]])

        def stageB_pe(g):
            S = sets[g % 2]
            xcb = S["xcb"]
            for cl in range(CG):
                c = g * CG + cl
                pra, bra = bank(cl * 2)
                pia, bia = bank(cl * 2 + 1)

                def mg(e, c=c, cl=cl, pra=pra, pia=pia, xcb=xcb):
                    e.matmul(pra[:, 0:N], lhsT=wg[:, 0, c, :], rhs=xcb[:, cl, 0:N], start=True, stop=True)
                    return e.matmul(pia[:, 0:N], lhsT=wg[:, 1, c, :], rhs=xcb[:, cl, 0:N], start=True, stop=True)
                P.op("pe", mg, reads=[S["b_xcb"], b_w], writes=[bra, bia])

        def stageB_el(g):
            S = sets[g % 2]
            rr, ig, aa, xc, sg, xcb = S["rr"], S["ig"], S["aa"], S["xc"], S["sg"], S["xcb"]
            for cl in range(CG):
                c = g * CG + cl
                pra, bra = bank(cl * 2)
                pia, bia = bank(cl * 2 + 1)
                P.op("act", lambda e, c=c, cl=cl, pra=pra, rr=rr: e.activation(out=rr[:, cl, 0:N], in_=pra[:, 0:N], func=AF.Sigmoid, bias=lcol[:, c, 5:6]), reads=[bra, b_const], writes=[S["b_rr"]])
                P.op("act", lambda e, c=c, cl=cl, pia=pia, ig=ig: e.activation(out=ig[:, cl, 0:N], in_=pia[:, 0:N], func=AF.Sigmoid, bias=lcol[:, c, 6:7]), reads=[bia, b_const], writes=[S["b_ig"]])
            for cl in range(CG):
                c = g * CG + cl
                P.op("act", lambda e, c=c, cl=cl, rr=rr, aa=aa: e.activation(out=aa[:, cl, 0:N], in_=rr[:, cl, 0:N], func=AF.Exp, scale=lcol[:, c, 8:9]), reads=[S["b_rr"], b_const], writes=[S["b_aa"]])
                P.op(ENG_A2, lambda e, cl=cl, rr=rr, aa=aa: e.tensor_tensor(out=rr[:, cl, 0:N], in0=aa[:, cl, 0:N], in1=aa[:, cl, 0:N], op=ALU.mult), reads=[S["b_aa"], S["b_rr"]], writes=[S["b_rr"]])
                P.op("pool", lambda e, cl=cl, ig=ig, xc=xc: e.tensor_tensor(out=ig[:, cl, 0:N], in0=ig[:, cl, 0:N], in1=xc[:, cl, 0:N], op=ALU.mult), reads=[S["b_ig"], S["b_xc"]], writes=[S["b_ig"]])
            for cl in range(CG):
                c = g * CG + cl
                P.op("act", lambda e, cl=cl, rr=rr: e.activation(out=rr[:, cl, 0:N], in_=rr[:, cl, 0:N], func=AF.Sqrt, scale=-1.0, bias=onec[:, 0:1]), reads=[S["b_rr"], b_const], writes=[S["b_rr"]])
                P.op(ENG_IM, lambda e, cl=cl, ig=ig, rr=rr: e.tensor_tensor(out=ig[:, cl, 0:N], in0=ig[:, cl, 0:N], in1=rr[:, cl, 0:N], op=ALU.mult), reads=[S["b_ig"], S["b_rr"]], writes=[S["b_ig"]])
                for sgm in range(nseg):
                    P.op("dve", lambda e, c=c, cl=cl, sgm=sgm, xc=xc, aa=aa, ig=ig: e.tensor_tensor_scan(out=xc[:, cl, sgm * L:(sgm + 1) * L], data0=aa[:, cl, sgm * L:(sgm + 1) * L], data1=ig[:, cl, sgm * L:(sgm + 1) * L], initial=hst[:, c, sgm:sgm + 1], op0=ALU.mult, op1=ALU.add),
                         reads=[S["b_aa"], S["b_ig"], b_hst, S["b_xc"]], writes=[S["b_xc"]])
                    P.op("dve", lambda e, c=c, cl=cl, sgm=sgm, xc=xc: e.tensor_copy(hst[:, c, sgm:sgm + 1], xc[:, cl, (sgm + 1) * L - 1:(sgm + 1) * L]), reads=[S["b_xc"], b_hst], writes=[b_hst])
                P.op("pool", lambda e, c=c, cl=cl, xc=xc, sg=sg: e.tensor_tensor(out=zT[:, c, 0:N], in0=xc[:, cl, 0:N], in1=sg[:, cl, 0:N], op=ALU.mult), reads=[S["b_xc"], S["b_sg"]], writes=[b_zT])

        stageA_pe(0)
        stageA_el(0)
        for g in range(NG):
            if not (dbg & 8192):
                if g + 1 < NG:
                    stageA_pe(g + 1)
                    stageA_el(g + 1)
                stageB_pe(g)
                stageB_el(g)
                continue
            stageB_pe(g)
            if g + 1 < NG:
                stageA_pe(g + 1)
            stageB_el(g)
            if g + 1 < NG:
                stageA_el(g + 1)

    P.op("pool", lambda e: e.memset(halo[:], 0.0), writes=b_halo)
    P.op("pool", lambda e: e.memset(hst[:], 0.0), writes=[b_hst])
    for i in range(nt_prompt):
        for s4 in range(4):
            P.dma("sp", lambda e, i=i, s4=s4: e.dma_start(out=xt_a1[:, s4, :], in_=i_xp[i * TT + s4 * 128:i * TT + (s4 + 1) * 128, :]), writes=[b_xts[s4]])
        front(TT, 4, 128, 0, [(0, TT, 0)], xt_a1, b_xts)
        layer0(TT, 1, TT, 0)
        post(TT, 4, 128, 0, 0, wout, b_w, NCH, zT, b_zT, 128, xt_a1, b_xts)
        for s4 in range(4):
            P.dma("pool", lambda e, i=i, s4=s4: e.dma_start(out=s_x1[i * TT + s4 * 128:i * TT + (s4 + 1) * 128, :], in_=xt_a1[:, s4, :]), reads=[b_xts[s4]], writes=[b_sx1])
        if dbg == 31 and i < NQ:
            P.dma("sp", lambda e, i=i: e.dma_start(out=o_yp[i * TT:(i + 1) * TT, :].rearrange("(s p) f -> p s f", p=128), in_=xt_a1[:]), reads=b_xts)
    P.dma("sp", lambda e: e.dma_start(out=o_lhp.rearrange("(c p) -> p c", p=128), in_=hst[:, :, 0], allow_slow_non_contiguous=True), reads=[b_hst])
    for k3 in range(3):
        P.dma("sp", lambda e, k3=k3: e.dma_start(out=o_lcp[k3].rearrange("(c p) -> p c", p=128), in_=halo[:, :, 0, k3], allow_slow_non_contiguous=True), reads=b_halo)
    sc = [(0, ST, 1), (ST, 2 * ST, 2)]
    if do_sample:
        for sq in range(2):
            P.dma("sp", lambda e, sq=sq: e.dma_start(out=hst[:, :, sq], in_=i_sth[sq].rearrange("(c p) -> p c", p=128), allow_slow_non_contiguous=True), reads=[b_hst], writes=[b_hst])
            for k3 in range(3):
                P.dma("sp", lambda e, sq=sq, k3=k3: e.dma_start(out=halo[:, :, sq, k3], in_=i_stc[sq, k3].rearrange("(c p) -> p c", p=128), allow_slow_non_contiguous=True), reads=b_halo, writes=b_halo)
        P.dma("sp", lambda e: e.dma_start(out=xs_t[0:64, 0, :], in_=i_xs[:, :]), writes=[b_xs])
        front(2 * ST, 1, 64, 0, sc, xs_t, b_xs)
        layer0(2 * ST, 2, ST, 40)
        post(2 * ST, 1, 64, 0, 1, wout, b_w, NCH, zT, b_zT, 128, xs_t, b_xs)
        for sq in range(2):
            P.dma("sp", lambda e, sq=sq: e.dma_start(out=o_lhs[sq].rearrange("(c p) -> p c", p=128), in_=hst[:, :, sq], allow_slow_non_contiguous=True), reads=[b_hst])
            for k3 in range(3):
                P.dma("sp", lambda e, sq=sq, k3=k3: e.dma_start(out=o_lcs[sq, k3].rearrange("(c p) -> p c", p=128), in_=halo[:, :, sq, k3], allow_slow_non_contiguous=True), reads=b_halo)
    P.barrier()
    A.reset(m_persist)
    if stop == 1:
        P.emit()
        return nc

    fwin = A.alloc([128, 8, 2 * D + H], BF16, "fwin")
    b_w = P.buf("weights2")
    for kc in range(8):
        for hf in range(2):
            P.dma("pool", lambda e, kc=kc, hf=hf: e.dma_start(out=fwin[:, kc, hf * D:(hf + 1) * D], in_=i_fwin[kc * 128:(kc + 1) * 128, D + hf * D:D + (hf + 1) * D]), writes=[b_w])
        P.dma("pool", lambda e, kc=kc: e.dma_start(out=fwin[:, kc, 2 * D:2 * D + H], in_=i_fwin[kc * 128:(kc + 1) * 128, 4 * D:4 * D + H]), writes=[b_w])
    xt_a2s = [A.alloc([128, 4, D], F32, "xt_a2")]
    xt_a2 = xt_a2s[0]
    cx["xn"] = A.alloc([128, D], F32, "xn")
    cx["junk"] = A.alloc([128, D], BF16, "junk")
    cx["stat"] = A.alloc([128, 16], F32, "stat")
    cx["hT"] = A.alloc([128, 8, TT], BF16, "hT")
    kst = A.alloc([128, 4, D], F32, "kst")
    kstb = A.alloc([128, 4, D], BF16, "kstb")
    b_kstb = P.buf("kstb")
    vst = A.alloc([128, 4, D], F32, "vst")
    v1st = A.alloc([128, H, 4, VS], BF16, "v1st")
    kTst = A.alloc([KA, H, TT], BF16, "kTst")
    lft = A.alloc([128, 4, H], F32, "lft")
    lfT = A.alloc([16, TT], F32, "lfT")
    cumT = A.alloc([16, TT], F32, "cumT")
    ccar = A.alloc([16, 2], F32, "ccar")
    cx["spl"] = A.alloc([16, 3, TT], BF16, "spl")
    cx["spf"] = A.alloc([16, 2, TT], F32, "spf")
    nb("xn", "junk", "stat", "hT", "spl", "spf")
    b_xt, b_kst, b_vst, b_v1st, b_kTst, b_lft, b_lfT, b_cumT, b_ccar = [P.buf(n) for n in range(9)]
    P.op("pool", lambda e: e.memset(v1st[:], 1.0), writes=[b_v1st])
    P.op("pool", lambda e: e.memset(kTst[:], 1.0), writes=[b_kTst])
    P.op("pool", lambda e: e.memset(ccar[:], 0.0), writes=[b_ccar])

    def ktrans(N, nsub, Pt, cast=False):
        if cast:
            for s_ in range(nsub):
                P.op("dve", lambda e, s_=s_: e.tensor_copy(kstb[0:Pt, s_, :], kst[0:Pt, s_, :]), reads=[b_kst], writes=[b_kstb])
        for h in range(H):
            pk, bk = bank(h % 2)

            def trk(e, h=h, pk=pk):
                ins = None
                for s in range(nsub):
                    ins = e.matmul(pk[0:64, s * Pt:(s + 1) * Pt], lhsT=kstb[0:Pt, s, h * 64:(h + 1) * 64], rhs=ident_bf[0:Pt, 0:Pt], start=True, stop=True)
                return ins
            P.op("pe", trk, reads=[b_kstb, b_const], writes=[bk])
            if h % 2 == 0:
                P.op("act", lambda e, h=h, pk=pk: e.activation(out=kTst[0:64, h, 0:N], in_=pk[0:64, 0:N], func=AF.Identity), reads=[bk], writes=[b_kTst])
            else:
                P.op("dve", lambda e, h=h, pk=pk: e.tensor_copy(kTst[0:64, h, 0:N], pk[0:64, 0:N]), reads=[bk], writes=[b_kTst])

    def ckrows(N):
        spl, b_spl = cx["spl"], cx["b_spl"]
        for j in range(3):
            P.dma("sp", lambda e, j=j: e.dma_start(out=kTst[67 + j:68 + j, :, 0:N], in_=spl[:, j, 0:N]), reads=[b_spl, b_kTst], writes=[b_kTst])

    def kvproj(N, nsub, Pt, nseg, L, x1src, b_x1, seqcols, o_fk, o_fv, o_fl, tok0, kT_dsts, vv_dst, cum_dst):
        front(N, nsub, Pt, 1, seqcols, x1src, b_x1)
        if dbg == 33:
            for q_ in range(2):
                P.dma("pool", lambda e, q_=q_: e.dma_start(out=o_fvp[2048:2176, q_ * 2048:(q_ + 1) * 2048].rearrange("p (k n) -> p k n", k=4) if False else o_fvp[2048 + q_ * 128:2176 + q_ * 128, 0:1024].rearrange("p (k n) -> p k n", k=4)[:, :, 0:N // 2 if False else 256], in_=cx["hT"][:, q_ * 4:(q_ + 1) * 4, 0:256]), reads=[cx["b_hT"]])
            return
        if dbg == 1:
            return
        hT, b_hT = cx["hT"], cx["b_hT"]
        for s in range(nsub):
            for which, (wofs, stg, b_stg, o_d) in enumerate([(0, kst, b_kst, o_fk), (D, vst, b_vst, o_fv)]):
                pp = PS[2 + which]
                bpp = bPS[2 + which]

                def mm(e, s=s, pp=pp, wofs=wofs):
                    ins = None
                    for hf in range(2):
                        for kc in range(8):
                            ins = e.matmul(pp[0:Pt, hf * 512:(hf + 1) * 512], lhsT=hT[:, kc, s * Pt:(s + 1) * Pt], rhs=fwin[:, kc, wofs + hf * 512:wofs + (hf + 1) * 512], start=(kc == 0), stop=(kc == 7))
                    return ins
                P.op("pe", mm, reads=[b_hT, b_w], writes=bpp)
                P.op("act", lambda e, s=s, pp=pp, stg=stg: e.activation(out=stg[0:Pt, s, :], in_=pp[0:Pt, :], func=AF.Identity), reads=bpp, writes=[b_stg])
                if which == 0:
                    P.op("dve", lambda e, s=s: e.tensor_copy(kstb[0:Pt, s, :], kst[0:Pt, s, :]), reads=[b_kst], writes=[b_kstb])
                if which == 1 and dbg != 21:
                    if dbg == 22:
                        P.op("act", lambda e, s=s, pp=pp: e.activation(out=v1st[0:Pt, :, s, 0:64], in_=pp[0:Pt, :].rearrange("p (h d) -> p h d", h=H), func=AF.Identity), reads=bpp, writes=[b_v1st])
                    elif dbg == 23:
                        P.op("pool", lambda e, s=s: e.tensor_copy(v1st[0:Pt, :, s, 0:64], vst[0:Pt, s, :].rearrange("p (h d) -> p h d", h=H)), reads=[b_vst], writes=[b_v1st])
                    else:
                        P.op("dve", lambda e, s=s: e.tensor_copy(v1st[0:Pt, :, s, 0:64], vst[0:Pt, s, :].rearrange("p (h d) -> p h d", h=H)), reads=[b_vst], writes=[b_v1st])
                P.dma("sp", lambda e, s=s, stg=stg, o_d=o_d: e.dma_start(out=o_d[tok0 + s * Pt:tok0 + (s + 1) * Pt, :], in_=stg[0:Pt, s, :]), reads=[b_stg])
            if dbg in (2, 21, 22, 23):
                continue
            pf, bf_ = bank(0)

            def mfl(e, s=s, pf=pf):
                ins = None
                for kc in range(8):
                    ins = e.matmul(pf[0:Pt, 0:H], lhsT=hT[:, kc, s * Pt:(s + 1) * Pt], rhs=fwin[:, kc, 2 * D:2 * D + H], start=(kc == 0), stop=(kc == 7))
                return ins
            P.op("pe", mfl, reads=[b_hT, b_w], writes=[bf_])
            P.op("dve", lambda e, s=s, pf=pf: e.tensor_tensor(out=lft[0:Pt, s, :], in0=pf[0:Pt, 0:H], in1=bfrow[0:Pt, :], op=ALU.add), reads=[bf_, b_const], writes=[b_lft])
        if dbg in (2, 21, 22, 23):
            return
        P.op("act", lambda e: e.activation(out=lft[0:Pt, 0:nsub, :], in_=lft[0:Pt, 0:nsub, :], func=AF.Exp, scale=-1.0), reads=[b_lft], writes=[b_lft])
        P.op("act", lambda e: e.activation(out=lft[0:Pt, 0:nsub, :], in_=lft[0:Pt, 0:nsub, :], func=AF.Ln, bias=onec[0:Pt, 0:1]), reads=[b_lft, b_const], writes=[b_lft])
        P.op("dve", lambda e: e.tensor_scalar(out=lft[0:Pt, 0:nsub, :], in0=lft[0:Pt, 0:nsub, :], scalar1=-1.0, scalar2=None, op0=ALU.mult), reads=[b_lft], writes=[b_lft])
        with nc.allow_non_contiguous_dma(reason="64B rows"):
            P.dma("sp", lambda e: e.dma_start(out=o_fl[tok0:tok0 + nsub * Pt, :].rearrange("(s p) h -> p s h", p=Pt), in_=lft[0:Pt, 0:nsub, :], allow_slow_non_contiguous=True), reads=[b_lft])
        if dbg == 3:
            return
        pf, bf_ = bank(1)

        def mflT(e, pf=pf):
            ins = None
            for kc in range(8):
                ins = e.matmul(pf[0:H, 0:N], lhsT=fwin[:, kc, 2 * D:2 * D + H], rhs=hT[:, kc, 0:N], start=(kc == 0), stop=(kc == 7))
            return ins
        P.op("pe", mflT, reads=[b_hT, b_w], writes=[bf_])
        P.op("act", lambda e, pf=pf: e.activation(out=lfT[:, 0:N], in_=pf[0:H, 0:N], func=AF.Exp, scale=-1.0, bias=bfcol[:, 0:1]), reads=[bf_, b_const], writes=[b_lfT])
        P.op("act", lambda e: e.activation(out=lfT[:, 0:N], in_=lfT[:, 0:N], func=AF.Ln, bias=onec[0:16, 0:1]), reads=[b_lfT, b_const], writes=[b_lfT])
        for sgm in range(nseg):
            P.op("dve", lambda e, sgm=sgm: e.tensor_tensor_scan(out=cumT[:, sgm * L:(sgm + 1) * L], data0=ones_f[0:16, 0:L], data1=lfT[:, sgm * L:(sgm + 1) * L], initial=ccar[:, sgm:sgm + 1], op0=ALU.mult, op1=ALU.subtract),
                 reads=[b_lfT, b_ccar, b_const, b_cumT], writes=[b_cumT])
            P.op("dve", lambda e, sgm=sgm: e.tensor_copy(ccar[:, sgm:sgm + 1], cumT[:, (sgm + 1) * L - 1:(sgm + 1) * L]), reads=[b_cumT, b_ccar], writes=[b_ccar])
        if cum_dst is not None:
            P.dma("sp", lambda e: e.dma_start(out=cum_dst, in_=cumT[:, 0:N]), reads=[b_cumT])
        if dbg == 4:
            return
        split3(cumT[:, 0:N], b_cumT, N, -1.0)
        if dbg == 5:
            return
        ktrans(N, nsub, Pt)
        if dbg == 6:
            return
        ckrows(N)
        if dbg == 7:
            return
        for sgm, kd in enumerate(kT_dsts if not (dbg == 9 and Pt == 64) else []):
            P.dma("sp", lambda e, sgm=sgm, kd=kd: e.dma_start(out=kd, in_=kTst[:, :, sgm * L:(sgm + 1) * L]), reads=[b_kTst])
        for (vd, vsrc) in (vv_dst if not (dbg == 8 and Pt == 64) else []):
            P.dma("sp", lambda e, vd=vd, vsrc=vsrc: e.dma_start(out=vd, in_=vsrc), reads=[b_v1st])

    m_a2 = A.mark()
    xt_a2s.append(A.alloc([128, 4, D], F32, "xt_a2b"))
    hT_a2 = [cx["hT"], A.alloc([128, 8, TT], BF16, "hT2b")]
    b_hT_a2 = [cx["b_hT"], P.buf("hT2b")]
    b_xt2 = [b_xt, P.buf("xt2b")]
    for i in range(nt_prompt):
        xt_a2, b_xt = xt_a2s[i % 2], b_xt2[i % 2]
        cx["hT"], cx["b_hT"] = hT_a2[i % 2], b_hT_a2[i % 2]
        P.dma("pool", lambda e, i=i, xt_a2=xt_a2: e.dma_start(out=xt_a2[:], in_=s_x1[i * TT:(i + 1) * TT, :].rearrange("(s p) f -> p s f", p=128)), reads=[b_sx1], writes=[b_xt])
        if dbg == 34 and i < NQ:
            P.dma("sp", lambda e, i=i, xt_a2=xt_a2: e.dma_start(out=o_yp[i * TT:(i + 1) * TT, :].rearrange("(s p) f -> p s f", p=128), in_=xt_a2[:]), reads=[b_xt])
        m_ = i // 4
        vvd = [(s_vv[:, m_, :, (i % 4) * 4 * VS:(i % 4 + 1) * 4 * VS].rearrange("h p x -> p h x"), v1st[:].rearrange("p h s d -> p h (s d)"))]
        kvproj(TT, 4, 128, 1, TT, xt_a2, b_xt, [(0, TT, 0)], o_fkp, o_fvp, o_flp, i * TT,
               [s_kT[:, :, i * TT:(i + 1) * TT].rearrange("h r n -> r h n")], vvd, s_cum[:, i * TT:(i + 1) * TT])
    cx["hT"], cx["b_hT"] = hT_a2[0], b_hT_a2[0]
    P.barrier()
    A.reset(m_a2)
    if do_sample:
        clf = A.alloc([128, 2, PAST // 128, H], F32, "clf")
        pcum = A.alloc([16, 2, PAST], F32, "pcum")
        b_clf, b_pcum = P.buf("clf"), P.buf("pcum")
        with nc.allow_non_contiguous_dma(reason="64B rows"):
            for sq in range(2):
                P.dma("sp", lambda e, sq=sq: e.dma_start(out=clf[:, sq, :, :], in_=i_clf[sq].rearrange("(j p) h -> p j h", p=128), allow_slow_non_contiguous=True), writes=[b_clf])
        for sq in range(2):
            for q8 in range(PAST // 512):
                pk, bk = bank(q8 % 2)

                def trl(e, sq=sq, q8=q8, pk=pk):
                    ins = None
                    for j in range(4):
                        ins = e.matmul(pk[0:16, j * 128:(j + 1) * 128], lhsT=clf[:, sq, q8 * 4 + j, :], rhs=identf[:, :], start=True, stop=True)
                    return ins
                P.op("pe", trl, reads=[b_clf, b_const], writes=[bk])
                P.op("act", lambda e, sq=sq, q8=q8, pk=pk: e.activation(out=pcum[:, sq, q8 * 512:(q8 + 1) * 512], in_=pk[0:16, 0:512], func=AF.Identity), reads=[bk], writes=[b_pcum])
            for q8 in range(PAST // 512):
                init = 0.0 if q8 == 0 else pcum[:, sq, q8 * 512 - 1:q8 * 512]
                P.op("dve", lambda e, sq=sq, q8=q8, init=init: e.tensor_tensor_scan(out=pcum[:, sq, q8 * 512:(q8 + 1) * 512], data0=ones_f[0:16, 0:512], data1=pcum[:, sq, q8 * 512:(q8 + 1) * 512], initial=init, op0=ALU.mult, op1=ALU.add),
                     reads=[b_pcum, b_const], writes=[b_pcum])
            P.op("dve", lambda e, sq=sq: e.tensor_copy(ccar[:, sq:sq + 1], pcum[:, sq, PAST - 1:PAST]), reads=[b_pcum, b_ccar], writes=[b_ccar])
        for sq in range(2 if dbg != 41 else 0):
            for q8 in range(PAST // 512):
                P.dma("sp", lambda e, sq=sq, q8=q8: e.dma_start(out=kst[:], in_=i_ck[sq, q8 * 512:(q8 + 1) * 512, :].rearrange("(s p) f -> p s f", p=128)), writes=[b_kst])
                P.dma("sp", lambda e, sq=sq, q8=q8: e.dma_start(out=vst[:], in_=i_cv[sq, q8 * 512:(q8 + 1) * 512, :].rearrange("(s p) f -> p s f", p=128)), writes=[b_vst])
                for s_ in range(4):
                    P.op("act", lambda e, s_=s_: e.activation(out=v1st[:, :, s_, 0:64], in_=vst[:, s_, :].rearrange("p (h d) -> p h d", h=H), func=AF.Identity), reads=[b_vst], writes=[b_v1st])
                P.dma("sp", lambda e, sq=sq, q8=q8: e.dma_start(out=s_vvs[sq, :, q8 // 4, :, (q8 % 4) * 4 * VS:(q8 % 4 + 1) * 4 * VS].rearrange("h p x -> p h x"), in_=v1st[:].rearrange("p h s d -> p h (s d)")), reads=[b_v1st])
                ktrans(512, 4, 128, cast=True)
                split3(pcum[:, sq, q8 * 512:(q8 + 1) * 512], b_pcum, 512, -1.0)
                ckrows(512)
                P.dma("sp", lambda e, sq=sq, q8=q8: e.dma_start(out=s_kTs[sq, :, :, q8 * 512:(q8 + 1) * 512].rearrange("h r n -> r h n"), in_=kTst[:]), reads=[b_kTst])
        vvd = [(s_vvs[sq, :, NPR, 0:ST, 0:VS].rearrange("h p x -> p h x"), v1st[sq * ST:(sq + 1) * ST, :, 0, :]) for sq in range(2)]
        if dbg not in (41, 42):
          kvproj(2 * ST, 1, 64, 2, ST, xs_t, b_xs, sc, o_fks, o_fvs, o_fls, 0,
               [s_kTs[sq, :, :, PAST:PAST + ST].rearrange("h r n -> r h n") for sq in range(2)], vvd, None)
        P.op("dve", lambda e: e.tensor_copy(scum[:, :], cumT[:, 0:2 * ST]), reads=[b_cumT], writes=[b_scum])
    P.barrier()
    A.reset(m_persist)
    if not do_attn:
        P.emit()
        return nc

    wq = A.alloc([128, 8, 2 * D], BF16, "wq")
    wo = A.alloc([64, H, D], BF16, "wo")
    pmask = A.alloc([128, 16, TT], BF16, "pmask")
    smask = A.alloc([ST, ST], BF16, "smask")
    b_w2 = P.buf("w2")
    for kc in range(8):
        P.dma("pool", lambda e, kc=kc: e.dma_start(out=wq[:, kc, 0:D], in_=i_fwin[kc * 128:(kc + 1) * 128, 0:D]), writes=[b_w2])
        P.dma("pool", lambda e, kc=kc: e.dma_start(out=wq[:, kc, D:2 * D], in_=i_fwin[kc * 128:(kc + 1) * 128, 3 * D:4 * D]), writes=[b_w2])
    for h in range(H):
        P.dma("pool", lambda e, h=h: e.dma_start(out=wo[:, h, :], in_=i_fwout[h * 64:(h + 1) * 64, :]), writes=[b_w2])
    P.dma("sp", lambda e: e.dma_start(out=pmask[:], in_=i_pmask[:, :, :]), writes=[b_w2])
    P.dma("sp", lambda e: e.dma_start(out=smask[:], in_=i_smask[:, :]), writes=[b_w2])
    xt_c = A.alloc([128, 4, D], F32, "xt2")
    cx["xn"] = A.alloc([128, D], F32, "xn")
    cx["junk"] = A.alloc([128, D], BF16, "junk")
    cx["stat"] = A.alloc([128, 16], F32, "stat")
    cx["hT"] = A.alloc([128, 8, TT], BF16, "hT")
    cx["spl"] = A.alloc([16, 3, TT], BF16, "spl")
    cx["spf"] = A.alloc([16, 2, TT], F32, "spf")
    nb("xn", "junk", "stat", "hT", "spl", "spf")
    cx["ytmp"] = cx["xn"]
    cx["b_ytmp"] = cx["b_xn"]
    xl = cx["xn"]
    qaug = A.alloc([KA, H, TT], BF16, "qaug")
    sgT = A.alloc([64, H, TT], BF16, "sgT")
    zT2 = sgT
    csel = A.alloc([16, 2, TT], F32, "csel")
    NKB = 2
    kch = [A.alloc([KA, 2048], BF16, "kch") for _ in range(NKB)]
    vch = [A.alloc([128, 16 * VS], BF16, "vch") for _ in range(NKB)]
    NSG = 4
    pT = [A.alloc([128, 512], BF16, "pT") for _ in range(NSG)]
    osb = A.alloc([65, TT], F32, "osb")
    rl = A.alloc([65, TT], F32, "rl")
    b_xt, b_xl, b_qaug, b_sgT, b_zT2, b_csel, b_osb, b_rl = [P.buf(n) for n in range(8)]
    b_xl = cx["b_xn"]
    b_zT2 = b_sgT
    b_kch = [P.buf("kch") for _ in range(NKB)]
    b_vch = [P.buf("vch") for _ in range(NKB)]
    b_pT = [P.buf("pT") for _ in range(NSG)]
    P.op("pool", lambda e: e.memset(qaug[:], 1.0), writes=[b_qaug])
    chunk_ctr = [0]
    grp_ctr = [0]

    def attend(N, nsub, Pt, seqs, x_res, b_xres, layer_grp, o_y, y_tok0):
        flat = []
        item_ctr = [0]
        pending = []
        EPI_DELAY = 3
        for h in range(H):
            for sq in seqs:
                q0, q1 = sq["q0"], sq["q1"]
                nq = q1 - q0
                work = [(k_ap, v_ap, 128, 16, None) for (k_ap, v_ap) in sq["rows"]] + list(sq["diag"])
                first = True
                for (k_src, v_src, nk, ntile, mask_fn) in work:
                    item = dict(k_src=k_src, v_src=v_src, nk=nk, ntile=ntile, cb=None, idx=item_ctr[0])
                    item_ctr[0] += 1
                    G = min(ntile, 512 // nq if nq > 256 else 16)
                    for g0 in range(0, ntile, G):
                        flat.append(dict(h=h, q0=q0, nq=nq, item=item, g0=g0, G=G, nk=nk, mask_fn=mask_fn, first=first, last_of_head=False))
                        first = False
            flat[-1]["last_of_head"] = True

        def emit_S(en):
            it = en["item"]
            h, nk, ntile = en["h"], en["nk"], it["ntile"]
            if it["cb"] is None:
                cb_ = chunk_ctr[0] % NKB
                chunk_ctr[0] += 1
                it["cb"] = cb_
                P.dma("sp", lambda e, cb_=cb_, k_src=it["k_src"], nk=nk, ntile=ntile, h=h: e.dma_start(out=kch[cb_][:, 0:nk * ntile], in_=k_src[h]), writes=[b_kch[cb_]])
                P.dma("sp", lambda e, cb_=cb_, v_src=it["v_src"], nk=nk, ntile=ntile, h=h: e.dma_start(out=vch[cb_][0:nk, 0:ntile * VS], in_=v_src[h]), writes=[b_vch[cb_]])
            cb_ = it["cb"]
            gi = grp_ctr[0] % NSG
            grp_ctr[0] += 1
            en["gi"] = gi
            psg, bpsg = bank(gi)

            def mmS(e, cb_=cb_, g0=en["g0"], psg=psg, h=h, q0=en["q0"], nq=en["nq"], G=en["G"], nk=nk, mask_fn=en["mask_fn"]):
                ins = None
                for j in range(G):
                    kt = g0 + j
                    ins = e.matmul(psg[0:nk, j * nq:(j + 1) * nq], lhsT=kch[cb_][:, kt * nk:(kt + 1) * nk], rhs=qaug[:, h, q0:q0 + nq], start=True, stop=(mask_fn is None))
                    if mask_fn is not None:
                        ins = e.matmul(psg[0:nk, j * nq:(j + 1) * nq], lhsT=ident_bf[0:nk, 0:nk], rhs=mask_fn(kt), start=False, stop=True)
                return ins
            P.op("pe", mmS, reads=[b_kch[cb_], b_qaug, b_w2, b_const], writes=[bpsg])

        def emit_EV(en):
            h, nk, nq, G, gi, q0 = en["h"], en["nk"], en["nq"], en["G"], en["gi"], en["q0"]
            cb_ = en["item"]["cb"]
            psg, bpsg = bank(gi)
            po, bo = bank(6 + (h % 2))
            P.op("act", lambda e, psg=psg, gi=gi, G=G, nq=nq, nk=nk: e.activation(out=pT[gi][0:nk, 0:G * nq], in_=psg[0:nk, 0:G * nq], func=AF.Exp), reads=[bpsg], writes=[b_pT[gi]])

            def mmV(e, cb_=cb_, g0=en["g0"], gi=gi, po=po, q0=q0, nq=nq, G=G, nk=nk, fst=en["first"]):
                ins = None
                for j in range(G):
                    kt = g0 + j
                    ins = e.matmul(po[0:65, q0:q0 + nq], lhsT=vch[cb_][0:nk, kt * VS:kt * VS + 65], rhs=pT[gi][0:nk, j * nq:(j + 1) * nq], start=(fst and j == 0), stop=False, skip_group_check=True)
                return ins
            P.op("pe", mmV, reads=[b_vch[cb_], b_pT[gi]], writes=[bo])
            if en["last_of_head"]:
                P.op("act", lambda e, po=po: e.activation(out=osb[0:65, 0:N], in_=po[0:65, 0:N], func=AF.Identity), reads=[bo], writes=[b_osb])
                P.op("dve", lambda e: e.reciprocal(rl[64:65, 0:N], osb[64:65, 0:N]), reads=[b_osb], writes=[b_rl])
                pending.append([EPI_DELAY, h])

        def emit_epi_tail(h):
            if True:
                pbq, bbq = bank(5)
                P.op("pe", lambda e, pbq=pbq: e.matmul(pbq[0:64, 0:N], lhsT=ones_f[64:65, 0:64], rhs=rl[64:65, 0:N], start=True, stop=True), reads=[b_rl, b_const], writes=[bbq])
                P.op("dve", lambda e, pbq=pbq: e.tensor_tensor(out=osb[0:64, 0:N], in0=osb[0:64, 0:N], in1=pbq[0:64, 0:N], op=ALU.mult), reads=[b_osb, bbq], writes=[b_osb])
                P.op("pool", lambda e, h=h: e.tensor_tensor(out=zT2[:, h, 0:N], in0=osb[0:64, 0:N], in1=sgT[:, h, 0:N], op=ALU.mult), reads=[b_osb, b_sgT], writes=[b_zT2])

        nxt = 0
        for i, en in enumerate(flat):
            while nxt < len(flat) and nxt <= i + NSG - 1 and flat[nxt]["item"]["idx"] <= en["item"]["idx"] + NKB - 1:
                emit_S(flat[nxt])
                nxt += 1
            for p_ in pending:
                p_[0] -= 1
            while pending and pending[0][0] <= 0:
                emit_epi_tail(pending.pop(0)[1])
            emit_EV(en)
        while pending:
            emit_epi_tail(pending.pop(0)[1])
        post(N, nsub, Pt, 1, layer_grp, wo, b_w2, H, zT2, b_zT2, 64, x_res, b_xres)
        P.dma("sp", lambda e: e.dma_start(out=o_y[y_tok0:y_tok0 + nsub * Pt, :].rearrange("(s p) f -> p s f", p=Pt), in_=x_res[0:Pt, 0:nsub, :]), reads=[b_xres])

    def qproj(N, seqcols, xsrc, b_x, nsub, Pt, cq_src, b_cq):
        front(N, nsub, Pt, 1, seqcols, xsrc, b_x)
        hT, b_hT = cx["hT"], cx["b_hT"]
        spl, b_spl = cx["spl"], cx["b_spl"]
        for h in range(H):
            pq, bq = bank(6)
            pg, bg = bank(7)

            def mq(e, h=h, pq=pq, pg=pg):
                ins = None
                for kc in range(8):
                    ins = e.matmul(pq[0:64, 0:N], lhsT=wq[:, kc, h * 64:(h + 1) * 64], rhs=hT[:, kc, 0:N], start=(kc == 0), stop=(kc == 7))
                for kc in range(8):
                    ins = e.matmul(pg[0:64, 0:N], lhsT=wq[:, kc, D + h * 64:D + (h + 1) * 64], rhs=hT[:, kc, 0:N], start=(kc == 0), stop=(kc == 7))
                return ins
            P.op("pe", mq, reads=[b_hT, b_w2], writes=[bq, bg])
            P.op("dve", lambda e, h=h, pq=pq: e.tensor_scalar(out=qaug[0:64, h, 0:N], in0=pq[0:64, 0:N], scalar1=0.125, scalar2=None, op0=ALU.mult), reads=[bq], writes=[b_qaug])
            P.op("act", lambda e, h=h, pg=pg: e.activation(out=sgT[:, h, 0:N], in_=pg[0:64, 0:N], func=AF.Silu), reads=[bg], writes=[b_sgT])
        split3(cq_src, b_cq, N, 1.0)
        for j in range(3):
            P.dma("sp", lambda e, j=j: e.dma_start(out=qaug[64 + j:65 + j, :, 0:N], in_=spl[:, j, 0:N]), reads=[b_spl, b_qaug], writes=[b_qaug])

    if do_sample:
        qproj(2 * ST, sc, xs_t, b_xs, 1, 64, scum[:, :], b_scum)
        seqs = []
        for sq in range(2):
            rows = [(s_kTs[sq, :, :, rw * 2048:(rw + 1) * 2048], s_vvs[sq, :, rw, :, :]) for rw in range(NPR)]
            diag = [(s_kTs[sq, :, :, PAST:PAST + ST], s_vvs[sq, :, NPR, 0:ST, 0:VS], ST, 1, (lambda kt: smask[:, :]))]
            seqs.append(dict(q0=sq * ST, q1=(sq + 1) * ST, rows=rows, diag=diag))
        attend(2 * ST, 1, 64, seqs, xs_t, b_xs, 1, o_ys, 0)
    for m in range(nq_tiles):
        for r_ in range(4):
            i = 4 * m + r_
            P.dma("sp", lambda e, i=i: e.dma_start(out=csel[:, 1, :], in_=s_cum[:, i * TT:(i + 1) * TT]), writes=[b_csel])
            if r_ == 0:
                P.op("dve", lambda e: e.tensor_scalar(out=csel[:, 0, :], in0=csel[:, 1, :], scalar1=onehot[0:16, 0:1], scalar2=None, op0=ALU.mult), reads=[b_csel, b_const], writes=[b_csel])
            else:
                P.op("dve", lambda e, r_=r_: e.scalar_tensor_tensor(out=csel[:, 0, :], in0=csel[:, 1, :], scalar=onehot[0:16, r_:r_ + 1], in1=csel[:, 0, :], op0=ALU.mult, op1=ALU.add), reads=[b_csel, b_const], writes=[b_csel])
            for s4 in range(4):
                P.dma("act", lambda e, i=i, s4=s4: e.dma_start(out=xl[:, :], in_=s_x1[i * TT + s4 * 128:i * TT + (s4 + 1) * 128, :]), writes=[b_xl])
                if r_ == 0:
                    P.op("dve", lambda e, s4=s4: e.tensor_scalar(out=xt_c[:, s4, :], in0=xl[:, :], scalar1=onehot[:, 0:1], scalar2=None, op0=ALU.mult), reads=[b_xl, b_const], writes=[b_xt])
                else:
                    P.op("dve", lambda e, r_=r_, s4=s4: e.scalar_tensor_tensor(out=xt_c[:, s4, :], in0=xl[:, :], scalar=onehot[:, r_:r_ + 1], in1=xt_c[:, s4, :], op0=ALU.mult, op1=ALU.add), reads=[b_xl, b_const, b_xt], writes=[b_xt])
        qproj(TT, [(0, TT, 0)], xt_c, b_xt, 4, 128, csel[:, 0, :], b_csel)
        rows = [(s_kT[:, :, mm_ * 2048:(mm_ + 1) * 2048], s_vv[:, mm_, :, :]) for mm_ in range(m)]
        diag = [(s_kT[:, :, m * 2048:(m + 1) * 2048], s_vv[:, m, :, :], 128, 16, (lambda kt: pmask[:, kt, :]))]
        attend(TT, 4, 128, [dict(q0=0, q1=TT, rows=rows, diag=diag)], xt_c, b_xt, 0, o_yp, m * TT)
    P.emit()
    return nc


_CACHE = {}


def _get_nc(**kw):
    key = tuple(sorted(kw.items()))
    if key not in _CACHE:
        _CACHE[key] = build(**kw)
    return _CACHE[key]


def _consts(core):
    r = core % 4
    ident = np.eye(128, dtype=np.float32)
    sel = np.zeros((3, 2, 128), np.float32)
    sel[0, 0, :] = 1.0
    sel[1, 1, 0:ST] = 1.0
    sel[2, 1, ST:2 * ST] = 1.0
    kpos = (np.arange(16)[None, :, None] * 128 + np.arange(128)[:, None, None])
    qpos = r * TT + np.arange(TT)[None, None, :]
    pmask = np.where(kpos > qpos, NEG, 0.0).astype(ml_dtypes.bfloat16)
    smask = np.where(np.arange(ST)[:, None] > np.arange(ST)[None, :], NEG, 0.0).astype(ml_dtypes.bfloat16)
    onehot = np.zeros((128, 4), np.float32)
    onehot[:, r] = 1.0
    return dict(c_ident=ident, c_sel=sel, c_pmask=np.ascontiguousarray(pmask), c_smask=smask, c_onehot=onehot)


def _in_map(c, I, shared, T, PAST):
    f = lambda a: np.ascontiguousarray(np.asarray(a, dtype=np.float32))
    b = c // 4
    s0 = 2 * c
    m = dict(shared)
    m.update(_consts(c))
    m["xp"] = f(I["x_prompt"][b])
    m["xs"] = f(I["x_sample"][s0:s0 + 2]).reshape(2 * ST, D)
    m["cc"] = np.concatenate([f(I["c_prompt"])[b:b + 1], f(I["c_sample"])[s0:s0 + 2]], axis=0)
    m["st_h"] = f(I["state_lru_h"][0, s0:s0 + 2])
    m["st_conv"] = f(I["state_lru_conv"][0, s0:s0 + 2])
    m["ck_k"] = f(I["cache_fox_k"][0, s0:s0 + 2]).reshape(2, PAST, D)
    m["ck_v"] = f(I["cache_fox_v"][0, s0:s0 + 2]).reshape(2, PAST, D)
    m["ck_lf"] = f(I["cache_fox_logf"][0, s0:s0 + 2])
    return m


def _shared(I):
    f = lambda a: np.ascontiguousarray(np.asarray(a, dtype=np.float32))
    lvec = np.concatenate([f(I["lru_conv_w"])[0], f(I["lru_conv_b"]), f(I["lru_b_a"]), f(I["lru_b_x"]), f(I["lru_lambda"])], axis=0)
    return dict(norm_pre=f(I["norm_pre"]), norm_post=f(I["norm_post"]), ada_w=f(I["ada_w"]), ada_b=f(I["ada_b"]), lru_w_in=f(I["lru_w_in"])[0],
                lru_vecs=f(lvec), lru_w_a=f(I["lru_w_a"])[0], lru_w_x=f(I["lru_w_x"])[0], lru_w_out=f(I["lru_w_out"])[0],
                fox_w_in=f(I["fox_w_in"])[0], fox_b_f=f(I["fox_b_f"]), fox_w_out=f(I["fox_w_out"])[0])


def run_cores(I, cores, build_kw):
    T = I["x_prompt"].shape[1]
    PAST = I["cache_fox_k"].shape[2]
    kw = dict(build_kw)
    kw.update(Tn=T, PASTn=PAST)
    nc = _get_nc(**kw)
    shared = _shared(I)
    in_maps = [_in_map(c, I, shared, T, PAST) for c in cores]
    res = run_bass_kernel_spmd(nc, in_maps, core_ids=list(range(len(cores)))).results
    return {c: res[i] for i, c in enumerate(cores)}


def kernel(x_prompt, x_sample, c_prompt, c_sample, state_lru_h, state_lru_conv, cache_fox_k, cache_fox_v, cache_fox_logf,
           norm_pre, norm_post, ada_w, ada_b, lru_w_in, lru_conv_w, lru_conv_b, lru_w_a, lru_b_a, lru_w_x, lru_b_x,
           lru_lambda, lru_w_out, fox_w_in, fox_b_f, fox_w_out, _build_kw=None):
    I = dict(x_prompt=x_prompt, x_sample=x_sample, c_prompt=c_prompt, c_sample=c_sample, state_lru_h=state_lru_h, state_lru_conv=state_lru_conv,
             cache_fox_k=cache_fox_k, cache_fox_v=cache_fox_v, cache_fox_logf=cache_fox_logf, norm_pre=norm_pre, norm_post=norm_post,
             ada_w=ada_w, ada_b=ada_b, lru_w_in=lru_w_in, lru_conv_w=lru_conv_w, lru_conv_b=lru_conv_b, lru_w_a=lru_w_a, lru_b_a=lru_b_a,
             lru_w_x=lru_w_x, lru_b_x=lru_b_x, lru_lambda=lru_lambda, lru_w_out=lru_w_out, fox_w_in=fox_w_in, fox_b_f=fox_b_f, fox_w_out=fox_w_out)
    I = {k: np.asarray(v) for k, v in I.items()}
    res = run_cores(I, list(range(8)), _build_kw if _build_kw is not None else {})
    return assemble(res, I)


def assemble(res, I):
    B, T = I["x_prompt"].shape[0], I["x_prompt"].shape[1]
    NQ = T // 2048
    cores = sorted(res.keys())
    y_p = np.zeros((B, T, D), np.float32)
    for c in cores:
        b, r = c // 4, c % 4
        yp = res[c]["y_p"].reshape(NQ, TT, D)
        for m_ in range(NQ):
            i = 4 * m_ + r
            y_p[b, i * TT:(i + 1) * TT] = yp[m_]
    nb = len(cores) // 4 if len(cores) >= 4 else 1
    bs = sorted(set(c // 4 for c in cores))
    first = {b: min(c for c in cores if c // 4 == b) for b in bs}
    y_s = np.concatenate([res[c]["y_s"].reshape(2, ST, D) for c in cores], axis=0)
    lru_h_p = np.stack([res[first[b]]["lru_h_p"] for b in bs])[None]
    lru_c_p = np.stack([res[first[b]]["lru_conv_p"] for b in bs])[None]
    fk_p = np.stack([res[first[b]]["fk_p"].reshape(T, H, DH) for b in bs])[None]
    fv_p = np.stack([res[first[b]]["fv_p"].reshape(T, H, DH) for b in bs])[None]
    fl_p = np.stack([res[first[b]]["flf_p"] for b in bs])[None]
    lru_h_s = np.concatenate([res[c]["lru_h_s"] for c in cores], axis=0)[None]
    lru_c_s = np.concatenate([res[c]["lru_conv_s"] for c in cores], axis=0)[None]
    fk_s = np.concatenate([res[c]["fk_s"].reshape(2, ST, H, DH) for c in cores], axis=0)[None]
    fv_s = np.concatenate([res[c]["fv_s"].reshape(2, ST, H, DH) for c in cores], axis=0)[None]
    fl_s = np.concatenate([res[c]["flf_s"].reshape(2, ST, H) for c in cores], axis=0)[None]
    return (y_p, y_s, lru_h_p, lru_c_p, fk_p, fv_p, fl_p, lru_h_s, lru_c_s, fk_s, fv_s, fl_s)
```

```python
import contextlib
import numpy as np
import ml_dtypes
import concourse.bass as bass
import concourse.mybir as mybir
from concourse.bass_utils import run_bass_kernel_spmd

F32 = mybir.dt.float32
BF16 = mybir.dt.bfloat16
AF = mybir.ActivationFunctionType
ALU = mybir.AluOpType

D = 1024
R = 1536
NCH = 12
H = 16
DH = 64
T = 16384
TT = 512
NT = T // TT
ST = 32
PAST = 4096
EPS = 1e-6
NEG = -30000.0
VS = 66
KA = 70
COMPUTE = ("pe", "act", "dve", "pool", "sp")


class Buf:
    __slots__ = ("name", "last_w", "readers")

    def __init__(self, name):
        self.name = name
        self.last_w = None
        self.readers = []


class Op:
    __slots__ = ("eng", "fn", "deps", "is_dma", "idx", "need_inc", "val", "sem", "semval", "prev_on_sem", "inc")

    def __init__(self, eng, fn, is_dma, inc=16):
        self.eng = eng
        self.fn = fn
        self.deps = set()
        self.is_dma = is_dma
        self.need_inc = False
        self.val = None
        self.sem = None
        self.semval = None
        self.prev_on_sem = None
        self.inc = inc


class Prog:
    def __init__(self, nc, n_dma_sems=(("sp", 32), ("act", 12), ("pool", 12))):
        self.nc = nc
        self.ops = []
        self.by_eng = {e: [] for e in COMPUTE}
        self.dma_ring = {e: n for e, n in n_dma_sems}
        self.dma_count = {e: 0 for e, _ in n_dma_sems}
        self.dma_last_on_slot = {}
        self.all_bufs = []

    def buf(self, name=""):
        b = Buf(name)
        self.all_bufs.append(b)
        return b

    def _add(self, op, reads, writes):
        op.idx = len(self.ops)
        for b in reads:
            if b.last_w is not None:
                op.deps.add(b.last_w)
        for b in writes:
            if b.last_w is not None:
                op.deps.add(b.last_w)
            for r in b.readers:
                op.deps.add(r)
        for b in reads:
            b.readers.append(op)
        for b in writes:
            b.last_w = op
            b.readers = []
        op.deps.discard(op)
        self.ops.append(op)
        self.by_eng[op.eng].append(op)
        return op

    def op(self, eng, fn, reads=(), writes=()):
        return self._add(Op(eng, fn, False), reads, writes)

    def dma(self, eng, fn, reads=(), writes=(), inc=16):
        op = Op(eng, fn, True, inc)
        n = self.dma_count[eng]
        self.dma_count[eng] = n + 1
        slot = (eng, n % self.dma_ring[eng])
        op.sem = slot
        op.prev_on_sem = self.dma_last_on_slot.get(slot)
        op.semval = (op.prev_on_sem.semval if op.prev_on_sem else 0) + inc
        self.dma_last_on_slot[slot] = op
        return self._add(op, reads, writes)

    def barrier(self):
        lasts = [self.by_eng[e][-1] for e in COMPUTE if self.by_eng[e]]
        lasts += list(self.dma_last_on_slot.values())
        for e in COMPUTE:
            o = Op(e, None, False)
            o.idx = len(self.ops)
            o.deps = set(lasts)
            self.ops.append(o)
            self.by_eng[e].append(o)
        for b in self.all_bufs:
            if str(b.name).startswith("dram_"):
                continue
            b.last_w = None
            b.readers = []

    def emit(self, final_wait_eng="sp"):
        nc = self.nc
        lasts = [self.by_eng[e][-1] for e in COMPUTE if self.by_eng[e]]
        lasts += list(self.dma_last_on_slot.values())
        fin = Op(final_wait_eng, None, False)
        fin.idx = len(self.ops)
        fin.deps = set(lasts)
        self.ops.append(fin)
        self.by_eng[final_wait_eng].append(fin)
        for o in self.ops:
            for d in o.deps:
                if not d.is_dma:
                    if d.eng == o.eng and o.eng == "pe":
                        continue
                    d.need_inc = True
        for e in COMPUTE:
            c = 0
            for o in self.by_eng[e]:
                if not o.is_dma and o.need_inc:
                    c += 1
                    o.val = c
        with contextlib.ExitStack() as st:
            esem = {e: st.enter_context(nc.semaphore("s_" + e)) for e in COMPUTE}
            dsem = {}
            for e, n in self.dma_ring.items():
                for i in range(n):
                    dsem[(e, i)] = st.enter_context(nc.semaphore("d_%s_%d" % (e, i)))
            block = st.enter_context(nc.Block())
            engobj = {"pe": nc.tensor, "act": nc.scalar, "dve": nc.vector, "pool": nc.gpsimd, "sp": nc.sync}

            def run_engine(e):
                eng = engobj[e]
                waited = {}

                def wait(key, sem, val):
                    if waited.get(key, 0) >= val:
                        return
                    waited[key] = val
                    eng.wait_ge(sem, val)

                for o in self.by_eng[e]:
                    for d in sorted(o.deps, key=lambda d: d.idx):
                        if d.is_dma:
                            wait(d.sem, dsem[d.sem], d.semval)
                        else:
                            if d.eng == e and e == "pe":
                                continue
                            if d.val is None:
                                continue
                            wait(d.eng, esem[d.eng], d.val)
                    if o.is_dma and o.prev_on_sem is not None:
                        wait(o.sem, dsem[o.sem], o.prev_on_sem.semval)
                    if o.fn is None:
                        if o.need_inc:
                            eng.nop().then_inc(esem[e], 1)
                        continue
                    ins = o.fn(eng)
                    if o.is_dma:
                        ins.then_inc(dsem[o.sem], o.inc)
                    elif o.need_inc:
                        ins.then_inc(esem[e], 1)

            @block.tensor
            def _(x):
                run_engine("pe")

            @block.scalar
            def _(x):
                run_engine("act")

            @block.vector
            def _(x):
                run_engine("dve")

            @block.gpsimd
            def _(x):
                run_engine("pool")

            @block.sync
            def _(x):
                run_engine("sp")


class Arena:
    def __init__(self, nc, base=16640, limit=229376 - 2048):
        self.nc = nc
        self.ptr = base
        self.limit = limit
        self.n = 0

    def alloc(self, shape, dtype, name="t"):
        size = int(np.prod(shape[1:])) * (4 if dtype == F32 else 2)
        size = (size + 63) // 64 * 64
        off = self.ptr
        self.ptr += size
        assert self.ptr <= self.limit, ("SBUF overflow", name, self.ptr)
        self.n += 1
        return self.nc.alloc_sbuf_tensor_at("%s_%d" % (name, self.n), list(shape), dtype, offset=off)

    def mark(self):
        return self.ptr

    def reset(self, m):
        self.ptr = m


def build(nq_tiles=None, do_attn=True, do_sample=True, nt_prompt=None, stop=9, Tn=16384, PASTn=4096, dbg=0):
    T, PAST = Tn, PASTn
    NPR = PAST // 2048
    NQ = T // 2048
    if nq_tiles is None:
        nq_tiles = NQ
    if nt_prompt is None:
        nt_prompt = T // TT
    nc = bass.Bass("TRN2", target_bir_lowering=False)
    P = Prog(nc)
    A = Arena(nc)

    def din(name, shape, dt=F32):
        return nc.dram_tensor(name, list(shape), dt, kind="ExternalInput").ap()

    def dout(name, shape, dt=F32):
        return nc.dram_tensor(name, list(shape), dt, kind="ExternalOutput").ap()

    def dscr(name, shape, dt=F32):
        return nc.dram_tensor(name, list(shape), dt).ap()

    i_xp = din("xp", [T, D])
    i_xs = din("xs", [2 * ST, D])
    i_cc = din("cc", [3, D])
    i_sth = din("st_h", [2, R])
    i_stc = din("st_conv", [2, 3, R])
    i_ck = din("ck_k", [2, PAST, D])
    i_cv = din("ck_v", [2, PAST, D])
    i_clf = din("ck_lf", [2, PAST, H])
    i_npre = din("norm_pre", [2, D])
    i_npost = din("norm_post", [2, D])
    i_adaw = din("ada_w", [2, D, 3 * D])
    i_adab = din("ada_b", [2, 3 * D])
    i_win = din("lru_w_in", [D, 2 * R])
    i_lvec = din("lru_vecs", [8, R])
    i_wa = din("lru_w_a", [NCH, 128, 128])
    i_wx = din("lru_w_x", [NCH, 128, 128])
    i_wout = din("lru_w_out", [R, D])
    i_fwin = din("fox_w_in", [D, 4 * D + H])
    i_fbf = din("fox_b_f", [1, H])
    i_fwout = din("fox_w_out", [D, D])
    i_ident = din("c_ident", [128, 128])
    i_sel = din("c_sel", [3, 2, 128])
    i_pmask = din("c_pmask", [128, 16, TT], BF16)
    i_smask = din("c_smask", [ST, ST], BF16)
    i_onehot = din("c_onehot", [128, 4])
    o_yp = dout("y_p", [NQ * TT, D])
    o_ys = dout("y_s", [2 * ST, D])
    o_lhp = dout("lru_h_p", [R])
    o_lcp = dout("lru_conv_p", [3, R])
    o_fkp = dout("fk_p", [T, D])
    o_fvp = dout("fv_p", [T, D])
    o_flp = dout("flf_p", [T, H])
    o_lhs = dout("lru_h_s", [2, R])
    o_lcs = dout("lru_conv_s", [2, 3, R])
    o_fks = dout("fk_s", [2 * ST, D])
    o_fvs = dout("fv_s", [2 * ST, D])
    o_fls = dout("flf_s", [2 * ST, H])
    s_x1 = dscr("s_x1", [T, D])
    s_kT = dscr("s_kT", [H, KA, T], BF16)
    s_vv = dscr("s_vv", [H, T // 2048, 128, 16 * VS], BF16)
    s_cum = dscr("s_cum", [H, T])
    NK_S = PAST + 128
    s_kTs = dscr("s_kTs", [2, H, KA, NK_S], BF16)
    s_vvs = dscr("s_vvs", [2, H, NPR + 1, 128, 16 * VS], BF16)

    b_sx1, b_skT, b_svv, b_scm, b_skTs, b_svvs = [P.buf("dram_%d" % i) for i in range(6)]
    PS = [nc.alloc_psum_tensor("ps%d" % i, [128, 1024], F32) for i in range(4)]
    bPS = [[P.buf("ps%d_%d" % (i, j)) for j in range(2)] for i in range(4)]

    def bank(i):
        return PS[i // 2][:, (i % 2) * 512:(i % 2) * 512 + 512], bPS[i // 2][i % 2]

    identf = A.alloc([128, 128], F32, "ident")
    b_const = P.buf("const")
    P.dma("sp", lambda e: e.dma_start(out=identf[:], in_=i_ident[:, :]), writes=[b_const])
    sel = A.alloc([3, 2, 128], F32, "sel")
    P.dma("sp", lambda e: e.dma_start(out=sel[:], in_=i_sel[:, :, :]), writes=[b_const])
    onehot = A.alloc([128, 4], F32, "onehot")
    P.dma("sp", lambda e: e.dma_start(out=onehot[:], in_=i_onehot[:, :]), writes=[b_const])
    ones_bf = A.alloc([128, 512], BF16, "ones")
    ones_f = A.alloc([128, 512], F32, "onesf")
    P.op("pool", lambda e: e.memset(ones_bf[:], 1.0), writes=[b_const])
    P.op("pool", lambda e: e.memset(ones_f[:], 1.0), writes=[b_const])
    ident_bf = A.alloc([128, 128], BF16, "identb")
    P.op("dve", lambda e: e.tensor_copy(ident_bf[:], identf[:]), reads=[b_const], writes=[b_const])
    gcol = A.alloc([128, 2, 2, 8, 3], F32, "gcol")
    gprow = A.alloc([128, 2, 2, D], F32, "gprow")
    lcol = A.alloc([128, NCH, 10], F32, "lcol")
    bfcol = A.alloc([16, 2], F32, "bfcol")
    bfrow = A.alloc([128, H], F32, "bfrow")
    b_ada = P.buf("ada")
    epsc = A.alloc([128, 1], F32, "epsc")
    onec = A.alloc([128, 1], F32, "onec")
    P.op("pool", lambda e: e.memset(epsc[:], EPS), writes=[b_const])
    P.op("pool", lambda e: e.memset(onec[:], 1.0), writes=[b_const])
    xs_t = A.alloc([128, 1, D], F32, "xs_t")
    b_xs = P.buf("xs")
    scum = A.alloc([16, 2 * ST], F32, "scum")
    b_scum = P.buf("scum")

    cx = {}

    def nb(*names):
        for n in names:
            cx["b_" + n] = P.buf(n)

    def front(N, nsub, Pt, layer, seqcols, xsrc, b_x):
        xn, junk, stat, hT = cx["xn"], cx["junk"], cx["stat"], cx["hT"]
        b_xn, b_junk, b_stat, b_hT = cx["b_xn"], cx["b_junk"], cx["b_stat"], cx["b_hT"]
        bxs = b_x if isinstance(b_x, list) else [b_x] * nsub
        for half in range(1):
            for s in range(nsub):
                b_x = bxs[s]
                if half == 0:
                    P.op("dve", lambda e, s=s: e.scalar_tensor_tensor(out=junk[0:Pt, :], in0=xsrc[0:Pt, s, :], scalar=1.0, in1=xsrc[0:Pt, s, :], op0=ALU.mult, op1=ALU.mult, accum_out=stat[0:Pt, s:s + 1]),
                         reads=[b_x], writes=[b_junk, b_stat])
                    P.op("act", lambda e, s=s: e.activation(out=stat[0:Pt, 4 + s:5 + s], in_=stat[0:Pt, s:s + 1], func=AF.Sqrt, scale=1.0 / D, bias=epsc[0:Pt, 0:1]), reads=[b_stat, b_const], writes=[b_stat])
                    P.op("dve", lambda e, s=s: e.reciprocal(stat[0:Pt, 8 + s:9 + s], stat[0:Pt, 4 + s:5 + s]), reads=[b_stat], writes=[b_stat])
                P.op("dve", lambda e, s=s: e.tensor_scalar(out=xn[0:Pt, :], in0=xsrc[0:Pt, s, :], scalar1=stat[0:Pt, 8 + s:9 + s], scalar2=None, op0=ALU.mult), reads=[b_x, b_stat], writes=[b_xn])

                def trs(e, s=s, half=half):
                    ins = None
                    for j in range(8):
                        kc = j
                        ins = e.transpose(out=bank(j)[0][:, s * Pt:(s + 1) * Pt], in_=xn[0:Pt, kc * 128:(kc + 1) * 128], identity=identf[0:Pt, 0:Pt])
                    return ins
                P.op("pe", trs, reads=[b_xn, b_const], writes=[bank(j)[1] for j in range(8)])
            for j in range(8):
                kc = j
                for (c0, c1, sq) in seqcols:
                    P.op("act", lambda e, j=j, kc=kc, c0=c0, c1=c1, sq=sq: e.activation(out=hT[:, kc, c0:c1], in_=bank(j)[0][:, c0:c1], func=AF.Identity, scale=gcol[:, layer, 0, kc, sq:sq + 1], bias=gcol[:, layer, 1, kc, sq:sq + 1]),
                         reads=[bank(j)[1], b_ada], writes=[b_hT])

    def post(N, nsub, Pt, layer, grp, wsb, b_wsb, nkc, srcT, b_srcT, ksz, xres, b_xres):
        junk, stat, ytmp = cx["junk"], cx["stat"], cx["ytmp"]
        b_junk, b_stat, b_ytmp = cx["b_junk"], cx["b_stat"], cx["b_ytmp"]
        bxr = b_xres if isinstance(b_xres, list) else [b_xres] * nsub
        for s in range(nsub):
            b_xres = bxr[s]
            pp = PS[2 + (s % 2)]
            bpp = bPS[2 + (s % 2)]

            def mm(e, s=s, pp=pp):
                ins = None
                for hf in range(2):
                    for c in range(nkc):
                        ins = e.matmul(pp[0:Pt, hf * 512:(hf + 1) * 512], lhsT=srcT[0:ksz, c, s * Pt:(s + 1) * Pt], rhs=wsb[0:ksz, c, hf * 512:(hf + 1) * 512], start=(c == 0), stop=(c == nkc - 1))
                return ins
            P.op("pe", mm, reads=[b_srcT, b_wsb], writes=bpp)
            P.op("act", lambda e, pp=pp: e.activation(out=junk[0:Pt, :], in_=pp[0:Pt, :], func=AF.Square, accum_out=stat[0:Pt, 12:13]), reads=bpp, writes=[b_junk, b_stat])
            P.op("act", lambda e: e.activation(out=stat[0:Pt, 13:14], in_=stat[0:Pt, 12:13], func=AF.Sqrt, scale=1.0 / D, bias=epsc[0:Pt, 0:1]), reads=[b_stat, b_const], writes=[b_stat])
            P.op("dve", lambda e: e.reciprocal(stat[0:Pt, 14:15], stat[0:Pt, 13:14]), reads=[b_stat], writes=[b_stat])
            P.op("dve", lambda e, pp=pp: e.scalar_tensor_tensor(out=ytmp[0:Pt, :], in0=pp[0:Pt, :], scalar=stat[0:Pt, 14:15], in1=gprow[0:Pt, layer, grp, :], op0=ALU.mult, op1=ALU.mult), reads=bpp + [b_stat, b_ada], writes=[b_ytmp])
            P.op("dve", lambda e, s=s: e.tensor_tensor(out=xres[0:Pt, s, :], in0=ytmp[0:Pt, :], in1=xres[0:Pt, s, :], op=ALU.add), reads=[b_ytmp, b_xres], writes=[b_xres])

    def split3(src, b_src, N, sign):
        spl, spf, b_spl, b_spf = cx["spl"], cx["spf"], cx["b_spl"], cx["b_spf"]
        P.op("dve", lambda e: e.tensor_scalar(out=spf[:, 0, 0:N], in0=src, scalar1=sign, scalar2=None, op0=ALU.mult), reads=[b_src], writes=[b_spf])
        for j in range(3):
            P.op("dve", lambda e, j=j: e.tensor_copy(spl[:, j, 0:N], spf[:, 0, 0:N]), reads=[b_spf], writes=[b_spl])
            if j < 2:
                P.op("dve", lambda e, j=j: e.tensor_copy(spf[:, 1, 0:N], spl[:, j, 0:N]), reads=[b_spl], writes=[b_spf])
                P.op("dve", lambda e: e.tensor_tensor(out=spf[:, 0, 0:N], in0=spf[:, 0, 0:N], in1=spf[:, 1, 0:N], op=ALU.subtract), reads=[b_spf], writes=[b_spf])

    m_persist = A.mark()
    adaw = A.alloc([128, 8, 3 * D], F32, "adaw")
    crow = A.alloc([3, D], F32, "crow")
    ccol = A.alloc([128, 8, 3], F32, "ccol")
    vrow = A.alloc([12, R], F32, "vrow")
    adab = A.alloc([1, 2, 3 * D], F32, "adab")
    grow = A.alloc([3, D], F32, "grow")
    npb = A.alloc([128, 2, D], F32, "npb")
    tmpc = A.alloc([128, 8, 3], F32, "tmpc")
    b_crow, b_ccol, b_vrow, b_adaw, b_adab, b_grow, b_npb = [P.buf(n) for n in "crow ccol vrow adaw adab grow npb".split()]
    P.dma("sp", lambda e: e.dma_start(out=crow[:], in_=i_cc[:, :]), writes=[b_crow])
    P.op("pool", lambda e: e.memset(vrow[:], 0.0), writes=[b_vrow])
    P.dma("sp", lambda e: e.dma_start(out=vrow[0:8, :], in_=i_lvec[:, :]), reads=[b_vrow], writes=[b_vrow])
    P.dma("sp", lambda e: e.dma_start(out=vrow[8:10, 0:D], in_=i_npre[:, :]), reads=[b_vrow], writes=[b_vrow])
    P.dma("sp", lambda e: e.dma_start(out=adab[:], in_=i_adab.rearrange("(o l) f -> o l f", o=1)), writes=[b_adab])
    P.dma("sp", lambda e: e.dma_start(out=npb[:], in_=i_npost.rearrange("(o l) f -> o l f", o=1).broadcast_to([128, 2, D])), writes=[b_npb])
    P.dma("sp", lambda e: e.dma_start(out=bfrow[:], in_=i_fbf.broadcast_to([128, H])), writes=[b_const])
    P.op("act", lambda e: e.activation(out=crow[:], in_=crow[:], func=AF.Silu), reads=[b_crow], writes=[b_crow])
    pa, ba = bank(0)

    def tr_c(e):
        ins = None
        for kc in range(8):
            ins = e.transpose(out=pa[:, kc * 3:kc * 3 + 3], in_=crow[0:3, kc * 128:(kc + 1) * 128], identity=identf[0:3, 0:3])
        return ins
    P.op("pe", tr_c, reads=[b_crow, b_const], writes=[ba])
    P.op("dve", lambda e: e.tensor_copy(ccol[:].rearrange("p k s -> p (k s)"), pa[:, 0:24]), reads=[ba], writes=[b_ccol])
    pb_, bb = bank(1)

    def tr_v(e):
        ins = None
        for c in range(NCH):
            ins = e.transpose(out=pb_[:, c * 10:c * 10 + 10], in_=vrow[0:10, c * 128:(c + 1) * 128], identity=identf[0:10, 0:10])
        return ins
    P.op("pe", tr_v, reads=[b_vrow, b_const], writes=[bb])
    vcol = A.alloc([128, NCH, 10], F32, "vcol")
    b_vcol = P.buf("vcol")
    P.op("dve", lambda e: e.tensor_copy(vcol[:].rearrange("p c v -> p (c v)"), pb_[:, 0:NCH * 10]), reads=[bb], writes=[b_vcol])
    P.op("dve", lambda e: e.tensor_copy(lcol[:, :, 0:8], vcol[:, :, 0:8]), reads=[b_vcol], writes=[b_const])
    P.op("act", lambda e: e.activation(out=lcol[:, :, 8], in_=vcol[:, :, 7], func=AF.Exp, scale=-1.0), reads=[b_vcol], writes=[b_const])
    P.op("act", lambda e: e.activation(out=lcol[:, :, 8], in_=lcol[:, :, 8], func=AF.Ln, bias=onec[:, 0:1]), reads=[b_const], writes=[b_const])
    P.op("dve", lambda e: e.tensor_scalar(out=lcol[:, :, 8], in0=lcol[:, :, 8], scalar1=-8.0, scalar2=None, op0=ALU.mult), reads=[b_const], writes=[b_const])
    pc_, bc = bank(2)
    P.op("pe", lambda e: e.matmul(pc_[0:16, 0:1], lhsT=bfrow[0:1, 0:16], rhs=ones_f[0:1, 0:1], start=True, stop=True), reads=[b_const], writes=[bc])
    P.op("dve", lambda e: e.tensor_scalar(out=bfcol[:, 0:1], in0=pc_[0:16, 0:1], scalar1=-1.0, scalar2=None, op0=ALU.mult), reads=[bc], writes=[b_const])
    for l in range(2):
        for q4 in range(4):
            P.dma("sp" if q4 % 2 == 0 else "act",
                  lambda e, l=l, q4=q4: e.dma_start(out=adaw[:, 2 * q4:2 * q4 + 2, :], in_=i_adaw[l, q4 * 256:(q4 + 1) * 256, :].rearrange("(k p) f -> p k f", p=128)),
                  writes=[b_adaw])
        pm, bm = bank(4 + l * 2)

        def ada_cols(e, l=l, pm=pm):
            ins = None
            for fc in range(16):
                for kc in range(8):
                    ins = e.matmul(pm[:, fc * 3:fc * 3 + 3], lhsT=adaw[:, kc, fc * 128:(fc + 1) * 128], rhs=ccol[:, kc, :], start=(kc == 0), stop=False)
                ins = e.matmul(pm[:, fc * 3:fc * 3 + 3], lhsT=adab[0:1, l, fc * 128:(fc + 1) * 128], rhs=ones_f[0:1, 0:3], start=False, stop=True)
            return ins
        P.op("pe", ada_cols, reads=[b_adaw, b_ccol, b_adab, b_const], writes=[bm])
        P.op("dve", lambda e, l=l, pm=pm: e.tensor_copy(gcol[:, l, 1, :, :].rearrange("p k s -> p (k s)"), pm[:, 0:24]), reads=[bm], writes=[b_ada])
        P.op("dve", lambda e, l=l, pm=pm: e.tensor_scalar(out=tmpc[:].rearrange("p k s -> p (k s)"), in0=pm[:, 24:48], scalar1=1.0, scalar2=None, op0=ALU.add), reads=[bm], writes=[b_grow])
        for s in range(3):
            P.op("dve", lambda e, l=l, s=s: e.tensor_tensor(out=gcol[:, l, 0, :, s], in0=tmpc[:, :, s], in1=vcol[0:128, 0:8, 8 + l], op=ALU.mult), reads=[b_grow, b_vcol], writes=[b_ada])
        for hf in range(2):
            prh, brh = bank(5 + l * 2) if hf == 0 else bank(3)

            def ada_rowh(e, l=l, hf=hf, prh=prh):
                ins = None
                for kc in range(8):
                    ins = e.matmul(prh[0:3, 0:512], lhsT=ccol[:, kc, :], rhs=adaw[:, kc, 2 * D + hf * 512:2 * D + (hf + 1) * 512], start=(kc == 0), stop=False)
                ins = e.matmul(prh[0:3, 0:512], lhsT=ones_f[0:1, 0:3], rhs=adab[0:1, l, 2 * D + hf * 512:2 * D + (hf + 1) * 512], start=False, stop=True)
                return ins
            P.op("pe", ada_rowh, reads=[b_adaw, b_ccol, b_adab, b_const], writes=[brh])
            P.op("dve", lambda e, hf=hf, prh=prh: e.tensor_copy(grow[:, hf * 512:(hf + 1) * 512], prh[0:3, 0:512]), reads=[brh], writes=[b_grow])
        for g in range(2):
            for hf in range(2):
                pg, bg = bank(0 + hf)
                P.op("pe", lambda e, g=g, hf=hf, pg=pg: e.matmul(pg[:, 0:512], lhsT=sel[0:3, g, :], rhs=grow[0:3, hf * 512:(hf + 1) * 512], start=True, stop=True), reads=[b_grow, b_const], writes=[bg])
                P.op("dve", lambda e, g=g, hf=hf, pg=pg, l=l: e.tensor_tensor(out=gprow[:, l, g, hf * 512:(hf + 1) * 512], in0=pg[:, 0:512], in1=npb[:, l, hf * 512:(hf + 1) * 512], op=ALU.mult), reads=[bg, b_npb], writes=[b_ada])
    P.barrier()
    A.reset(m_persist)
    if dbg == 32:
        P.dma("sp", lambda e: e.dma_start(out=o_fkp[2048:2176, 0:96], in_=gcol[:].rearrange("p a b c d -> p (a b c d)")), reads=[b_ada])
        P.dma("sp", lambda e: e.dma_start(out=o_fkp[2176:2304, 0:2 * 2 * D], in_=gprow[:].rearrange("p a b c -> p (a b c)")), reads=[b_ada]) if False else None
    if stop == 0:
        P.emit()
        return nc

    win = A.alloc([128, 8, 2 * R], BF16, "win")
    wg = A.alloc([128, 2, NCH, 128], BF16, "wg")
    wout = A.alloc([128, NCH, D], BF16, "wout")
    b_w = P.buf("weights")
    for kc in range(8):
        for hf in range(2):
            P.dma("pool", lambda e, kc=kc, hf=hf: e.dma_start(out=win[:, kc, hf * R:(hf + 1) * R], in_=i_win[kc * 128:(kc + 1) * 128, hf * R:(hf + 1) * R]), writes=[b_w])
    P.dma("pool", lambda e: e.dma_start(out=wg[:, 0, :, :], in_=i_wa.rearrange("n d e -> d n e")), writes=[b_w])
    P.dma("pool", lambda e: e.dma_start(out=wg[:, 1, :, :], in_=i_wx.rearrange("n d e -> d n e")), writes=[b_w])
    for c in range(NCH):
        P.dma("pool", lambda e, c=c: e.dma_start(out=wout[:, c, :], in_=i_wout[c * 128:(c + 1) * 128, :]), writes=[b_w])
    CG = 2
    NG = NCH // CG
    ENG_CAST = "pool" if dbg & 1024 else "dve"
    ENG_A2 = "pool" if dbg & 2048 else "dve"
    ENG_IM = "pool" if dbg & 4096 else "dve"
    xt_a1 = A.alloc([128, 4, D], F32, "xt_a1")
    cx["xn"] = A.alloc([128, D], F32, "xn")
    cx["junk"] = A.alloc([128, D], BF16, "junk")
    cx["stat"] = A.alloc([128, 16], F32, "stat")
    cx["hT"] = A.alloc([128, 8, TT], BF16, "hT")
    nb("xn", "junk", "stat", "hT")
    cx["ytmp"] = cx["xn"]
    cx["b_ytmp"] = cx["b_xn"]
    halo = A.alloc([128, NCH, 2, 4], F32, "halo")
    hst = A.alloc([128, NCH, 2], F32, "hst")
    XBW = 520
    sets = []
    for k_ in range(2):
        S_ = dict(xb=A.alloc([128, CG, XBW], F32, "xb"), sg=A.alloc([128, CG, TT], BF16, "sg"), xc=A.alloc([128, CG, TT], F32, "xc"),
                  xcb=A.alloc([128, CG, TT], BF16, "xcb"), rr=A.alloc([128, CG, TT], F32, "rr"), ig=A.alloc([128, CG, TT], F32, "ig"),
                  aa=A.alloc([128, CG, TT], F32, "aa"))
        for n_ in ("xb", "xbh", "sg", "xc", "xcb", "rr", "ig", "aa"):
            S_["b_" + n_] = P.buf("%s%d" % (n_, k_))
        sets.append(S_)
    zT = A.alloc([128, NCH, TT], BF16, "zT")
    b_xts = [P.buf("xt%d" % s_) for s_ in range(4)]
    b_halo = [P.buf("halo%d" % c_) for c_ in range(NCH)]
    b_hst, b_zT = P.buf("hst"), P.buf("zT")

    def layer0(N, nseg, L, segw):
        hT, b_hT = cx["hT"], cx["b_hT"]

        def xbv(S, cl, sgm, a_, b__):
            return S["xb"][:, cl, sgm * segw + a_:sgm * segw + b__]

        def stageA_pe(g):
            for cl in range(CG):
                c = g * CG + cl
                pxa, bxa = bank(4 + cl * 2)
                pga, bga = bank(5 + cl * 2)

                def mm(e, c=c, pxa=pxa, pga=pga):
                    ins = None
                    for kc in range(8):
                        ins = e.matmul(pxa[:, 0:N], lhsT=win[:, kc, c * 128:(c + 1) * 128], rhs=hT[:, kc, 0:N], start=(kc == 0), stop=(kc == 7))
                    for kc in range(8):
                        ins = e.matmul(pga[:, 0:N], lhsT=win[:, kc, R + c * 128:R + (c + 1) * 128], rhs=hT[:, kc, 0:N], start=(kc == 0), stop=(kc == 7))
                    return ins
                P.op("pe", mm, reads=[b_hT, b_w], writes=[bxa, bga])

        def stageA_el(g):
            S = sets[g % 2]
            for cl in range(CG):
                c = g * CG + cl
                pxa, bxa = bank(4 + cl * 2)
                pga, bga = bank(5 + cl * 2)
                P.op("act", lambda e, S=S, cl=cl, pga=pga: e.activation(out=S["sg"][:, cl, 0:N], in_=pga[:, 0:N], func=AF.Silu), reads=[bga], writes=[S["b_sg"]])
                for sgm in range(nseg):
                    P.op("dve", lambda e, S=S, cl=cl, sgm=sgm, pxa=pxa: e.tensor_copy(xbv(S, cl, sgm, 3, 3 + L), pxa[:, sgm * L:(sgm + 1) * L]), reads=[bxa], writes=[S["b_xb"]])
                    P.op("pool", lambda e, S=S, c=c, cl=cl, sgm=sgm: e.tensor_copy(xbv(S, cl, sgm, 0, 3), halo[:, c, sgm, 0:3]), reads=[b_halo[c]], writes=[S["b_xbh"]])
            for cl in range(CG):
                c = g * CG + cl
                for sgm in range(nseg):
                    o = S["xc"][:, cl, sgm * L:(sgm + 1) * L]
                    P.op("dve", lambda e, S=S, c=c, cl=cl, sgm=sgm, o=o: e.tensor_scalar(out=o, in0=xbv(S, cl, sgm, 0, L), scalar1=lcol[:, c, 0:1], scalar2=lcol[:, c, 4:5], op0=ALU.mult, op1=ALU.add),
                         reads=[S["b_xb"], S["b_xbh"], b_const], writes=[S["b_xc"]])
                    for k in range(1, 4):
                        P.op("dve", lambda e, S=S, c=c, cl=cl, sgm=sgm, o=o, k=k: e.scalar_tensor_tensor(out=o, in0=xbv(S, cl, sgm, k, k + L), scalar=lcol[:, c, k:k + 1], in1=o, op0=ALU.mult, op1=ALU.add),
                             reads=[S["b_xb"], S["b_xbh"], b_const, S["b_xc"]], writes=[S["b_xc"]])
                    P.op("pool", lambda e, S=S, c=c, cl=cl, sgm=sgm: e.tensor_copy(halo[:, c, sgm, 0:3], xbv(S, cl, sgm, L, L + 3)), reads=[S["b_xb"]], writes=[b_halo[c]])
                P.op(ENG_CAST, lambda e, S=S, cl=cl: e.tensor_copy(S["xcb"][:, cl, 0:N], S["xc"][:, cl, 0:N]), reads=[S["b_xc"]], writes=[S["b_xcb"]])

        def stageB_pe(g):
            S = sets[g % 2]
            xcb = S["xcb"]
            for cl in range(CG):
                c = g * CG + cl
                pra, bra = bank(cl * 2)
                pia, bia = bank(cl * 2 + 1)

                def mg(e, c=c, cl=cl, pra=pra, pia=pia, xcb=xcb):
                    e.matmul(pra[:, 0:N], lhsT=wg[:, 0, c, :], rhs=xcb[:, cl, 0:N], start=True, stop=True)
                    return e.matmul(pia[:, 0:N], lhsT=wg[:, 1, c, :], rhs=xcb[:, cl, 0:N], start=True, stop=True)
                P.op("pe", mg, reads=[S["b_xcb"], b_w], writes=[bra, bia])

        def stageB_el(g):
            S = sets[g % 2]
            rr, ig, aa, xc, sg, xcb = S["rr"], S["ig"], S["aa"], S["xc"], S["sg"], S["xcb"]
            for cl in range(CG):
                c = g * CG + cl
                pra, bra = bank(cl * 2)
                pia, bia = bank(cl * 2 + 1)
                P.op("act", lambda e, c=c, cl=cl, pra=pra, rr=rr: e.activation(out=rr[:, cl, 0:N], in_=pra[:, 0:N], func=AF.Sigmoid, bias=lcol[:, c, 5:6]), reads=[bra, b_const], writes=[S["b_rr"]])
                P.op("act", lambda e, c=c, cl=cl, pia=pia, ig=ig: e.activation(out=ig[:, cl, 0:N], in_=pia[:, 0:N], func=AF.Sigmoid, bias=lcol[:, c, 6:7]), reads=[bia, b_const], writes=[S["b_ig"]])
            for cl in range(CG):
                c = g * CG + cl
                P.op("act", lambda e, c=c, cl=cl, rr=rr, aa=aa: e.activation(out=aa[:, cl, 0:N], in_=rr[:, cl, 0:N], func=AF.Exp, scale=lcol[:, c, 8:9]), reads=[S["b_rr"], b_const], writes=[S["b_aa"]])
                P.op(ENG_A2, lambda e, cl=cl, rr=rr, aa=aa: e.tensor_tensor(out=rr[:, cl, 0:N], in0=aa[:, cl, 0:N], in1=aa[:, cl, 0:N], op=ALU.mult), reads=[S["b_aa"], S["b_rr"]], writes=[S["b_rr"]])
                P.op("pool", lambda e, cl=cl, ig=ig, xc=xc: e.tensor_tensor(out=ig[:, cl, 0:N], in0=ig[:, cl, 0:N], in1=xc[:, cl, 0:N], op=ALU.mult), reads=[S["b_ig"], S["b_xc"]], writes=[S["b_ig"]])
            for cl in range(CG):
                c = g * CG + cl
                P.op("act", lambda e, cl=cl, rr=rr: e.activation(out=rr[:, cl, 0:N], in_=rr[:, cl, 0:N], func=AF.Sqrt, scale=-1.0, bias=onec[:, 0:1]), reads=[S["b_rr"], b_const], writes=[S["b_rr"]])
                P.op(ENG_IM, lambda e, cl=cl, ig=ig, rr=rr: e.tensor_tensor(out=ig[:, cl, 0:N], in0=ig[:, cl, 0:N], in1=rr[:, cl, 0:N], op=ALU.mult), reads=[S["b_ig"], S["b_rr"]], writes=[S["b_ig"]])
                for sgm in range(nseg):
                    P.op("dve", lambda e, c=c, cl=cl, sgm=sgm, xc=xc, aa=aa, ig=ig: e.tensor_tensor_scan(out=xc[:, cl, sgm * L:(sgm + 1) * L], data0=aa[:, cl, sgm * L:(sgm + 1) * L], data1=ig[:, cl, sgm * L:(sgm + 1) * L], initial=hst[:, c, sgm:sgm + 1], op0=ALU.mult, op1=ALU.add),
                         reads=[S["b_aa"], S["b_ig"], b_hst, S["b_xc"]], writes=[S["b_xc"]])
                    P.op("dve", lambda e, c=c, cl=cl, sgm=sgm, xc=xc: e.tensor_copy(hst[:, c, sgm:sgm + 1], xc[:, cl, (sgm + 1) * L - 1:(sgm + 1) * L]), reads=[S["b_xc"], b_hst], writes=[b_hst])
                P.op("pool", lambda e, c=c, cl=cl, xc=xc, sg=sg: e.tensor_tensor(out=zT[:, c, 0:N], in0=xc[:, cl, 0:N], in1=sg[:, cl, 0:N], op=ALU.mult), reads=[S["b_xc"], S["b_sg"]], writes=[b_zT])

        stageA_pe(0)
        stageA_el(0)
        for g in range(NG):
            if not (dbg & 8192):
                if g + 1 < NG:
                    stageA_pe(g + 1)
                    stageA_el(g + 1)
                stageB_pe(g)
                stageB_el(g)
                continue
            stageB_pe(g)
            if g + 1 < NG:
                stageA_pe(g + 1)
            stageB_el(g)
            if g + 1 < NG:
                stageA_el(g + 1)

    P.op("pool", lambda e: e.memset(halo[:], 0.0), writes=b_halo)
    P.op("pool", lambda e: e.memset(hst[:], 0.0), writes=[b_hst])
    for i in range(nt_prompt):
        for s4 in range(4):
            P.dma("sp", lambda e, i=i, s4=s4: e.dma_start(out=xt_a1[:, s4, :], in_=i_xp[i * TT + s4 * 128:i * TT + (s4 + 1) * 128, :]), writes=[b_xts[s4]])
        front(TT, 4, 128, 0, [(0, TT, 0)], xt_a1, b_xts)
        layer0(TT, 1, TT, 0)
        post(TT, 4, 128, 0, 0, wout, b_w, NCH, zT, b_zT, 128, xt_a1, b_xts)
        for s4 in range(4):
            P.dma("pool", lambda e, i=i, s4=s4: e.dma_start(out=s_x1[i * TT + s4 * 128:i * TT + (s4 + 1) * 128, :], in_=xt_a1[:, s4, :]), reads=[b_xts[s4]], writes=[b_sx1])
        if dbg == 31 and i < NQ:
            P.dma("sp", lambda e, i=i: e.dma_start(out=o_yp[i * TT:(i + 1) * TT, :].rearrange("(s p) f -> p s f", p=128), in_=xt_a1[:]), reads=b_xts)
    P.dma("sp", lambda e: e.dma_start(out=o_lhp.rearrange("(c p) -> p c", p=128), in_=hst[:, :, 0], allow_slow_non_contiguous=True), reads=[b_hst])
    for k3 in range(3):
        P.dma("sp", lambda e, k3=k3: e.dma_start(out=o_lcp[k3].rearrange("(c p) -> p c", p=128), in_=halo[:, :, 0, k3], allow_slow_non_contiguous=True), reads=b_halo)
    sc = [(0, ST, 1), (ST, 2 * ST, 2)]
    if do_sample:
        for sq in range(2):
            P.dma("sp", lambda e, sq=sq: e.dma_start(out=hst[:, :, sq], in_=i_sth[sq].rearrange("(c p) -> p c", p=128), allow_slow_non_contiguous=True), reads=[b_hst], writes=[b_hst])
            for k3 in range(3):
                P.dma("sp", lambda e, sq=sq, k3=k3: e.dma_start(out=halo[:, :, sq, k3], in_=i_stc[sq, k3].rearrange("(c p) -> p c", p=128), allow_slow_non_contiguous=True), reads=b_halo, writes=b_halo)
        P.dma("sp", lambda e: e.dma_start(out=xs_t[0:64, 0, :], in_=i_xs[:, :]), writes=[b_xs])
        front(2 * ST, 1, 64, 0, sc, xs_t, b_xs)
        layer0(2 * ST, 2, ST, 40)
        post(2 * ST, 1, 64, 0, 1, wout, b_w, NCH, zT, b_zT, 128, xs_t, b_xs)
        for sq in range(2):
            P.dma("sp", lambda e, sq=sq: e.dma_start(out=o_lhs[sq].rearrange("(c p) -> p c", p=128), in_=hst[:, :, sq], allow_slow_non_contiguous=True), reads=[b_hst])
            for k3 in range(3):
                P.dma("sp", lambda e, sq=sq, k3=k3: e.dma_start(out=o_lcs[sq, k3].rearrange("(c p) -> p c", p=128), in_=halo[:, :, sq, k3], allow_slow_non_contiguous=True), reads=b_halo)
    P.barrier()
    A.reset(m_persist)
    if stop == 1:
        P.emit()
        return nc

    fwin = A.alloc([128, 8, 2 * D + H], BF16, "fwin")
    b_w = P.buf("weights2")
    for kc in range(8):
        for hf in range(2):
            P.dma("pool", lambda e, kc=kc, hf=hf: e.dma_start(out=fwin[:, kc, hf * D:(hf + 1) * D], in_=i_fwin[kc * 128:(kc + 1) * 128, D + hf * D:D + (hf + 1) * D]), writes=[b_w])
        P.dma("pool", lambda e, kc=kc: e.dma_start(out=fwin[:, kc, 2 * D:2 * D + H], in_=i_fwin[kc * 128:(kc + 1) * 128, 4 * D:4 * D + H]), writes=[b_w])
    xt_a2s = [A.alloc([128, 4, D], F32, "xt_a2")]
    xt_a2 = xt_a2s[0]
    cx["xn"] = A.alloc([128, D], F32, "xn")
    cx["junk"] = A.alloc([128, D], BF16, "junk")
    cx["stat"] = A.alloc([128, 16], F32, "stat")
    cx["hT"] = A.alloc([128, 8, TT], BF16, "hT")
    kst = A.alloc([128, 4, D], F32, "kst")
    kstb = A.alloc([128, 4, D], BF16, "kstb")
    b_kstb = P.buf("kstb")
    vst = A.alloc([128, 4, D], F32, "vst")
    v1st = A.alloc([128, H, 4, VS], BF16, "v1st")
    kTst = A.alloc([KA, H, TT], BF16, "kTst")
    lft = A.alloc([128, 4, H], F32, "lft")
    lfT = A.alloc([16, TT], F32, "lfT")
    cumT = A.alloc([16, TT], F32, "cumT")
    ccar = A.alloc([16, 2], F32, "ccar")
    cx["spl"] = A.alloc([16, 3, TT], BF16, "spl")
    cx["spf"] = A.alloc([16, 2, TT], F32, "spf")
    nb("xn", "junk", "stat", "hT", "spl", "spf")
    b_xt, b_kst, b_vst, b_v1st, b_kTst, b_lft, b_lfT, b_cumT, b_ccar = [P.buf(n) for n in range(9)]
    P.op("pool", lambda e: e.memset(v1st[:], 1.0), writes=[b_v1st])
    P.op("pool", lambda e: e.memset(kTst[:], 1.0), writes=[b_kTst])
    P.op("pool", lambda e: e.memset(ccar[:], 0.0), writes=[b_ccar])

    def ktrans(N, nsub, Pt, cast=False):
        if cast:
            for s_ in range(nsub):
                P.op("dve", lambda e, s_=s_: e.tensor_copy(kstb[0:Pt, s_, :], kst[0:Pt, s_, :]), reads=[b_kst], writes=[b_kstb])
        for h in range(H):
            pk, bk = bank(h % 2)

            def trk(e, h=h, pk=pk):
                ins = None
                for s in range(nsub):
                    ins = e.matmul(pk[0:64, s * Pt:(s + 1) * Pt], lhsT=kstb[0:Pt, s, h * 64:(h + 1) * 64], rhs=ident_bf[0:Pt, 0:Pt], start=True, stop=True)
                return ins
            P.op("pe", trk, reads=[b_kstb, b_const], writes=[bk])
            if h % 2 == 0:
                P.op("act", lambda e, h=h, pk=pk: e.activation(out=kTst[0:64, h, 0:N], in_=pk[0:64, 0:N], func=AF.Identity), reads=[bk], writes=[b_kTst])
            else:
                P.op("dve", lambda e, h=h, pk=pk: e.tensor_copy(kTst[0:64, h, 0:N], pk[0:64, 0:N]), reads=[bk], writes=[b_kTst])

    def ckrows(N):
        spl, b_spl = cx["spl"], cx["b_spl"]
        for j in range(3):
            P.dma("sp", lambda e, j=j: e.dma_start(out=kTst[67 + j:68 + j, :, 0:N], in_=spl[:, j, 0:N]), reads=[b_spl, b_kTst], writes=[b_kTst])

    def kvproj(N, nsub, Pt, nseg, L, x1src, b_x1, seqcols, o_fk, o_fv, o_fl, tok0, kT_dsts, vv_dst, cum_dst):
        front(N, nsub, Pt, 1, seqcols, x1src, b_x1)
        if dbg == 33:
            for q_ in range(2):
                P.dma("pool", lambda e, q_=q_: e.dma_start(out=o_fvp[2048:2176, q_ * 2048:(q_ + 1) * 2048].rearrange("p (k n) -> p k n", k=4) if False else o_fvp[2048 + q_ * 128:2176 + q_ * 128, 0:1024].rearrange("p (k n) -> p k n", k=4)[:, :, 0:N // 2 if False else 256], in_=cx["hT"][:, q_ * 4:(q_ + 1) * 4, 0:256]), reads=[cx["b_hT"]])
            return
        if dbg == 1:
            return
        hT, b_hT = cx["hT"], cx["b_hT"]
        for s in range(nsub):
            for which, (wofs, stg, b_stg, o_d) in enumerate([(0, kst, b_kst, o_fk), (D, vst, b_vst, o_fv)]):
                pp = PS[2 + which]
                bpp = bPS[2 + which]

                def mm(e, s=s, pp=pp, wofs=wofs):
                    ins = None
                    for hf in range(2):
                        for kc in range(8):
                            ins = e.matmul(pp[0:Pt, hf * 512:(hf + 1) * 512], lhsT=hT[:, kc, s * Pt:(s + 1) * Pt], rhs=fwin[:, kc, wofs + hf * 512:wofs + (hf + 1) * 512], start=(kc == 0), stop=(kc == 7))
                    return ins
                P.op("pe", mm, reads=[b_hT, b_w], writes=bpp)
                P.op("act", lambda e, s=s, pp=pp, stg=stg: e.activation(out=stg[0:Pt, s, :], in_=pp[0:Pt, :], func=AF.Identity), reads=bpp, writes=[b_stg])
                if which == 0:
                    P.op("dve", lambda e, s=s: e.tensor_copy(kstb[0:Pt, s, :], kst[0:Pt, s, :]), reads=[b_kst], writes=[b_kstb])
                if which == 1 and dbg != 21:
                    if dbg == 22:
                        P.op("act", lambda e, s=s, pp=pp: e.activation(out=v1st[0:Pt, :, s, 0:64], in_=pp[0:Pt, :].rearrange("p (h d) -> p h d", h=H), func=AF.Identity), reads=bpp, writes=[b_v1st])
                    elif dbg == 23:
                        P.op("pool", lambda e, s=s: e.tensor_copy(v1st[0:Pt, :, s, 0:64], vst[0:Pt, s, :].rearrange("p (h d) -> p h d", h=H)), reads=[b_vst], writes=[b_v1st])
                    else:
                        P.op("dve", lambda e, s=s: e.tensor_copy(v1st[0:Pt, :, s, 0:64], vst[0:Pt, s, :].rearrange("p (h d) -> p h d", h=H)), reads=[b_vst], writes=[b_v1st])
                P.dma("sp", lambda e, s=s, stg=stg, o_d=o_d: e.dma_start(out=o_d[tok0 + s * Pt:tok0 + (s + 1) * Pt, :], in_=stg[0:Pt, s, :]), reads=[b_stg])
            if dbg in (2, 21, 22, 23):
                continue
            pf, bf_ = bank(0)

            def mfl(e, s=s, pf=pf):
                ins = None
                for kc in range(8):
                    ins = e.matmul(pf[0:Pt, 0:H], lhsT=hT[:, kc, s * Pt:(s + 1) * Pt], rhs=fwin[:, kc, 2 * D:2 * D + H], start=(kc == 0), stop=(kc == 7))
                return ins
            P.op("pe", mfl, reads=[b_hT, b_w], writes=[bf_])
            P.op("dve", lambda e, s=s, pf=pf: e.tensor_tensor(out=lft[0:Pt, s, :], in0=pf[0:Pt, 0:H], in1=bfrow[0:Pt, :], op=ALU.add), reads=[bf_, b_const], writes=[b_lft])
        if dbg in (2, 21, 22, 23):
            return
        P.op("act", lambda e: e.activation(out=lft[0:Pt, 0:nsub, :], in_=lft[0:Pt, 0:nsub, :], func=AF.Exp, scale=-1.0), reads=[b_lft], writes=[b_lft])
        P.op("act", lambda e: e.activation(out=lft[0:Pt, 0:nsub, :], in_=lft[0:Pt, 0:nsub, :], func=AF.Ln, bias=onec[0:Pt, 0:1]), reads=[b_lft, b_const], writes=[b_lft])
        P.op("dve", lambda e: e.tensor_scalar(out=lft[0:Pt, 0:nsub, :], in0=lft[0:Pt, 0:nsub, :], scalar1=-1.0, scalar2=None, op0=ALU.mult), reads=[b_lft], writes=[b_lft])
        with nc.allow_non_contiguous_dma(reason="64B rows"):
            P.dma("sp", lambda e: e.dma_start(out=o_fl[tok0:tok0 + nsub * Pt, :].rearrange("(s p) h -> p s h", p=Pt), in_=lft[0:Pt, 0:nsub, :], allow_slow_non_contiguous=True), reads=[b_lft])
        if dbg == 3:
            return
        pf, bf_ = bank(1)

        def mflT(e, pf=pf):
            ins = None
            for kc in range(8):
                ins = e.matmul(pf[0:H, 0:N], lhsT=fwin[:, kc, 2 * D:2 * D + H], rhs=hT[:, kc, 0:N], start=(kc == 0), stop=(kc == 7))
            return ins
        P.op("pe", mflT, reads=[b_hT, b_w], writes=[bf_])
        P.op("act", lambda e, pf=pf: e.activation(out=lfT[:, 0:N], in_=pf[0:H, 0:N], func=AF.Exp, scale=-1.0, bias=bfcol[:, 0:1]), reads=[bf_, b_const], writes=[b_lfT])
        P.op("act", lambda e: e.activation(out=lfT[:, 0:N], in_=lfT[:, 0:N], func=AF.Ln, bias=onec[0:16, 0:1]), reads=[b_lfT, b_const], writes=[b_lfT])
        for sgm in range(nseg):
            P.op("dve", lambda e, sgm=sgm: e.tensor_tensor_scan(out=cumT[:, sgm * L:(sgm + 1) * L], data0=ones_f[0:16, 0:L], data1=lfT[:, sgm * L:(sgm + 1) * L], initial=ccar[:, sgm:sgm + 1], op0=ALU.mult, op1=ALU.subtract),
                 reads=[b_lfT, b_ccar, b_const, b_cumT], writes=[b_cumT])
            P.op("dve", lambda e, sgm=sgm: e.tensor_copy(ccar[:, sgm:sgm + 1], cumT[:, (sgm + 1) * L - 1:(sgm + 1) * L]), reads=[b_cumT, b_ccar], writes=[b_ccar])
        if cum_dst is not None:
            P.dma("sp", lambda e: e.dma_start(out=cum_dst, in_=cumT[:, 0:N]), reads=[b_cumT])
        if dbg == 4:
            return
        split3(cumT[:, 0:N], b_cumT, N, -1.0)
        if dbg == 5:
            return
        ktrans(N, nsub, Pt)
        if dbg == 6:
            return
        ckrows(N)
        if dbg == 7:
            return
        for sgm, kd in enumerate(kT_dsts if not (dbg == 9 and Pt == 64) else []):
            P.dma("sp", lambda e, sgm=sgm, kd=kd: e.dma_start(out=kd, in_=kTst[:, :, sgm * L:(sgm + 1) * L]), reads=[b_kTst])
        for (vd, vsrc) in (vv_dst if not (dbg == 8 and Pt == 64) else []):
            P.dma("sp", lambda e, vd=vd, vsrc=vsrc: e.dma_start(out=vd, in_=vsrc), reads=[b_v1st])

    m_a2 = A.mark()
    xt_a2s.append(A.alloc([128, 4, D], F32, "xt_a2b"))
    hT_a2 = [cx["hT"], A.alloc([128, 8, TT], BF16, "hT2b")]
    b_hT_a2 = [cx["b_hT"], P.buf("hT2b")]
    b_xt2 = [b_xt, P.buf("xt2b")]
    for i in range(nt_prompt):
        xt_a2, b_xt = xt_a2s[i % 2], b_xt2[i % 2]
        cx["hT"], cx["b_hT"] = hT_a2[i % 2], b_hT_a2[i % 2]
        P.dma("pool", lambda e, i=i, xt_a2=xt_a2: e.dma_start(out=xt_a2[:], in_=s_x1[i * TT:(i + 1) * TT, :].rearrange("(s p) f -> p s f", p=128)), reads=[b_sx1], writes=[b_xt])
        if dbg == 34 and i < NQ:
            P.dma("sp", lambda e, i=i, xt_a2=xt_a2: e.dma_start(out=o_yp[i * TT:(i + 1) * TT, :].rearrange("(s p) f -> p s f", p=128), in_=xt_a2[:]), reads=[b_xt])
        m_ = i // 4
        vvd = [(s_vv[:, m_, :, (i % 4) * 4 * VS:(i % 4 + 1) * 4 * VS].rearrange("h p x -> p h x"), v1st[:].rearrange("p h s d -> p h (s d)"))]
        kvproj(TT, 4, 128, 1, TT, xt_a2, b_xt, [(0, TT, 0)], o_fkp, o_fvp, o_flp, i * TT,
               [s_kT[:, :, i * TT:(i + 1) * TT].rearrange("h r n -> r h n")], vvd, s_cum[:, i * TT:(i + 1) * TT])
    cx["hT"], cx["b_hT"] = hT_a2[0], b_hT_a2[0]
    P.barrier()
    A.reset(m_a2)
    if do_sample:
        clf = A.alloc([128, 2, PAST // 128, H], F32, "clf")
        pcum = A.alloc([16, 2, PAST], F32, "pcum")
        b_clf, b_pcum = P.buf("clf"), P.buf("pcum")
        with nc.allow_non_contiguous_dma(reason="64B rows"):
            for sq in range(2):
                P.dma("sp", lambda e, sq=sq: e.dma_start(out=clf[:, sq, :, :], in_=i_clf[sq].rearrange("(j p) h -> p j h", p=128), allow_slow_non_contiguous=True), writes=[b_clf])
        for sq in range(2):
            for q8 in range(PAST // 512):
                pk, bk = bank(q8 % 2)

                def trl(e, sq=sq, q8=q8, pk=pk):
                    ins = None
                    for j in range(4):
                        ins = e.matmul(pk[0:16, j * 128:(j + 1) * 128], lhsT=clf[:, sq, q8 * 4 + j, :], rhs=identf[:, :], start=True, stop=True)
                    return ins
                P.op("pe", trl, reads=[b_clf, b_const], writes=[bk])
                P.op("act", lambda e, sq=sq, q8=q8, pk=pk: e.activation(out=pcum[:, sq, q8 * 512:(q8 + 1) * 512], in_=pk[0:16, 0:512], func=AF.Identity), reads=[bk], writes=[b_pcum])
            for q8 in range(PAST // 512):
                init = 0.0 if q8 == 0 else pcum[:, sq, q8 * 512 - 1:q8 * 512]
                P.op("dve", lambda e, sq=sq, q8=q8, init=init: e.tensor_tensor_scan(out=pcum[:, sq, q8 * 512:(q8 + 1) * 512], data0=ones_f[0:16, 0:512], data1=pcum[:, sq, q8 * 512:(q8 + 1) * 512], initial=init, op0=ALU.mult, op1=ALU.add),
                     reads=[b_pcum, b_const], writes=[b_pcum])
            P.op("dve", lambda e, sq=sq: e.tensor_copy(ccar[:, sq:sq + 1], pcum[:, sq, PAST - 1:PAST]), reads=[b_pcum, b_ccar], writes=[b_ccar])
        for sq in range(2 if dbg != 41 else 0):
            for q8 in range(PAST // 512):
                P.dma("sp", lambda e, sq=sq, q8=q8: e.dma_start(out=kst[:], in_=i_ck[sq, q8 * 512:(q8 + 1) * 512, :].rearrange("(s p) f -> p s f", p=128)), writes=[b_kst])
                P.dma("sp", lambda e, sq=sq, q8=q8: e.dma_start(out=vst[:], in_=i_cv[sq, q8 * 512:(q8 + 1) * 512, :].rearrange("(s p) f -> p s f", p=128)), writes=[b_vst])
                for s_ in range(4):
                    P.op("act", lambda e, s_=s_: e.activation(out=v1st[:, :, s_, 0:64], in_=vst[:, s_, :].rearrange("p (h d) -> p h d", h=H), func=AF.Identity), reads=[b_vst], writes=[b_v1st])
                P.dma("sp", lambda e, sq=sq, q8=q8: e.dma_start(out=s_vvs[sq, :, q8 // 4, :, (q8 % 4) * 4 * VS:(q8 % 4 + 1) * 4 * VS].rearrange("h p x -> p h x"), in_=v1st[:].rearrange("p h s d -> p h (s d)")), reads=[b_v1st])
                ktrans(512, 4, 128, cast=True)
                split3(pcum[:, sq, q8 * 512:(q8 + 1) * 512], b_pcum, 512, -1.0)
                ckrows(512)
                P.dma("sp", lambda e, sq=sq, q8=q8: e.dma_start(out=s_kTs[sq, :, :, q8 * 512:(q8 + 1) * 512].rearrange("h r n -> r h n"), in_=kTst[:]), reads=[b_kTst])
        vvd = [(s_vvs[sq, :, NPR, 0:ST, 0:VS].rearrange("h p x -> p h x"), v1st[sq * ST:(sq + 1) * ST, :, 0, :]) for sq in range(2)]
        if dbg not in (41, 42):
          kvproj(2 * ST, 1, 64, 2, ST, xs_t, b_xs, sc, o_fks, o_fvs, o_fls, 0,
               [s_kTs[sq, :, :, PAST:PAST + ST].rearrange("h r n -> r h n") for sq in range(2)], vvd, None)
        P.op("dve", lambda e: e.tensor_copy(scum[:, :], cumT[:, 0:2 * ST]), reads=[b_cumT], writes=[b_scum])
    P.barrier()
    A.reset(m_persist)
    if not do_attn:
        P.emit()
        return nc

    wq = A.alloc([128, 8, 2 * D], BF16, "wq")
    wo = A.alloc([64, H, D], BF16, "wo")
    pmask = A.alloc([128, 16, TT], BF16, "pmask")
    smask = A.alloc([ST, ST], BF16, "smask")
    b_w2 = P.buf("w2")
    for kc in range(8):
        P.dma("pool", lambda e, kc=kc: e.dma_start(out=wq[:, kc, 0:D], in_=i_fwin[kc * 128:(kc + 1) * 128, 0:D]), writes=[b_w2])
        P.dma("pool", lambda e, kc=kc: e.dma_start(out=wq[:, kc, D:2 * D], in_=i_fwin[kc * 128:(kc + 1) * 128, 3 * D:4 * D]), writes=[b_w2])
    for h in range(H):
        P.dma("pool", lambda e, h=h: e.dma_start(out=wo[:, h, :], in_=i_fwout[h * 64:(h + 1) * 64, :]), writes=[b_w2])
    P.dma("sp", lambda e: e.dma_start(out=pmask[:], in_=i_pmask[:, :, :]), writes=[b_w2])
    P.dma("sp", lambda e: e.dma_start(out=smask[:], in_=i_smask[:, :]), writes=[b_w2])
    xt_c = A.alloc([128, 4, D], F32, "xt2")
    cx["xn"] = A.alloc([128, D], F32, "xn")
    cx["junk"] = A.alloc([128, D], BF16, "junk")
    cx["stat"] = A.alloc([128, 16], F32, "stat")
    cx["hT"] = A.alloc([128, 8, TT], BF16, "hT")
    cx["spl"] = A.alloc([16, 3, TT], BF16, "spl")
    cx["spf"] = A.alloc([16, 2, TT], F32, "spf")
    nb("xn", "junk", "stat", "hT", "spl", "spf")
    cx["ytmp"] = cx["xn"]
    cx["b_ytmp"] = cx["b_xn"]
    xl = cx["xn"]
    qaug = A.alloc([KA, H, TT], BF16, "qaug")
    sgT = A.alloc([64, H, TT], BF16, "sgT")
    zT2 = sgT
    csel = A.alloc([16, 2, TT], F32, "csel")
    NKB = 2
    kch = [A.alloc([KA, 2048], BF16, "kch") for _ in range(NKB)]
    vch = [A.alloc([128, 16 * VS], BF16, "vch") for _ in range(NKB)]
    NSG = 4
    pT = [A.alloc([128, 512], BF16, "pT") for _ in range(NSG)]
    osb = A.alloc([65, TT], F32, "osb")
    rl = A.alloc([65, TT], F32, "rl")
    b_xt, b_xl, b_qaug, b_sgT, b_zT2, b_csel, b_osb, b_rl = [P.buf(n) for n in range(8)]
    b_xl = cx["b_xn"]
    b_zT2 = b_sgT
    b_kch = [P.buf("kch") for _ in range(NKB)]
    b_vch = [P.buf("vch") for _ in range(NKB)]
    b_pT = [P.buf("pT") for _ in range(NSG)]
    P.op("pool", lambda e: e.memset(qaug[:], 1.0), writes=[b_qaug])
    chunk_ctr = [0]
    grp_ctr = [0]

    def attend(N, nsub, Pt, seqs, x_res, b_xres, layer_grp, o_y, y_tok0):
        flat = []
        item_ctr = [0]
        pending = []
        EPI_DELAY = 3
        for h in range(H):
            for sq in seqs:
                q0, q1 = sq["q0"], sq["q1"]
                nq = q1 - q0
                work = [(k_ap, v_ap, 128, 16, None) for (k_ap, v_ap) in sq["rows"]] + list(sq["diag"])
                first = True
                for (k_src, v_src, nk, ntile, mask_fn) in work:
                    item = dict(k_src=k_src, v_src=v_src, nk=nk, ntile=ntile, cb=None, idx=item_ctr[0])
                    item_ctr[0] += 1
                    G = min(ntile, 512 // nq if nq > 256 else 16)
                    for g0 in range(0, ntile, G):
                        flat.append(dict(h=h, q0=q0, nq=nq, item=item, g0=g0, G=G, nk=nk, mask_fn=mask_fn, first=first, last_of_head=False))
                        first = False
            flat[-1]["last_of_head"] = True

        def emit_S(en):
            it = en["item"]
            h, nk, ntile = en["h"], en["nk"], it["ntile"]
            if it["cb"] is None:
                cb_ = chunk_ctr[0] % NKB
                chunk_ctr[0] += 1
                it["cb"] = cb_
                P.dma("sp", lambda e, cb_=cb_, k_src=it["k_src"], nk=nk, ntile=ntile, h=h: e.dma_start(out=kch[cb_][:, 0:nk * ntile], in_=k_src[h]), writes=[b_kch[cb_]])
                P.dma("sp", lambda e, cb_=cb_, v_src=it["v_src"], nk=nk, ntile=ntile, h=h: e.dma_start(out=vch[cb_][0:nk, 0:ntile * VS], in_=v_src[h]), writes=[b_vch[cb_]])
            cb_ = it["cb"]
            gi = grp_ctr[0] % NSG
            grp_ctr[0] += 1
            en["gi"] = gi
            psg, bpsg = bank(gi)

            def mmS(e, cb_=cb_, g0=en["g0"], psg=psg, h=h, q0=en["q0"], nq=en["nq"], G=en["G"], nk=nk, mask_fn=en["mask_fn"]):
                ins = None
                for j in range(G):
                    kt = g0 + j
                    ins = e.matmul(psg[0:nk, j * nq:(j + 1) * nq], lhsT=kch[cb_][:, kt * nk:(kt + 1) * nk], rhs=qaug[:, h, q0:q0 + nq], start=True, stop=(mask_fn is None))
                    if mask_fn is not None:
                        ins = e.matmul(psg[0:nk, j * nq:(j + 1) * nq], lhsT=ident_bf[0:nk, 0:nk], rhs=mask_fn(kt), start=False, stop=True)
                return ins
            P.op("pe", mmS, reads=[b_kch[cb_], b_qaug, b_w2, b_const], writes=[bpsg])

        def emit_EV(en):
            h, nk, nq, G, gi, q0 = en["h"], en["nk"], en["nq"], en["G"], en["gi"], en["q0"]
            cb_ = en["item"]["cb"]
            psg, bpsg = bank(gi)
            po, bo = bank(6 + (h % 2))
            P.op("act", lambda e, psg=psg, gi=gi, G=G, nq=nq, nk=nk: e.activation(out=pT[gi][0:nk, 0:G * nq], in_=psg[0:nk, 0:G * nq], func=AF.Exp), reads=[bpsg], writes=[b_pT[gi]])

            def mmV(e, cb_=cb_, g0=en["g0"], gi=gi, po=po, q0=q0, nq=nq, G=G, nk=nk, fst=en["first"]):
                ins = None
                for j in range(G):
                    kt = g0 + j
                    ins = e.matmul(po[0:65, q0:q0 + nq], lhsT=vch[cb_][0:nk, kt * VS:kt * VS + 65], rhs=pT[gi][0:nk, j * nq:(j + 1) * nq], start=(fst and j == 0), stop=False, skip_group_check=True)
                return ins
            P.op("pe", mmV, reads=[b_vch[cb_], b_pT[gi]], writes=[bo])
            if en["last_of_head"]:
                P.op("act", lambda e, po=po: e.activation(out=osb[0:65, 0:N], in_=po[0:65, 0:N], func=AF.Identity), reads=[bo], writes=[b_osb])
                P.op("dve", lambda e: e.reciprocal(rl[64:65, 0:N], osb[64:65, 0:N]), reads=[b_osb], writes=[b_rl])
                pending.append([EPI_DELAY, h])

        def emit_epi_tail(h):
            if True:
                pbq, bbq = bank(5)
                P.op("pe", lambda e, pbq=pbq: e.matmul(pbq[0:64, 0:N], lhsT=ones_f[64:65, 0:64], rhs=rl[64:65, 0:N], start=True, stop=True), reads=[b_rl, b_const], writes=[bbq])
                P.op("dve", lambda e, pbq=pbq: e.tensor_tensor(out=osb[0:64, 0:N], in0=osb[0:64, 0:N], in1=pbq[0:64, 0:N], op=ALU.mult), reads=[b_osb, bbq], writes=[b_osb])
                P.op("pool", lambda e, h=h: e.tensor_tensor(out=zT2[:, h, 0:N], in0=osb[0:64, 0:N], in1=sgT[:, h, 0:N], op=ALU.mult), reads=[b_osb, b_sgT], writes=[b_zT2])

        nxt = 0
        for i, en in enumerate(flat):
            while nxt < len(flat) and nxt <= i + NSG - 1 and flat[nxt]["item"]["idx"] <= en["item"]["idx"] + NKB - 1:
                emit_S(flat[nxt])
                nxt += 1
            for p_ in pending:
                p_[0] -= 1
            while pending and pending[0][0] <= 0:
                emit_epi_tail(pending.pop(0)[1])
            emit_EV(en)
        while pending:
            emit_epi_tail(pending.pop(0)[1])
        post(N, nsub, Pt, 1, layer_grp, wo, b_w2, H, zT2, b_zT2, 64, x_res, b_xres)
        P.dma("sp", lambda e: e.dma_start(out=o_y[y_tok0:y_tok0 + nsub * Pt, :].rearrange("(s p) f -> p s f", p=Pt), in_=x_res[0:Pt, 0:nsub, :]), reads=[b_xres])

    def qproj(N, seqcols, xsrc, b_x, nsub, Pt, cq_src, b_cq):
        front(N, nsub, Pt, 1, seqcols, xsrc, b_x)
        hT, b_hT = cx["hT"], cx["b_hT"]
        spl, b_spl = cx["spl"], cx["b_spl"]
        for h in range(H):
            pq, bq = bank(6)
            pg, bg = bank(7)

            def mq(e, h=h, pq=pq, pg=pg):
                ins = None
                for kc in range(8):
                    ins = e.matmul(pq[0:64, 0:N], lhsT=wq[:, kc, h * 64:(h + 1) * 64], rhs=hT[:, kc, 0:N], start=(kc == 0), stop=(kc == 7))
                for kc in range(8):
                    ins = e.matmul(pg[0:64, 0:N], lhsT=wq[:, kc, D + h * 64:D + (h + 1) * 64], rhs=hT[:, kc, 0:N], start=(kc == 0), stop=(kc == 7))
                return ins
            P.op("pe", mq, reads=[b_hT, b_w2], writes=[bq, bg])
            P.op("dve", lambda e, h=h, pq=pq: e.tensor_scalar(out=qaug[0:64, h, 0:N], in0=pq[0:64, 0:N], scalar1=0.125, scalar2=None, op0=ALU.mult), reads=[bq], writes=[b_qaug])
            P.op("act", lambda e, h=h, pg=pg: e.activation(out=sgT[:, h, 0:N], in_=pg[0:64, 0:N], func=AF.Silu), reads=[bg], writes=[b_sgT])
        split3(cq_src, b_cq, N, 1.0)
        for j in range(3):
            P.dma("sp", lambda e, j=j: e.dma_start(out=qaug[64 + j:65 + j, :, 0:N], in_=spl[:, j, 0:N]), reads=[b_spl, b_qaug], writes=[b_qaug])

    if do_sample:
        qproj(2 * ST, sc, xs_t, b_xs, 1, 64, scum[:, :], b_scum)
        seqs = []
        for sq in range(2):
            rows = [(s_kTs[sq, :, :, rw * 2048:(rw + 1) * 2048], s_vvs[sq, :, rw, :, :]) for rw in range(NPR)]
            diag = [(s_kTs[sq, :, :, PAST:PAST + ST], s_vvs[sq, :, NPR, 0:ST, 0:VS], ST, 1, (lambda kt: smask[:, :]))]
            seqs.append(dict(q0=sq * ST, q1=(sq + 1) * ST, rows=rows, diag=diag))
        attend(2 * ST, 1, 64, seqs, xs_t, b_xs, 1, o_ys, 0)
    for m in range(nq_tiles):
        for r_ in range(4):
            i = 4 * m + r_
            P.dma("sp", lambda e, i=i: e.dma_start(out=csel[:, 1, :], in_=s_cum[:, i * TT:(i + 1) * TT]), writes=[b_csel])
            if r_ == 0:
                P.op("dve", lambda e: e.tensor_scalar(out=csel[:, 0, :], in0=csel[:, 1, :], scalar1=onehot[0:16, 0:1], scalar2=None, op0=ALU.mult), reads=[b_csel, b_const], writes=[b_csel])
            else:
                P.op("dve", lambda e, r_=r_: e.scalar_tensor_tensor(out=csel[:, 0, :], in0=csel[:, 1, :], scalar=onehot[0:16, r_:r_ + 1], in1=csel[:, 0, :], op0=ALU.mult, op1=ALU.add), reads=[b_csel, b_const], writes=[b_csel])
            for s4 in range(4):
                P.dma("act", lambda e, i=i, s4=s4: e.dma_start(out=xl[:, :], in_=s_x1[i * TT + s4 * 128:i * TT + (s4 + 1) * 128, :]), writes=[b_xl])
                if r_ == 0:
                    P.op("dve", lambda e, s4=s4: e.tensor_scalar(out=xt_c[:, s4, :], in0=xl[:, :], scalar1=onehot[:, 0:1], scalar2=None, op0=ALU.mult), reads=[b_xl, b_const], writes=[b_xt])
                else:
                    P.op("dve", lambda e, r_=r_, s4=s4: e.scalar_tensor_tensor(out=xt_c[:, s4, :], in0=xl[:, :], scalar=onehot[:, r_:r_ + 1], in1=xt_c[:, s4, :], op0=ALU.mult, op1=ALU.add), reads=[b_xl, b_const, b_xt], writes=[b_xt])
        qproj(TT, [(0, TT, 0)], xt_c, b_xt, 4, 128, csel[:, 0, :], b_csel)
        rows = [(s_kT[:, :, mm_ * 2048:(mm_ + 1) * 2048], s_vv[:, mm_, :, :]) for mm_ in range(m)]
        diag = [(s_kT[:, :, m * 2048:(m + 1) * 2048], s_vv[:, m, :, :], 128, 16, (lambda kt: pmask[:, kt, :]))]
        attend(TT, 4, 128, [dict(q0=0, q1=TT, rows=rows, diag=diag)], xt_c, b_xt, 0, o_yp, m * TT)
    P.emit()
    return nc


_CACHE = {}


def _get_nc(**kw):
    key = tuple(sorted(kw.items()))
    if key not in _CACHE:
        _CACHE[key] = build(**kw)
    return _CACHE[key]


def _consts(core):
    r = core % 4
    ident = np.eye(128, dtype=np.float32)
    sel = np.zeros((3, 2, 128), np.float32)
    sel[0, 0, :] = 1.0
    sel[1, 1, 0:ST] = 1.0
    sel[2, 1, ST:2 * ST] = 1.0
    kpos = (np.arange(16)[None, :, None] * 128 + np.arange(128)[:, None, None])
    qpos = r * TT + np.arange(TT)[None, None, :]
    pmask = np.where(kpos > qpos, NEG, 0.0).astype(ml_dtypes.bfloat16)
    smask = np.where(np.arange(ST)[:, None] > np.arange(ST)[None, :], NEG, 0.0).astype(ml_dtypes.bfloat16)
    onehot = np.zeros((128, 4), np.float32)
    onehot[:, r] = 1.0
    return dict(c_ident=ident, c_sel=sel, c_pmask=np.ascontiguousarray(pmask), c_smask=smask, c_onehot=onehot)


def _in_map(c, I, shared, T, PAST):
    f = lambda a: np.ascontiguousarray(np.asarray(a, dtype=np.float32))
    b = c // 4
    s0 = 2 * c
    m = dict(shared)
    m.update(_consts(c))
    m["xp"] = f(I["x_prompt"][b])
    m["xs"] = f(I["x_sample"][s0:s0 + 2]).reshape(2 * ST, D)
    m["cc"] = np.concatenate([f(I["c_prompt"])[b:b + 1], f(I["c_sample"])[s0:s0 + 2]], axis=0)
    m["st_h"] = f(I["state_lru_h"][0, s0:s0 + 2])
    m["st_conv"] = f(I["state_lru_conv"][0, s0:s0 + 2])
    m["ck_k"] = f(I["cache_fox_k"][0, s0:s0 + 2]).reshape(2, PAST, D)
    m["ck_v"] = f(I["cache_fox_v"][0, s0:s0 + 2]).reshape(2, PAST, D)
    m["ck_lf"] = f(I["cache_fox_logf"][0, s0:s0 + 2])
    return m


def _shared(I):
    f = lambda a: np.ascontiguousarray(np.asarray(a, dtype=np.float32))
    lvec = np.concatenate([f(I["lru_conv_w"])[0], f(I["lru_conv_b"]), f(I["lru_b_a"]), f(I["lru_b_x"]), f(I["lru_lambda"])], axis=0)
    return dict(norm_pre=f(I["norm_pre"]), norm_post=f(I["norm_post"]), ada_w=f(I["ada_w"]), ada_b=f(I["ada_b"]), lru_w_in=f(I["lru_w_in"])[0],
                lru_vecs=f(lvec), lru_w_a=f(I["lru_w_a"])[0], lru_w_x=f(I["lru_w_x"])[0], lru_w_out=f(I["lru_w_out"])[0],
                fox_w_in=f(I["fox_w_in"])[0], fox_b_f=f(I["fox_b_f"]), fox_w_out=f(I["fox_w_out"])[0])


def run_cores(I, cores, build_kw):
    T = I["x_prompt"].shape[1]
    PAST = I["cache_fox_k"].shape[2]
    kw = dict(build_kw)
    kw.update(Tn=T, PASTn=PAST)
    nc = _get_nc(**kw)
    shared = _shared(I)
    in_maps = [_in_map(c, I, shared, T, PAST) for c in cores]
    res = run_bass_kernel_spmd(nc, in_maps, core_ids=list(range(len(cores)))).results
    return {c: res[i] for i, c in enumerate(cores)}


def kernel(x_prompt, x_sample, c_prompt, c_sample, state_lru_h, state_lru_conv, cache_fox_k, cache_fox_v, cache_fox_logf,
           norm_pre, norm_post, ada_w, ada_b, lru_w_in, lru_conv_w, lru_conv_b, lru_w_a, lru_b_a, lru_w_x, lru_b_x,
           lru_lambda, lru_w_out, fox_w_in, fox_b_f, fox_w_out, _build_kw=None):
    I = dict(x_prompt=x_prompt, x_sample=x_sample, c_prompt=c_prompt, c_sample=c_sample, state_lru_h=state_lru_h, state_lru_conv=state_lru_conv,
             cache_fox_k=cache_fox_k, cache_fox_v=cache_fox_v, cache_fox_logf=cache_fox_logf, norm_pre=norm_pre, norm_post=norm_post,
             ada_w=ada_w, ada_b=ada_b, lru_w_in=lru_w_in, lru_conv_w=lru_conv_w, lru_conv_b=lru_conv_b, lru_w_a=lru_w_a, lru_b_a=lru_b_a,
             lru_w_x=lru_w_x, lru_b_x=lru_b_x, lru_lambda=lru_lambda, lru_w_out=lru_w_out, fox_w_in=fox_w_in, fox_b_f=fox_b_f, fox_w_out=fox_w_out)
    I = {k: np.asarray(v) for k, v in I.items()}
    res = run_cores(I, list(range(8)), _build_kw if _build_kw is not None else {})
    return assemble(res, I)


def assemble(res, I):
    B, T = I["x_prompt"].shape[0], I["x_prompt"].shape[1]
    NQ = T // 2048
    cores = sorted(res.keys())
    y_p = np.zeros((B, T, D), np.float32)
    for c in cores:
        b, r = c // 4, c % 4
        yp = res[c]["y_p"].reshape(NQ, TT, D)
        for m_ in range(NQ):
            i = 4 * m_ + r
            y_p[b, i * TT:(i + 1) * TT] = yp[m_]
    nb = len(cores) // 4 if len(cores) >= 4 else 1
    bs = sorted(set(c // 4 for c in cores))
    first = {b: min(c for c in cores if c // 4 == b) for b in bs}
    y_s = np.concatenate([res[c]["y_s"].reshape(2, ST, D) for c in cores], axis=0)
    lru_h_p = np.stack([res[first[b]]["lru_h_p"] for b in bs])[None]
    lru_c_p = np.stack([res[first[b]]["lru_conv_p"] for b in bs])[None]
    fk_p = np.stack([res[first[b]]["fk_p"].reshape(T, H, DH) for b in bs])[None]
    fv_p = np.stack([res[first[b]]["fv_p"].reshape(T, H, DH) for b in bs])[None]
    fl_p = np.stack([res[first[b]]["flf_p"] for b in bs])[None]
    lru_h_s = np.concatenate([res[c]["lru_h_s"] for c in cores], axis=0)[None]
    lru_c_s = np.concatenate([res[c]["lru_conv_s"] for c in cores], axis=0)[None]
    fk_s = np.concatenate([res[c]["fk_s"].reshape(2, ST, H, DH) for c in cores], axis=0)[None]
    fv_s = np.concatenate([res[c]["fv_s"].reshape(2, ST, H, DH) for c in cores], axis=0)[None]
    fl_s = np.concatenate([res[c]["flf_s"].reshape(2, ST, H) for c in cores], axis=0)[None]
    return (y_p, y_s, lru_h_p, lru_c_p, fk_p, fv_p, fl_p, lru_h_s, lru_c_s, fk_s, fv_s, fl_s)
```
